# Optimizing a Trainium2 kernel written in Bass

```python
import jax, jax.numpy as jnp
from jax import lax
import numpy as np

D_MODEL = 1024
BATCH = 8
SEQ = 2048
DEPTH = 2

MIX_W = D_MODEL
N_BRANCH = 3
CONV_WIDTH = 4
DN_HEADS = 8
DN_HEAD_DIM = MIX_W // DN_HEADS
DN_CHUNK = 64
SB_HEADS = 16
SB_HEAD_DIM = MIX_W // SB_HEADS
SB_BLOCK = 128
SSM_HEADS = 16
SSM_HEAD_DIM = MIX_W // SSM_HEADS
SSM_STATE = 128
SSM_GROUPS = 4
SSM_CHUNK = 64
D_FF = 4 * D_MODEL
EPS = 1e-6

DN_QKV = 3 * MIX_W
SSM_CONV_DIM = MIX_W + 2 * SSM_GROUPS * SSM_STATE
IN_SIZES = (DN_QKV, MIX_W, DN_HEADS, DN_HEADS, 3 * MIX_W, MIX_W, SSM_CONV_DIM, SSM_HEADS, N_BRANCH * D_MODEL)
IN_DIM = sum(IN_SIZES)

kernel_name = "hybrid_gdn_stickbreak_mamba2_block"


def rms_norm(x, w):
    xf = x.astype(jnp.float32)
    xf = xf * lax.rsqrt(jnp.mean(xf * xf, axis=-1, keepdims=True) + EPS)
    return (xf * w.astype(jnp.float32)).astype(x.dtype)


def group_rms_norm(x, w, groups):
    shp = x.shape
    xf = x.astype(jnp.float32).reshape(*shp[:-1], groups, shp[-1] // groups)
    xf = xf * lax.rsqrt(jnp.mean(xf * xf, axis=-1, keepdims=True) + EPS)
    return (xf.reshape(shp) * w.astype(jnp.float32)).astype(x.dtype)


def l2_normalize(x):
    return x * lax.rsqrt(jnp.sum(x * x, axis=-1, keepdims=True) + EPS)


def split_columns(t, sizes):
    out, start = [], 0
    for s in sizes:
        out.append(t[..., start:start + s])
        start += s
    return out


def causal_dwconv(x, w, b=None):
    k_width, seq = w.shape[0], x.shape[1]
    xp = jnp.pad(x, ((0, 0), (k_width - 1, 0), (0, 0)))
    y = xp[:, 0:seq] * w[0]
    for k in range(1, k_width):
        y = y + xp[:, k:k + seq] * w[k]
    return y if b is None else y + b


def gated_delta_rule(q, k, v, g, beta):
    dtype = v.dtype
    q, k, v, g, beta = (t.astype(jnp.float32) for t in (q, k, v, g, beta))
    bsz, seq, h, dk = q.shape
    dv = v.shape[-1]
    c = DN_CHUNK
    n = seq // c

    def to_chunks(t):
        t = t.reshape(bsz, n, c, *t.shape[2:])
        return jnp.swapaxes(t, 2, 3)

    q = to_chunks(q) * dk ** -0.5
    k, v = to_chunks(k), to_chunks(v)
    g, beta = to_chunks(g), to_chunks(beta)
    gc = jnp.cumsum(g, axis=-1)
    causal = jnp.tril(jnp.ones((c, c), dtype=bool))
    strict = jnp.tril(jnp.ones((c, c), dtype=bool), -1)
    decay = jnp.exp(jnp.where(causal, gc[..., :, None] - gc[..., None, :], -jnp.inf))

    kb = k * beta[..., None]
    vb = v * beta[..., None]
    lower = jnp.where(strict, jnp.einsum('bnhid,bnhjd->bnhij', kb, k) * decay, 0.0)
    tmat = lower + jnp.eye(c, dtype=jnp.float32)
    rhs = jnp.concatenate([vb, kb * jnp.exp(gc)[..., None]], axis=-1)
    sol = lax.linalg.triangular_solve(tmat, rhs, left_side=True, lower=True, unit_diagonal=True)
    u, w = sol[..., :dv], sol[..., dv:]

    attn = jnp.einsum('bnhid,bnhjd->bnhij', q, k) * decay
    qg = q * jnp.exp(gc)[..., None]
    kd = k * jnp.exp(gc[..., -1:] - gc)[..., None]
    glast = jnp.exp(gc[..., -1])

    def step(state, xs):
        u_c, w_c, attn_c, qg_c, kd_c, gl_c = xs
        v_new = u_c - jnp.einsum('bhcd,bhde->bhce', w_c, state)
        o_c = jnp.einsum('bhcd,bhde->bhce', qg_c, state) + jnp.einsum('bhij,bhje->bhie', attn_c, v_new)
        state = state * gl_c[..., None, None] + jnp.einsum('bhcd,bhce->bhde', kd_c, v_new)
        return state, o_c

    xs = tuple(jnp.moveaxis(t, 1, 0) for t in (u, w, attn, qg, kd, glast))
    s0 = jnp.zeros((bsz, h, dk, dv), jnp.float32)
    _, o = lax.scan(step, s0, xs)
    o = jnp.transpose(o, (1, 0, 3, 2, 4)).reshape(bsz, seq, h, dv)
    return o.astype(dtype)


def gated_deltanet_branch(qkv, gate, a, b, conv_w, a_log, dt_bias, norm_w):
    bsz, seq, _ = qkv.shape
    qkv = jax.nn.silu(causal_dwconv(qkv, conv_w))
    q, k, v = (t.reshape(bsz, seq, DN_HEADS, DN_HEAD_DIM) for t in split_columns(qkv, (MIX_W, MIX_W, MIX_W)))
    q, k = l2_normalize(q), l2_normalize(k)
    beta = jax.nn.sigmoid(b.astype(jnp.float32))
    g = -jnp.exp(a_log.astype(jnp.float32)) * jax.nn.softplus(a.astype(jnp.float32) + dt_bias.astype(jnp.float32))
    o = gated_delta_rule(q, k, v, g, beta)
    o = rms_norm(o, norm_w) * jax.nn.silu(gate.reshape(bsz, seq, DN_HEADS, DN_HEAD_DIM))
    return o.reshape(bsz, seq, MIX_W)


def stick_breaking_branch(qkv):
    bsz, seq, _ = qkv.shape
    dtype = qkv.dtype
    q, k, v = (t.reshape(bsz, seq, SB_HEADS, SB_HEAD_DIM).astype(jnp.float32)
               for t in split_columns(qkv, (MIX_W, MIX_W, MIX_W)))
    scale = SB_HEAD_DIM ** -0.5
    outs = []
    for i in range(seq // SB_BLOCK):
        t0, t1 = i * SB_BLOCK, (i + 1) * SB_BLOCK
        qb, kb, vb = q[:, t0:t1], k[:, :t1], v[:, :t1]
        z = jnp.einsum('bthd,bshd->bhts', qb, kb) * scale
        t_idx = t0 + jnp.arange(SB_BLOCK)
        s_idx = jnp.arange(t1)
        mask = s_idx[None, :] < t_idx[:, None]
        log_keep = jnp.where(mask, -jax.nn.softplus(z), 0.0)
        reach = lax.cumsum(log_keep, axis=3, reverse=True) - log_keep
        log_a = jax.nn.log_sigmoid(z) + reach
        weights = jnp.exp(jnp.where(mask, log_a, -jnp.inf))
        outs.append(jnp.einsum('bhts,bshd->bthd', weights, vb))
    o = jnp.concatenate(outs, axis=1)
    return o.reshape(bsz, seq, MIX_W).astype(dtype)


def ssd_chunked(x, a, bm, cm):
    dtype = x.dtype
    x, a, bm, cm = (t.astype(jnp.float32) for t in (x, a, bm, cm))
    bsz, seq, h, p = x.shape
    g, n_state = bm.shape[2], bm.shape[3]
    r = h // g
    c = SSM_CHUNK
    nc = seq // c
    x = x.reshape(bsz, nc, c, g, r, p)
    a = a.reshape(bsz, nc, c, g, r)
    bm = bm.reshape(bsz, nc, c, g, n_state)
    cm = cm.reshape(bsz, nc, c, g, n_state)
    a_cum = jnp.cumsum(a, axis=2)
    causal = jnp.tril(jnp.ones((c, c), dtype=bool))
    seg = a_cum[:, :, :, None] - a_cum[:, :, None, :]
    lmat = jnp.exp(jnp.where(causal[:, :, None, None], seg, -jnp.inf))
    scores = jnp.einsum('bclgn,bcsgn->bclsg', cm, bm)
    y_diag = jnp.einsum('bclsg,bclsgr,bcsgrp->bclgrp', scores, lmat, x)
    decay_states = jnp.exp(a_cum[:, :, -1:] - a_cum)
    chunk_states = jnp.einsum('bclgn,bclgr,bclgrp->bcgrpn', bm, decay_states, x)
    chunk_decay = jnp.exp(a_cum[:, :, -1])

    def step(state, xs):
        st, dec = xs
        return state * dec[..., None, None] + st, state

    h0 = jnp.zeros((bsz, g, r, p, n_state), jnp.float32)
    _, h_prev = lax.scan(step, h0, (jnp.moveaxis(chunk_states, 1, 0), jnp.moveaxis(chunk_decay, 1, 0)))
    h_prev = jnp.moveaxis(h_prev, 0, 1)
    y_off = jnp.einsum('bclgn,bcgrpn,bclgr->bclgrp', cm, h_prev, jnp.exp(a_cum))
    return (y_diag + y_off).reshape(bsz, seq, h, p).astype(dtype)


def mamba2_branch(z, xbc, dt, conv_w, conv_b, a_log, dt_bias, d_skip, norm_w):
    bsz, seq, _ = z.shape
    gn = SSM_GROUPS * SSM_STATE
    xbc = jax.nn.silu(causal_dwconv(xbc, conv_w, conv_b))
    xs, bm, cm = split_columns(xbc, (MIX_W, gn, gn))
    xs = xs.reshape(bsz, seq, SSM_HEADS, SSM_HEAD_DIM)
    bm = bm.reshape(bsz, seq, SSM_GROUPS, SSM_STATE)
    cm = cm.reshape(bsz, seq, SSM_GROUPS, SSM_STATE)
    dt = jax.nn.softplus(dt.astype(jnp.float32) + dt_bias.astype(jnp.float32))
    a = -jnp.exp(a_log.astype(jnp.float32)) * dt
    y = ssd_chunked(xs * dt[..., None].astype(xs.dtype), a, bm, cm)
    y = y + xs * d_skip[:, None]
    y = y.reshape(bsz, seq, MIX_W) * jax.nn.silu(z)
    return group_rms_norm(y, norm_w, SSM_GROUPS)


def hybrid_mixer(xn, w_in, dn_conv_w, dn_a_log, dn_dt_bias, dn_norm_w,
                 ssm_conv_w, ssm_conv_b, ssm_a_log, ssm_dt_bias, ssm_d, ssm_norm_w,
                 w_branch, w_out):
    bsz, seq, _ = xn.shape
    proj = xn @ w_in
    (dn_qkv, dn_gate, dn_a, dn_b, sb_qkv, ssm_z, ssm_xbc, ssm_dt, gate_logits) = split_columns(proj, IN_SIZES)
    o_dn = gated_deltanet_branch(dn_qkv, dn_gate, dn_a, dn_b, dn_conv_w, dn_a_log, dn_dt_bias, dn_norm_w)
    o_sb = stick_breaking_branch(sb_qkv)
    o_ssm = mamba2_branch(ssm_z, ssm_xbc, ssm_dt, ssm_conv_w, ssm_conv_b, ssm_a_log, ssm_dt_bias, ssm_d, ssm_norm_w)
    branches = jnp.stack([o_dn, o_sb, o_ssm], axis=2)
    projected = jnp.einsum('bsim,imd->bsid', branches, w_branch)
    gates = jax.nn.sigmoid(gate_logits.reshape(bsz, seq, N_BRANCH, D_MODEL))
    merged = jnp.sum(gates * projected, axis=2)
    return merged @ w_out


def setup_inputs(seed: int = 0) -> dict:
    key = jax.random.key(seed)
    ks = jax.random.split(key, 20)
    L = DEPTH

    def nrm(k, shape, scale):
        return jax.random.normal(k, shape, jnp.float32) * scale

    def log_uniform_a(k, shape):
        return jnp.log(jax.random.uniform(k, shape, jnp.float32, minval=1.0, maxval=16.0))

    def dt_bias_init(k, shape):
        dt = jnp.exp(jax.random.uniform(k, shape, jnp.float32, minval=np.log(1e-3), maxval=np.log(1e-1)))
        return dt + jnp.log(-jnp.expm1(-dt))

    return {
        "x": nrm(ks[0], (BATCH, SEQ, D_MODEL), 1.0),
        "norm_mix": 1.0 + nrm(ks[1], (L, D_MODEL), 0.02),
        "w_in": nrm(ks[2], (L, D_MODEL, IN_DIM), D_MODEL ** -0.5),
        "dn_conv_w": nrm(ks[3], (L, CONV_WIDTH, DN_QKV), CONV_WIDTH ** -0.5),
        "dn_a_log": log_uniform_a(ks[4], (L, DN_HEADS)),
        "dn_dt_bias": dt_bias_init(ks[5], (L, DN_HEADS)),
        "dn_norm_w": 1.0 + nrm(ks[6], (L, DN_HEAD_DIM), 0.02),
        "ssm_conv_w": nrm(ks[7], (L, CONV_WIDTH, SSM_CONV_DIM), CONV_WIDTH ** -0.5),
        "ssm_conv_b": nrm(ks[8], (L, SSM_CONV_DIM), 0.02),
        "ssm_a_log": log_uniform_a(ks[9], (L, SSM_HEADS)),
        "ssm_dt_bias": dt_bias_init(ks[10], (L, SSM_HEADS)),
        "ssm_d": 1.0 + nrm(ks[11], (L, SSM_HEADS), 0.02),
        "ssm_norm_w": 1.0 + nrm(ks[12], (L, MIX_W), 0.02),
        "w_branch": nrm(ks[13], (L, N_BRANCH, MIX_W, D_MODEL), MIX_W ** -0.5),
        "w_out": nrm(ks[14], (L, D_MODEL, D_MODEL), D_MODEL ** -0.5),
        "norm_mlp": 1.0 + nrm(ks[15], (L, D_MODEL), 0.02),
        "w_up": nrm(ks[16], (L, D_MODEL, D_FF), D_MODEL ** -0.5),
        "w_down": nrm(ks[17], (L, D_FF, D_MODEL), D_FF ** -0.5),
        "norm_final": 1.0 + nrm(ks[18], (D_MODEL,), 0.02),
    }


def reference(x, norm_mix, w_in, dn_conv_w, dn_a_log, dn_dt_bias, dn_norm_w,
              ssm_conv_w, ssm_conv_b, ssm_a_log, ssm_dt_bias, ssm_d, ssm_norm_w,
              w_branch, w_out, norm_mlp, w_up, w_down, norm_final):
    for l in range(DEPTH):
        h = rms_norm(x, norm_mix[l])
        x = x + hybrid_mixer(h, w_in[l], dn_conv_w[l], dn_a_log[l], dn_dt_bias[l], dn_norm_w[l],
                             ssm_conv_w[l], ssm_conv_b[l], ssm_a_log[l], ssm_dt_bias[l], ssm_d[l],
                             ssm_norm_w[l], w_branch[l], w_out[l])
        h = rms_norm(x, norm_mlp[l])
        x = x + jnp.square(jax.nn.relu(h @ w_up[l])) @ w_down[l]
    return rms_norm(x, norm_final)
```

```python
import numpy as np
import concourse.bass as bass
import concourse.mybir as mybir
from concourse.bass_utils import run_bass_kernel_spmd

F32 = mybir.dt.float32
BF16 = mybir.dt.bfloat16
AF = mybir.ActivationFunctionType
ALU = mybir.AluOpType

S_LEN = 2048
D = 1024
KC = 8
NCORES = 8
DEPTH = 2
D_FF = 4096
EPS = 1e-6
IN_SIZES = (3072, 1024, 8, 8, 3072, 1024, 2048, 16, 3072)
IN_DIM = sum(IN_SIZES)
OFF = [0]
for _s in IN_SIZES:
    OFF.append(OFF[-1] + _s)
(O_DNQKV, O_DNGATE, O_DNA, O_DNB, O_SBQKV, O_SSMZ, O_SSMXBC, O_SSMDT, O_GATE, _) = OFF


class _Op:
    __slots__ = ("id", "eng", "fn", "dma", "waits", "idx", "sig", "reuse", "seen")


class Sched:
    ENG = ("pe", "act", "dve", "pool", "sp")
    NSLOT = 12
    SEM_LIMIT = 20000

    def __init__(self, nc):
        self.nc = nc
        self.ops = []
        self.by_eng = {e: [] for e in self.ENG}
        self.kstate = {}
        self.seen = {e: {p: -1 for p in self.ENG} for e in self.ENG}
        self.seen_dma = {e: set() for e in self.ENG}
        self.pending = {e: set() for e in self.ENG}
        self.open_dma = []

    def op(self, eng, fn, reads=(), writes=(), dma=False):
        o = _Op()
        o.id = len(self.ops)
        o.eng = eng
        o.fn = fn
        o.dma = dma
        o.sig = None
        o.reuse = None
        deps = {}
        for k in reads:
            st = self.kstate.get(k)
            if st is not None and st[0] is not None:
                deps[st[0]] = True
        for k in writes:
            st = self.kstate.get(k)
            if st is not None:
                if st[0] is not None:
                    deps.setdefault(st[0], False)
                for r in st[1].values():
                    deps.setdefault(r, False)
                for r in st[2]:
                    deps.setdefault(r, False)
        for d in self.pending[eng]:
            deps[d] = True
        self.pending[eng] = set()
        o.idx = len(self.by_eng[eng])
        seen = self.seen[eng]
        best = {}
        waits = []
        for d, raw in deps.items():
            p = self.ops[d]
            if p.dma:
                if d in self.seen_dma[eng]:
                    continue
                self.seen_dma[eng].add(d)
                waits.append(d)
            else:
                if p.eng == eng and not dma:
                    if eng == "pe" or not raw:
                        continue
                if p.idx <= seen[p.eng]:
                    continue
                if p.eng not in best or self.ops[best[p.eng]].idx < p.idx:
                    best[p.eng] = d
        for pe, d in best.items():
            p = self.ops[d]
            waits.append(d)
            for e2, v in p.seen.items():
                if v > seen[e2]:
                    seen[e2] = v
            if p.idx > seen[pe]:
                seen[pe] = p.idx
        o.waits = waits
        o.seen = dict(seen)
        for k in reads:
            st = self.kstate.get(k)
            if st is None:
                st = [None, {}, []]
                self.kstate[k] = st
            if dma:
                st[2].append(o.id)
            else:
                st[1][eng] = o.id
        for k in writes:
            self.kstate[k] = [o.id, {}, []]
        self.ops.append(o)
        self.by_eng[eng].append(o)
        if dma:
            self.open_dma.append(o.id)
        return o

    def barrier(self):
        last = [self.by_eng[e][-1].id for e in self.ENG if self.by_eng[e] and not self.by_eng[e][-1].dma]
        for e in self.ENG:
            self.pending[e] |= set(last) | set(self.open_dma[-self.NSLOT:])
        self.open_dma = self.open_dma[-self.NSLOT:]

    def finalize(self):
        nc = self.nc
        needed = set()
        for o in self.ops:
            needed.update(o.waits)
        for e in self.ENG:
            sem = None
            cnt = 0
            ndma = 0
            slots = None
            for o in self.by_eng[e]:
                if o.dma:
                    if slots is None:
                        slots = [nc.alloc_semaphore(f"dq_{e}_{i}") for i in range(self.NSLOT)]
                    s = ndma % self.NSLOT
                    r = ndma // self.NSLOT
                    o.sig = (slots[s], 16 * (r + 1))
                    if r > 0:
                        o.reuse = (slots[s], 16 * r)
                    ndma += 1
                elif o.id in needed:
                    if sem is None or cnt >= self.SEM_LIMIT:
                        sem = nc.alloc_semaphore(f"pg_{e}_{o.id}")
                        cnt = 0
                    cnt += 1
                    o.sig = (sem, cnt)

    def emit(self, ename, eng):
        ops = self.ops
        for o in self.by_eng[ename]:
            for d in o.waits:
                s, v = ops[d].sig
                eng.wait_ge(s, v)
            if o.reuse is not None:
                eng.wait_ge(o.reuse[0], o.reuse[1])
            if o.fn is None:
                continue
            ins = o.fn(eng)
            if o.sig is not None:
                ins.then_inc(o.sig[0], 16 if o.dma else 1)


class Arena:
    def __init__(self, nc, limit=208 * 1024):
        self.nc = nc
        self.off = 16 * 1024
        self.limit = limit
        self.n = 0
        self.peak = 0

    def alloc(self, name, shape, dtype):
        per = 1
        for s in shape[1:]:
            per *= s
        nbytes = per * (4 if dtype == F32 else 2)
        nbytes = (nbytes + 63) // 64 * 64
        assert self.off + nbytes <= self.limit, f"SBUF arena overflow at {name}: {self.off}+{nbytes}"
        self.n += 1
        t = self.nc.alloc_sbuf_tensor_at(f"{name}_{self.n}", list(shape), dtype, offset=self.off)
        self.off += nbytes
        self.peak = max(self.peak, self.off)
        return t

    def mark(self):
        return self.off

    def reset(self, m):
        self.off = m


class Ring:
    def __init__(self, tiles, name):
        self.tiles = tiles
        self.name = name
        self.i = 0

    def next(self):
        j = self.i % len(self.tiles)
        self.i += 1
        return self.tiles[j], (self.name, j)


class Builder:
    def __init__(self, nlayers=DEPTH, mixers=("dn", "sb", "ssm"), do_mlp=True, debug=()):
        self.nlayers = nlayers
        self.mixers = mixers
        self.do_mlp = do_mlp
        self.debug = debug
        nc = bass.Bass("TRN2", target_bir_lowering=False)
        self.nc = nc
        self.S = Sched(nc)
        self.A = Arena(nc)
        self.uid = 0

    def dram_in(self, name, shape, dtype=F32):
        return self.nc.dram_tensor(name, list(shape), dtype, kind="ExternalInput").ap()

    def dram_out(self, name, shape, dtype=F32):
        return self.nc.dram_tensor(name, list(shape), dtype, kind="ExternalOutput").ap()

    def dram_tmp(self, name, shape, dtype):
        return self.nc.dram_tensor(name, list(shape), dtype, kind="Internal").ap()

    def key(self, base):
        self.uid += 1
        return (base, self.uid)

    def ring(self, name, n, shape, dtype):
        return Ring([self.A.alloc(name, shape, dtype) for _ in range(n)], self.key(name))

    def dma(self, out, in_, reads, writes, slow=False):
        if slow:
            fn = lambda e, out=out, in_=in_: e.dma_start(out=out, in_=in_, allow_slow_non_contiguous=True)
        else:
            fn = lambda e, out=out, in_=in_: e.dma_start(out=out, in_=in_)
        return self.S.op("sp", fn, reads=reads, writes=writes, dma=True)

    def mm(self, out, lhsT, rhs, start, stop, reads, writes, **kw):
        return self.S.op(
            "pe",
            lambda e, out=out, lhsT=lhsT, rhs=rhs, start=start, stop=stop, kw=kw: e.matmul(
                out, lhsT=lhsT, rhs=rhs, start=start, stop=stop, **kw
            ),
            reads=reads,
            writes=writes,
        )

    def tr(self, out, in_, ident, reads, writes):
        return self.S.op(
            "pe",
            lambda e, out=out, in_=in_, ident=ident: e.transpose(out, in_, ident),
            reads=reads,
            writes=writes,
        )

    def act(self, out, in_, func, reads, writes, eng="act", **kw):
        return self.S.op(
            eng,
            lambda e, out=out, in_=in_, func=func, kw=kw: e.activation(out=out, in_=in_, func=func, **kw),
            reads=reads,
            writes=writes,
        )

    def tt(self, out, in0, in1, op, reads, writes, eng="dve"):
        return self.S.op(
            eng,
            lambda e, out=out, in0=in0, in1=in1, op=op: e.tensor_tensor(out=out, in0=in0, in1=in1, op=op),
            reads=reads,
            writes=writes,
        )

    def ts(self, out, in0, s1, op0, reads, writes, s2=None, op1=None, eng="dve"):
        def fn(e, out=out, in0=in0, s1=s1, op0=op0, s2=s2, op1=op1):
            if op1 is None:
                return e.tensor_scalar(out=out, in0=in0, scalar1=s1, scalar2=None, op0=op0)
            return e.tensor_scalar(out=out, in0=in0, scalar1=s1, scalar2=s2, op0=op0, op1=op1)

        return self.S.op(eng, fn, reads=reads, writes=writes)

    def stt(self, out, in0, scalar, in1, op0, op1, reads, writes):
        return self.S.op(
            "dve",
            lambda e, out=out, in0=in0, scalar=scalar, in1=in1, op0=op0, op1=op1: e.scalar_tensor_tensor(
                out=out, in0=in0, scalar=scalar, in1=in1, op0=op0, op1=op1
            ),
            reads=reads,
            writes=writes,
        )

    def copy(self, out, in_, reads, writes, eng="dve"):
        if eng == "act":
            return self.act(out, in_, AF.Copy, reads, writes)
        return self.S.op(
            eng, lambda e, out=out, in_=in_: e.tensor_copy(out=out, in_=in_), reads=reads, writes=writes
        )

    def memset(self, ap, val, writes, eng="dve"):
        return self.S.op(eng, lambda e, ap=ap, val=val: e.memset(ap, val), reads=(), writes=writes)

    def build(self):
        nc, S, A = self.nc, self.S, self.A
        L = self.nlayers
        self.x_d = self.dram_in("x", [S_LEN, D])
        self.consts_d = self.dram_in("consts", [128, NCONST * 128])
        self.w_in_d = self.dram_in("w_in", [DEPTH, D, IN_DIM])
        self.norm_mix_d = self.dram_in("norm_mix", [DEPTH, D])
        self.norm_mlp_d = self.dram_in("norm_mlp", [DEPTH, D])
        self.norm_final_d = self.dram_in("norm_final", [1, D])
        self.w_branch_d = self.dram_in("w_branch", [DEPTH, 3, D, D])
        self.w_out_d = self.dram_in("w_out", [DEPTH, D, D])
        self.w_up_d = self.dram_in("w_up", [DEPTH, D, D_FF])
        self.w_down_d = self.dram_in("w_down", [DEPTH, D_FF, D])
        self.ssm_conv_w_d = self.dram_in("ssm_conv_w", [DEPTH, 4, 2048])
        self.ssm_conv_b_d = self.dram_in("ssm_conv_b", [DEPTH, 2048])
        self.ssm_a_log_d = self.dram_in("ssm_a_log", [DEPTH, 16])
        self.ssm_dt_bias_d = self.dram_in("ssm_dt_bias", [DEPTH, 16])
        self.ssm_d_d = self.dram_in("ssm_d", [DEPTH, 16])
        self.ssm_norm_w_d = self.dram_in("ssm_norm_w", [DEPTH, D])
        self.dn_conv_w_d = self.dram_in("dn_conv_w", [DEPTH, 4, 3072])
        self.dn_a_log_d = self.dram_in("dn_a_log", [DEPTH, 8])
        self.dn_dt_bias_d = self.dram_in("dn_dt_bias", [DEPTH, 8])
        self.dn_norm_w_d = self.dram_in("dn_norm_w", [DEPTH, 128])
        self.out_d = self.dram_out("out", [S_LEN, D])
        self.u_scr = self.dram_tmp("u_scr", [D_FF, S_LEN], BF16)
        if self.debug:
            self.o_scr = self.dram_out("o_scr", [3, D, S_LEN], BF16)
        else:
            self.o_scr = self.dram_tmp("o_scr", [3, D, S_LEN], BF16)
        self.g_scr = self.dram_tmp("g_scr", [3, D, S_LEN], BF16)

        pst = [nc.alloc_psum_tensor(f"ps{i}", [128, 512], F32) for i in range(8)]
        self.psum = Ring(pst[:6], "psum")
        self.psum_acc = Ring(pst[6:], "psacc")

        self.consts = A.alloc("consts", [128, NCONST, 128], F32)
        self.cbf = A.alloc("cbf", [128, NCONST, 128], BF16)
        self.xT = A.alloc("xT", [128, KC, S_LEN], F32)
        self.nw = A.alloc("nw", [128, 2 * DEPTH + 1, KC], F32)
        KCON = "consts"
        self.dma(self.consts[:].rearrange("p a b -> p (a b)"), self.consts_d, reads=[], writes=[KCON])
        self.copy(self.cbf[:], self.consts[:], reads=[KCON], writes=["cbf"])
        for l in range(DEPTH):
            self.dma(self.nw[:, 2 * l, :], self.norm_mix_d[l].rearrange("(k p) -> p k", p=128), [], ["nw"], slow=True)
            self.dma(self.nw[:, 2 * l + 1, :], self.norm_mlp_d[l].rearrange("(k p) -> p k", p=128), [], ["nw"], slow=True)
        self.dma(self.nw[:, 2 * DEPTH, :], self.norm_final_d[0].rearrange("(k p) -> p k", p=128), [], ["nw"], slow=True)

        self.load_x()
        for l in range(L):
            self.layer(l)
        self.final_out()
        S.barrier()
        S.op("sp", None)
        S.finalize()

        with nc.Block() as block:

            @block.tensor
            def _(e):
                S.emit("pe", e)

            @block.scalar
            def _(e):
                S.emit("act", e)

            @block.vector
            def _(e):
                S.emit("dve", e)

            @block.gpsimd
            def _(e):
                S.emit("pool", e)

            @block.sync
            def _(e):
                S.emit("sp", e)

        return nc

    def ident_f(self):
        return self.consts[:, C_IDENT, :]

    def ident_b(self):
        return self.cbf[:, C_IDENT, :]

    def ones_b(self):
        return self.cbf[:, C_ONES, :]

    def load_x(self):
        A, S = self.A, self.S
        m = A.mark()
        stg = self.ring("xstg", 2, [128, 4, D], F32)
        for tg in range(4):
            st, sk = stg.next()
            self.dma(
                st[:],
                self.x_d[tg * 512:(tg + 1) * 512, :].rearrange("(t p) d -> p t d", p=128),
                reads=[],
                writes=[sk],
            )
            for kc in range(KC):
                ps, pk = self.psum.next()
                for t in range(4):
                    self.tr(ps[:, t * 128:(t + 1) * 128], st[:, t, kc * 128:(kc + 1) * 128], self.ident_f(),
                            reads=[sk, "consts"], writes=[pk])
                self.copy(self.xT[:, kc, tg * 512:(tg + 1) * 512], ps[:], reads=[pk], writes=[("xT", kc, tg)],
                          eng=("act" if kc % 2 else "dve"))
        S.barrier()
        A.reset(m)

    def rmsnorm(self):
        A, S = self.A, self.S
        m = A.mark()
        sq = self.ring("sq", 3, [128, 512], BF16)
        rs = self.ring("rs", 2, [128, 512], F32)
        for g in range(4):
            ps, pk = self.psum.next()
            sl = slice(g * 512, (g + 1) * 512)
            for kc in range(KC):
                q, qk = sq.next()
                self.act(q[:], self.xT[:, kc, sl], AF.Square, reads=[("xT", kc, g)], writes=[qk])
                self.mm(ps[:], self.ones_b(), q[:], kc == 0, kc == KC - 1, reads=[qk, "cbf"], writes=[pk])
            r, rk = rs.next()
            self.act(r[:], ps[:], AF.Ln, reads=[pk], writes=[rk], scale=1.0 / D, bias=self.eps_ap())
            self.act(r[:], r[:], AF.Exp, reads=[rk], writes=[rk], scale=-0.5)
            for kc in range(KC):
                self.tt(self.hT[:, kc, sl], self.xT[:, kc, sl], r[:], ALU.mult,
                        reads=[("xT", kc, g), rk], writes=[("hT", kc, g)])
        self.rstd_ring = rs
        S.barrier()
        A.reset(m)

    def eps_ap(self):
        return self.consts[:, C_EPS, 0:1]

    def wload(self, w_ap, kcn, n, wst_ring, wbf_ring, scale=None):
        st, sk = wst_ring.next()
        wb, wk = wbf_ring.next()
        self.dma(st[:, :kcn, :n], w_ap.rearrange("(k p) n -> p k n", p=128), reads=[], writes=[sk])
        for kc in range(kcn):
            if scale is not None:
                self.act(wb[:, kc, :n], st[:, kc, :n], AF.Copy, reads=[sk, "nw"], writes=[wk],
                         scale=scale[:, kc:kc + 1])
            else:
                self.act(wb[:, kc, :n], st[:, kc, :n], AF.Copy, reads=[sk], writes=[wk])
        return wb, wk

    def hT_keys(self, g):
        return [("hT", kc, g) for kc in range(KC)]

    def linear_T(self, wb, wk, ncols, kcn, rhs_fn, rhs_keys_fn, evac, groups=range(4)):
        for c0 in range(0, ncols, 128):
            n = min(128, ncols - c0)
            for g in groups:
                ps, pk = self.psum.next()
                for kc in range(kcn):
                    self.mm(ps[:n, :], wb[:, kc, c0:c0 + n], rhs_fn(kc, g), kc == 0, kc == kcn - 1,
                            reads=[wk] + rhs_keys_fn(g), writes=[pk])
                evac(c0 // 128, g, ps, pk, n)

    def mlp(self, l):
        A, S = self.A, self.S
        m = A.mark()
        self.hT = A.alloc("hT", [128, KC, S_LEN], BF16)
        self.rmsnorm()
        wst = self.ring("wst", 2, [128, KC, 512], F32)
        wbf = self.ring("wbf", 2, [128, KC, 512], BF16)
        ub = self.ring("ub", 3, [128, S_LEN], BF16)
        rr = self.ring("rr", 3, [128, 512], F32)
        scale = self.nw[:, 2 * l + 1, :]
        for cb in range(D_FF // 512):
            wb, wk = self.wload(self.w_up_d[l][:, cb * 512:(cb + 1) * 512], KC, 512, wst, wbf, scale=scale)
            cur = {}

            def evac(ci, g, ps, pk, n, cb=cb, cur=cur):
                if g == 0:
                    cur["u"] = ub.next()
                u, uk = cur["u"]
                r, rk = rr.next()
                self.ts(r[:], ps[:], 0.0, ALU.max, reads=[pk], writes=[rk])
                self.act(u[:, g * 512:(g + 1) * 512], r[:], AF.Square, reads=[rk], writes=[uk])
                if g == 3:
                    f = cb * 4 + ci
                    self.dma(self.u_scr[f * 128:(f + 1) * 128, :], u[:], reads=[uk], writes=[("u_scr", f)])

            self.linear_T(wb, wk, 512, KC, lambda kc, g: self.hT[:, kc, g * 512:(g + 1) * 512],
                          self.hT_keys, evac)
        S.barrier()
        A.reset(m)
        wst = self.ring("wdst", 2, [128, 4, 512], F32)
        wbf = self.ring("wdbf", 1, [128, 32, 512], BF16)
        ur = self.ring("ur", 1, [128, 32, 512], BF16)
        for half in range(2):
            wb, wk = wbf.next()
            for q in range(8):
                st, sk = wst.next()
                self.dma(st[:], self.w_down_d[l][q * 512:(q + 1) * 512, half * 512:(half + 1) * 512]
                         .rearrange("(k p) n -> p k n", p=128), reads=[], writes=[sk])
                for k4 in range(4):
                    self.act(wb[:, q * 4 + k4, :], st[:, k4, :], AF.Copy, reads=[sk], writes=[wk])
            for g in range(4):
                u, uk = ur.next()
                self.dma(u[:], self.u_scr[:, g * 512:(g + 1) * 512].rearrange("(k p) t -> p k t", p=128),
                         reads=[("u_scr", f) for f in range(32)], writes=[uk])
                for ci in range(4):
                    ps, pk = self.psum.next()
                    for kc in range(32):
                        self.mm(ps[:], wb[:, kc, ci * 128:(ci + 1) * 128], u[:, kc, :], kc == 0, kc == 31,
                                reads=[wk, uk], writes=[pk])
                    oc = half * 4 + ci
                    sl = slice(g * 512, (g + 1) * 512)
                    self.tt(self.xT[:, oc, sl], self.xT[:, oc, sl], ps[:], ALU.add,
                            reads=[pk, ("xT", oc, g)], writes=[("xT", oc, g)])
        S.barrier()
        A.reset(m)

    def layer(self, l):
        if self.mixers:
            self.mixer(l)
        if self.do_mlp:
            self.mlp(l)


    def mixer(self, l):
        A, S = self.A, self.S
        m0 = A.mark()
        self.hT = A.alloc("hT", [128, KC, S_LEN], BF16)
        self.rmsnorm()
        m1 = A.mark()
        if "sb" in self.mixers:
            self.sb_phase(l)
            S.barrier()
            A.reset(m1)
        if "ssm" in self.mixers:
            self.ssm_phase(l)
            S.barrier()
            A.reset(m1)
        if "dn" in self.mixers:
            self.dn_phase(l)
            S.barrier()
            A.reset(m1)
        self.gates_phase(l)
        S.barrier()
        A.reset(m0)
        self.merge_phase(l)
        S.barrier()
        A.reset(m0)

    def branches(self):
        return [i for i, n in enumerate(("dn", "sb", "ssm")) if n in self.mixers]

    def gates_phase(self, l):
        wst = self.ring("gwst", 2, [128, KC, 512], F32)
        wbf = self.ring("gwbf", 2, [128, KC, 512], BF16)
        gb = self.ring("gb", 3, [128, S_LEN], BF16)
        scale = self.nw[:, 2 * l, :]
        for i in self.branches():
            for cb in range(2):
                c0 = O_GATE + i * D + cb * 512
                wb, wk = self.wload(self.w_in_d[l][:, c0:c0 + 512], KC, 512, wst, wbf, scale=scale)
                cur = {}

                def evac(ci, g, ps, pk, n, cb=cb, i=i, cur=cur):
                    if g == 0:
                        cur["t"] = gb.next()
                    t, tk = cur["t"]
                    self.act(t[:, g * 512:(g + 1) * 512], ps[:], AF.Sigmoid, reads=[pk], writes=[tk])
                    if g == 3:
                        oc = cb * 4 + ci
                        self.dma(self.g_scr[i][oc * 128:(oc + 1) * 128, :], t[:], reads=[tk],
                                 writes=[("g_scr", i, oc)])

                self.linear_T(wb, wk, 512, KC, lambda kc, g: self.hT[:, kc, g * 512:(g + 1) * 512],
                              self.hT_keys, evac)

    def merge_phase(self, l):
        A, S = self.A, self.S
        wst = self.ring("mwst", 1, [128, KC, 512], F32)
        wbf = self.ring("mwbf", 2, [128, KC, 512], BF16)
        mg = A.alloc("mg", [128, KC, 1024], F32)
        mgb = A.alloc("mgb", [128, KC, 1024], BF16)
        ob = self.ring("ob", 1, [128, KC, 1024], BF16)
        gt = self.ring("gt", 3, [128, 1024], BF16)
        self.mtmp = self.ring("mtmp", 3, [128, 512], F32)
        brs = self.branches()
        for half in range(2):
            tsl = slice(half * 1024, (half + 1) * 1024)
            for bi, i in enumerate(brs):
                o, ok = ob.next()
                self.dma(o[:], self.o_scr[i][:, tsl].rearrange("(k p) t -> p k t", p=128),
                         reads=[("o_scr", i, c) for c in range(8)], writes=[ok])
                for cb in range(2):
                    wb, wk = self.wload(self.w_branch_d[l][i][:, cb * 512:(cb + 1) * 512], KC, 512, wst, wbf)
                    cur = {}

                    def evac(ci, g, ps, pk, n, cb=cb, i=i, bi=bi, cur=cur, half=half, tsl=tsl):
                        oc = cb * 4 + ci
                        gl = g - 2 * half
                        if gl == 0:
                            cur["g"] = gt.next()
                            t, tk = cur["g"]
                            self.dma(t[:], self.g_scr[i][oc * 128:(oc + 1) * 128, tsl],
                                     reads=[("g_scr", i, oc)], writes=[tk])
                        t, tk = cur["g"]
                        dst = mg[:, oc, gl * 512:(gl + 1) * 512]
                        mk = ("mg", oc, gl)
                        if bi == 0:
                            self.tt(dst, ps[:], t[:, gl * 512:(gl + 1) * 512], ALU.mult,
                                    reads=[pk, tk], writes=[mk])
                        else:
                            tmp, tmk = self.mtmp.next()
                            self.tt(tmp[:], ps[:], t[:, gl * 512:(gl + 1) * 512], ALU.mult,
                                    reads=[pk, tk], writes=[tmk])
                            self.tt(dst, dst, tmp[:], ALU.add, reads=[tmk, mk], writes=[mk], eng="pool")
                        if bi == len(brs) - 1:
                            self.copy(mgb[:, oc, gl * 512:(gl + 1) * 512], dst, reads=[mk],
                                      writes=[("mgb", oc, gl)], eng="act")

                    self.linear_T(wb, wk, 512, KC, lambda kc, g, o=o, half=half: o[:, kc, (g - 2 * half) * 512:(g - 2 * half + 1) * 512],
                                  lambda g, ok=ok: [ok], evac, groups=(2 * half, 2 * half + 1))
            for cb in range(2):
                wb, wk = self.wload(self.w_out_d[l][:, cb * 512:(cb + 1) * 512], KC, 512, wst, wbf)

                def evac2(ci, g, ps, pk, n, cb=cb):
                    oc = cb * 4 + ci
                    sl = slice(g * 512, (g + 1) * 512)
                    self.tt(self.xT[:, oc, sl], self.xT[:, oc, sl], ps[:], ALU.add,
                            reads=[pk, ("xT", oc, g)], writes=[("xT", oc, g)])

                self.linear_T(wb, wk, 512, KC,
                              lambda kc, g, half=half: mgb[:, kc, (g - 2 * half) * 512:(g - 2 * half + 1) * 512],
                              lambda g, half=half: [("mgb", kc, g - 2 * half) for kc in range(KC)], evac2,
                              groups=(2 * half, 2 * half + 1))

    def sb_phase(self, l):
        A, S = self.A, self.S
        R = {}
        R["wst"] = self.ring("sbwst", 1, [128, KC, 384], F32)
        R["wbf"] = self.ring("sbwbf", 2, [128, KC, 384], BF16)
        R["qT"] = self.ring("sbq", 2, [128, S_LEN], BF16)
        R["kT"] = self.ring("sbk", 2, [128, S_LEN], BF16)
        R["v"] = self.ring("sbv", 2, [128, 16, 128], BF16)
        R["osb"] = self.ring("sbo", 2, [128, S_LEN], BF16)
        R["e"] = self.ring("sbe", 3, [128, 512], F32)
        R["spb"] = self.ring("sbsp", 3, [128, 512], BF16)
        R["cum"] = self.ring("sbcum", 2, [128, 512], F32)
        R["cumb"] = self.ring("sbcumb", 2, [128, 512], BF16)
        R["xa"] = self.ring("sbxa", 3, [128, 512], F32)
        R["w"] = self.ring("sbw", 3, [128, 512], BF16)
        for hp in range(8):
            self.sb_unit(l, hp, R)

    def sb_unit(self, l, hp, R):
        scale = self.nw[:, 2 * l, :]
        st, sk = R["wst"].next()
        wb, wk = R["wbf"].next()
        for j in range(3):
            base = O_SBQKV + j * 1024 + 128 * hp
            self.dma(st[:, :, j * 128:(j + 1) * 128],
                     self.w_in_d[l][:, base:base + 128].rearrange("(k p) n -> p k n", p=128),
                     reads=[], writes=[(sk, j)])
        for kc in range(KC):
            self.act(wb[:, kc, :], st[:, kc, :], AF.Copy, reads=[(sk, 0), (sk, 1), (sk, 2), "nw"], writes=[wk],
                     scale=scale[:, kc:kc + 1])
        qT, qk = R["qT"].next()
        kT, kk = R["kT"].next()
        v, vk = R["v"].next()

        def evac_qk(ci, g, ps, pk, n):
            dst, dk = (qT, qk) if ci == 0 else (kT, kk)
            self.copy(dst[:, g * 512:(g + 1) * 512], ps[:], reads=[pk], writes=[(dk, g)],
                      eng=("act" if g % 2 else "dve"))

        self.linear_T(wb, wk, 256, KC, lambda kc, g: self.hT[:, kc, g * 512:(g + 1) * 512], self.hT_keys, evac_qk)
        for tq in range(4):
            ps, pk = self.psum.next()
            for t in range(4):
                tt_ = tq * 4 + t
                for kc in range(KC):
                    self.mm(ps[:, t * 128:(t + 1) * 128], self.hT[:, kc, tt_ * 128:(tt_ + 1) * 128],
                            wb[:, kc, 256:384], kc == 0, kc == KC - 1, reads=[wk, ("hT", kc, tq)], writes=[pk])
            self.copy(v[:, tq * 4:(tq + 1) * 4, :], ps[:].rearrange("p (t c) -> p t c", t=4), reads=[pk],
                      writes=[(vk, tq)], eng=("act" if tq % 2 else "dve"))
        osb, ok = R["osb"].next()
        mstrict = self.consts[:, C_MSTRICT, :]
        tinc = self.cbf[:, C_TINC, :]
        one_col = self.consts[:, C_ONES, 0:1]
        for e in range(2):
            pb = 64 * e
            for g in range(4):
                cum, ck = R["cum"].next()
                cumb, cbk = R["cumb"].next()
                self.memset(cum[:], 0.0, writes=[ck], eng="pool")
                self.memset(cumb[:], 0.0, writes=[cbk], eng="pool")
                po, pok = self.psum_acc.next()
                first = True
                for kb in range(4 * g + 3, -1, -1):
                    t0 = max(kb * 128, g * 512)
                    N = (g + 1) * 512 - t0
                    c0 = t0 - g * 512
                    diag = kb * 128 >= g * 512
                    pa, pak = self.psum.next()
                    self.mm(pa[:, :N], kT[pb:pb + 64, kb * 128:(kb + 1) * 128], qT[pb:pb + 64, t0:t0 + N], True, True,
                            reads=[(kk, kb // 4), (qk, g)], writes=[pak])
                    ee, ek = R["e"].next()
                    self.act(ee[:, :N], pa[:, :N], AF.Exp, reads=[pak], writes=[ek], scale=0.125)
                    if diag:
                        self.tt(ee[:, :128], ee[:, :128], mstrict, ALU.mult, reads=[ek, "consts"], writes=[ek])
                    spb, spk = R["spb"].next()
                    self.act(spb[:, :N], ee[:, :N], AF.Ln, reads=[ek, "consts"], writes=[spk], bias=one_col, scale=1.0)
                    pB, pBk = self.psum.next()
                    self.mm(pB[:, :N], tinc, spb[:, :N], True, first, reads=[spk, "cbf"], writes=[pBk])
                    if not first:
                        self.mm(pB[:, :N], self.ones_b(), cumb[:, c0:c0 + N], False, True, reads=[cbk, "cbf"],
                                writes=[pBk])
                    xa, xk = R["xa"].next()
                    self.act(xa[:, :N], pB[:, :N], AF.Exp, reads=[pBk], writes=[xk], scale=-1.0)
                    w, wwk = R["w"].next()
                    self.tt(w[:, :N], ee[:, :N], xa[:, :N], ALU.mult, reads=[ek, xk], writes=[wwk])
                    self.mm(po[pb:pb + 64, c0:512], v[:, kb, pb:pb + 64], w[:, :N], first, kb == 0,
                            reads=[(vk, kb // 4), wwk], writes=[pok], skip_group_check=True)
                    if kb > 0:
                        self.tt(cum[:, c0:512], cum[:, c0:512], spb[:, :N], ALU.add, reads=[ck, spk], writes=[ck],
                                eng="pool")
                        self.copy(cumb[:, c0:512], cum[:, c0:512], reads=[ck], writes=[cbk], eng="pool")
                    first = False
                self.copy(osb[pb:pb + 64, g * 512:(g + 1) * 512], po[pb:pb + 64, :], reads=[pok],
                          writes=[(ok, e, g)], eng="dve")
        self.dma(self.o_scr[1][hp * 128:(hp + 1) * 128, :], osb[:],
                 reads=[(ok, e, g) for e in range(2) for g in range(4)], writes=[("o_scr", 1, hp)])


    def bload(self, name, dram_row_ap, n):
        t = self.A.alloc(name, [128, n], F32)
        k = self.key(name)
        self.dma(t[:], dram_row_ap.partition_broadcast(128), reads=[], writes=[k])
        return t, k

    def conv_silu(self, l, wb, wk, wcol, conv_w_d, conv_b_d, ch, raw, rawk, acc, acck, cw, dst, dstk, func=AF.Silu):
        c, ck = cw.next()
        self.dma(c[:, 0:4], conv_w_d[l][:, ch:ch + 128].rearrange("k c -> c k"), reads=[], writes=[(ck, 0)], slow=True)
        if conv_b_d is not None:
            self.dma(c[:, 4:5], conv_b_d[l][ch:ch + 128].rearrange("(c o) -> c o", o=1), reads=[], writes=[(ck, 1)],
                     slow=True)
        else:
            self.memset(c[:, 4:5], 0.0, writes=[(ck, 1)])

        def evac(ci, g, ps, pk, n):
            self.copy(raw[:, 3 + g * 512:3 + (g + 1) * 512], ps[:], reads=[pk], writes=[(rawk, g)],
                      eng=("act" if g % 2 else "dve"))

        for g in range(4):
            ps, pk = self.psum.next()
            for kc in range(KC):
                self.mm(ps[:], wb[:, kc, wcol:wcol + 128], self.hT[:, kc, g * 512:(g + 1) * 512], kc == 0, kc == KC - 1,
                        reads=[wk] + self.hT_keys(g), writes=[pk])
            evac(0, g, ps, pk, 128)
        rk = [(rawk, g) for g in range(4)] + [(rawk, "pad")]
        self.ts(acc[:], raw[:, 0:S_LEN], c[:, 0:1], ALU.mult, reads=rk + [(ck, 0), (ck, 1)], writes=[acck],
                s2=c[:, 4:5], op1=ALU.add)
        for k in range(1, 4):
            self.stt(acc[:], raw[:, k:k + S_LEN], c[:, k:k + 1], acc[:], ALU.mult, ALU.add,
                     reads=rk + [(ck, 0), acck], writes=[acck])
        self.act(dst, acc[:], func, reads=[acck], writes=[dstk])

    def to_tok(self, src, srck, dst_fn, dstk):
        for tq in range(4):
            ps, pk = self.psum.next()
            pb = ps[:].bitcast(BF16)
            for t in range(4):
                tt_ = tq * 4 + t
                self.tr(pb[:, t * 128:(t + 1) * 128], src[:, tt_ * 128:(tt_ + 1) * 128], self.ident_b(),
                        reads=(list(srck) if isinstance(srck, list) else [srck]) + ["cbf"], writes=[pk])
            self.copy(dst_fn(tq), pb[:, 0:512].rearrange("p (t c) -> p t c", t=4), reads=[pk], writes=[(dstk, tq)],
                      eng=("act" if tq % 2 else "dve"))

    def ssm_phase(self, l):
        A, S = self.A, self.S
        scale = self.nw[:, 2 * l, :]
        cst = self.consts
        wdst = A.alloc("wdtst", [128, KC, 16], F32)
        wdt = A.alloc("wdt", [128, KC, 16], BF16)
        self.dma(wdst[:], self.w_in_d[l][:, O_SSMDT:O_SSMDT + 16].rearrange("(k p) n -> p k n", p=128), [], ["wdtst"])
        for kc in range(KC):
            self.act(wdt[:, kc, :], wdst[:, kc, :], AF.Copy, reads=["wdtst", "nw"], writes=["wdt"], scale=scale[:, kc:kc + 1])
        dtb, dtbk = self.bload("dtb", self.ssm_dt_bias_d[l], 16)
        alog, alogk = self.bload("alog", self.ssm_a_log_d[l], 16)
        dbc, dbck = self.bload("dbc", self.ssm_d_d[l], 16)
        dt = A.alloc("dt", [128, 16, 16], F32)
        av = A.alloc("av", [128, 16, 16], F32)
        acum = A.alloc("acum", [128, 16, 16], F32)
        eacum = A.alloc("eacum", [128, 16, 16], F32)
        dtds = A.alloc("dtds", [128, 16, 16], F32)
        eatot = A.alloc("eatot", [128, 32, 16], F32)
        tmp = A.alloc("ptmp", [128, 16, 16], F32)
        ps, pk = self.psum.next()
        for t in range(16):
            for kc in range(KC):
                self.mm(ps[:, t * 16:(t + 1) * 16], self.hT[:, kc, t * 128:(t + 1) * 128], wdt[:, kc, :], kc == 0,
                        kc == KC - 1, reads=["wdt", ("hT", kc, t // 4)], writes=[pk])
        self.tt(dt[:], ps[:, 0:256].rearrange("p (t h) -> p t h", t=16),
                dtb[:].unsqueeze(1).to_broadcast([128, 16, 16]), ALU.add, reads=[pk, dtbk], writes=["dt"])
        one_col = cst[:, C_ONES, 0:1]
        self.act(tmp[:], dt[:], AF.Exp, reads=["dt"], writes=["ptmp"])
        self.act(dt[:], tmp[:], AF.Ln, reads=["ptmp", "consts"], writes=["dt"], bias=one_col, scale=1.0)
        self.act(alog[:], alog[:], AF.Exp, reads=[alogk], writes=[alogk])
        self.S.op("dve", lambda e: e.scalar_tensor_tensor(out=av[:], in0=dt[:], scalar=-1.0,
                                                          in1=alog[:].unsqueeze(1).to_broadcast([128, 16, 16]),
                                                          op0=ALU.mult, op1=ALU.mult),
                  reads=["dt", alogk], writes=["av"])
        ps, pk = self.psum.next()
        for t in range(16):
            self.mm(ps[:, t * 16:(t + 1) * 16], cst[:, C_TRI2, :], av[:, t, :], True, True, reads=["av", "consts"], writes=[pk])
        self.copy(acum[:], ps[:, 0:256].rearrange("p (t h) -> p t h", t=16), reads=[pk], writes=["acum"])
        self.act(eacum[:], acum[:], AF.Exp, reads=["acum"], writes=["eacum"])
        ps, pk = self.psum.next()
        for t in range(16):
            self.mm(ps[:, t * 16:(t + 1) * 16], cst[:, C_BD, :], av[:, t, :], True, True, reads=["av", "consts"], writes=[pk])
        self.tt(tmp[:], ps[:, 0:256].rearrange("p (t h) -> p t h", t=16), acum[:], ALU.subtract, reads=[pk, "acum"],
                writes=["ptmp"])
        self.act(tmp[:], tmp[:], AF.Exp, reads=["ptmp"], writes=["ptmp"])
        self.tt(dtds[:], tmp[:], dt[:], ALU.mult, reads=["ptmp", "dt"], writes=["dtds"])
        ps, pk = self.psum.next()
        for t in range(16):
            for c in range(2):
                j = 2 * t + c
                self.mm(ps[:, j * 16:(j + 1) * 16], cst[:, C_IND0 + c, :], av[:, t, :], True, True,
                        reads=["av", "consts"], writes=[pk])
        self.act(eatot[:].rearrange("p j h -> p (j h)"), ps[:], AF.Exp, reads=[pk], writes=["eatot"])

        mG = A.mark()
        for g in range(4):
            A.reset(mG)
            S.barrier()
            wbf = A.alloc("swz", [128, KC, 256], BF16)
            BT = A.alloc("sBT", [128, S_LEN], BF16)
            CT = A.alloc("sCT", [128, S_LEN], BF16)
            x_tok = A.alloc("sxtok", [128, 16, 256], BF16)
            B_tok = A.alloc("sBtok", [128, 16, 128], BF16)
            nwb, nwbk = self.bload("snwb", self.ssm_norm_w_d[l][256 * g:256 * (g + 1)], 256)
            mA = A.mark()
            wst = self.ring("swst", 2, [128, KC, 128], F32)
            wbx = A.alloc("swbx", [128, KC, 512], BF16)
            raw = A.alloc("sraw", [128, S_LEN + 4], F32)
            acc = A.alloc("sacc", [128, S_LEN], F32)
            xc = self.ring("sxc", 2, [128, S_LEN], BF16)
            cw = self.ring("scw", 2, [128, 8], F32)
            rawk = self.key("sraw")
            self.memset(raw[:, 0:3], 0.0, writes=[(rawk, "pad")])
            cols = [O_SSMZ + 256 * g, O_SSMZ + 256 * g + 128, O_SSMXBC + 256 * g, O_SSMXBC + 256 * g + 128,
                    O_SSMXBC + 1024 + 128 * g, O_SSMXBC + 1536 + 128 * g]
            wk = self.key("swbf")
            for j, c0 in enumerate(cols):
                st, sk = wst.next()
                self.dma(st[:], self.w_in_d[l][:, c0:c0 + 128].rearrange("(k p) n -> p k n", p=128), [], [sk])
                for kc in range(KC):
                    wdst_ = wbf[:, kc, j * 128:(j + 1) * 128] if j < 2 else wbx[:, kc, (j - 2) * 128:(j - 1) * 128]
                    self.act(wdst_, st[:, kc, :], AF.Copy, reads=[sk, "nw"], writes=[(wk, j)],
                             scale=scale[:, kc:kc + 1])
            chs = [256 * g, 256 * g + 128, 1024 + 128 * g, 1536 + 128 * g]
            acck = self.key("sacc")
            xtk = self.key("sxtok")
            btk = self.key("sBtok")
            for jj, ch in enumerate(chs):
                j = jj + 2
                if jj < 2:
                    dst, dk = xc.next()
                elif jj == 2:
                    dst, dk = BT, "sBT"
                else:
                    dst, dk = CT, "sCT"
                self.conv_silu(l, wbx, (wk, j), jj * 128, self.ssm_conv_w_d, self.ssm_conv_b_d, ch, raw, rawk, acc, acck,
                               cw, dst[:], dk)
                if jj < 2:
                    self.to_tok(dst, dk, lambda tq, jj=jj: x_tok[:, tq * 4:(tq + 1) * 4, jj * 128:(jj + 1) * 128], (xtk, jj))
                elif jj == 2:
                    self.to_tok(dst, dk, lambda tq: B_tok[:, tq * 4:(tq + 1) * 4, :], btk)
            S.barrier()
            A.reset(mA)
            xdt = A.alloc("sxdt", [128, 16, 256], BF16)
            xdtd = A.alloc("sxdtd", [128, 16, 256], BF16)
            xD = A.alloc("sxD", [128, 16, 256], BF16)
            oT = A.alloc("soT", [128, 2, S_LEN], BF16)
            state = A.alloc("sstate", [128, 256], F32)
            state_bf = A.alloc("sstatebf", [128, 256], BF16)
            abc = self.ring("sabc", 2, [128, 128], F32)
            dm = self.ring("sdm", 2, [128, 4, 128], F32)
            MT = self.ring("sMT", 2, [128, 4, 128], BF16)
            szr = self.ring("ssz", 1, [128, 256], F32)
            t1r = self.ring("st1", 1, [128, 256], F32)
            yr = self.ring("sy", 2, [128, 256], F32)
            jr = self.ring("sjunk", 1, [128, 256], F32)
            ssr = self.ring("sssq", 4, [128, 2], F32)
            obr = self.ring("sob", 2, [128, 256], BF16)
            xtks = [(xtk, jj, tq) for jj in range(2) for tq in range(4)]
            x4 = x_tok[:].rearrange("p t (h c) -> p t h c", h=4)
            hs = slice(4 * g, 4 * g + 4)
            self.tt(xdt[:].rearrange("p t (h c) -> p t h c", h=4), x4,
                    dt[:, :, hs].unsqueeze(3).to_broadcast([128, 16, 4, 64]), ALU.mult, reads=xtks + ["dt"], writes=["sxdt"])
            self.tt(xdtd[:].rearrange("p t (h c) -> p t h c", h=4), x4,
                    dtds[:, :, hs].unsqueeze(3).to_broadcast([128, 16, 4, 64]), ALU.mult, reads=xtks + ["dtds"],
                    writes=["sxdtd"])
            for t in range(16):
                self.tt(xD[:, t, :].rearrange("p (h c) -> p h c", h=4), x_tok[:, t, :].rearrange("p (h c) -> p h c", h=4),
                        dbc[:, hs].unsqueeze(2).to_broadcast([128, 4, 64]), ALU.mult, reads=xtks + [dbck],
                        writes=[("sxD", t)], eng="pool")
            stk = self.key("sstate")
            sbk = self.key("sstatebf")
            self.memset(state[:], 0.0, writes=[stk])
            self.memset(state_bf[:], 0.0, writes=[sbk])
            ones_f = cst[:, C_ONES, :]
            for t in range(16):
                tsl = slice(t * 128, (t + 1) * 128)
                psS, psSk = self.psum.next()
                self.mm(psS[:, 0:128], BT[:, tsl], CT[:, tsl], True, True, reads=["sBT", "sCT"], writes=[psSk])
                psD, psDk = self.psum.next()
                for hh in range(4):
                    ab, abk = abc.next()
                    self.ts(ab[:], ones_f, av[:, t, 4 * g + hh:4 * g + hh + 1], ALU.mult, reads=["av", "consts"], writes=[abk])
                    self.mm(psD[:, hh * 128:(hh + 1) * 128], ab[:], cst[:, C_TRI2, :], True, False, reads=[abk, "consts"],
                            writes=[psDk])
                    self.mm(psD[:, hh * 128:(hh + 1) * 128], cst[:, C_TRI2NEG, :], ab[:], False, True,
                            reads=[abk, "consts"], writes=[psDk])
                d_, dk_ = dm.next()
                self.tt(d_[:], psD[:].rearrange("p (h c) -> p h c", h=4),
                        cst[:, C_NEGINCL, :].unsqueeze(1).to_broadcast([128, 4, 128]), ALU.add, reads=[psDk, "consts"],
                        writes=[dk_])
                L_, Lk_ = d_, dk_
                self.act(L_[:], d_[:], AF.Exp, reads=[dk_], writes=[Lk_])
                M_, Mk_ = MT.next()
                self.tt(M_[:], L_[:], psS[:, 0:128].unsqueeze(1).to_broadcast([128, 4, 128]), ALU.mult,
                        reads=[Lk_, psSk], writes=[Mk_])
                psY, psYk = self.psum.next()
                for hh in range(4):
                    cs = slice(hh * 64, (hh + 1) * 64)
                    self.mm(psY[:, cs], M_[:, hh, :], xdt[:, t, cs], True, False, reads=[Mk_, "sxdt"], writes=[psYk])
                    self.mm(psY[:, cs], self.ident_b(), xD[:, t, cs], False, True, reads=["cbf", ("sxD", t)], writes=[psYk])
                psZ, psZk = self.psum.next()
                for kc in range(KC):
                    self.mm(psZ[:, 0:256], self.hT[:, kc, tsl], wbf[:, kc, 0:256], kc == 0, kc == KC - 1,
                            reads=[(wk, 0), (wk, 1), ("hT", kc, t // 4)], writes=[psZk])
                sz, szk = szr.next()
                self.act(sz[:], psZ[:, 0:256], AF.Silu, reads=[psZk], writes=[szk])
                psO, psOk = self.psum_acc.next()
                for c in range(2):
                    j = 2 * t + c
                    self.mm(psO[64 * c:64 * c + 64, 0:256], CT[:, t * 128 + 64 * c:t * 128 + 64 * c + 64], state_bf[:], True, True,
                            reads=["sCT", sbk], writes=[psOk])
                    psT, psTk = self.psum.next()
                    self.mm(psT[:, 0:256], B_tok[64 * c:64 * c + 64, t, :], xdtd[64 * c:64 * c + 64, t, :], True, True,
                            reads=[(btk, t // 4), "sxdtd"], writes=[psTk])
                    self.tt(state[:].rearrange("p (h c) -> p h c", h=4), state[:].rearrange("p (h c) -> p h c", h=4),
                            eatot[:, j, hs].unsqueeze(2).to_broadcast([128, 4, 64]), ALU.mult, reads=[stk, "eatot"],
                            writes=[stk])
                    self.tt(state[:], state[:], psT[:, 0:256], ALU.add, reads=[stk, psTk], writes=[stk])
                    self.copy(state_bf[:], state[:], reads=[stk], writes=[sbk], eng="act")
                t1, t1k = t1r.next()
                self.tt(t1[:].rearrange("p (h c) -> p h c", h=4), psO[:, 0:256].rearrange("p (h c) -> p h c", h=4),
                        eacum[:, t, hs].unsqueeze(2).to_broadcast([128, 4, 64]), ALU.mult, reads=[psOk, "eacum"],
                        writes=[t1k])
                y, yk = yr.next()
                self.tt(y[:], t1[:], psY[:, 0:256], ALU.add, reads=[t1k, psYk], writes=[yk])
                self.tt(y[:], y[:], sz[:], ALU.mult, reads=[yk, szk], writes=[yk])
                jk_, jkk = jr.next()
                ss, ssk = ssr.next()
                self.S.op("act", lambda e, jk_=jk_, y=y, ss=ss: e.activation(out=jk_[:], in_=y[:], func=AF.Square,
                                                                          accum_out=ss[:, 0:1]),
                          reads=[yk], writes=[jkk, ssk])
                self.act(ss[:, 1:2], ss[:, 0:1], AF.Ln, reads=[ssk, "consts"], writes=[ssk], scale=1.0 / 256,
                         bias=self.eps_ap())
                self.act(ss[:, 1:2], ss[:, 1:2], AF.Exp, reads=[ssk], writes=[ssk], scale=-0.5)
                ob, obk = obr.next()
                self.stt(ob[:], y[:], ss[:, 1:2], nwb[:], ALU.mult, ALU.mult, reads=[yk, ssk, nwbk], writes=[obk])
                psX, psXk = self.psum.next()
                pxb = psX[:].bitcast(BF16)
                for ch in range(2):
                    self.tr(pxb[:, ch * 128:(ch + 1) * 128], ob[:, ch * 128:(ch + 1) * 128], self.ident_b(),
                            reads=[obk, "cbf"], writes=[psXk])
                self.copy(oT[:, :, tsl], pxb[:, 0:256].rearrange("p (c t) -> p c t", c=2), reads=[psXk],
                          writes=[("soT", t)], eng="act")
            for ch in range(2):
                oc = 2 * g + ch
                self.dma(self.o_scr[2][oc * 128:(oc + 1) * 128, :], oT[:, ch, :], reads=[("soT", t) for t in range(16)],
                         writes=[("o_scr", 2, oc)])


    def dn_phase(self, l):
        A, S = self.A, self.S
        scale = self.nw[:, 2 * l, :]
        cst = self.consts
        one_col = cst[:, C_ONES, 0:1]
        ones_f = cst[:, C_ONES, :]
        wast = A.alloc("dwast", [128, KC, 16], F32)
        wa = A.alloc("dwa", [128, KC, 16], BF16)
        self.dma(wast[:], self.w_in_d[l][:, O_DNA:O_DNA + 16].rearrange("(k p) n -> p k n", p=128), [], ["dwast"])
        for kc in range(KC):
            self.act(wa[:, kc, :], wast[:, kc, :], AF.Copy, reads=["dwast", "nw"], writes=["dwa"], scale=scale[:, kc:kc + 1])
        dtb, dtbk = self.bload("ddtb", self.dn_dt_bias_d[l], 8)
        alog, alogk = self.bload("dalog", self.dn_a_log_d[l], 8)
        nwb, nwbk = self.bload("dnwb", self.dn_norm_w_d[l], 128)
        gv = A.alloc("dg", [128, 16, 8], F32)
        beta = A.alloc("dbeta", [128, 16, 8], F32)
        gc = A.alloc("dgc", [128, 16, 8], F32)
        egc = A.alloc("degc", [128, 16, 8], F32)
        bg = A.alloc("dbg", [128, 16, 8], F32)
        kdec = A.alloc("dkdec", [128, 16, 8], F32)
        eglast = A.alloc("deglast", [128, 32, 8], F32)
        tmp = A.alloc("dtmp", [128, 16, 8], F32)
        ps, pk = self.psum.next()
        for t in range(16):
            for kc in range(KC):
                self.mm(ps[:, t * 16:(t + 1) * 16], self.hT[:, kc, t * 128:(t + 1) * 128], wa[:, kc, :], kc == 0,
                        kc == KC - 1, reads=["dwa", ("hT", kc, t // 4)], writes=[pk])
        pv = ps[:, 0:256].rearrange("p (t h) -> p t h", t=16)
        self.act(beta[:], pv[:, :, 8:16], AF.Exp, reads=[pk], writes=["dbeta"], scale=-1.0)
        self.ts(beta[:], beta[:], 1.0, ALU.add, reads=["dbeta"], writes=["dbeta"])
        self.S.op("dve", lambda e: e.reciprocal(out=beta[:], in_=beta[:]), reads=["dbeta"], writes=["dbeta"])
        self.tt(gv[:], pv[:, :, 0:8], dtb[:].unsqueeze(1).to_broadcast([128, 16, 8]), ALU.add, reads=[pk, dtbk], writes=["dg"])
        self.act(tmp[:], gv[:], AF.Exp, reads=["dg"], writes=["dtmp"])
        self.act(gv[:], tmp[:], AF.Ln, reads=["dtmp", "consts"], writes=["dg"], bias=one_col, scale=1.0)
        self.act(alog[:], alog[:], AF.Exp, reads=[alogk], writes=[alogk])
        self.S.op("dve", lambda e: e.scalar_tensor_tensor(out=gv[:], in0=gv[:], scalar=-1.0,
                                                          in1=alog[:].unsqueeze(1).to_broadcast([128, 16, 8]),
                                                          op0=ALU.mult, op1=ALU.mult),
                  reads=["dg", alogk], writes=["dg"])
        ps, pk = self.psum.next()
        for t in range(16):
            self.mm(ps[:, t * 8:(t + 1) * 8], cst[:, C_TRI2, :], gv[:, t, :], True, True, reads=["dg", "consts"], writes=[pk])
        self.copy(gc[:], ps[:, 0:128].rearrange("p (t h) -> p t h", t=16), reads=[pk], writes=["dgc"])
        self.act(egc[:], gc[:], AF.Exp, reads=["dgc"], writes=["degc"])
        self.tt(bg[:], egc[:], beta[:], ALU.mult, reads=["degc", "dbeta"], writes=["dbg"])
        ps, pk = self.psum.next()
        for t in range(16):
            self.mm(ps[:, t * 8:(t + 1) * 8], cst[:, C_BD, :], gv[:, t, :], True, True, reads=["dg", "consts"], writes=[pk])
        self.tt(kdec[:], ps[:, 0:128].rearrange("p (t h) -> p t h", t=16), gc[:], ALU.subtract, reads=[pk, "dgc"],
                writes=["dkdec"])
        self.act(kdec[:], kdec[:], AF.Exp, reads=["dkdec"], writes=["dkdec"])
        ps, pk = self.psum.next()
        for t in range(16):
            for c in range(2):
                j = 2 * t + c
                self.mm(ps[:, j * 8:(j + 1) * 8], cst[:, C_IND0 + c, :], gv[:, t, :], True, True,
                        reads=["dg", "consts"], writes=[pk])
        self.act(eglast[:].rearrange("p j h -> p (j h)"), ps[:, 0:256], AF.Exp, reads=[pk], writes=["deglast"])

        mG = A.mark()
        for h in range(8):
            A.reset(mG)
            S.barrier()
            wg = A.alloc("dwg", [128, KC, 128], BF16)
            qTn = A.alloc("dqTn", [128, S_LEN], BF16)
            kTn = A.alloc("dkTn", [128, S_LEN], BF16)
            k_tok = A.alloc("dktok", [128, 16, 128], BF16)
            v_tok = A.alloc("dvtok", [128, 16, 128], BF16)
            mA = A.mark()
            wst = self.ring("dwst", 2, [128, KC, 128], F32)
            wbx = A.alloc("dwbx", [128, KC, 384], BF16)
            raw = A.alloc("draw", [128, S_LEN + 4], F32)
            acc = A.alloc("dacc", [128, S_LEN], F32)
            vT = A.alloc("dvT", [128, S_LEN], BF16)
            cw = self.ring("dcw", 2, [128, 8], F32)
            sqr = self.ring("dsq", 2, [128, 512], BF16)
            rsr = self.ring("drs", 2, [128, 512], F32)
            rawk = self.key("draw")
            self.memset(raw[:, 0:3], 0.0, writes=[(rawk, "pad")])
            cols = [O_DNQKV + 128 * h, O_DNQKV + 1024 + 128 * h, O_DNQKV + 2048 + 128 * h, O_DNGATE + 128 * h]
            wk = self.key("dwb")
            for j, c0 in enumerate(cols):
                st, sk = wst.next()
                self.dma(st[:], self.w_in_d[l][:, c0:c0 + 128].rearrange("(k p) n -> p k n", p=128), [], [sk])
                for kc in range(KC):
                    wd_ = wbx[:, kc, j * 128:(j + 1) * 128] if j < 3 else wg[:, kc, :]
                    self.act(wd_, st[:, kc, :], AF.Copy, reads=[sk, "nw"], writes=[(wk, j)], scale=scale[:, kc:kc + 1])
            acck = self.key("dacc")
            ktk = self.key("dktok")
            vtk = self.key("dvtok")
            for j in range(3):
                ch = j * 1024 + 128 * h
                if j == 2:
                    self.conv_silu(l, wbx, (wk, j), j * 128, self.dn_conv_w_d, None, ch, raw, rawk, acc, acck, cw, vT[:], "dvT")
                    self.to_tok(vT, "dvT", lambda tq: v_tok[:, tq * 4:(tq + 1) * 4, :], vtk)
                    continue
                self.conv_silu(l, wbx, (wk, j), j * 128, self.dn_conv_w_d, None, ch, raw, rawk, acc, acck, cw, acc[:], acck)
                dst, dk = (qTn, "dqTn") if j == 0 else (kTn, "dkTn")
                for g in range(4):
                    sl = slice(g * 512, (g + 1) * 512)
                    q_, qk_ = sqr.next()
                    self.act(q_[:], acc[:, sl], AF.Square, reads=[acck], writes=[qk_])
                    ps, pk = self.psum.next()
                    self.mm(ps[:], self.ones_b(), q_[:], True, True, reads=[qk_, "cbf"], writes=[pk])
                    r_, rk_ = rsr.next()
                    self.act(r_[:], ps[:], AF.Ln, reads=[pk, "consts"], writes=[rk_], scale=1.0, bias=self.eps_ap())
                    self.act(r_[:], r_[:], AF.Exp, reads=[rk_], writes=[rk_], scale=-0.5)
                    if j == 0:
                        self.stt(dst[:, sl], acc[:, sl], 128.0 ** -0.5, r_[:], ALU.mult, ALU.mult, reads=[acck, rk_],
                                 writes=[(dk, g)])
                    else:
                        self.tt(dst[:, sl], acc[:, sl], r_[:], ALU.mult, reads=[acck, rk_], writes=[(dk, g)])
                if j == 1:
                    self.to_tok(kTn, [("dkTn", g) for g in range(4)], lambda tq: k_tok[:, tq * 4:(tq + 1) * 4, :], ktk)
            S.barrier()
            A.reset(mA)
            attnT = A.alloc("dattnT", [128, 16, 128], BF16)
            P = A.alloc("dP", [128, 16, 128], BF16)
            PL = A.alloc("dPL", [128, 16, 128], BF16)
            R32 = A.alloc("dR32", [128, 16, 128], F32)
            Rb = A.alloc("dRb", [128, 16, 128], BF16)
            vb = v_tok
            kbg = k_tok
            kd = A.alloc("dkd", [128, 16, 128], BF16)
            u = A.alloc("du", [128, 16, 128], F32)
            wT = A.alloc("dwT", [128, 16, 128], BF16)
            oT = A.alloc("doT", [128, S_LEN], BF16)
            Sst = A.alloc("dS", [128, 128], F32)
            Sbf = A.alloc("dSbf", [128, 128], BF16)
            abr = self.ring("dab", 3, [128, 128], F32)
            dcr = self.ring("ddec", 1, [128, 4, 128], F32)
            tmr = self.ring("dtm", 1, [128, 4, 128], F32)
            bmr = self.ring("dbm", 1, [128, 4, 128], F32)
            vnr = self.ring("dvn", 2, [128, 128], BF16)
            t1r = self.ring("dt1", 2, [128, 128], F32)
            otr = self.ring("dot", 2, [128, 128], F32)
            sgr = self.ring("dsg", 2, [128, 128], F32)
            jr = self.ring("djunk", 1, [128, 128], F32)
            ssr = self.ring("dssq", 4, [128, 2], F32)
            obr = self.ring("dob", 2, [128, 128], BF16)
            ktks = [(ktk, tq) for tq in range(4)]
            vtks = [(vtk, tq) for tq in range(4)]
            bc3 = lambda t_: t_[:, :, h:h + 1].to_broadcast([128, 16, 128])
            self.tt(vb[:], v_tok[:], bc3(beta), ALU.mult, reads=vtks + ["dbeta"], writes=vtks + ["dvb"])
            self.tt(kd[:], k_tok[:], bc3(kdec), ALU.mult, reads=ktks + ["dkdec"], writes=["dkd"])
            self.tt(kbg[:], k_tok[:], bc3(bg), ALU.mult, reads=ktks + ["dbg", "dkd"], writes=ktks + ["dkbg"])
            kTk = [("dkTn", g) for g in range(4)]
            identf4 = cst[:, C_IDENT, :].unsqueeze(1).to_broadcast([128, 4, 128])
            for q in range(4):
                psK, psKk = self.psum.next()
                psQ, psQk = self.psum.next()
                psD, psDk = self.psum.next()
                psB, psBk = self.psum.next()
                for i4 in range(4):
                    t = 4 * q + i4
                    tsl = slice(t * 128, (t + 1) * 128)
                    cs = slice(i4 * 128, (i4 + 1) * 128)
                    self.mm(psK[:, cs], kTn[:, tsl], kTn[:, tsl], True, True, reads=[("dkTn", q)], writes=[psKk])
                    self.mm(psQ[:, cs], kTn[:, tsl], qTn[:, tsl], True, True, reads=[("dkTn", q), ("dqTn", q)], writes=[psQk])
                    ab, abk = abr.next()
                    self.ts(ab[:], ones_f, gv[:, t, h:h + 1], ALU.mult, reads=["dg", "consts"], writes=[abk])
                    self.mm(psD[:, cs], ab[:], cst[:, C_TRI2, :], True, False, reads=[abk, "consts"], writes=[psDk])
                    self.mm(psD[:, cs], cst[:, C_TRI2NEG, :], ab[:], False, True, reads=[abk, "consts"], writes=[psDk])
                    db, dbk = abr.next()
                    self.ts(db[:], cst[:, C_IDENT, :], beta[:, t, h:h + 1], ALU.mult, reads=["dbeta", "consts"], writes=[dbk])
                    self.mm(psB[:, cs], ones_f, db[:], True, True, reads=[dbk, "consts"], writes=[psBk])
                dc, dck = dcr.next()
                self.tt(dc[:], psD[:].rearrange("p (a b) -> p a b", a=4),
                        cst[:, C_NEGINCL, :].unsqueeze(1).to_broadcast([128, 4, 128]), ALU.add, reads=[psDk, "consts"],
                        writes=[dck])
                self.act(dc[:], dc[:], AF.Exp, reads=[dck], writes=[dck])
                self.tt(attnT[:, 4 * q:4 * q + 4, :], psQ[:].rearrange("p (a b) -> p a b", a=4), dc[:], ALU.mult,
                        reads=[psQk, dck], writes=[("dattnT", q)])
                bm, bmk = bmr.next()
                self.tt(bm[:], psB[:].rearrange("p (a b) -> p a b", a=4),
                        cst[:, C_MSTRICT2, :].unsqueeze(1).to_broadcast([128, 4, 128]), ALU.mult, reads=[psBk, "consts"],
                        writes=[bmk])
                tm, tmk = tmr.next()
                self.tt(tm[:], psK[:].rearrange("p (a b) -> p a b", a=4), dc[:], ALU.mult, reads=[psKk, dck], writes=[tmk])
                self.tt(P[:, 4 * q:4 * q + 4, :], tm[:], bm[:], ALU.mult, reads=[tmk, bmk], writes=[("dP", q)])
                psT, psTk = self.psum.next()
                ptb = psT[:].bitcast(BF16)
                for i4 in range(4):
                    self.tr(ptb[:, i4 * 128:(i4 + 1) * 128], P[:, 4 * q + i4, :], self.ident_b(), reads=[("dP", q), "cbf"],
                            writes=[psTk])
                self.copy(PL[:, 4 * q:4 * q + 4, :], ptb[:, 0:512].rearrange("p (a b) -> p a b", a=4), reads=[psTk],
                          writes=[("dPL", q)], eng="act")
                self.tt(R32[:, 4 * q:4 * q + 4, :], identf4, P[:, 4 * q:4 * q + 4, :], ALU.subtract,
                        reads=[("dP", q), "consts"], writes=[("dR32", q)])
                self.copy(Rb[:, 4 * q:4 * q + 4, :], R32[:, 4 * q:4 * q + 4, :], reads=[("dR32", q)], writes=[("dRb", q)],
                          eng="act")
            for lev in range(5):
                last = lev == 4
                for q in range(4):
                    if not last:
                        psP, psPk = self.psum.next()
                    psL, psLk = self.psum.next()
                    for i4 in range(4):
                        t = 4 * q + i4
                        cs = slice(i4 * 128, (i4 + 1) * 128)
                        if not last:
                            self.mm(psP[:, cs], PL[:, t, :], P[:, t, :], True, True, reads=[("dPL", q), ("dP", q)],
                                    writes=[psPk])
                        self.mm(psL[:, cs], P[:, t, :], PL[:, t, :], True, True, reads=[("dPL", q), ("dP", q)],
                                writes=[psLk])
                    if not last:
                        self.copy(P[:, 4 * q:4 * q + 4, :], psP[:].rearrange("p (a b) -> p a b", a=4), reads=[psPk],
                                  writes=[("dP", q)], eng="dve")
                    self.copy(PL[:, 4 * q:4 * q + 4, :], psL[:].rearrange("p (a b) -> p a b", a=4), reads=[psLk],
                              writes=[("dPL", q)], eng="act")
                for q in range(4):
                    psR, psRk = self.psum.next()
                    for i4 in range(4):
                        t = 4 * q + i4
                        cs = slice(i4 * 128, (i4 + 1) * 128)
                        self.mm(psR[:, cs], PL[:, t, :], Rb[:, t, :], True, True, reads=[("dPL", q), ("dRb", q)],
                                writes=[psRk])
                    self.tt(R32[:, 4 * q:4 * q + 4, :], R32[:, 4 * q:4 * q + 4, :],
                            psR[:].rearrange("p (a b) -> p a b", a=4), ALU.add, reads=[psRk, ("dR32", q)],
                            writes=[("dR32", q)])
                    self.copy(Rb[:, 4 * q:4 * q + 4, :], R32[:, 4 * q:4 * q + 4, :], reads=[("dR32", q)],
                              writes=[("dRb", q)], eng="act")
            for q in range(4):
                psU, psUk = self.psum.next()
                psW, psWk = self.psum.next()
                for i4 in range(4):
                    t = 4 * q + i4
                    cs = slice(i4 * 128, (i4 + 1) * 128)
                    self.mm(psU[:, cs], Rb[:, t, :], vb[:, t, :], True, True, reads=[("dRb", q), "dvb"], writes=[psUk])
                    self.mm(psW[:, cs], kbg[:, t, :], Rb[:, t, :], True, True, reads=[("dRb", q), "dkbg"], writes=[psWk])
                self.copy(u[:, 4 * q:4 * q + 4, :], psU[:].rearrange("p (a b) -> p a b", a=4), reads=[psUk],
                          writes=[("du", q)], eng="dve")
                self.copy(wT[:, 4 * q:4 * q + 4, :], psW[:].rearrange("p (a b) -> p a b", a=4), reads=[psWk],
                          writes=[("dwT", q)], eng="act")
            stk = self.key("dS")
            sbk = self.key("dSbf")
            self.memset(Sst[:], 0.0, writes=[stk])
            self.memset(Sbf[:], 0.0, writes=[sbk])
            for t in range(16):
                q = t // 4
                tsl = slice(t * 128, (t + 1) * 128)
                psG, psGk = self.psum.next()
                for kc in range(KC):
                    self.mm(psG[:, 0:128], self.hT[:, kc, tsl], wg[:, kc, :], kc == 0, kc == KC - 1,
                            reads=[(wk, 3), ("hT", kc, q)], writes=[psGk])
                sg, sgk = sgr.next()
                self.act(sg[:], psG[:, 0:128], AF.Silu, reads=[psGk], writes=[sgk])
                psO, psOk = self.psum_acc.next()
                vn, vnk = vnr.next()
                for c in range(2):
                    j = 2 * t + c
                    pb = 64 * c
                    prt = slice(pb, pb + 64)
                    psW2, psW2k = self.psum.next()
                    self.mm(psW2[prt, 0:128], wT[:, t, prt], Sbf[:], True, True, reads=[("dwT", q), sbk], writes=[psW2k])
                    self.tt(vn[prt, :], u[prt, t, :], psW2[prt, 0:128], ALU.subtract, reads=[("du", q), psW2k],
                            writes=[(vnk, c)])
                    self.mm(psO[prt, 0:128], qTn[:, t * 128 + pb:t * 128 + pb + 64], Sbf[:], True, True,
                            reads=[("dqTn", q), sbk], writes=[(psOk, c)])
                    self.mm(psO[prt, 128:256], attnT[prt, t, prt], vn[prt, :], True, True,
                            reads=[("dattnT", q), (vnk, c)], writes=[(psOk, c)])
                    psS, psSk = self.psum.next()
                    self.mm(psS[:, 0:128], kd[prt, t, :], vn[prt, :], True, True, reads=["dkd", (vnk, c)], writes=[psSk])
                    self.stt(Sst[:], Sst[:], eglast[:, j, h:h + 1], psS[:, 0:128], ALU.mult, ALU.add,
                             reads=[stk, psSk, "deglast"], writes=[stk])
                    self.copy(Sbf[:], Sst[:], reads=[stk], writes=[sbk], eng="act")
                t1, t1k = t1r.next()
                self.ts(t1[:], psO[:, 0:128], egc[:, t, h:h + 1], ALU.mult, reads=[(psOk, 0), (psOk, 1), "degc"], writes=[t1k])
                ot, otk = otr.next()
                self.tt(ot[:], t1[:], psO[:, 128:256], ALU.add, reads=[t1k, (psOk, 0), (psOk, 1)], writes=[otk])
                jk_, jkk = jr.next()
                ss, ssk = ssr.next()
                self.S.op("act", lambda e, jk_=jk_, ot=ot, ss=ss: e.activation(out=jk_[:], in_=ot[:], func=AF.Square,
                                                                            accum_out=ss[:, 0:1]),
                          reads=[otk], writes=[jkk, ssk])
                self.act(ss[:, 1:2], ss[:, 0:1], AF.Ln, reads=[ssk, "consts"], writes=[ssk], scale=1.0 / 128,
                         bias=self.eps_ap())
                self.act(ss[:, 1:2], ss[:, 1:2], AF.Exp, reads=[ssk], writes=[ssk], scale=-0.5)
                self.stt(ot[:], ot[:], ss[:, 1:2], nwb[:], ALU.mult, ALU.mult, reads=[otk, ssk, nwbk], writes=[otk])
                ob, obk = obr.next()
                self.tt(ob[:], ot[:], sg[:], ALU.mult, reads=[otk, sgk], writes=[obk])
                psX, psXk = self.psum.next()
                pxb = psX[:].bitcast(BF16)
                self.tr(pxb[:, 0:128], ob[:], self.ident_b(), reads=[obk, "cbf"], writes=[psXk])
                self.copy(oT[:, tsl], pxb[:, 0:128], reads=[psXk], writes=[("doT", t)], eng="act")
            self.dma(self.o_scr[0][h * 128:(h + 1) * 128, :], oT[:], reads=[("doT", t) for t in range(16)],
                     writes=[("o_scr", 0, h)])

    def final_out(self):
        A, S = self.A, self.S
        m = A.mark()
        sq = self.ring("fsq", 3, [128, 512], BF16)
        rs = self.ring("frs", 2, [128, 512], F32)
        hf = self.ring("hf", 3, [128, 512], F32)
        ost = self.ring("ost", 2, [128, 4, D], F32)
        for g in range(4):
            ps, pk = self.psum.next()
            sl = slice(g * 512, (g + 1) * 512)
            for kc in range(KC):
                q, qk = sq.next()
                self.act(q[:], self.xT[:, kc, sl], AF.Square, reads=[("xT", kc, g)], writes=[qk])
                self.mm(ps[:], self.ones_b(), q[:], kc == 0, kc == KC - 1, reads=[qk, "cbf"], writes=[pk])
            r, rk = rs.next()
            self.act(r[:], ps[:], AF.Ln, reads=[pk], writes=[rk], scale=1.0 / D, bias=self.eps_ap())
            self.act(r[:], r[:], AF.Exp, reads=[rk], writes=[rk], scale=-0.5)
            o, ok = ost.next()
            for kc in range(KC):
                h, hk = hf.next()
                self.stt(h[:], self.xT[:, kc, sl], self.nw[:, 2 * DEPTH, kc:kc + 1], r[:], ALU.mult, ALU.mult,
                         reads=[("xT", kc, g), rk, "nw"], writes=[hk])
                ps2, pk2 = self.psum.next()
                for t in range(4):
                    self.tr(ps2[:, t * 128:(t + 1) * 128], h[:, t * 128:(t + 1) * 128], self.ident_f(),
                            reads=[hk, "consts"], writes=[pk2])
                self.copy(o[:, :, kc * 128:(kc + 1) * 128], ps2[:].rearrange("p (t c) -> p t c", t=4),
                          reads=[pk2], writes=[ok], eng=("act" if kc % 2 else "dve"))
            self.dma(self.out_d[g * 512:(g + 1) * 512, :].rearrange("(t p) d -> p t d", p=128), o[:],
                     reads=[ok], writes=[("out", g)])
        S.barrier()
        A.reset(m)


C_IDENT, C_ONES, C_EPS, C_MSTRICT, C_TINC = 0, 1, 2, 3, 4
C_TRI2, C_TRI2NEG, C_BD, C_IND0, C_IND1, C_NEGINCL, C_NEGSTRICT, C_MINCL2, C_MSTRICT2, C_MSTRICT2T = 5, 6, 7, 8, 9, 10, 11, 12, 13, 14
NCONST = 15
NEG = -30000.0


def make_consts():
    c = np.zeros((128, NCONST, 128), np.float32)
    c[:, C_IDENT, :] = np.eye(128, dtype=np.float32)
    c[:, C_ONES, :] = 1.0
    c[:, C_EPS, :] = EPS
    ii = np.arange(128)
    c[:, C_MSTRICT, :] = (ii[:, None] < ii[None, :]).astype(np.float32)
    c[:, C_TINC, :] = (ii[:, None] >= ii[None, :]).astype(np.float32)
    same = (ii[:, None] // 64) == (ii[None, :] // 64)
    le = ii[:, None] <= ii[None, :]
    lt = ii[:, None] < ii[None, :]
    c[:, C_TRI2, :] = (same & le).astype(np.float32)
    c[:, C_TRI2NEG, :] = -c[:, C_TRI2, :]
    c[:, C_BD, :] = same.astype(np.float32)
    c[:, C_IND0, :] = (ii[:, None] < 64).astype(np.float32) * np.ones((1, 128), np.float32)
    c[:, C_IND1, :] = (ii[:, None] >= 64).astype(np.float32) * np.ones((1, 128), np.float32)
    c[:, C_NEGINCL, :] = np.where(same & le, 0.0, NEG)
    c[:, C_NEGSTRICT, :] = np.where(same & lt, 0.0, NEG)
    c[:, C_MINCL2, :] = (same & le).astype(np.float32)
    c[:, C_MSTRICT2, :] = (same & lt).astype(np.float32)
    c[:, C_MSTRICT2T, :] = (same & lt).T.astype(np.float32)
    return c.reshape(128, NCONST * 128)


_CACHE = {}


def get_nc(**kw):
    key = tuple(sorted((k, str(v)) for k, v in kw.items()))
    if key not in _CACHE:
        b = Builder(**kw)
        b.build()
        _CACHE[key] = b
    return _CACHE[key]


def run(inputs, **kw):
    b = get_nc(**kw)
    consts = make_consts()
    common = {
        "consts": consts,
        "w_in": np.ascontiguousarray(inputs["w_in"], dtype=np.float32),
        "norm_mix": np.ascontiguousarray(inputs["norm_mix"], dtype=np.float32),
        "norm_mlp": np.ascontiguousarray(inputs["norm_mlp"], dtype=np.float32),
        "norm_final": np.ascontiguousarray(inputs["norm_final"], dtype=np.float32).reshape(1, D),
        "w_branch": np.ascontiguousarray(inputs["w_branch"], dtype=np.float32),
        "w_out": np.ascontiguousarray(inputs["w_out"], dtype=np.float32),
        "w_up": np.ascontiguousarray(inputs["w_up"], dtype=np.float32),
        "w_down": np.ascontiguousarray(inputs["w_down"], dtype=np.float32),
    }
    for nm in ("ssm_conv_w", "ssm_conv_b", "ssm_a_log", "ssm_dt_bias", "ssm_d", "ssm_norm_w", "dn_conv_w", "dn_a_log",
               "dn_dt_bias", "dn_norm_w"):
        common[nm] = np.ascontiguousarray(inputs[nm], dtype=np.float32)
    x = np.asarray(inputs["x"], dtype=np.float32)
    in_maps = []
    for c in range(NCORES):
        m = dict(common)
        m["x"] = np.ascontiguousarray(x[c])
        in_maps.append(m)
    res = run_bass_kernel_spmd(b.nc, in_maps, core_ids=list(range(NCORES)))
    return res


def kernel(**inputs):
    res = run(inputs)
    out = np.stack([np.asarray(r["out"], dtype=np.float32) for r in res.results], axis=0)
    return out
```

```python
import numpy as np
import concourse.bass as bass
import concourse.mybir as mybir
from concourse.bass_utils import run_bass_kernel_spmd

F32 = mybir.dt.float32
BF16 = mybir.dt.bfloat16
AF = mybir.ActivationFunctionType
ALU = mybir.AluOpType

S_LEN = 2048
D = 1024
KC = 8
NCORES = 8
DEPTH = 2
D_FF = 4096
EPS = 1e-6
IN_SIZES = (3072, 1024, 8, 8, 3072, 1024, 2048, 16, 3072)
IN_DIM = sum(IN_SIZES)
OFF = [0]
for _s in IN_SIZES:
    OFF.append(OFF[-1] + _s)
(O_DNQKV, O_DNGATE, O_DNA, O_DNB, O_SBQKV, O_SSMZ, O_SSMXBC, O_SSMDT, O_GATE, _) = OFF


class _Op:
    __slots__ = ("id", "eng", "fn", "dma", "waits", "idx", "sig", "reuse", "seen")


class Sched:
    ENG = ("pe", "act", "dve", "pool", "sp")
    NSLOT = 12
    SEM_LIMIT = 20000

    def __init__(self, nc):
        self.nc = nc
        self.ops = []
        self.by_eng = {e: [] for e in self.ENG}
        self.kstate = {}
        self.seen = {e: {p: -1 for p in self.ENG} for e in self.ENG}
        self.seen_dma = {e: set() for e in self.ENG}
        self.pending = {e: set() for e in self.ENG}
        self.open_dma = []

    def op(self, eng, fn, reads=(), writes=(), dma=False):
        o = _Op()
        o.id = len(self.ops)
        o.eng = eng
        o.fn = fn
        o.dma = dma
        o.sig = None
        o.reuse = None
        deps = {}
        for k in reads:
            st = self.kstate.get(k)
            if st is not None and st[0] is not None:
                deps[st[0]] = True
        for k in writes:
            st = self.kstate.get(k)
            if st is not None:
                if st[0] is not None:
                    deps.setdefault(st[0], False)
                for r in st[1].values():
                    deps.setdefault(r, False)
                for r in st[2]:
                    deps.setdefault(r, False)
        for d in self.pending[eng]:
            deps[d] = True
        self.pending[eng] = set()
        o.idx = len(self.by_eng[eng])
        seen = self.seen[eng]
        best = {}
        waits = []
        for d, raw in deps.items():
            p = self.ops[d]
            if p.dma:
                if d in self.seen_dma[eng]:
                    continue
                self.seen_dma[eng].add(d)
                waits.append(d)
            else:
                if p.eng == eng and not dma:
                    if eng == "pe" or (not raw and eng != "pool"):
                        continue
                if p.idx <= seen[p.eng]:
                    continue
                if p.eng not in best or self.ops[best[p.eng]].idx < p.idx:
                    best[p.eng] = d
        for pe, d in best.items():
            p = self.ops[d]
            waits.append(d)
            for e2, v in p.seen.items():
                if v > seen[e2]:
                    seen[e2] = v
            if p.idx > seen[pe]:
                seen[pe] = p.idx
        o.waits = waits
        o.seen = dict(seen)
        for k in reads:
            st = self.kstate.get(k)
            if st is None:
                st = [None, {}, []]
                self.kstate[k] = st
            if dma:
                st[2].append(o.id)
            else:
                st[1][eng] = o.id
        for k in writes:
            self.kstate[k] = [o.id, {}, []]
        self.ops.append(o)
        self.by_eng[eng].append(o)
        if dma:
            self.open_dma.append(o.id)
        return o

    def barrier(self):
        last = [self.by_eng[e][-1].id for e in self.ENG if self.by_eng[e] and not self.by_eng[e][-1].dma]
        for e in self.ENG:
            self.pending[e] |= set(last) | set(self.open_dma[-self.NSLOT:])
        self.open_dma = self.open_dma[-self.NSLOT:]

    def finalize(self):
        nc = self.nc
        needed = set()
        for o in self.ops:
            needed.update(o.waits)
        for e in self.ENG:
            sem = None
            cnt = 0
            ndma = 0
            slots = None
            for o in self.by_eng[e]:
                if o.dma:
                    if slots is None:
                        slots = [nc.alloc_semaphore(f"dq_{e}_{i}") for i in range(self.NSLOT)]
                    s = ndma % self.NSLOT
                    r = ndma // self.NSLOT
                    o.sig = (slots[s], 16 * (r + 1))
                    if r > 0:
                        o.reuse = (slots[s], 16 * r)
                    ndma += 1
                elif o.id in needed:
                    if sem is None or cnt >= self.SEM_LIMIT:
                        sem = nc.alloc_semaphore(f"pg_{e}_{o.id}")
                        cnt = 0
                    cnt += 1
                    o.sig = (sem, cnt)

    def emit(self, ename, eng):
        ops = self.ops
        for o in self.by_eng[ename]:
            for d in o.waits:
                s, v = ops[d].sig
                eng.wait_ge(s, v)
            if o.reuse is not None:
                eng.wait_ge(o.reuse[0], o.reuse[1])
            if o.fn is None:
                continue
            ins = o.fn(eng)
            if o.sig is not None:
                ins.then_inc(o.sig[0], 16 if o.dma else 1)


class Arena:
    def __init__(self, nc, limit=208 * 1024):
        self.nc = nc
        self.off = 16 * 1024
        self.limit = limit
        self.n = 0
        self.peak = 0

    def alloc(self, name, shape, dtype):
        per = 1
        for s in shape[1:]:
            per *= s
        nbytes = per * (4 if dtype == F32 else 2)
        nbytes = (nbytes + 63) // 64 * 64
        assert self.off + nbytes <= self.limit, f"SBUF arena overflow at {name}: {self.off}+{nbytes}"
        self.n += 1
        t = self.nc.alloc_sbuf_tensor_at(f"{name}_{self.n}", list(shape), dtype, offset=self.off)
        self.off += nbytes
        self.peak = max(self.peak, self.off)
        return t

    def mark(self):
        return self.off

    def reset(self, m):
        self.off = m


class Ring:
    def __init__(self, tiles, name):
        self.tiles = tiles
        self.name = name
        self.i = 0

    def next(self):
        j = self.i % len(self.tiles)
        self.i += 1
        return self.tiles[j], (self.name, j)


class Builder:
    def __init__(self, nlayers=DEPTH, mixers=("dn", "sb", "ssm"), do_mlp=True, debug=()):
        self.nlayers = nlayers
        self.mixers = mixers
        self.do_mlp = do_mlp
        self.debug = debug
        nc = bass.Bass("TRN2", target_bir_lowering=False)
        self.nc = nc
        self.S = Sched(nc)
        self.A = Arena(nc)
        self.uid = 0

    def dram_in(self, name, shape, dtype=F32):
        return self.nc.dram_tensor(name, list(shape), dtype, kind="ExternalInput").ap()

    def dram_out(self, name, shape, dtype=F32):
        return self.nc.dram_tensor(name, list(shape), dtype, kind="ExternalOutput").ap()

    def dram_tmp(self, name, shape, dtype):
        return self.nc.dram_tensor(name, list(shape), dtype, kind="Internal").ap()

    def key(self, base):
        self.uid += 1
        return (base, self.uid)

    def ring(self, name, n, shape, dtype):
        return Ring([self.A.alloc(name, shape, dtype) for _ in range(n)], self.key(name))

    def dma(self, out, in_, reads, writes, slow=False):
        if slow:
            fn = lambda e, out=out, in_=in_: e.dma_start(out=out, in_=in_, allow_slow_non_contiguous=True)
        else:
            fn = lambda e, out=out, in_=in_: e.dma_start(out=out, in_=in_)
        return self.S.op("sp", fn, reads=reads, writes=writes, dma=True)

    def mm(self, out, lhsT, rhs, start, stop, reads, writes, **kw):
        return self.S.op(
            "pe",
            lambda e, out=out, lhsT=lhsT, rhs=rhs, start=start, stop=stop, kw=kw: e.matmul(
                out, lhsT=lhsT, rhs=rhs, start=start, stop=stop, **kw
            ),
            reads=reads,
            writes=writes,
        )

    def tr(self, out, in_, ident, reads, writes):
        return self.S.op(
            "pe",
            lambda e, out=out, in_=in_, ident=ident: e.transpose(out, in_, ident),
            reads=reads,
            writes=writes,
        )

    def act(self, out, in_, func, reads, writes, eng="act", **kw):
        return self.S.op(
            eng,
            lambda e, out=out, in_=in_, func=func, kw=kw: e.activation(out=out, in_=in_, func=func, **kw),
            reads=reads,
            writes=writes,
        )

    def tt(self, out, in0, in1, op, reads, writes, eng="dve"):
        return self.S.op(
            eng,
            lambda e, out=out, in0=in0, in1=in1, op=op: e.tensor_tensor(out=out, in0=in0, in1=in1, op=op),
            reads=reads,
            writes=writes,
        )

    def ts(self, out, in0, s1, op0, reads, writes, s2=None, op1=None, eng="dve"):
        def fn(e, out=out, in0=in0, s1=s1, op0=op0, s2=s2, op1=op1):
            if op1 is None:
                return e.tensor_scalar(out=out, in0=in0, scalar1=s1, scalar2=None, op0=op0)
            return e.tensor_scalar(out=out, in0=in0, scalar1=s1, scalar2=s2, op0=op0, op1=op1)

        return self.S.op(eng, fn, reads=reads, writes=writes)

    def stt(self, out, in0, scalar, in1, op0, op1, reads, writes):
        return self.S.op(
            "dve",
            lambda e, out=out, in0=in0, scalar=scalar, in1=in1, op0=op0, op1=op1: e.scalar_tensor_tensor(
                out=out, in0=in0, scalar=scalar, in1=in1, op0=op0, op1=op1
            ),
            reads=reads,
            writes=writes,
        )

    def copy(self, out, in_, reads, writes, eng="dve"):
        if eng == "act":
            return self.act(out, in_, AF.Copy, reads, writes)
        return self.S.op(
            eng, lambda e, out=out, in_=in_: e.tensor_copy(out=out, in_=in_), reads=reads, writes=writes
        )

    def memset(self, ap, val, writes, eng="dve"):
        return self.S.op(eng, lambda e, ap=ap, val=val: e.memset(ap, val), reads=(), writes=writes)

    def build(self):
        nc, S, A = self.nc, self.S, self.A
        L = self.nlayers
        self.x_d = self.dram_in("x", [S_LEN, D])
        self.consts_d = self.dram_in("consts", [128, NCONST * 128])
        self.w_in_d = self.dram_in("w_in", [DEPTH, D, IN_DIM])
        self.norm_mix_d = self.dram_in("norm_mix", [DEPTH, D])
        self.norm_mlp_d = self.dram_in("norm_mlp", [DEPTH, D])
        self.norm_final_d = self.dram_in("norm_final", [1, D])
        self.w_branch_d = self.dram_in("w_branch", [DEPTH, 3, D, D])
        self.w_out_d = self.dram_in("w_out", [DEPTH, D, D])
        self.w_up_d = self.dram_in("w_up", [DEPTH, D, D_FF])
        self.w_down_d = self.dram_in("w_down", [DEPTH, D_FF, D])
        self.ssm_conv_w_d = self.dram_in("ssm_conv_w", [DEPTH, 4, 2048])
        self.ssm_conv_b_d = self.dram_in("ssm_conv_b", [DEPTH, 2048])
        self.ssm_a_log_d = self.dram_in("ssm_a_log", [DEPTH, 16])
        self.ssm_dt_bias_d = self.dram_in("ssm_dt_bias", [DEPTH, 16])
        self.ssm_d_d = self.dram_in("ssm_d", [DEPTH, 16])
        self.ssm_norm_w_d = self.dram_in("ssm_norm_w", [DEPTH, D])
        self.dn_conv_w_d = self.dram_in("dn_conv_w", [DEPTH, 4, 3072])
        self.dn_a_log_d = self.dram_in("dn_a_log", [DEPTH, 8])
        self.dn_dt_bias_d = self.dram_in("dn_dt_bias", [DEPTH, 8])
        self.dn_norm_w_d = self.dram_in("dn_norm_w", [DEPTH, 128])
        self.out_d = self.dram_out("out", [S_LEN, D])
        self.u_scr = self.dram_tmp("u_scr", [D_FF, S_LEN], BF16)
        if self.debug:
            self.o_scr = self.dram_out("o_scr", [3, D, S_LEN], BF16)
        else:
            self.o_scr = self.dram_tmp("o_scr", [3, D, S_LEN], BF16)
        self.g_scr = self.dram_tmp("g_scr", [3, D, S_LEN], BF16)

        pst = [nc.alloc_psum_tensor(f"ps{i}", [128, 512], F32) for i in range(8)]
        self.pst = pst
        self.psum = Ring(pst[:6], "psum")
        self.psum_acc = Ring(pst[6:], "psacc")

        self.consts = A.alloc("consts", [128, NCONST, 128], F32)
        self.cbf = A.alloc("cbf", [128, NCONST, 128], BF16)
        self.xT = A.alloc("xT", [128, KC, S_LEN], F32)
        self.nw = A.alloc("nw", [128, 2 * DEPTH + 1, KC], F32)
        KCON = "consts"
        self.dma(self.consts[:].rearrange("p a b -> p (a b)"), self.consts_d, reads=[], writes=[KCON])
        self.copy(self.cbf[:], self.consts[:], reads=[KCON], writes=["cbf"])
        for l in range(DEPTH):
            self.dma(self.nw[:, 2 * l, :], self.norm_mix_d[l].rearrange("(k p) -> p k", p=128), [], ["nw"], slow=True)
            self.dma(self.nw[:, 2 * l + 1, :], self.norm_mlp_d[l].rearrange("(k p) -> p k", p=128), [], ["nw"], slow=True)
        self.dma(self.nw[:, 2 * DEPTH, :], self.norm_final_d[0].rearrange("(k p) -> p k", p=128), [], ["nw"], slow=True)

        self.load_x()
        for l in range(L):
            self.layer(l)
        self.final_out()
        S.barrier()
        S.op("sp", None)
        S.finalize()

        with nc.Block() as block:

            @block.tensor
            def _(e):
                S.emit("pe", e)

            @block.scalar
            def _(e):
                S.emit("act", e)

            @block.vector
            def _(e):
                S.emit("dve", e)

            @block.gpsimd
            def _(e):
                S.emit("pool", e)

            @block.sync
            def _(e):
                S.emit("sp", e)

        return nc

    def ident_f(self):
        return self.consts[:, C_IDENT, :]

    def ident_b(self):
        return self.cbf[:, C_IDENT, :]

    def ones_b(self):
        return self.cbf[:, C_ONES, :]

    def load_x(self):
        A, S = self.A, self.S
        m = A.mark()
        stg = self.ring("xstg", 2, [128, 4, D], F32)
        for tg in range(4):
            st, sk = stg.next()
            self.dma(
                st[:],
                self.x_d[tg * 512:(tg + 1) * 512, :].rearrange("(t p) d -> p t d", p=128),
                reads=[],
                writes=[sk],
            )
            for kc in range(KC):
                ps, pk = self.psum.next()
                for t in range(4):
                    self.tr(ps[:, t * 128:(t + 1) * 128], st[:, t, kc * 128:(kc + 1) * 128], self.ident_f(),
                            reads=[sk, "consts"], writes=[pk])
                self.copy(self.xT[:, kc, tg * 512:(tg + 1) * 512], ps[:], reads=[pk], writes=[("xT", kc, tg)],
                          eng=("act" if kc % 2 else "dve"))
        S.barrier()
        A.reset(m)

    def rmsnorm(self):
        A, S = self.A, self.S
        m = A.mark()
        sq = self.ring("sq", 3, [128, 512], BF16)
        rs = self.ring("rs", 2, [128, 512], F32)
        for g in range(4):
            ps, pk = self.psum.next()
            sl = slice(g * 512, (g + 1) * 512)
            for kc in range(KC):
                q, qk = sq.next()
                self.act(q[:], self.xT[:, kc, sl], AF.Square, reads=[("xT", kc, g)], writes=[qk])
                self.mm(ps[:], self.ones_b(), q[:], kc == 0, kc == KC - 1, reads=[qk, "cbf"], writes=[pk])
            r, rk = rs.next()
            self.act(r[:], ps[:], AF.Ln, reads=[pk], writes=[rk], scale=1.0 / D, bias=self.eps_ap())
            self.act(r[:], r[:], AF.Exp, reads=[rk], writes=[rk], scale=-0.5)
            for kc in range(KC):
                self.tt(self.hT[:, kc, sl], self.xT[:, kc, sl], r[:], ALU.mult,
                        reads=[("xT", kc, g), rk], writes=[("hT", kc, g)])
        self.rstd_ring = rs
        S.barrier()
        A.reset(m)

    def eps_ap(self):
        return self.consts[:, C_EPS, 0:1]

    def wload(self, w_ap, kcn, n, wst_ring, wbf_ring, scale=None):
        st, sk = wst_ring.next()
        wb, wk = wbf_ring.next()
        self.dma(st[:, :kcn, :n], w_ap.rearrange("(k p) n -> p k n", p=128), reads=[], writes=[sk])
        for kc in range(kcn):
            if scale is not None:
                self.act(wb[:, kc, :n], st[:, kc, :n], AF.Copy, reads=[sk, "nw"], writes=[wk],
                         scale=scale[:, kc:kc + 1])
            else:
                self.act(wb[:, kc, :n], st[:, kc, :n], AF.Copy, reads=[sk], writes=[wk])
        return wb, wk

    def hT_keys(self, g):
        return [("hT", kc, g) for kc in range(KC)]

    def linear_T(self, wb, wk, ncols, kcn, rhs_fn, rhs_keys_fn, evac, groups=range(4)):
        for c0 in range(0, ncols, 128):
            n = min(128, ncols - c0)
            for g in groups:
                ps, pk = self.psum.next()
                for kc in range(kcn):
                    self.mm(ps[:n, :], wb[:, kc, c0:c0 + n], rhs_fn(kc, g), kc == 0, kc == kcn - 1,
                            reads=[wk] + rhs_keys_fn(g), writes=[pk])
                evac(c0 // 128, g, ps, pk, n)

    def mlp(self, l):
        A, S = self.A, self.S
        m = A.mark()
        self.hT = A.alloc("hT", [128, KC, S_LEN], BF16)
        self.rmsnorm()
        wst = self.ring("wst", 2, [128, KC, 512], F32)
        wbf = self.ring("wbf", 2, [128, KC, 512], BF16)
        ub = self.ring("ub", 3, [128, S_LEN], BF16)
        rr = self.ring("rr", 3, [128, 512], F32)
        scale = self.nw[:, 2 * l + 1, :]
        for cb in range(D_FF // 512):
            wb, wk = self.wload(self.w_up_d[l][:, cb * 512:(cb + 1) * 512], KC, 512, wst, wbf, scale=scale)
            cur = {}

            def evac(ci, g, ps, pk, n, cb=cb, cur=cur):
                if g == 0:
                    cur["u"] = ub.next()
                u, uk = cur["u"]
                r, rk = rr.next()
                self.ts(r[:], ps[:], 0.0, ALU.max, reads=[pk], writes=[rk])
                self.act(u[:, g * 512:(g + 1) * 512], r[:], AF.Square, reads=[rk], writes=[uk])
                if g == 3:
                    f = cb * 4 + ci
                    self.dma(self.u_scr[f * 128:(f + 1) * 128, :], u[:], reads=[uk], writes=[("u_scr", f)])

            self.linear_T(wb, wk, 512, KC, lambda kc, g: self.hT[:, kc, g * 512:(g + 1) * 512],
                          self.hT_keys, evac)
        S.barrier()
        A.reset(m)
        wst = self.ring("wdst", 2, [128, 4, 512], F32)
        wbf = self.ring("wdbf", 1, [128, 32, 512], BF16)
        ur = self.ring("ur", 1, [128, 32, 512], BF16)
        for half in range(2):
            wb, wk = wbf.next()
            for q in range(8):
                st, sk = wst.next()
                self.dma(st[:], self.w_down_d[l][q * 512:(q + 1) * 512, half * 512:(half + 1) * 512]
                         .rearrange("(k p) n -> p k n", p=128), reads=[], writes=[sk])
                for k4 in range(4):
                    self.act(wb[:, q * 4 + k4, :], st[:, k4, :], AF.Copy, reads=[sk], writes=[wk])
            for g in range(4):
                u, uk = ur.next()
                self.dma(u[:], self.u_scr[:, g * 512:(g + 1) * 512].rearrange("(k p) t -> p k t", p=128),
                         reads=[("u_scr", f) for f in range(32)], writes=[uk])
                for ci in range(4):
                    ps, pk = self.psum.next()
                    for kc in range(32):
                        self.mm(ps[:], wb[:, kc, ci * 128:(ci + 1) * 128], u[:, kc, :], kc == 0, kc == 31,
                                reads=[wk, uk], writes=[pk])
                    oc = half * 4 + ci
                    sl = slice(g * 512, (g + 1) * 512)
                    self.tt(self.xT[:, oc, sl], self.xT[:, oc, sl], ps[:], ALU.add,
                            reads=[pk, ("xT", oc, g)], writes=[("xT", oc, g)])
        S.barrier()
        A.reset(m)

    def layer(self, l):
        if self.mixers:
            self.mixer(l)
        if self.do_mlp:
            self.mlp(l)


    def mixer(self, l):
        A, S = self.A, self.S
        m0 = A.mark()
        self.hT = A.alloc("hT", [128, KC, S_LEN], BF16)
        self.rmsnorm()
        m1 = A.mark()
        if "sb" in self.mixers:
            self.sb_phase(l)
            S.barrier()
            A.reset(m1)
        if "ssm" in self.mixers:
            self.ssm_phase(l)
            S.barrier()
            A.reset(m1)
        if "dn" in self.mixers:
            self.dn_phase(l)
            S.barrier()
            A.reset(m1)
        self.gates_phase(l)
        S.barrier()
        A.reset(m0)
        self.merge_phase(l)
        S.barrier()
        A.reset(m0)

    def branches(self):
        return [i for i, n in enumerate(("dn", "sb", "ssm")) if n in self.mixers]

    def gates_phase(self, l):
        wst = self.ring("gwst", 2, [128, KC, 512], F32)
        wbf = self.ring("gwbf", 2, [128, KC, 512], BF16)
        gb = self.ring("gb", 3, [128, S_LEN], BF16)
        scale = self.nw[:, 2 * l, :]
        for i in self.branches():
            for cb in range(2):
                c0 = O_GATE + i * D + cb * 512
                wb, wk = self.wload(self.w_in_d[l][:, c0:c0 + 512], KC, 512, wst, wbf, scale=scale)
                cur = {}

                def evac(ci, g, ps, pk, n, cb=cb, i=i, cur=cur):
                    if g == 0:
                        cur["t"] = gb.next()
                    t, tk = cur["t"]
                    self.act(t[:, g * 512:(g + 1) * 512], ps[:], AF.Sigmoid, reads=[pk], writes=[tk])
                    if g == 3:
                        oc = cb * 4 + ci
                        self.dma(self.g_scr[i][oc * 128:(oc + 1) * 128, :], t[:], reads=[tk],
                                 writes=[("g_scr", i, oc)])

                self.linear_T(wb, wk, 512, KC, lambda kc, g: self.hT[:, kc, g * 512:(g + 1) * 512],
                              self.hT_keys, evac)

    def merge_phase(self, l):
        A, S = self.A, self.S
        wst = self.ring("mwst", 1, [128, KC, 512], F32)
        wbf = self.ring("mwbf", 2, [128, KC, 512], BF16)
        mg = A.alloc("mg", [128, KC, 1024], F32)
        mgb = A.alloc("mgb", [128, KC, 1024], BF16)
        ob = self.ring("ob", 1, [128, KC, 1024], BF16)
        gt = self.ring("gt", 3, [128, 1024], BF16)
        self.mtmp = self.ring("mtmp", 3, [128, 512], F32)
        brs = self.branches()
        for half in range(2):
            tsl = slice(half * 1024, (half + 1) * 1024)
            for bi, i in enumerate(brs):
                o, ok = ob.next()
                self.dma(o[:], self.o_scr[i][:, tsl].rearrange("(k p) t -> p k t", p=128),
                         reads=[("o_scr", i, c) for c in range(8)], writes=[ok])
                for cb in range(2):
                    wb, wk = self.wload(self.w_branch_d[l][i][:, cb * 512:(cb + 1) * 512], KC, 512, wst, wbf)
                    cur = {}

                    def evac(ci, g, ps, pk, n, cb=cb, i=i, bi=bi, cur=cur, half=half, tsl=tsl):
                        oc = cb * 4 + ci
                        gl = g - 2 * half
                        if gl == 0:
                            cur["g"] = gt.next()
                            t, tk = cur["g"]
                            self.dma(t[:], self.g_scr[i][oc * 128:(oc + 1) * 128, tsl],
                                     reads=[("g_scr", i, oc)], writes=[tk])
                        t, tk = cur["g"]
                        dst = mg[:, oc, gl * 512:(gl + 1) * 512]
                        mk = ("mg", oc, gl)
                        if bi == 0:
                            self.tt(dst, ps[:], t[:, gl * 512:(gl + 1) * 512], ALU.mult,
                                    reads=[pk, tk], writes=[mk])
                        else:
                            tmp, tmk = self.mtmp.next()
                            self.tt(tmp[:], ps[:], t[:, gl * 512:(gl + 1) * 512], ALU.mult,
                                    reads=[pk, tk], writes=[tmk])
                            self.tt(dst, dst, tmp[:], ALU.add, reads=[tmk, mk], writes=[mk], eng="pool")
                        if bi == len(brs) - 1:
                            self.copy(mgb[:, oc, gl * 512:(gl + 1) * 512], dst, reads=[mk],
                                      writes=[("mgb", oc, gl)], eng="act")

                    self.linear_T(wb, wk, 512, KC, lambda kc, g, o=o, half=half: o[:, kc, (g - 2 * half) * 512:(g - 2 * half + 1) * 512],
                                  lambda g, ok=ok: [ok], evac, groups=(2 * half, 2 * half + 1))
            for cb in range(2):
                wb, wk = self.wload(self.w_out_d[l][:, cb * 512:(cb + 1) * 512], KC, 512, wst, wbf)

                def evac2(ci, g, ps, pk, n, cb=cb):
                    oc = cb * 4 + ci
                    sl = slice(g * 512, (g + 1) * 512)
                    self.tt(self.xT[:, oc, sl], self.xT[:, oc, sl], ps[:], ALU.add,
                            reads=[pk, ("xT", oc, g)], writes=[("xT", oc, g)])

                self.linear_T(wb, wk, 512, KC,
                              lambda kc, g, half=half: mgb[:, kc, (g - 2 * half) * 512:(g - 2 * half + 1) * 512],
                              lambda g, half=half: [("mgb", kc, g - 2 * half) for kc in range(KC)], evac2,
                              groups=(2 * half, 2 * half + 1))

    def sb_phase(self, l):
        A, S = self.A, self.S
        R = {}
        R["wst"] = self.ring("sbwst", 1, [128, KC, 384], F32)
        R["wbf"] = self.ring("sbwbf", 2, [128, KC, 384], BF16)
        R["qT"] = self.ring("sbq", 2, [128, S_LEN], BF16)
        R["kT"] = self.ring("sbk", 2, [128, S_LEN], BF16)
        R["v"] = self.ring("sbv", 2, [128, 16, 128], BF16)
        R["osb"] = self.ring("sbo", 2, [128, S_LEN], BF16)
        R["e"] = self.ring("sbe", 4, [128, 512], F32)
        R["spb"] = self.ring("sbsp", 4, [128, 512], BF16)
        R["xa"] = self.ring("sbxa", 2, [128, 512], F32)
        R["w"] = self.ring("sbw", 3, [128, 512], BF16)
        R["pa"] = Ring(self.pst[0:4], "psum")
        for hp in range(8):
            self.sb_unit(l, hp, R)

    def sb_unit(self, l, hp, R):
        scale = self.nw[:, 2 * l, :]
        st, sk = R["wst"].next()
        wb, wk = R["wbf"].next()
        for j in range(3):
            base = O_SBQKV + j * 1024 + 128 * hp
            self.dma(st[:, :, j * 128:(j + 1) * 128],
                     self.w_in_d[l][:, base:base + 128].rearrange("(k p) n -> p k n", p=128),
                     reads=[], writes=[(sk, j)])
        for kc in range(KC):
            self.act(wb[:, kc, :], st[:, kc, :], AF.Copy, reads=[(sk, 0), (sk, 1), (sk, 2), "nw"], writes=[wk],
                     scale=scale[:, kc:kc + 1])
        qT, qk = R["qT"].next()
        kT, kk = R["kT"].next()
        v, vk = R["v"].next()

        def evac_qk(ci, g, ps, pk, n):
            dst, dk = (qT, qk) if ci == 0 else (kT, kk)
            self.copy(dst[:, g * 512:(g + 1) * 512], ps[:], reads=[pk], writes=[(dk, g)],
                      eng=("act" if g % 2 else "dve"))

        self.linear_T(wb, wk, 256, KC, lambda kc, g: self.hT[:, kc, g * 512:(g + 1) * 512], self.hT_keys, evac_qk)
        for tq in range(4):
            ps, pk = self.psum.next()
            for t in range(4):
                tt_ = tq * 4 + t
                for kc in range(KC):
                    self.mm(ps[:, t * 128:(t + 1) * 128], self.hT[:, kc, tt_ * 128:(tt_ + 1) * 128],
                            wb[:, kc, 256:384], kc == 0, kc == KC - 1, reads=[wk, ("hT", kc, tq)], writes=[pk])
            self.copy(v[:, tq * 4:(tq + 1) * 4, :], ps[:].rearrange("p (t c) -> p t c", t=4), reads=[pk],
                      writes=[(vk, tq)], eng=("act" if tq % 2 else "dve"))
        osb, ok = R["osb"].next()
        mstrict = self.consts[:, C_MSTRICT, :]
        tinc = self.cbf[:, C_TINC, :]
        tlow = self.cbf[:, C_MSTRICT, :]
        one_col = self.consts[:, C_ONES, 0:1]
        pst = self.pst
        pa_ring = R["pa"]
        for g in range(4):
            racc = [(pst[4], ("psum", 4)), (pst[5], ("psum", 5))]
            po = [(pst[6], ("psacc", 0)), (pst[7], ("psacc", 1))]
            tiles = []
            for kb in range(4 * g + 3, -1, -1):
                for e in range(2):
                    t0 = max(kb * 128, g * 512)
                    tiles.append(dict(e=e, kb=kb, pb=64 * e, t0=t0, N=(g + 1) * 512 - t0, c0=t0 - g * 512,
                                      diag=kb * 128 >= g * 512, first=(kb == 4 * g + 3), last=(kb == 0)))

            def stage0(T):
                pa, pak = pa_ring.next()
                pb, N, t0, kb = T["pb"], T["N"], T["t0"], T["kb"]
                self.mm(pa[:, :N], kT[pb:pb + 64, kb * 128:(kb + 1) * 128], qT[pb:pb + 64, t0:t0 + N], True, True,
                        reads=[(kk, kb // 4), (qk, g)], writes=[pak])
                T["pa"], T["pak"] = pa, pak

            def stage1(T):
                N, c0 = T["N"], T["c0"]
                ee, ek = R["e"].next()
                self.act(ee[:, :N], T["pa"][:, :N], AF.Exp, reads=[T["pak"]], writes=[ek], scale=0.125)
                if T["diag"]:
                    self.tt(ee[:, :128], ee[:, :128], mstrict, ALU.mult, reads=[ek, "consts"], writes=[ek])
                spb, spk = R["spb"].next()
                self.act(spb[:, :N], ee[:, :N], AF.Ln, reads=[ek, "consts"], writes=[spk], bias=one_col, scale=1.0)
                ra, rak = racc[T["e"]]
                self.mm(ra[:, c0:512], tinc, spb[:, :N], T["first"], False, reads=[spk, "cbf"], writes=[rak],
                        skip_group_check=True)
                T.update(ee=ee, ek=ek, spb=spb, spk=spk)

            def stage2(T):
                N, c0, pb, kb = T["N"], T["c0"], T["pb"], T["kb"]
                ra, rak = racc[T["e"]]
                xa, xk = R["xa"].next()
                self.act(xa[:, :N], ra[:, c0:512], AF.Exp, reads=[rak], writes=[xk], scale=-1.0)
                if not T["last"]:
                    self.mm(ra[:, c0:512], tlow, T["spb"][:, :N], False, True, reads=[T["spk"], "cbf"], writes=[rak],
                            skip_group_check=True)
                w, wwk = R["w"].next()
                self.tt(w[:, :N], T["ee"][:, :N], xa[:, :N], ALU.mult, reads=[T["ek"], xk], writes=[wwk])
                pp, ppk = po[T["e"]]
                self.mm(pp[pb:pb + 64, c0:512], v[:, kb, pb:pb + 64], w[:, :N], T["first"], T["last"],
                        reads=[(vk, kb // 4), wwk], writes=[ppk], skip_group_check=True)

            n = len(tiles)
            for i in range(min(2, n)):
                stage0(tiles[i])
            for i in range(n + 2):
                if i - 2 >= 0:
                    stage2(tiles[i - 2])
                if i < n:
                    stage1(tiles[i])
                if i + 2 < n:
                    stage0(tiles[i + 2])
            for e in range(2):
                pp, ppk = po[e]
                pb = 64 * e
                self.copy(osb[pb:pb + 64, g * 512:(g + 1) * 512], pp[pb:pb + 64, :], reads=[ppk],
                          writes=[(ok, e, g)], eng="dve")
        self.dma(self.o_scr[1][hp * 128:(hp + 1) * 128, :], osb[:],
                 reads=[(ok, e, g) for e in range(2) for g in range(4)], writes=[("o_scr", 1, hp)])


    def bload(self, name, dram_row_ap, n):
        t = self.A.alloc(name, [128, n], F32)
        k = self.key(name)
        self.dma(t[:], dram_row_ap.partition_broadcast(128), reads=[], writes=[k])
        return t, k

    def conv_silu(self, l, wb, wk, wcol, conv_w_d, conv_b_d, ch, raw, rawk, acc, acck, cw, dst, dstk, func=AF.Silu):
        c, ck = cw.next()
        self.dma(c[:, 0:4], conv_w_d[l][:, ch:ch + 128].rearrange("k c -> c k"), reads=[], writes=[(ck, 0)], slow=True)
        if conv_b_d is not None:
            self.dma(c[:, 4:5], conv_b_d[l][ch:ch + 128].rearrange("(c o) -> c o", o=1), reads=[], writes=[(ck, 1)],
                     slow=True)
        else:
            self.memset(c[:, 4:5], 0.0, writes=[(ck, 1)])

        def evac(ci, g, ps, pk, n):
            self.copy(raw[:, 3 + g * 512:3 + (g + 1) * 512], ps[:], reads=[pk], writes=[(rawk, g)],
                      eng=("act" if g % 2 else "dve"))

        for g in range(4):
            ps, pk = self.psum.next()
            for kc in range(KC):
                self.mm(ps[:], wb[:, kc, wcol:wcol + 128], self.hT[:, kc, g * 512:(g + 1) * 512], kc == 0, kc == KC - 1,
                        reads=[wk] + self.hT_keys(g), writes=[pk])
            evac(0, g, ps, pk, 128)
        rk = [(rawk, g) for g in range(4)] + [(rawk, "pad")]
        self.ts(acc[:], raw[:, 0:S_LEN], c[:, 0:1], ALU.mult, reads=rk + [(ck, 0), (ck, 1)], writes=[acck],
                s2=c[:, 4:5], op1=ALU.add)
        for k in range(1, 4):
            self.stt(acc[:], raw[:, k:k + S_LEN], c[:, k:k + 1], acc[:], ALU.mult, ALU.add,
                     reads=rk + [(ck, 0), acck], writes=[acck])
        self.act(dst, acc[:], func, reads=[acck], writes=[dstk])

    def to_tok(self, src, srck, dst_fn, dstk):
        for tq in range(4):
            ps, pk = self.psum.next()
            pb = ps[:].bitcast(BF16)
            for t in range(4):
                tt_ = tq * 4 + t
                self.tr(pb[:, t * 128:(t + 1) * 128], src[:, tt_ * 128:(tt_ + 1) * 128], self.ident_b(),
                        reads=(list(srck) if isinstance(srck, list) else [srck]) + ["cbf"], writes=[pk])
            self.copy(dst_fn(tq), pb[:, 0:512].rearrange("p (t c) -> p t c", t=4), reads=[pk], writes=[(dstk, tq)],
                      eng=("act" if tq % 2 else "dve"))

    def ssm_phase(self, l):
        A, S = self.A, self.S
        scale = self.nw[:, 2 * l, :]
        cst = self.consts
        wdst = A.alloc("wdtst", [128, KC, 16], F32)
        wdt = A.alloc("wdt", [128, KC, 16], BF16)
        self.dma(wdst[:], self.w_in_d[l][:, O_SSMDT:O_SSMDT + 16].rearrange("(k p) n -> p k n", p=128), [], ["wdtst"])
        for kc in range(KC):
            self.act(wdt[:, kc, :], wdst[:, kc, :], AF.Copy, reads=["wdtst", "nw"], writes=["wdt"], scale=scale[:, kc:kc + 1])
        dtb, dtbk = self.bload("dtb", self.ssm_dt_bias_d[l], 16)
        alog, alogk = self.bload("alog", self.ssm_a_log_d[l], 16)
        dbc, dbck = self.bload("dbc", self.ssm_d_d[l], 16)
        dt = A.alloc("dt", [128, 16, 16], F32)
        av = A.alloc("av", [128, 16, 16], F32)
        acum = A.alloc("acum", [128, 16, 16], F32)
        eacum = A.alloc("eacum", [128, 16, 16], F32)
        dtds = A.alloc("dtds", [128, 16, 16], F32)
        eatot = A.alloc("eatot", [128, 32, 16], F32)
        tmp = A.alloc("ptmp", [128, 16, 16], F32)
        ps, pk = self.psum.next()
        for t in range(16):
            for kc in range(KC):
                self.mm(ps[:, t * 16:(t + 1) * 16], self.hT[:, kc, t * 128:(t + 1) * 128], wdt[:, kc, :], kc == 0,
                        kc == KC - 1, reads=["wdt", ("hT", kc, t // 4)], writes=[pk])
        self.tt(dt[:], ps[:, 0:256].rearrange("p (t h) -> p t h", t=16),
                dtb[:].unsqueeze(1).to_broadcast([128, 16, 16]), ALU.add, reads=[pk, dtbk], writes=["dt"])
        one_col = cst[:, C_ONES, 0:1]
        self.act(tmp[:], dt[:], AF.Exp, reads=["dt"], writes=["ptmp"])
        self.act(dt[:], tmp[:], AF.Ln, reads=["ptmp", "consts"], writes=["dt"], bias=one_col, scale=1.0)
        self.act(alog[:], alog[:], AF.Exp, reads=[alogk], writes=[alogk])
        self.S.op("dve", lambda e: e.scalar_tensor_tensor(out=av[:], in0=dt[:], scalar=-1.0,
                                                          in1=alog[:].unsqueeze(1).to_broadcast([128, 16, 16]),
                                                          op0=ALU.mult, op1=ALU.mult),
                  reads=["dt", alogk], writes=["av"])
        ps, pk = self.psum.next()
        for t in range(16):
            self.mm(ps[:, t * 16:(t + 1) * 16], cst[:, C_TRI2, :], av[:, t, :], True, True, reads=["av", "consts"], writes=[pk])
        self.copy(acum[:], ps[:, 0:256].rearrange("p (t h) -> p t h", t=16), reads=[pk], writes=["acum"])
        self.act(eacum[:], acum[:], AF.Exp, reads=["acum"], writes=["eacum"])
        ps, pk = self.psum.next()
        for t in range(16):
            self.mm(ps[:, t * 16:(t + 1) * 16], cst[:, C_BD, :], av[:, t, :], True, True, reads=["av", "consts"], writes=[pk])
        self.tt(tmp[:], ps[:, 0:256].rearrange("p (t h) -> p t h", t=16), acum[:], ALU.subtract, reads=[pk, "acum"],
                writes=["ptmp"])
        self.act(tmp[:], tmp[:], AF.Exp, reads=["ptmp"], writes=["ptmp"])
        self.tt(dtds[:], tmp[:], dt[:], ALU.mult, reads=["ptmp", "dt"], writes=["dtds"])
        ps, pk = self.psum.next()
        for t in range(16):
            for c in range(2):
                j = 2 * t + c
                self.mm(ps[:, j * 16:(j + 1) * 16], cst[:, C_IND0 + c, :], av[:, t, :], True, True,
                        reads=["av", "consts"], writes=[pk])
        self.act(eatot[:].rearrange("p j h -> p (j h)"), ps[:], AF.Exp, reads=[pk], writes=["eatot"])

        mG = A.mark()
        for g in range(4):
            A.reset(mG)
            S.barrier()
            wbf = A.alloc("swz", [128, KC, 256], BF16)
            BT = A.alloc("sBT", [128, S_LEN], BF16)
            CT = A.alloc("sCT", [128, S_LEN], BF16)
            x_tok = A.alloc("sxtok", [128, 16, 256], BF16)
            B_tok = A.alloc("sBtok", [128, 16, 128], BF16)
            nwb, nwbk = self.bload("snwb", self.ssm_norm_w_d[l][256 * g:256 * (g + 1)], 256)
            mA = A.mark()
            wst = self.ring("swst", 2, [128, KC, 128], F32)
            wbx = A.alloc("swbx", [128, KC, 512], BF16)
            raw = A.alloc("sraw", [128, S_LEN + 4], F32)
            acc = A.alloc("sacc", [128, S_LEN], F32)
            xc = self.ring("sxc", 2, [128, S_LEN], BF16)
            cw = self.ring("scw", 2, [128, 8], F32)
            rawk = self.key("sraw")
            self.memset(raw[:, 0:3], 0.0, writes=[(rawk, "pad")])
            cols = [O_SSMZ + 256 * g, O_SSMZ + 256 * g + 128, O_SSMXBC + 256 * g, O_SSMXBC + 256 * g + 128,
                    O_SSMXBC + 1024 + 128 * g, O_SSMXBC + 1536 + 128 * g]
            wk = self.key("swbf")
            for j, c0 in enumerate(cols):
                st, sk = wst.next()
                self.dma(st[:], self.w_in_d[l][:, c0:c0 + 128].rearrange("(k p) n -> p k n", p=128), [], [sk])
                for kc in range(KC):
                    wdst_ = wbf[:, kc, j * 128:(j + 1) * 128] if j < 2 else wbx[:, kc, (j - 2) * 128:(j - 1) * 128]
                    self.act(wdst_, st[:, kc, :], AF.Copy, reads=[sk, "nw"], writes=[(wk, j)],
                             scale=scale[:, kc:kc + 1])
            chs = [256 * g, 256 * g + 128, 1024 + 128 * g, 1536 + 128 * g]
            acck = self.key("sacc")
            xtk = self.key("sxtok")
            btk = self.key("sBtok")
            for jj, ch in enumerate(chs):
                j = jj + 2
                if jj < 2:
                    dst, dk = xc.next()
                elif jj == 2:
                    dst, dk = BT, "sBT"
                else:
                    dst, dk = CT, "sCT"
                self.conv_silu(l, wbx, (wk, j), jj * 128, self.ssm_conv_w_d, self.ssm_conv_b_d, ch, raw, rawk, acc, acck,
                               cw, dst[:], dk)
                if jj < 2:
                    self.to_tok(dst, dk, lambda tq, jj=jj: x_tok[:, tq * 4:(tq + 1) * 4, jj * 128:(jj + 1) * 128], (xtk, jj))
                elif jj == 2:
                    self.to_tok(dst, dk, lambda tq: B_tok[:, tq * 4:(tq + 1) * 4, :], btk)
            S.barrier()
            A.reset(mA)
            xdt = A.alloc("sxdt", [128, 16, 256], BF16)
            xdtd = A.alloc("sxdtd", [128, 16, 256], BF16)
            xD = A.alloc("sxD", [128, 16, 256], BF16)
            oT = A.alloc("soT", [128, 2, S_LEN], BF16)
            state = A.alloc("sstate", [128, 256], F32)
            state_bf = A.alloc("sstatebf", [128, 256], BF16)
            abc = self.ring("sabc", 2, [128, 128], F32)
            dm = self.ring("sdm", 2, [128, 4, 128], F32)
            MT = self.ring("sMT", 2, [128, 4, 128], BF16)
            szr = self.ring("ssz", 1, [128, 256], F32)
            t1r = self.ring("st1", 1, [128, 256], F32)
            yr = self.ring("sy", 2, [128, 256], F32)
            jr = self.ring("sjunk", 1, [128, 256], F32)
            ssr = self.ring("sssq", 4, [128, 2], F32)
            obr = self.ring("sob", 2, [128, 256], BF16)
            xtks = [(xtk, jj, tq) for jj in range(2) for tq in range(4)]
            x4 = x_tok[:].rearrange("p t (h c) -> p t h c", h=4)
            hs = slice(4 * g, 4 * g + 4)
            self.tt(xdt[:].rearrange("p t (h c) -> p t h c", h=4), x4,
                    dt[:, :, hs].unsqueeze(3).to_broadcast([128, 16, 4, 64]), ALU.mult, reads=xtks + ["dt"], writes=["sxdt"])
            self.tt(xdtd[:].rearrange("p t (h c) -> p t h c", h=4), x4,
                    dtds[:, :, hs].unsqueeze(3).to_broadcast([128, 16, 4, 64]), ALU.mult, reads=xtks + ["dtds"],
                    writes=["sxdtd"])
            for t in range(16):
                self.tt(xD[:, t, :].rearrange("p (h c) -> p h c", h=4), x_tok[:, t, :].rearrange("p (h c) -> p h c", h=4),
                        dbc[:, hs].unsqueeze(2).to_broadcast([128, 4, 64]), ALU.mult, reads=xtks + [dbck],
                        writes=[("sxD", t)], eng="pool")
            stk = self.key("sstate")
            sbk = self.key("sstatebf")
            self.memset(state[:], 0.0, writes=[stk])
            self.memset(state_bf[:], 0.0, writes=[sbk])
            ones_f = cst[:, C_ONES, :]
            for t in range(16):
                tsl = slice(t * 128, (t + 1) * 128)
                psS, psSk = self.psum.next()
                self.mm(psS[:, 0:128], BT[:, tsl], CT[:, tsl], True, True, reads=["sBT", "sCT"], writes=[psSk])
                psD, psDk = self.psum.next()
                for hh in range(4):
                    ab, abk = abc.next()
                    self.ts(ab[:], ones_f, av[:, t, 4 * g + hh:4 * g + hh + 1], ALU.mult, reads=["av", "consts"], writes=[abk])
                    self.mm(psD[:, hh * 128:(hh + 1) * 128], ab[:], cst[:, C_TRI2, :], True, False, reads=[abk, "consts"],
                            writes=[psDk])
                    self.mm(psD[:, hh * 128:(hh + 1) * 128], cst[:, C_TRI2NEG, :], ab[:], False, True,
                            reads=[abk, "consts"], writes=[psDk])
                d_, dk_ = dm.next()
                self.tt(d_[:], psD[:].rearrange("p (h c) -> p h c", h=4),
                        cst[:, C_NEGINCL, :].unsqueeze(1).to_broadcast([128, 4, 128]), ALU.add, reads=[psDk, "consts"],
                        writes=[dk_])
                L_, Lk_ = d_, dk_
                self.act(L_[:], d_[:], AF.Exp, reads=[dk_], writes=[Lk_])
                M_, Mk_ = MT.next()
                self.tt(M_[:], L_[:], psS[:, 0:128].unsqueeze(1).to_broadcast([128, 4, 128]), ALU.mult,
                        reads=[Lk_, psSk], writes=[Mk_])
                psY, psYk = self.psum.next()
                for hh in range(4):
                    cs = slice(hh * 64, (hh + 1) * 64)
                    self.mm(psY[:, cs], M_[:, hh, :], xdt[:, t, cs], True, False, reads=[Mk_, "sxdt"], writes=[psYk])
                    self.mm(psY[:, cs], self.ident_b(), xD[:, t, cs], False, True, reads=["cbf", ("sxD", t)], writes=[psYk])
                psZ, psZk = self.psum.next()
                for kc in range(KC):
                    self.mm(psZ[:, 0:256], self.hT[:, kc, tsl], wbf[:, kc, 0:256], kc == 0, kc == KC - 1,
                            reads=[(wk, 0), (wk, 1), ("hT", kc, t // 4)], writes=[psZk])
                sz, szk = szr.next()
                self.act(sz[:], psZ[:, 0:256], AF.Silu, reads=[psZk], writes=[szk])
                psO, psOk = self.psum_acc.next()
                for c in range(2):
                    j = 2 * t + c
                    self.mm(psO[64 * c:64 * c + 64, 0:256], CT[:, t * 128 + 64 * c:t * 128 + 64 * c + 64], state_bf[:], True, True,
                            reads=["sCT", sbk], writes=[psOk])
                    psT, psTk = self.psum.next()
                    self.mm(psT[:, 0:256], B_tok[64 * c:64 * c + 64, t, :], xdtd[64 * c:64 * c + 64, t, :], True, True,
                            reads=[(btk, t // 4), "sxdtd"], writes=[psTk])
                    self.tt(state[:].rearrange("p (h c) -> p h c", h=4), state[:].rearrange("p (h c) -> p h c", h=4),
                            eatot[:, j, hs].unsqueeze(2).to_broadcast([128, 4, 64]), ALU.mult, reads=[stk, "eatot"],
                            writes=[stk])
                    self.tt(state[:], state[:], psT[:, 0:256], ALU.add, reads=[stk, psTk], writes=[stk])
                    self.copy(state_bf[:], state[:], reads=[stk], writes=[sbk], eng="act")
                t1, t1k = t1r.next()
                self.tt(t1[:].rearrange("p (h c) -> p h c", h=4), psO[:, 0:256].rearrange("p (h c) -> p h c", h=4),
                        eacum[:, t, hs].unsqueeze(2).to_broadcast([128, 4, 64]), ALU.mult, reads=[psOk, "eacum"],
                        writes=[t1k])
                y, yk = yr.next()
                self.tt(y[:], t1[:], psY[:, 0:256], ALU.add, reads=[t1k, psYk], writes=[yk])
                self.tt(y[:], y[:], sz[:], ALU.mult, reads=[yk, szk], writes=[yk])
                jk_, jkk = jr.next()
                ss, ssk = ssr.next()
                self.S.op("act", lambda e, jk_=jk_, y=y, ss=ss: e.activation(out=jk_[:], in_=y[:], func=AF.Square,
                                                                          accum_out=ss[:, 0:1]),
                          reads=[yk], writes=[jkk, ssk])
                self.act(ss[:, 1:2], ss[:, 0:1], AF.Ln, reads=[ssk, "consts"], writes=[ssk], scale=1.0 / 256,
                         bias=self.eps_ap())
                self.act(ss[:, 1:2], ss[:, 1:2], AF.Exp, reads=[ssk], writes=[ssk], scale=-0.5)
                ob, obk = obr.next()
                self.stt(ob[:], y[:], ss[:, 1:2], nwb[:], ALU.mult, ALU.mult, reads=[yk, ssk, nwbk], writes=[obk])
                psX, psXk = self.psum.next()
                pxb = psX[:].bitcast(BF16)
                for ch in range(2):
                    self.tr(pxb[:, ch * 128:(ch + 1) * 128], ob[:, ch * 128:(ch + 1) * 128], self.ident_b(),
                            reads=[obk, "cbf"], writes=[psXk])
                self.copy(oT[:, :, tsl], pxb[:, 0:256].rearrange("p (c t) -> p c t", c=2), reads=[psXk],
                          writes=[("soT", t)], eng="act")
            for ch in range(2):
                oc = 2 * g + ch
                self.dma(self.o_scr[2][oc * 128:(oc + 1) * 128, :], oT[:, ch, :], reads=[("soT", t) for t in range(16)],
                         writes=[("o_scr", 2, oc)])


    def dn_phase(self, l):
        A, S = self.A, self.S
        scale = self.nw[:, 2 * l, :]
        cst = self.consts
        one_col = cst[:, C_ONES, 0:1]
        ones_f = cst[:, C_ONES, :]
        wast = A.alloc("dwast", [128, KC, 16], F32)
        wa = A.alloc("dwa", [128, KC, 16], BF16)
        self.dma(wast[:], self.w_in_d[l][:, O_DNA:O_DNA + 16].rearrange("(k p) n -> p k n", p=128), [], ["dwast"])
        for kc in range(KC):
            self.act(wa[:, kc, :], wast[:, kc, :], AF.Copy, reads=["dwast", "nw"], writes=["dwa"], scale=scale[:, kc:kc + 1])
        dtb, dtbk = self.bload("ddtb", self.dn_dt_bias_d[l], 8)
        alog, alogk = self.bload("dalog", self.dn_a_log_d[l], 8)
        nwb, nwbk = self.bload("dnwb", self.dn_norm_w_d[l], 128)
        gv = A.alloc("dg", [128, 16, 8], F32)
        beta = A.alloc("dbeta", [128, 16, 8], F32)
        gc = A.alloc("dgc", [128, 16, 8], F32)
        egc = A.alloc("degc", [128, 16, 8], F32)
        bg = A.alloc("dbg", [128, 16, 8], F32)
        kdec = A.alloc("dkdec", [128, 16, 8], F32)
        eglast = A.alloc("deglast", [128, 32, 8], F32)
        tmp = A.alloc("dtmp", [128, 16, 8], F32)
        ps, pk = self.psum.next()
        for t in range(16):
            for kc in range(KC):
                self.mm(ps[:, t * 16:(t + 1) * 16], self.hT[:, kc, t * 128:(t + 1) * 128], wa[:, kc, :], kc == 0,
                        kc == KC - 1, reads=["dwa", ("hT", kc, t // 4)], writes=[pk])
        pv = ps[:, 0:256].rearrange("p (t h) -> p t h", t=16)
        self.act(beta[:], pv[:, :, 8:16], AF.Exp, reads=[pk], writes=["dbeta"], scale=-1.0)
        self.ts(beta[:], beta[:], 1.0, ALU.add, reads=["dbeta"], writes=["dbeta"])
        self.S.op("dve", lambda e: e.reciprocal(out=beta[:], in_=beta[:]), reads=["dbeta"], writes=["dbeta"])
        self.tt(gv[:], pv[:, :, 0:8], dtb[:].unsqueeze(1).to_broadcast([128, 16, 8]), ALU.add, reads=[pk, dtbk], writes=["dg"])
        self.act(tmp[:], gv[:], AF.Exp, reads=["dg"], writes=["dtmp"])
        self.act(gv[:], tmp[:], AF.Ln, reads=["dtmp", "consts"], writes=["dg"], bias=one_col, scale=1.0)
        self.act(alog[:], alog[:], AF.Exp, reads=[alogk], writes=[alogk])
        self.S.op("dve", lambda e: e.scalar_tensor_tensor(out=gv[:], in0=gv[:], scalar=-1.0,
                                                          in1=alog[:].unsqueeze(1).to_broadcast([128, 16, 8]),
                                                          op0=ALU.mult, op1=ALU.mult),
                  reads=["dg", alogk], writes=["dg"])
        ps, pk = self.psum.next()
        for t in range(16):
            self.mm(ps[:, t * 8:(t + 1) * 8], cst[:, C_TRI2, :], gv[:, t, :], True, True, reads=["dg", "consts"], writes=[pk])
        self.copy(gc[:], ps[:, 0:128].rearrange("p (t h) -> p t h", t=16), reads=[pk], writes=["dgc"])
        self.act(egc[:], gc[:], AF.Exp, reads=["dgc"], writes=["degc"])
        self.tt(bg[:], egc[:], beta[:], ALU.mult, reads=["degc", "dbeta"], writes=["dbg"])
        ps, pk = self.psum.next()
        for t in range(16):
            self.mm(ps[:, t * 8:(t + 1) * 8], cst[:, C_BD, :], gv[:, t, :], True, True, reads=["dg", "consts"], writes=[pk])
        self.tt(kdec[:], ps[:, 0:128].rearrange("p (t h) -> p t h", t=16), gc[:], ALU.subtract, reads=[pk, "dgc"],
                writes=["dkdec"])
        self.act(kdec[:], kdec[:], AF.Exp, reads=["dkdec"], writes=["dkdec"])
        ps, pk = self.psum.next()
        for t in range(16):
            for c in range(2):
                j = 2 * t + c
                self.mm(ps[:, j * 8:(j + 1) * 8], cst[:, C_IND0 + c, :], gv[:, t, :], True, True,
                        reads=["dg", "consts"], writes=[pk])
        self.act(eglast[:].rearrange("p j h -> p (j h)"), ps[:, 0:256], AF.Exp, reads=[pk], writes=["deglast"])

        mG = A.mark()
        for h in range(8):
            A.reset(mG)
            S.barrier()
            wg = A.alloc("dwg", [128, KC, 128], BF16)
            qTn = A.alloc("dqTn", [128, S_LEN], BF16)
            kTn = A.alloc("dkTn", [128, S_LEN], BF16)
            k_tok = A.alloc("dktok", [128, 16, 128], BF16)
            v_tok = A.alloc("dvtok", [128, 16, 128], BF16)
            mA = A.mark()
            wst = self.ring("dwst", 2, [128, KC, 128], F32)
            wbx = A.alloc("dwbx", [128, KC, 384], BF16)
            raw = A.alloc("draw", [128, S_LEN + 4], F32)
            acc = A.alloc("dacc", [128, S_LEN], F32)
            vT = A.alloc("dvT", [128, S_LEN], BF16)
            cw = self.ring("dcw", 2, [128, 8], F32)
            sqr = self.ring("dsq", 2, [128, 512], BF16)
            rsr = self.ring("drs", 2, [128, 512], F32)
            rawk = self.key("draw")
            self.memset(raw[:, 0:3], 0.0, writes=[(rawk, "pad")])
            cols = [O_DNQKV + 128 * h, O_DNQKV + 1024 + 128 * h, O_DNQKV + 2048 + 128 * h, O_DNGATE + 128 * h]
            wk = self.key("dwb")
            for j, c0 in enumerate(cols):
                st, sk = wst.next()
                self.dma(st[:], self.w_in_d[l][:, c0:c0 + 128].rearrange("(k p) n -> p k n", p=128), [], [sk])
                for kc in range(KC):
                    wd_ = wbx[:, kc, j * 128:(j + 1) * 128] if j < 3 else wg[:, kc, :]
                    self.act(wd_, st[:, kc, :], AF.Copy, reads=[sk, "nw"], writes=[(wk, j)], scale=scale[:, kc:kc + 1])
            acck = self.key("dacc")
            ktk = self.key("dktok")
            vtk = self.key("dvtok")
            for j in range(3):
                ch = j * 1024 + 128 * h
                if j == 2:
                    self.conv_silu(l, wbx, (wk, j), j * 128, self.dn_conv_w_d, None, ch, raw, rawk, acc, acck, cw, vT[:], "dvT")
                    self.to_tok(vT, "dvT", lambda tq: v_tok[:, tq * 4:(tq + 1) * 4, :], vtk)
                    continue
                self.conv_silu(l, wbx, (wk, j), j * 128, self.dn_conv_w_d, None, ch, raw, rawk, acc, acck, cw, acc[:], acck)
                dst, dk = (qTn, "dqTn") if j == 0 else (kTn, "dkTn")
                for g in range(4):
                    sl = slice(g * 512, (g + 1) * 512)
                    q_, qk_ = sqr.next()
                    self.act(q_[:], acc[:, sl], AF.Square, reads=[acck], writes=[qk_])
                    ps, pk = self.psum.next()
                    self.mm(ps[:], self.ones_b(), q_[:], True, True, reads=[qk_, "cbf"], writes=[pk])
                    r_, rk_ = rsr.next()
                    self.act(r_[:], ps[:], AF.Ln, reads=[pk, "consts"], writes=[rk_], scale=1.0, bias=self.eps_ap())
                    self.act(r_[:], r_[:], AF.Exp, reads=[rk_], writes=[rk_], scale=-0.5)
                    if j == 0:
                        self.stt(dst[:, sl], acc[:, sl], 128.0 ** -0.5, r_[:], ALU.mult, ALU.mult, reads=[acck, rk_],
                                 writes=[(dk, g)])
                    else:
                        self.tt(dst[:, sl], acc[:, sl], r_[:], ALU.mult, reads=[acck, rk_], writes=[(dk, g)])
                if j == 1:
                    self.to_tok(kTn, [("dkTn", g) for g in range(4)], lambda tq: k_tok[:, tq * 4:(tq + 1) * 4, :], ktk)
            S.barrier()
            A.reset(mA)
            attnT = A.alloc("dattnT", [128, 16, 128], BF16)
            P = A.alloc("dP", [128, 16, 128], BF16)
            PL = A.alloc("dPL", [128, 16, 128], BF16)
            R32 = A.alloc("dR32", [128, 16, 128], F32)
            Rb = A.alloc("dRb", [128, 16, 128], BF16)
            vb = v_tok
            kbg = k_tok
            kd = A.alloc("dkd", [128, 16, 128], BF16)
            u = A.alloc("du", [128, 16, 128], F32)
            wT = A.alloc("dwT", [128, 16, 128], BF16)
            oT = A.alloc("doT", [128, S_LEN], BF16)
            Sst = A.alloc("dS", [128, 128], F32)
            Sbf = A.alloc("dSbf", [128, 128], BF16)
            abr = self.ring("dab", 3, [128, 128], F32)
            dcr = self.ring("ddec", 1, [128, 4, 128], F32)
            tmr = self.ring("dtm", 1, [128, 4, 128], F32)
            bmr = self.ring("dbm", 1, [128, 4, 128], F32)
            vnr = self.ring("dvn", 2, [128, 128], BF16)
            t1r = self.ring("dt1", 2, [128, 128], F32)
            otr = self.ring("dot", 2, [128, 128], F32)
            sgr = self.ring("dsg", 2, [128, 128], F32)
            jr = self.ring("djunk", 1, [128, 128], F32)
            ssr = self.ring("dssq", 4, [128, 2], F32)
            obr = self.ring("dob", 2, [128, 128], BF16)
            ktks = [(ktk, tq) for tq in range(4)]
            vtks = [(vtk, tq) for tq in range(4)]
            bc3 = lambda t_: t_[:, :, h:h + 1].to_broadcast([128, 16, 128])
            self.tt(vb[:], v_tok[:], bc3(beta), ALU.mult, reads=vtks + ["dbeta"], writes=vtks + ["dvb"])
            self.tt(kd[:], k_tok[:], bc3(kdec), ALU.mult, reads=ktks + ["dkdec"], writes=["dkd"])
            self.tt(kbg[:], k_tok[:], bc3(bg), ALU.mult, reads=ktks + ["dbg", "dkd"], writes=ktks + ["dkbg"])
            kTk = [("dkTn", g) for g in range(4)]
            identf4 = cst[:, C_IDENT, :].unsqueeze(1).to_broadcast([128, 4, 128])
            for q in range(4):
                psK, psKk = self.psum.next()
                psQ, psQk = self.psum.next()
                psD, psDk = self.psum.next()
                psB, psBk = self.psum.next()
                for i4 in range(4):
                    t = 4 * q + i4
                    tsl = slice(t * 128, (t + 1) * 128)
                    cs = slice(i4 * 128, (i4 + 1) * 128)
                    self.mm(psK[:, cs], kTn[:, tsl], kTn[:, tsl], True, True, reads=[("dkTn", q)], writes=[psKk])
                    self.mm(psQ[:, cs], kTn[:, tsl], qTn[:, tsl], True, True, reads=[("dkTn", q), ("dqTn", q)], writes=[psQk])
                    ab, abk = abr.next()
                    self.ts(ab[:], ones_f, gv[:, t, h:h + 1], ALU.mult, reads=["dg", "consts"], writes=[abk])
                    self.mm(psD[:, cs], ab[:], cst[:, C_TRI2, :], True, False, reads=[abk, "consts"], writes=[psDk])
                    self.mm(psD[:, cs], cst[:, C_TRI2NEG, :], ab[:], False, True, reads=[abk, "consts"], writes=[psDk])
                    db, dbk = abr.next()
                    self.ts(db[:], cst[:, C_IDENT, :], beta[:, t, h:h + 1], ALU.mult, reads=["dbeta", "consts"], writes=[dbk])
                    self.mm(psB[:, cs], ones_f, db[:], True, True, reads=[dbk, "consts"], writes=[psBk])
                dc, dck = dcr.next()
                self.tt(dc[:], psD[:].rearrange("p (a b) -> p a b", a=4),
                        cst[:, C_NEGINCL, :].unsqueeze(1).to_broadcast([128, 4, 128]), ALU.add, reads=[psDk, "consts"],
                        writes=[dck])
                self.act(dc[:], dc[:], AF.Exp, reads=[dck], writes=[dck])
                self.tt(attnT[:, 4 * q:4 * q + 4, :], psQ[:].rearrange("p (a b) -> p a b", a=4), dc[:], ALU.mult,
                        reads=[psQk, dck], writes=[("dattnT", q)])
                bm, bmk = bmr.next()
                self.tt(bm[:], psB[:].rearrange("p (a b) -> p a b", a=4),
                        cst[:, C_MSTRICT2, :].unsqueeze(1).to_broadcast([128, 4, 128]), ALU.mult, reads=[psBk, "consts"],
                        writes=[bmk])
                tm, tmk = tmr.next()
                self.tt(tm[:], psK[:].rearrange("p (a b) -> p a b", a=4), dc[:], ALU.mult, reads=[psKk, dck], writes=[tmk])
                self.tt(P[:, 4 * q:4 * q + 4, :], tm[:], bm[:], ALU.mult, reads=[tmk, bmk], writes=[("dP", q)])
                psT, psTk = self.psum.next()
                ptb = psT[:].bitcast(BF16)
                for i4 in range(4):
                    self.tr(ptb[:, i4 * 128:(i4 + 1) * 128], P[:, 4 * q + i4, :], self.ident_b(), reads=[("dP", q), "cbf"],
                            writes=[psTk])
                self.copy(PL[:, 4 * q:4 * q + 4, :], ptb[:, 0:512].rearrange("p (a b) -> p a b", a=4), reads=[psTk],
                          writes=[("dPL", q)], eng="act")
                self.tt(R32[:, 4 * q:4 * q + 4, :], identf4, P[:, 4 * q:4 * q + 4, :], ALU.subtract,
                        reads=[("dP", q), "consts"], writes=[("dR32", q)])
                self.copy(Rb[:, 4 * q:4 * q + 4, :], R32[:, 4 * q:4 * q + 4, :], reads=[("dR32", q)], writes=[("dRb", q)],
                          eng="act")
            for lev in range(5):
                last = lev == 4
                for q in range(4):
                    if not last:
                        psP, psPk = self.psum.next()
                    psL, psLk = self.psum.next()
                    for i4 in range(4):
                        t = 4 * q + i4
                        cs = slice(i4 * 128, (i4 + 1) * 128)
                        if not last:
                            self.mm(psP[:, cs], PL[:, t, :], P[:, t, :], True, True, reads=[("dPL", q), ("dP", q)],
                                    writes=[psPk])
                        self.mm(psL[:, cs], P[:, t, :], PL[:, t, :], True, True, reads=[("dPL", q), ("dP", q)],
                                writes=[psLk])
                    if not last:
                        self.copy(P[:, 4 * q:4 * q + 4, :], psP[:].rearrange("p (a b) -> p a b", a=4), reads=[psPk],
                                  writes=[("dP", q)], eng="dve")
                    self.copy(PL[:, 4 * q:4 * q + 4, :], psL[:].rearrange("p (a b) -> p a b", a=4), reads=[psLk],
                              writes=[("dPL", q)], eng="act")
                for q in range(4):
                    psR, psRk = self.psum.next()
                    for i4 in range(4):
                        t = 4 * q + i4
                        cs = slice(i4 * 128, (i4 + 1) * 128)
                        self.mm(psR[:, cs], PL[:, t, :], Rb[:, t, :], True, True, reads=[("dPL", q), ("dRb", q)],
                                writes=[psRk])
                    self.tt(R32[:, 4 * q:4 * q + 4, :], R32[:, 4 * q:4 * q + 4, :],
                            psR[:].rearrange("p (a b) -> p a b", a=4), ALU.add, reads=[psRk, ("dR32", q)],
                            writes=[("dR32", q)])
                    self.copy(Rb[:, 4 * q:4 * q + 4, :], R32[:, 4 * q:4 * q + 4, :], reads=[("dR32", q)],
                              writes=[("dRb", q)], eng="act")
            for q in range(4):
                psU, psUk = self.psum.next()
                psW, psWk = self.psum.next()
                for i4 in range(4):
                    t = 4 * q + i4
                    cs = slice(i4 * 128, (i4 + 1) * 128)
                    self.mm(psU[:, cs], Rb[:, t, :], vb[:, t, :], True, True, reads=[("dRb", q), "dvb"], writes=[psUk])
                    self.mm(psW[:, cs], kbg[:, t, :], Rb[:, t, :], True, True, reads=[("dRb", q), "dkbg"], writes=[psWk])
                self.copy(u[:, 4 * q:4 * q + 4, :], psU[:].rearrange("p (a b) -> p a b", a=4), reads=[psUk],
                          writes=[("du", q)], eng="dve")
                self.copy(wT[:, 4 * q:4 * q + 4, :], psW[:].rearrange("p (a b) -> p a b", a=4), reads=[psWk],
                          writes=[("dwT", q)], eng="act")
            stk = self.key("dS")
            sbk = self.key("dSbf")
            self.memset(Sst[:], 0.0, writes=[stk])
            self.memset(Sbf[:], 0.0, writes=[sbk])
            for t in range(16):
                q = t // 4
                tsl = slice(t * 128, (t + 1) * 128)
                psG, psGk = self.psum.next()
                for kc in range(KC):
                    self.mm(psG[:, 0:128], self.hT[:, kc, tsl], wg[:, kc, :], kc == 0, kc == KC - 1,
                            reads=[(wk, 3), ("hT", kc, q)], writes=[psGk])
                sg, sgk = sgr.next()
                self.act(sg[:], psG[:, 0:128], AF.Silu, reads=[psGk], writes=[sgk])
                psO, psOk = self.psum_acc.next()
                vn, vnk = vnr.next()
                for c in range(2):
                    j = 2 * t + c
                    pb = 64 * c
                    prt = slice(pb, pb + 64)
                    psW2, psW2k = self.psum.next()
                    self.mm(psW2[prt, 0:128], wT[:, t, prt], Sbf[:], True, True, reads=[("dwT", q), sbk], writes=[psW2k])
                    self.tt(vn[prt, :], u[prt, t, :], psW2[prt, 0:128], ALU.subtract, reads=[("du", q), psW2k],
                            writes=[(vnk, c)])
                    self.mm(psO[prt, 0:128], qTn[:, t * 128 + pb:t * 128 + pb + 64], Sbf[:], True, True,
                            reads=[("dqTn", q), sbk], writes=[(psOk, c)])
                    self.mm(psO[prt, 128:256], attnT[prt, t, prt], vn[prt, :], True, True,
                            reads=[("dattnT", q), (vnk, c)], writes=[(psOk, c)])
                    psS, psSk = self.psum.next()
                    self.mm(psS[:, 0:128], kd[prt, t, :], vn[prt, :], True, True, reads=["dkd", (vnk, c)], writes=[psSk])
                    self.stt(Sst[:], Sst[:], eglast[:, j, h:h + 1], psS[:, 0:128], ALU.mult, ALU.add,
                             reads=[stk, psSk, "deglast"], writes=[stk])
                    self.copy(Sbf[:], Sst[:], reads=[stk], writes=[sbk], eng="act")
                t1, t1k = t1r.next()
                self.ts(t1[:], psO[:, 0:128], egc[:, t, h:h + 1], ALU.mult, reads=[(psOk, 0), (psOk, 1), "degc"], writes=[t1k])
                ot, otk = otr.next()
                self.tt(ot[:], t1[:], psO[:, 128:256], ALU.add, reads=[t1k, (psOk, 0), (psOk, 1)], writes=[otk])
                jk_, jkk = jr.next()
                ss, ssk = ssr.next()
                self.S.op("act", lambda e, jk_=jk_, ot=ot, ss=ss: e.activation(out=jk_[:], in_=ot[:], func=AF.Square,
                                                                            accum_out=ss[:, 0:1]),
                          reads=[otk], writes=[jkk, ssk])
                self.act(ss[:, 1:2], ss[:, 0:1], AF.Ln, reads=[ssk, "consts"], writes=[ssk], scale=1.0 / 128,
                         bias=self.eps_ap())
                self.act(ss[:, 1:2], ss[:, 1:2], AF.Exp, reads=[ssk], writes=[ssk], scale=-0.5)
                self.stt(ot[:], ot[:], ss[:, 1:2], nwb[:], ALU.mult, ALU.mult, reads=[otk, ssk, nwbk], writes=[otk])
                ob, obk = obr.next()
                self.tt(ob[:], ot[:], sg[:], ALU.mult, reads=[otk, sgk], writes=[obk])
                psX, psXk = self.psum.next()
                pxb = psX[:].bitcast(BF16)
                self.tr(pxb[:, 0:128], ob[:], self.ident_b(), reads=[obk, "cbf"], writes=[psXk])
                self.copy(oT[:, tsl], pxb[:, 0:128], reads=[psXk], writes=[("doT", t)], eng="act")
            self.dma(self.o_scr[0][h * 128:(h + 1) * 128, :], oT[:], reads=[("doT", t) for t in range(16)],
                     writes=[("o_scr", 0, h)])

    def final_out(self):
        A, S = self.A, self.S
        m = A.mark()
        sq = self.ring("fsq", 3, [128, 512], BF16)
        rs = self.ring("frs", 2, [128, 512], F32)
        hf = self.ring("hf", 3, [128, 512], F32)
        ost = self.ring("ost", 2, [128, 4, D], F32)
        for g in range(4):
            ps, pk = self.psum.next()
            sl = slice(g * 512, (g + 1) * 512)
            for kc in range(KC):
                q, qk = sq.next()
                self.act(q[:], self.xT[:, kc, sl], AF.Square, reads=[("xT", kc, g)], writes=[qk])
                self.mm(ps[:], self.ones_b(), q[:], kc == 0, kc == KC - 1, reads=[qk, "cbf"], writes=[pk])
            r, rk = rs.next()
            self.act(r[:], ps[:], AF.Ln, reads=[pk], writes=[rk], scale=1.0 / D, bias=self.eps_ap())
            self.act(r[:], r[:], AF.Exp, reads=[rk], writes=[rk], scale=-0.5)
            o, ok = ost.next()
            for kc in range(KC):
                h, hk = hf.next()
                self.stt(h[:], self.xT[:, kc, sl], self.nw[:, 2 * DEPTH, kc:kc + 1], r[:], ALU.mult, ALU.mult,
                         reads=[("xT", kc, g), rk, "nw"], writes=[hk])
                ps2, pk2 = self.psum.next()
                for t in range(4):
                    self.tr(ps2[:, t * 128:(t + 1) * 128], h[:, t * 128:(t + 1) * 128], self.ident_f(),
                            reads=[hk, "consts"], writes=[pk2])
                self.copy(o[:, :, kc * 128:(kc + 1) * 128], ps2[:].rearrange("p (t c) -> p t c", t=4),
                          reads=[pk2], writes=[ok], eng=("act" if kc % 2 else "dve"))
            self.dma(self.out_d[g * 512:(g + 1) * 512, :].rearrange("(t p) d -> p t d", p=128), o[:],
                     reads=[ok], writes=[("out", g)])
        S.barrier()
        A.reset(m)


C_IDENT, C_ONES, C_EPS, C_MSTRICT, C_TINC = 0, 1, 2, 3, 4
C_TRI2, C_TRI2NEG, C_BD, C_IND0, C_IND1, C_NEGINCL, C_NEGSTRICT, C_MINCL2, C_MSTRICT2, C_MSTRICT2T = 5, 6, 7, 8, 9, 10, 11, 12, 13, 14
NCONST = 15
NEG = -30000.0


def make_consts():
    c = np.zeros((128, NCONST, 128), np.float32)
    c[:, C_IDENT, :] = np.eye(128, dtype=np.float32)
    c[:, C_ONES, :] = 1.0
    c[:, C_EPS, :] = EPS
    ii = np.arange(128)
    c[:, C_MSTRICT, :] = (ii[:, None] < ii[None, :]).astype(np.float32)
    c[:, C_TINC, :] = (ii[:, None] >= ii[None, :]).astype(np.float32)
    same = (ii[:, None] // 64) == (ii[None, :] // 64)
    le = ii[:, None] <= ii[None, :]
    lt = ii[:, None] < ii[None, :]
    c[:, C_TRI2, :] = (same & le).astype(np.float32)
    c[:, C_TRI2NEG, :] = -c[:, C_TRI2, :]
    c[:, C_BD, :] = same.astype(np.float32)
    c[:, C_IND0, :] = (ii[:, None] < 64).astype(np.float32) * np.ones((1, 128), np.float32)
    c[:, C_IND1, :] = (ii[:, None] >= 64).astype(np.float32) * np.ones((1, 128), np.float32)
    c[:, C_NEGINCL, :] = np.where(same & le, 0.0, NEG)
    c[:, C_NEGSTRICT, :] = np.where(same & lt, 0.0, NEG)
    c[:, C_MINCL2, :] = (same & le).astype(np.float32)
    c[:, C_MSTRICT2, :] = (same & lt).astype(np.float32)
    c[:, C_MSTRICT2T, :] = (same & lt).T.astype(np.float32)
    return c.reshape(128, NCONST * 128)


_CACHE = {}
_RUN_KW = {}


def get_nc(**kw):
    key = tuple(sorted((k, str(v)) for k, v in kw.items()))
    if key not in _CACHE:
        b = Builder(**kw)
        b.build()
        _CACHE[key] = b
    return _CACHE[key]


def run(inputs, **kw):
    b = get_nc(**kw)
    consts = make_consts()
    common = {
        "consts": consts,
        "w_in": np.ascontiguousarray(inputs["w_in"], dtype=np.float32),
        "norm_mix": np.ascontiguousarray(inputs["norm_mix"], dtype=np.float32),
        "norm_mlp": np.ascontiguousarray(inputs["norm_mlp"], dtype=np.float32),
        "norm_final": np.ascontiguousarray(inputs["norm_final"], dtype=np.float32).reshape(1, D),
        "w_branch": np.ascontiguousarray(inputs["w_branch"], dtype=np.float32),
        "w_out": np.ascontiguousarray(inputs["w_out"], dtype=np.float32),
        "w_up": np.ascontiguousarray(inputs["w_up"], dtype=np.float32),
        "w_down": np.ascontiguousarray(inputs["w_down"], dtype=np.float32),
    }
    for nm in ("ssm_conv_w", "ssm_conv_b", "ssm_a_log", "ssm_dt_bias", "ssm_d", "ssm_norm_w", "dn_conv_w", "dn_a_log",
               "dn_dt_bias", "dn_norm_w"):
        common[nm] = np.ascontiguousarray(inputs[nm], dtype=np.float32)
    x = np.asarray(inputs["x"], dtype=np.float32)
    in_maps = []
    for c in range(NCORES):
        m = dict(common)
        m["x"] = np.ascontiguousarray(x[c])
        in_maps.append(m)
    res = run_bass_kernel_spmd(b.nc, in_maps, core_ids=list(range(NCORES)), **_RUN_KW)
    return res


def kernel(**inputs):
    res = run(inputs)
    out = np.stack([np.asarray(r["out"], dtype=np.float32) for r in res.results], axis=0)
    return out
```

```python
import numpy as np
import concourse.bass as bass
import concourse.mybir as mybir
from concourse.bass_utils import run_bass_kernel_spmd

F32 = mybir.dt.float32
BF16 = mybir.dt.bfloat16
AF = mybir.ActivationFunctionType
ALU = mybir.AluOpType

S_LEN = 2048
D = 1024
KC = 8
NCORES = 8
DEPTH = 2
D_FF = 4096
EPS = 1e-6
IN_SIZES = (3072, 1024, 8, 8, 3072, 1024, 2048, 16, 3072)
IN_DIM = sum(IN_SIZES)
OFF = [0]
for _s in IN_SIZES:
    OFF.append(OFF[-1] + _s)
(O_DNQKV, O_DNGATE, O_DNA, O_DNB, O_SBQKV, O_SSMZ, O_SSMXBC, O_SSMDT, O_GATE, _) = OFF


class _Op:
    __slots__ = ("id", "eng", "fn", "dma", "waits", "idx", "sig", "reuse", "seen")


class Sched:
    ENG = ("pe", "act", "dve", "pool", "sp")
    NSLOT = 12
    SEM_LIMIT = 20000

    def __init__(self, nc):
        self.nc = nc
        self.ops = []
        self.by_eng = {e: [] for e in self.ENG}
        self.kstate = {}
        self.seen = {e: {p: -1 for p in self.ENG} for e in self.ENG}
        self.seen_dma = {e: set() for e in self.ENG}
        self.pending = {e: set() for e in self.ENG}
        self.open_dma = []

    def op(self, eng, fn, reads=(), writes=(), dma=False):
        o = _Op()
        o.id = len(self.ops)
        o.eng = eng
        o.fn = fn
        o.dma = dma
        o.sig = None
        o.reuse = None
        deps = {}
        for k in reads:
            st = self.kstate.get(k)
            if st is not None and st[0] is not None:
                deps[st[0]] = True
        for k in writes:
            st = self.kstate.get(k)
            if st is not None:
                if st[0] is not None:
                    deps.setdefault(st[0], False)
                for r in st[1].values():
                    deps.setdefault(r, False)
                for r in st[2]:
                    deps.setdefault(r, False)
        for d in self.pending[eng]:
            deps[d] = True
        self.pending[eng] = set()
        o.idx = len(self.by_eng[eng])
        seen = self.seen[eng]
        best = {}
        waits = []
        for d, raw in deps.items():
            p = self.ops[d]
            if p.dma:
                if d in self.seen_dma[eng]:
                    continue
                self.seen_dma[eng].add(d)
                waits.append(d)
            else:
                if p.eng == eng and not dma:
                    if eng == "pe" or (not raw and eng != "pool"):
                        continue
                if p.idx <= seen[p.eng]:
                    continue
                if p.eng not in best or self.ops[best[p.eng]].idx < p.idx:
                    best[p.eng] = d
        for pe, d in best.items():
            p = self.ops[d]
            waits.append(d)
            for e2, v in p.seen.items():
                if v > seen[e2]:
                    seen[e2] = v
            if p.idx > seen[pe]:
                seen[pe] = p.idx
        o.waits = waits
        o.seen = dict(seen)
        for k in reads:
            st = self.kstate.get(k)
            if st is None:
                st = [None, {}, []]
                self.kstate[k] = st
            if dma:
                st[2].append(o.id)
            else:
                st[1][eng] = o.id
        for k in writes:
            self.kstate[k] = [o.id, {}, []]
        self.ops.append(o)
        self.by_eng[eng].append(o)
        if dma:
            self.open_dma.append(o.id)
        return o

    def barrier(self):
        last = []
        for e in self.ENG:
            for o in reversed(self.by_eng[e]):
                if o.dma:
                    break
                if o.fn is not None:
                    last.append(o.id)
                    break
        for e in self.ENG:
            self.pending[e] |= set(last) | set(self.open_dma[-self.NSLOT:])
        self.open_dma = self.open_dma[-self.NSLOT:]

    def finalize(self):
        nc = self.nc
        needed = set()
        for o in self.ops:
            needed.update(o.waits)
        for e in self.ENG:
            sem = None
            cnt = 0
            ndma = 0
            slots = None
            for o in self.by_eng[e]:
                if o.dma:
                    if slots is None:
                        slots = [nc.alloc_semaphore(f"dq_{e}_{i}") for i in range(self.NSLOT)]
                    s = ndma % self.NSLOT
                    r = ndma // self.NSLOT
                    o.sig = (slots[s], 16 * (r + 1))
                    if r > 0:
                        o.reuse = (slots[s], 16 * r)
                    ndma += 1
                elif o.id in needed:
                    if sem is None or cnt >= self.SEM_LIMIT:
                        sem = nc.alloc_semaphore(f"pg_{e}_{o.id}")
                        cnt = 0
                    cnt += 1
                    o.sig = (sem, cnt)

    def emit(self, ename, eng):
        ops = self.ops
        for o in self.by_eng[ename]:
            for d in o.waits:
                s, v = ops[d].sig
                eng.wait_ge(s, v)
            if o.reuse is not None:
                eng.wait_ge(o.reuse[0], o.reuse[1])
            if o.fn is None:
                continue
            ins = o.fn(eng)
            if o.sig is not None:
                ins.then_inc(o.sig[0], 16 if o.dma else 1)


class Arena:
    def __init__(self, nc, limit=208 * 1024):
        self.nc = nc
        self.off = 16 * 1024
        self.limit = limit
        self.n = 0
        self.peak = 0

    def alloc(self, name, shape, dtype):
        per = 1
        for s in shape[1:]:
            per *= s
        nbytes = per * (4 if dtype == F32 else 2)
        nbytes = (nbytes + 63) // 64 * 64
        assert self.off + nbytes <= self.limit, f"SBUF arena overflow at {name}: {self.off}+{nbytes}"
        self.n += 1
        t = self.nc.alloc_sbuf_tensor_at(f"{name}_{self.n}", list(shape), dtype, offset=self.off)
        self.off += nbytes
        self.peak = max(self.peak, self.off)
        return t

    def mark(self):
        return self.off

    def reset(self, m):
        self.off = m


class Ring:
    def __init__(self, tiles, name):
        self.tiles = tiles
        self.name = name
        self.i = 0

    def next(self):
        j = self.i % len(self.tiles)
        self.i += 1
        return self.tiles[j], (self.name, j)


class Builder:
    def __init__(self, nlayers=DEPTH, mixers=("dn", "sb", "ssm"), do_mlp=True, debug=()):
        self.nlayers = nlayers
        self.mixers = mixers
        self.do_mlp = do_mlp
        self.debug = debug
        nc = bass.Bass("TRN2", target_bir_lowering=False)
        self.nc = nc
        self.S = Sched(nc)
        self.A = Arena(nc)
        self.uid = 0

    def dram_in(self, name, shape, dtype=F32):
        return self.nc.dram_tensor(name, list(shape), dtype, kind="ExternalInput").ap()

    def dram_out(self, name, shape, dtype=F32):
        return self.nc.dram_tensor(name, list(shape), dtype, kind="ExternalOutput").ap()

    def dram_tmp(self, name, shape, dtype):
        return self.nc.dram_tensor(name, list(shape), dtype, kind="Internal").ap()

    def key(self, base):
        self.uid += 1
        return (base, self.uid)

    def ring(self, name, n, shape, dtype):
        return Ring([self.A.alloc(name, shape, dtype) for _ in range(n)], self.key(name))

    def dma(self, out, in_, reads, writes, slow=False):
        if slow:
            fn = lambda e, out=out, in_=in_: e.dma_start(out=out, in_=in_, allow_slow_non_contiguous=True)
        else:
            fn = lambda e, out=out, in_=in_: e.dma_start(out=out, in_=in_)
        return self.S.op("sp", fn, reads=reads, writes=writes, dma=True)

    def mm(self, out, lhsT, rhs, start, stop, reads, writes, **kw):
        return self.S.op(
            "pe",
            lambda e, out=out, lhsT=lhsT, rhs=rhs, start=start, stop=stop, kw=kw: e.matmul(
                out, lhsT=lhsT, rhs=rhs, start=start, stop=stop, **kw
            ),
            reads=reads,
            writes=writes,
        )

    def tr(self, out, in_, ident, reads, writes):
        return self.S.op(
            "pe",
            lambda e, out=out, in_=in_, ident=ident: e.transpose(out, in_, ident),
            reads=reads,
            writes=writes,
        )

    def act(self, out, in_, func, reads, writes, eng="act", **kw):
        return self.S.op(
            eng,
            lambda e, out=out, in_=in_, func=func, kw=kw: e.activation(out=out, in_=in_, func=func, **kw),
            reads=reads,
            writes=writes,
        )

    def tt(self, out, in0, in1, op, reads, writes, eng="dve"):
        return self.S.op(
            eng,
            lambda e, out=out, in0=in0, in1=in1, op=op: e.tensor_tensor(out=out, in0=in0, in1=in1, op=op),
            reads=reads,
            writes=writes,
        )

    def ts(self, out, in0, s1, op0, reads, writes, s2=None, op1=None, eng="dve"):
        def fn(e, out=out, in0=in0, s1=s1, op0=op0, s2=s2, op1=op1):
            if op1 is None:
                return e.tensor_scalar(out=out, in0=in0, scalar1=s1, scalar2=None, op0=op0)
            return e.tensor_scalar(out=out, in0=in0, scalar1=s1, scalar2=s2, op0=op0, op1=op1)

        return self.S.op(eng, fn, reads=reads, writes=writes)

    def stt(self, out, in0, scalar, in1, op0, op1, reads, writes):
        return self.S.op(
            "dve",
            lambda e, out=out, in0=in0, scalar=scalar, in1=in1, op0=op0, op1=op1: e.scalar_tensor_tensor(
                out=out, in0=in0, scalar=scalar, in1=in1, op0=op0, op1=op1
            ),
            reads=reads,
            writes=writes,
        )

    def copy(self, out, in_, reads, writes, eng="dve"):
        if eng == "act":
            return self.act(out, in_, AF.Copy, reads, writes)
        return self.S.op(
            eng, lambda e, out=out, in_=in_: e.tensor_copy(out=out, in_=in_), reads=reads, writes=writes
        )

    def memset(self, ap, val, writes, eng="dve"):
        return self.S.op(eng, lambda e, ap=ap, val=val: e.memset(ap, val), reads=(), writes=writes)

    def build(self):
        nc, S, A = self.nc, self.S, self.A
        L = self.nlayers
        self.x_d = self.dram_in("x", [S_LEN, D])
        self.consts_d = self.dram_in("consts", [128, NCONST * 128])
        self.w_in_d = self.dram_in("w_in", [DEPTH, D, IN_DIM])
        self.norm_mix_d = self.dram_in("norm_mix", [DEPTH, D])
        self.norm_mlp_d = self.dram_in("norm_mlp", [DEPTH, D])
        self.norm_final_d = self.dram_in("norm_final", [1, D])
        self.w_branch_d = self.dram_in("w_branch", [DEPTH, 3, D, D])
        self.w_out_d = self.dram_in("w_out", [DEPTH, D, D])
        self.w_up_d = self.dram_in("w_up", [DEPTH, D, D_FF])
        self.w_down_d = self.dram_in("w_down", [DEPTH, D_FF, D])
        self.ssm_conv_w_d = self.dram_in("ssm_conv_w", [DEPTH, 4, 2048])
        self.ssm_conv_b_d = self.dram_in("ssm_conv_b", [DEPTH, 2048])
        self.ssm_a_log_d = self.dram_in("ssm_a_log", [DEPTH, 16])
        self.ssm_dt_bias_d = self.dram_in("ssm_dt_bias", [DEPTH, 16])
        self.ssm_d_d = self.dram_in("ssm_d", [DEPTH, 16])
        self.ssm_norm_w_d = self.dram_in("ssm_norm_w", [DEPTH, D])
        self.dn_conv_w_d = self.dram_in("dn_conv_w", [DEPTH, 4, 3072])
        self.dn_a_log_d = self.dram_in("dn_a_log", [DEPTH, 8])
        self.dn_dt_bias_d = self.dram_in("dn_dt_bias", [DEPTH, 8])
        self.dn_norm_w_d = self.dram_in("dn_norm_w", [DEPTH, 128])
        self.out_d = self.dram_out("out", [S_LEN, D])
        self.u_scr = self.dram_tmp("u_scr", [D_FF, S_LEN], BF16)
        if self.debug:
            self.o_scr = self.dram_out("o_scr", [3, D, S_LEN], BF16)
        else:
            self.o_scr = self.dram_tmp("o_scr", [3, D, S_LEN], BF16)
        self.g_scr = self.dram_tmp("g_scr", [3, D, S_LEN], BF16)
        self.dn_scr = {nm: self.dram_tmp("dn_" + nm, [8, 128, S_LEN], BF16) for nm in ("u", "wT", "q", "attnT", "kd", "sg")}

        pst = [nc.alloc_psum_tensor(f"ps{i}", [128, 512], F32) for i in range(8)]
        self.pst = pst
        self.psum = Ring(pst[:6], "psum")
        self.psum_acc = Ring(pst[6:], "psacc")

        self.consts = A.alloc("consts", [128, NCONST, 128], F32)
        self.cbf = A.alloc("cbf", [128, NCONST, 128], BF16)
        self.xT = A.alloc("xT", [128, KC, S_LEN], F32)
        self.nw = A.alloc("nw", [128, 2 * DEPTH + 1, KC], F32)
        KCON = "consts"
        self.dma(self.consts[:].rearrange("p a b -> p (a b)"), self.consts_d, reads=[], writes=[KCON])
        self.copy(self.cbf[:], self.consts[:], reads=[KCON], writes=["cbf"])
        for l in range(DEPTH):
            self.dma(self.nw[:, 2 * l, :], self.norm_mix_d[l].rearrange("(k p) -> p k", p=128), [], ["nw"], slow=True)
            self.dma(self.nw[:, 2 * l + 1, :], self.norm_mlp_d[l].rearrange("(k p) -> p k", p=128), [], ["nw"], slow=True)
        self.dma(self.nw[:, 2 * DEPTH, :], self.norm_final_d[0].rearrange("(k p) -> p k", p=128), [], ["nw"], slow=True)

        self.load_x()
        for l in range(L):
            self.layer(l)
        self.final_out()
        S.barrier()
        S.op("sp", None)
        S.finalize()

        with nc.Block() as block:

            @block.tensor
            def _(e):
                S.emit("pe", e)

            @block.scalar
            def _(e):
                S.emit("act", e)

            @block.vector
            def _(e):
                S.emit("dve", e)

            @block.gpsimd
            def _(e):
                S.emit("pool", e)

            @block.sync
            def _(e):
                S.emit("sp", e)

        return nc

    def ident_f(self):
        return self.consts[:, C_IDENT, :]

    def ident_b(self):
        return self.cbf[:, C_IDENT, :]

    def ones_b(self):
        return self.cbf[:, C_ONES, :]

    def load_x(self):
        A, S = self.A, self.S
        m = A.mark()
        stg = self.ring("xstg", 2, [128, 4, D], F32)
        for tg in range(4):
            st, sk = stg.next()
            self.dma(
                st[:],
                self.x_d[tg * 512:(tg + 1) * 512, :].rearrange("(t p) d -> p t d", p=128),
                reads=[],
                writes=[sk],
            )
            for kc in range(KC):
                ps, pk = self.psum.next()
                for t in range(4):
                    self.tr(ps[:, t * 128:(t + 1) * 128], st[:, t, kc * 128:(kc + 1) * 128], self.ident_f(),
                            reads=[sk, "consts"], writes=[pk])
                self.copy(self.xT[:, kc, tg * 512:(tg + 1) * 512], ps[:], reads=[pk], writes=[("xT", kc, tg)],
                          eng=("act" if kc % 2 else "dve"))
        S.barrier()
        A.reset(m)

    def rmsnorm(self):
        A, S = self.A, self.S
        m = A.mark()
        sq = self.ring("sq", 3, [128, 512], BF16)
        rs = self.ring("rs", 2, [128, 512], F32)
        for g in range(4):
            ps, pk = self.psum.next()
            sl = slice(g * 512, (g + 1) * 512)
            for kc in range(KC):
                q, qk = sq.next()
                self.act(q[:], self.xT[:, kc, sl], AF.Square, reads=[("xT", kc, g)], writes=[qk])
                self.mm(ps[:], self.ones_b(), q[:], kc == 0, kc == KC - 1, reads=[qk, "cbf"], writes=[pk])
            r, rk = rs.next()
            self.act(r[:], ps[:], AF.Ln, reads=[pk], writes=[rk], scale=1.0 / D, bias=self.eps_ap())
            self.act(r[:], r[:], AF.Exp, reads=[rk], writes=[rk], scale=-0.5)
            for kc in range(KC):
                self.tt(self.hT[:, kc, sl], self.xT[:, kc, sl], r[:], ALU.mult,
                        reads=[("xT", kc, g), rk], writes=[("hT", kc, g)])
        self.rstd_ring = rs
        S.barrier()
        A.reset(m)

    def eps_ap(self):
        return self.consts[:, C_EPS, 0:1]

    def wload(self, w_ap, kcn, n, wst_ring, wbf_ring, scale=None):
        st, sk = wst_ring.next()
        wb, wk = wbf_ring.next()
        self.dma(st[:, :kcn, :n], w_ap.rearrange("(k p) n -> p k n", p=128), reads=[], writes=[sk])
        for kc in range(kcn):
            if scale is not None:
                self.act(wb[:, kc, :n], st[:, kc, :n], AF.Copy, reads=[sk, "nw"], writes=[wk],
                         scale=scale[:, kc:kc + 1])
            else:
                self.act(wb[:, kc, :n], st[:, kc, :n], AF.Copy, reads=[sk], writes=[wk])
        return wb, wk

    def hT_keys(self, g):
        return [("hT", kc, g) for kc in range(KC)]

    def linear_T(self, wb, wk, ncols, kcn, rhs_fn, rhs_keys_fn, evac, groups=range(4)):
        for c0 in range(0, ncols, 128):
            n = min(128, ncols - c0)
            for g in groups:
                ps, pk = self.psum.next()
                for kc in range(kcn):
                    self.mm(ps[:n, :], wb[:, kc, c0:c0 + n], rhs_fn(kc, g), kc == 0, kc == kcn - 1,
                            reads=[wk] + rhs_keys_fn(g), writes=[pk])
                evac(c0 // 128, g, ps, pk, n)

    def mlp(self, l):
        A, S = self.A, self.S
        m = A.mark()
        self.hT = A.alloc("hT", [128, KC, S_LEN], BF16)
        self.rmsnorm()
        wst = self.ring("wst", 2, [128, KC, 512], F32)
        wbf = self.ring("wbf", 2, [128, KC, 512], BF16)
        ub = self.ring("ub", 3, [128, S_LEN], BF16)
        rr = self.ring("rr", 3, [128, 512], F32)
        scale = self.nw[:, 2 * l + 1, :]
        for cb in range(D_FF // 512):
            wb, wk = self.wload(self.w_up_d[l][:, cb * 512:(cb + 1) * 512], KC, 512, wst, wbf, scale=scale)
            cur = {}

            def evac(ci, g, ps, pk, n, cb=cb, cur=cur):
                if g == 0:
                    cur["u"] = ub.next()
                u, uk = cur["u"]
                r, rk = rr.next()
                self.ts(r[:], ps[:], 0.0, ALU.max, reads=[pk], writes=[rk])
                self.act(u[:, g * 512:(g + 1) * 512], r[:], AF.Square, reads=[rk], writes=[uk])
                if g == 3:
                    f = cb * 4 + ci
                    self.dma(self.u_scr[f * 128:(f + 1) * 128, :], u[:], reads=[uk], writes=[("u_scr", f)])

            self.linear_T(wb, wk, 512, KC, lambda kc, g: self.hT[:, kc, g * 512:(g + 1) * 512],
                          self.hT_keys, evac)
        S.barrier()
        A.reset(m)
        wst = self.ring("wdst", 2, [128, 4, 512], F32)
        wbf = self.ring("wdbf", 1, [128, 32, 512], BF16)
        ur = self.ring("ur", 1, [128, 32, 512], BF16)
        for half in range(2):
            wb, wk = wbf.next()
            for q in range(8):
                st, sk = wst.next()
                self.dma(st[:], self.w_down_d[l][q * 512:(q + 1) * 512, half * 512:(half + 1) * 512]
                         .rearrange("(k p) n -> p k n", p=128), reads=[], writes=[sk])
                for k4 in range(4):
                    self.act(wb[:, q * 4 + k4, :], st[:, k4, :], AF.Copy, reads=[sk], writes=[wk])
            for g in range(4):
                u, uk = ur.next()
                self.dma(u[:], self.u_scr[:, g * 512:(g + 1) * 512].rearrange("(k p) t -> p k t", p=128),
                         reads=[("u_scr", f) for f in range(32)], writes=[uk])
                for ci in range(4):
                    ps, pk = self.psum.next()
                    for kc in range(32):
                        self.mm(ps[:], wb[:, kc, ci * 128:(ci + 1) * 128], u[:, kc, :], kc == 0, kc == 31,
                                reads=[wk, uk], writes=[pk])
                    oc = half * 4 + ci
                    sl = slice(g * 512, (g + 1) * 512)
                    self.tt(self.xT[:, oc, sl], self.xT[:, oc, sl], ps[:], ALU.add,
                            reads=[pk, ("xT", oc, g)], writes=[("xT", oc, g)])
        S.barrier()
        A.reset(m)

    def layer(self, l):
        if self.mixers:
            self.mixer(l)
        if self.do_mlp:
            self.mlp(l)


    def mixer(self, l):
        A, S = self.A, self.S
        m00 = A.mark()
        if "dn" in self.mixers:
            self.dn_egc = A.alloc("degc", [128, 16, 8], F32)
            self.dn_eglast = A.alloc("deglast", [128, 32, 8], F32)
            self.dn_nwb, self.dn_nwbk = self.bload("dnwb", self.dn_norm_w_d[l], 128)
        m0 = A.mark()
        self.hT = A.alloc("hT", [128, KC, S_LEN], BF16)
        self.rmsnorm()
        m1 = A.mark()
        if "sb" in self.mixers:
            self.sb_phase(l)
            S.barrier()
            A.reset(m1)
        if "ssm" in self.mixers:
            self.ssm_phase(l)
            S.barrier()
            A.reset(m1)
        if "dn" in self.mixers:
            self.dn_phase(l)
            S.barrier()
            A.reset(m1)
        self.gates_phase(l)
        S.barrier()
        A.reset(m0)
        if "dn" in self.mixers:
            self.dn_phase2(l)
            S.barrier()
            A.reset(m0)
        self.merge_phase(l)
        S.barrier()
        A.reset(m00)

    def dn_phase2(self, l):
        A, S = self.A, self.S
        cst = self.consts
        pst = self.pst
        NH = 8
        names = ["u", "wT", "q", "attnT", "kd", "sg"]
        sets = [[A.alloc(f"d2{n}", [128, NH, 512], BF16) for n in names] for _ in range(1)]
        Sst = A.alloc("d2S", [128, NH, 128], F32)
        Sbf = A.alloc("d2Sbf", [128, NH, 128], BF16)
        vnr = self.ring("d2vn", 2, [128, NH, 128], BF16)
        t1 = A.alloc("d2t1", [128, NH, 128], F32)
        ot = A.alloc("d2ot", [128, NH, 128], F32)
        sq = A.alloc("d2sq", [128, NH, 128], F32)
        ssq = A.alloc("d2ssq", [128, 2, NH], F32)
        obr = self.ring("d2ob", 2, [128, NH, 128], BF16)
        otl = self.ring("d2otl", 2, [128, NH, 128], BF16)
        egc, eglast, nwb, nwbk = self.dn_egc, self.dn_eglast, self.dn_nwb, self.dn_nwbk
        self.memset(Sst[:], 0.0, writes=["d2S"])
        self.memset(Sbf[:], 0.0, writes=["d2Sbf"])
        bO = [(pst[i], ("p2O", i)) for i in range(4)]
        bW = [(pst[4], ("p2W", 0)), (pst[5], ("p2W", 1))]
        bS = [(pst[6], ("p2S", 0)), (pst[7], ("p2S", 1))]
        o_view = self.o_scr[0].rearrange("(h e) s -> e h s", h=NH)
        for q in range(4):
            st = sets[0]
            sk = ("d2set", 0)
            for ni, nm in enumerate(names):
                self.dma(st[ni][:], self.dn_scr[nm][:, :, q * 512:(q + 1) * 512].rearrange("h p t -> p h t"),
                         reads=[("dn_scr", nm, h) for h in range(NH)], writes=[(sk, ni)])
            U, WT, Q, AT, KD, SG = st
            for tl in range(4):
                t = 4 * q + tl
                cs = slice(tl * 128, (tl + 1) * 128)
                vn, vnk = vnr.next()
                for c in range(2):
                    j = 2 * t + c
                    pb = 64 * c
                    prt = slice(pb, pb + 64)
                    ccs = slice(tl * 128 + pb, tl * 128 + pb + 64)
                    for h in range(NH):
                        pw, pwk = bW[h // 4]
                        self.mm(pw[prt, (h % 4) * 128:(h % 4 + 1) * 128], WT[:, h, ccs], Sbf[:, h, :], True, True,
                                reads=[(sk, 1), ("d2Sbf", h // 4)], writes=[pwk])
                    for hb in range(2):
                        pw, pwk = bW[hb]
                        self.tt(vn[prt, 4 * hb:4 * hb + 4, :], U[prt, 4 * hb:4 * hb + 4, cs],
                                pw[prt, :].rearrange("p (a b) -> p a b", a=4), ALU.subtract, reads=[(sk, 0), pwk],
                                writes=[(vnk, c, hb)])
                    for h in range(NH):
                        po, pok = bO[h // 2]
                        co = (h % 2) * 256
                        self.mm(po[prt, co:co + 128], Q[:, h, ccs], Sbf[:, h, :], True, True,
                                reads=[(sk, 2), ("d2Sbf", h // 4)], writes=[(pok, c)])
                        self.mm(po[prt, co + 128:co + 256], AT[prt, h, ccs], vn[prt, h, :], True, True,
                                reads=[(sk, 3), (vnk, c, h // 4)], writes=[(pok, c)])
                        ps_, psk_ = bS[h // 4]
                        self.mm(ps_[:, (h % 4) * 128:(h % 4 + 1) * 128], KD[prt, h, cs], vn[prt, h, :], True, True,
                                reads=[(sk, 4), (vnk, c, h // 4)], writes=[psk_])
                    for hb in range(2):
                        ps_, psk_ = bS[hb]
                        hs = slice(4 * hb, 4 * hb + 4)
                        self.tt(Sst[:, hs, :], Sst[:, hs, :], eglast[:, j, hs].unsqueeze(2).to_broadcast([128, 4, 128]),
                                ALU.mult, reads=[("d2S", hb), "deglast"], writes=[("d2S", hb)])
                        self.tt(Sst[:, hs, :], Sst[:, hs, :], ps_[:].rearrange("p (a b) -> p a b", a=4), ALU.add,
                                reads=[("d2S", hb), psk_], writes=[("d2S", hb)])
                        self.copy(Sbf[:, hs, :], Sst[:, hs, :], reads=[("d2S", hb)], writes=[("d2Sbf", hb)], eng="act")
                for b4 in range(4):
                    po, pok = bO[b4]
                    hs = slice(2 * b4, 2 * b4 + 2)
                    pv = po[:].rearrange("p (a b) -> p a b", a=2)
                    self.tt(t1[:, hs, :], pv[:, :, 0:128], egc[:, t, hs].unsqueeze(2).to_broadcast([128, 2, 128]), ALU.mult,
                            reads=[(pok, 0), (pok, 1), "degc"], writes=[("d2t1", b4)])
                    self.tt(ot[:, hs, :], t1[:, hs, :], pv[:, :, 128:256], ALU.add,
                            reads=[("d2t1", b4), (pok, 0), (pok, 1)], writes=[("d2ot", b4)])
                otk = [("d2ot", b4) for b4 in range(4)]
                self.act(sq[:], ot[:], AF.Square, reads=otk, writes=["d2sq"])
                self.S.op("dve", lambda e: e.tensor_reduce(out=ssq[:, 0, :], in_=sq[:], op=ALU.add,
                                                           axis=mybir.AxisListType.X),
                          reads=["d2sq"], writes=["d2ssq"])
                self.act(ssq[:, 1, :], ssq[:, 0, :], AF.Ln, reads=["d2ssq", "consts"], writes=["d2ssq"], scale=1.0 / 128,
                         bias=self.eps_ap())
                self.act(ssq[:, 1, :], ssq[:, 1, :], AF.Exp, reads=["d2ssq"], writes=["d2ssq"], scale=-0.5)
                self.tt(ot[:], ot[:], ssq[:, 1, :].unsqueeze(2).to_broadcast([128, NH, 128]), ALU.mult,
                        reads=otk + ["d2ssq"], writes=otk)
                self.tt(ot[:], ot[:], nwb[:].unsqueeze(1).to_broadcast([128, NH, 128]), ALU.mult, reads=otk + [nwbk],
                        writes=otk)
                ob, obk = obr.next()
                self.tt(ob[:], ot[:], SG[:, :, cs], ALU.mult, reads=otk + [(sk, 5)], writes=[obk])
                px, pxk = bW[0]
                pxb = px[:].bitcast(BF16)
                for h in range(NH):
                    self.tr(pxb[:, h * 128:(h + 1) * 128], ob[:, h, :], self.ident_b(), reads=[obk, "cbf"], writes=[pxk])
                otile, otlk = otl.next()
                self.copy(otile[:], pxb[:].rearrange("p (a b) -> p a b", a=NH), reads=[pxk], writes=[otlk], eng="act")
                self.dma(o_view[:, :, t * 128:(t + 1) * 128], otile[:], reads=[otlk], writes=[("o_scr", 0, "t", t)])

    def branches(self):
        return [i for i, n in enumerate(("dn", "sb", "ssm")) if n in self.mixers]

    def gates_phase(self, l):
        wst = self.ring("gwst", 2, [128, KC, 512], F32)
        wbf = self.ring("gwbf", 2, [128, KC, 512], BF16)
        gb = self.ring("gb", 3, [128, S_LEN], BF16)
        scale = self.nw[:, 2 * l, :]
        for i in self.branches():
            for cb in range(2):
                c0 = O_GATE + i * D + cb * 512
                wb, wk = self.wload(self.w_in_d[l][:, c0:c0 + 512], KC, 512, wst, wbf, scale=scale)
                cur = {}

                def evac(ci, g, ps, pk, n, cb=cb, i=i, cur=cur):
                    if g == 0:
                        cur["t"] = gb.next()
                    t, tk = cur["t"]
                    self.act(t[:, g * 512:(g + 1) * 512], ps[:], AF.Sigmoid, reads=[pk], writes=[tk])
                    if g == 3:
                        oc = cb * 4 + ci
                        self.dma(self.g_scr[i][oc * 128:(oc + 1) * 128, :], t[:], reads=[tk],
                                 writes=[("g_scr", i, oc)])

                self.linear_T(wb, wk, 512, KC, lambda kc, g: self.hT[:, kc, g * 512:(g + 1) * 512],
                              self.hT_keys, evac)

    def merge_phase(self, l):
        A, S = self.A, self.S
        wst = self.ring("mwst", 1, [128, KC, 512], F32)
        wbf = self.ring("mwbf", 2, [128, KC, 512], BF16)
        mg = A.alloc("mg", [128, KC, 1024], F32)
        mgb = A.alloc("mgb", [128, KC, 1024], BF16)
        ob = self.ring("ob", 1, [128, KC, 1024], BF16)
        gt = self.ring("gt", 3, [128, 1024], BF16)
        self.mtmp = self.ring("mtmp", 3, [128, 512], F32)
        brs = self.branches()
        for half in range(2):
            tsl = slice(half * 1024, (half + 1) * 1024)
            for bi, i in enumerate(brs):
                o, ok = ob.next()
                self.dma(o[:], self.o_scr[i][:, tsl].rearrange("(k p) t -> p k t", p=128),
                         reads=[("o_scr", i, c) for c in range(8)], writes=[ok])
                for cb in range(2):
                    wb, wk = self.wload(self.w_branch_d[l][i][:, cb * 512:(cb + 1) * 512], KC, 512, wst, wbf)
                    cur = {}

                    def evac(ci, g, ps, pk, n, cb=cb, i=i, bi=bi, cur=cur, half=half, tsl=tsl):
                        oc = cb * 4 + ci
                        gl = g - 2 * half
                        if gl == 0:
                            cur["g"] = gt.next()
                            t, tk = cur["g"]
                            self.dma(t[:], self.g_scr[i][oc * 128:(oc + 1) * 128, tsl],
                                     reads=[("g_scr", i, oc)], writes=[tk])
                        t, tk = cur["g"]
                        dst = mg[:, oc, gl * 512:(gl + 1) * 512]
                        mk = ("mg", oc, gl)
                        if bi == 0:
                            self.tt(dst, ps[:], t[:, gl * 512:(gl + 1) * 512], ALU.mult,
                                    reads=[pk, tk], writes=[mk])
                        else:
                            tmp, tmk = self.mtmp.next()
                            self.tt(tmp[:], ps[:], t[:, gl * 512:(gl + 1) * 512], ALU.mult,
                                    reads=[pk, tk], writes=[tmk])
                            self.tt(dst, dst, tmp[:], ALU.add, reads=[tmk, mk], writes=[mk], eng="pool")
                        if bi == len(brs) - 1:
                            self.copy(mgb[:, oc, gl * 512:(gl + 1) * 512], dst, reads=[mk],
                                      writes=[("mgb", oc, gl)], eng="act")

                    self.linear_T(wb, wk, 512, KC, lambda kc, g, o=o, half=half: o[:, kc, (g - 2 * half) * 512:(g - 2 * half + 1) * 512],
                                  lambda g, ok=ok: [ok], evac, groups=(2 * half, 2 * half + 1))
            for cb in range(2):
                wb, wk = self.wload(self.w_out_d[l][:, cb * 512:(cb + 1) * 512], KC, 512, wst, wbf)

                def evac2(ci, g, ps, pk, n, cb=cb):
                    oc = cb * 4 + ci
                    sl = slice(g * 512, (g + 1) * 512)
                    self.tt(self.xT[:, oc, sl], self.xT[:, oc, sl], ps[:], ALU.add,
                            reads=[pk, ("xT", oc, g)], writes=[("xT", oc, g)])

                self.linear_T(wb, wk, 512, KC,
                              lambda kc, g, half=half: mgb[:, kc, (g - 2 * half) * 512:(g - 2 * half + 1) * 512],
                              lambda g, half=half: [("mgb", kc, g - 2 * half) for kc in range(KC)], evac2,
                              groups=(2 * half, 2 * half + 1))

    def sb_phase(self, l):
        A, S = self.A, self.S
        R = {}
        R["wst"] = self.ring("sbwst", 1, [128, KC, 384], F32)
        R["wbf"] = self.ring("sbwbf", 2, [128, KC, 384], BF16)
        R["qT"] = self.ring("sbq", 2, [128, S_LEN], BF16)
        R["kT"] = self.ring("sbk", 2, [128, S_LEN], BF16)
        R["v"] = self.ring("sbv", 2, [128, 16, 128], BF16)
        R["osb"] = self.ring("sbo", 2, [128, S_LEN], BF16)
        R["e"] = self.ring("sbe", 4, [128, 512], F32)
        R["spb"] = self.ring("sbsp", 4, [128, 512], BF16)
        R["xa"] = self.ring("sbxa", 2, [128, 512], F32)
        R["w"] = self.ring("sbw", 3, [128, 512], BF16)
        R["pa"] = Ring(self.pst[0:4], "psum")
        for hp in range(8):
            self.sb_unit(l, hp, R)

    def sb_unit(self, l, hp, R):
        scale = self.nw[:, 2 * l, :]
        st, sk = R["wst"].next()
        wb, wk = R["wbf"].next()
        for j in range(3):
            base = O_SBQKV + j * 1024 + 128 * hp
            self.dma(st[:, :, j * 128:(j + 1) * 128],
                     self.w_in_d[l][:, base:base + 128].rearrange("(k p) n -> p k n", p=128),
                     reads=[], writes=[(sk, j)])
        for kc in range(KC):
            self.act(wb[:, kc, :], st[:, kc, :], AF.Copy, reads=[(sk, 0), (sk, 1), (sk, 2), "nw"], writes=[wk],
                     scale=scale[:, kc:kc + 1])
        qT, qk = R["qT"].next()
        kT, kk = R["kT"].next()
        v, vk = R["v"].next()

        def evac_qk(ci, g, ps, pk, n):
            dst, dk = (qT, qk) if ci == 0 else (kT, kk)
            self.copy(dst[:, g * 512:(g + 1) * 512], ps[:], reads=[pk], writes=[(dk, g)],
                      eng=("act" if g % 2 else "dve"))

        self.linear_T(wb, wk, 256, KC, lambda kc, g: self.hT[:, kc, g * 512:(g + 1) * 512], self.hT_keys, evac_qk)
        for tq in range(4):
            ps, pk = self.psum.next()
            for t in range(4):
                tt_ = tq * 4 + t
                for kc in range(KC):
                    self.mm(ps[:, t * 128:(t + 1) * 128], self.hT[:, kc, tt_ * 128:(tt_ + 1) * 128],
                            wb[:, kc, 256:384], kc == 0, kc == KC - 1, reads=[wk, ("hT", kc, tq)], writes=[pk])
            self.copy(v[:, tq * 4:(tq + 1) * 4, :], ps[:].rearrange("p (t c) -> p t c", t=4), reads=[pk],
                      writes=[(vk, tq)], eng=("act" if tq % 2 else "dve"))
        osb, ok = R["osb"].next()
        mstrict = self.consts[:, C_MSTRICT, :]
        tinc = self.cbf[:, C_TINC, :]
        tlow = self.cbf[:, C_MSTRICT, :]
        one_col = self.consts[:, C_ONES, 0:1]
        pst = self.pst
        pa_ring = R["pa"]
        for g in range(4):
            racc = [(pst[4], ("psum", 4)), (pst[5], ("psum", 5))]
            po = [(pst[6], ("psacc", 0)), (pst[7], ("psacc", 1))]
            tiles = []
            for kb in range(4 * g + 3, -1, -1):
                for e in range(2):
                    t0 = max(kb * 128, g * 512)
                    tiles.append(dict(e=e, kb=kb, pb=64 * e, t0=t0, N=(g + 1) * 512 - t0, c0=t0 - g * 512,
                                      diag=kb * 128 >= g * 512, first=(kb == 4 * g + 3), last=(kb == 0)))

            def stage0(T):
                pa, pak = pa_ring.next()
                pb, N, t0, kb = T["pb"], T["N"], T["t0"], T["kb"]
                self.mm(pa[:, :N], kT[pb:pb + 64, kb * 128:(kb + 1) * 128], qT[pb:pb + 64, t0:t0 + N], True, True,
                        reads=[(kk, kb // 4), (qk, g)], writes=[pak])
                T["pa"], T["pak"] = pa, pak

            def stage1(T):
                N, c0 = T["N"], T["c0"]
                ee, ek = R["e"].next()
                self.act(ee[:, :N], T["pa"][:, :N], AF.Exp, reads=[T["pak"]], writes=[ek], scale=0.125)
                if T["diag"]:
                    self.tt(ee[:, :128], ee[:, :128], mstrict, ALU.mult, reads=[ek, "consts"], writes=[ek])
                spb, spk = R["spb"].next()
                self.act(spb[:, :N], ee[:, :N], AF.Ln, reads=[ek, "consts"], writes=[spk], bias=one_col, scale=1.0)
                ra, rak = racc[T["e"]]
                self.mm(ra[:, c0:512], tinc, spb[:, :N], T["first"], False, reads=[spk, "cbf"], writes=[rak],
                        skip_group_check=True)
                T.update(ee=ee, ek=ek, spb=spb, spk=spk)

            def stage2(T):
                N, c0, pb, kb = T["N"], T["c0"], T["pb"], T["kb"]
                ra, rak = racc[T["e"]]
                xa, xk = R["xa"].next()
                self.act(xa[:, :N], ra[:, c0:512], AF.Exp, reads=[rak], writes=[xk], scale=-1.0)
                if not T["last"]:
                    self.mm(ra[:, c0:512], tlow, T["spb"][:, :N], False, True, reads=[T["spk"], "cbf"], writes=[rak],
                            skip_group_check=True)
                w, wwk = R["w"].next()
                self.tt(w[:, :N], T["ee"][:, :N], xa[:, :N], ALU.mult, reads=[T["ek"], xk], writes=[wwk])
                pp, ppk = po[T["e"]]
                self.mm(pp[pb:pb + 64, c0:512], v[:, kb, pb:pb + 64], w[:, :N], T["first"], T["last"],
                        reads=[(vk, kb // 4), wwk], writes=[ppk], skip_group_check=True)

            n = len(tiles)
            for i in range(min(2, n)):
                stage0(tiles[i])
            for i in range(n + 2):
                if i - 2 >= 0:
                    stage2(tiles[i - 2])
                if i < n:
                    stage1(tiles[i])
                if i + 2 < n:
                    stage0(tiles[i + 2])
            for e in range(2):
                pp, ppk = po[e]
                pb = 64 * e
                self.copy(osb[pb:pb + 64, g * 512:(g + 1) * 512], pp[pb:pb + 64, :], reads=[ppk],
                          writes=[(ok, e, g)], eng="dve")
        self.dma(self.o_scr[1][hp * 128:(hp + 1) * 128, :], osb[:],
                 reads=[(ok, e, g) for e in range(2) for g in range(4)], writes=[("o_scr", 1, hp)])


    def bload(self, name, dram_row_ap, n):
        t = self.A.alloc(name, [128, n], F32)
        k = self.key(name)
        self.dma(t[:], dram_row_ap.partition_broadcast(128), reads=[], writes=[k])
        return t, k

    def conv_silu(self, l, wb, wk, wcol, conv_w_d, conv_b_d, ch, raw, rawk, acc, acck, cw, dst, dstk, func=AF.Silu):
        c, ck = cw.next()
        self.dma(c[:, 0:4], conv_w_d[l][:, ch:ch + 128].rearrange("k c -> c k"), reads=[], writes=[(ck, 0)], slow=True)
        if conv_b_d is not None:
            self.dma(c[:, 4:5], conv_b_d[l][ch:ch + 128].rearrange("(c o) -> c o", o=1), reads=[], writes=[(ck, 1)],
                     slow=True)
        else:
            self.memset(c[:, 4:5], 0.0, writes=[(ck, 1)])

        def evac(ci, g, ps, pk, n):
            self.copy(raw[:, 3 + g * 512:3 + (g + 1) * 512], ps[:], reads=[pk], writes=[(rawk, g)],
                      eng=("act" if g % 2 else "dve"))

        for g in range(4):
            ps, pk = self.psum.next()
            for kc in range(KC):
                self.mm(ps[:], wb[:, kc, wcol:wcol + 128], self.hT[:, kc, g * 512:(g + 1) * 512], kc == 0, kc == KC - 1,
                        reads=[wk] + self.hT_keys(g), writes=[pk])
            evac(0, g, ps, pk, 128)
        rk = [(rawk, g) for g in range(4)] + [(rawk, "pad")]
        self.ts(acc[:], raw[:, 0:S_LEN], c[:, 0:1], ALU.mult, reads=rk + [(ck, 0), (ck, 1)], writes=[acck],
                s2=c[:, 4:5], op1=ALU.add)
        for k in range(1, 4):
            self.stt(acc[:], raw[:, k:k + S_LEN], c[:, k:k + 1], acc[:], ALU.mult, ALU.add,
                     reads=rk + [(ck, 0), acck], writes=[acck])
        self.act(dst, acc[:], func, reads=[acck], writes=[dstk])

    def to_tok(self, src, srck, dst_fn, dstk):
        for tq in range(4):
            ps, pk = self.psum.next()
            pb = ps[:].bitcast(BF16)
            for t in range(4):
                tt_ = tq * 4 + t
                self.tr(pb[:, t * 128:(t + 1) * 128], src[:, tt_ * 128:(tt_ + 1) * 128], self.ident_b(),
                        reads=(list(srck) if isinstance(srck, list) else [srck]) + ["cbf"], writes=[pk])
            self.copy(dst_fn(tq), pb[:, 0:512].rearrange("p (t c) -> p t c", t=4), reads=[pk], writes=[(dstk, tq)],
                      eng=("act" if tq % 2 else "dve"))

    def ssm_phase(self, l):
        A, S = self.A, self.S
        scale = self.nw[:, 2 * l, :]
        cst = self.consts
        wdst = A.alloc("wdtst", [128, KC, 16], F32)
        wdt = A.alloc("wdt", [128, KC, 16], BF16)
        self.dma(wdst[:], self.w_in_d[l][:, O_SSMDT:O_SSMDT + 16].rearrange("(k p) n -> p k n", p=128), [], ["wdtst"])
        for kc in range(KC):
            self.act(wdt[:, kc, :], wdst[:, kc, :], AF.Copy, reads=["wdtst", "nw"], writes=["wdt"], scale=scale[:, kc:kc + 1])
        dtb, dtbk = self.bload("dtb", self.ssm_dt_bias_d[l], 16)
        alog, alogk = self.bload("alog", self.ssm_a_log_d[l], 16)
        dbc, dbck = self.bload("dbc", self.ssm_d_d[l], 16)
        dt = A.alloc("dt", [128, 16, 16], F32)
        av = A.alloc("av", [128, 16, 16], F32)
        acum = A.alloc("acum", [128, 16, 16], F32)
        eacum = A.alloc("eacum", [128, 16, 16], F32)
        dtds = A.alloc("dtds", [128, 16, 16], F32)
        eatot = A.alloc("eatot", [128, 32, 16], F32)
        tmp = A.alloc("ptmp", [128, 16, 16], F32)
        ps, pk = self.psum.next()
        for t in range(16):
            for kc in range(KC):
                self.mm(ps[:, t * 16:(t + 1) * 16], self.hT[:, kc, t * 128:(t + 1) * 128], wdt[:, kc, :], kc == 0,
                        kc == KC - 1, reads=["wdt", ("hT", kc, t // 4)], writes=[pk])
        self.tt(dt[:], ps[:, 0:256].rearrange("p (t h) -> p t h", t=16),
                dtb[:].unsqueeze(1).to_broadcast([128, 16, 16]), ALU.add, reads=[pk, dtbk], writes=["dt"])
        one_col = cst[:, C_ONES, 0:1]
        self.act(tmp[:], dt[:], AF.Exp, reads=["dt"], writes=["ptmp"])
        self.act(dt[:], tmp[:], AF.Ln, reads=["ptmp", "consts"], writes=["dt"], bias=one_col, scale=1.0)
        self.act(alog[:], alog[:], AF.Exp, reads=[alogk], writes=[alogk])
        self.S.op("dve", lambda e: e.scalar_tensor_tensor(out=av[:], in0=dt[:], scalar=-1.0,
                                                          in1=alog[:].unsqueeze(1).to_broadcast([128, 16, 16]),
                                                          op0=ALU.mult, op1=ALU.mult),
                  reads=["dt", alogk], writes=["av"])
        ps, pk = self.psum.next()
        for t in range(16):
            self.mm(ps[:, t * 16:(t + 1) * 16], cst[:, C_TRI2, :], av[:, t, :], True, True, reads=["av", "consts"], writes=[pk])
        self.copy(acum[:], ps[:, 0:256].rearrange("p (t h) -> p t h", t=16), reads=[pk], writes=["acum"])
        self.act(eacum[:], acum[:], AF.Exp, reads=["acum"], writes=["eacum"])
        ps, pk = self.psum.next()
        for t in range(16):
            self.mm(ps[:, t * 16:(t + 1) * 16], cst[:, C_BD, :], av[:, t, :], True, True, reads=["av", "consts"], writes=[pk])
        self.tt(tmp[:], ps[:, 0:256].rearrange("p (t h) -> p t h", t=16), acum[:], ALU.subtract, reads=[pk, "acum"],
                writes=["ptmp"])
        self.act(tmp[:], tmp[:], AF.Exp, reads=["ptmp"], writes=["ptmp"])
        self.tt(dtds[:], tmp[:], dt[:], ALU.mult, reads=["ptmp", "dt"], writes=["dtds"])
        ps, pk = self.psum.next()
        for t in range(16):
            for c in range(2):
                j = 2 * t + c
                self.mm(ps[:, j * 16:(j + 1) * 16], cst[:, C_IND0 + c, :], av[:, t, :], True, True,
                        reads=["av", "consts"], writes=[pk])
        self.act(eatot[:].rearrange("p j h -> p (j h)"), ps[:], AF.Exp, reads=[pk], writes=["eatot"])

        mG = A.mark()
        for g in range(4):
            A.reset(mG)
            S.barrier()
            wbf = A.alloc("swz", [128, KC, 256], BF16)
            BT = A.alloc("sBT", [128, S_LEN], BF16)
            CT = A.alloc("sCT", [128, S_LEN], BF16)
            x_tok = A.alloc("sxtok", [128, 16, 256], BF16)
            B_tok = A.alloc("sBtok", [128, 16, 128], BF16)
            nwb, nwbk = self.bload("snwb", self.ssm_norm_w_d[l][256 * g:256 * (g + 1)], 256)
            mA = A.mark()
            wst = self.ring("swst", 2, [128, KC, 128], F32)
            wbx = A.alloc("swbx", [128, KC, 512], BF16)
            raw = A.alloc("sraw", [128, S_LEN + 4], F32)
            acc = A.alloc("sacc", [128, S_LEN], F32)
            xc = self.ring("sxc", 2, [128, S_LEN], BF16)
            cw = self.ring("scw", 2, [128, 8], F32)
            rawk = self.key("sraw")
            self.memset(raw[:, 0:3], 0.0, writes=[(rawk, "pad")])
            cols = [O_SSMZ + 256 * g, O_SSMZ + 256 * g + 128, O_SSMXBC + 256 * g, O_SSMXBC + 256 * g + 128,
                    O_SSMXBC + 1024 + 128 * g, O_SSMXBC + 1536 + 128 * g]
            wk = self.key("swbf")
            for j, c0 in enumerate(cols):
                st, sk = wst.next()
                self.dma(st[:], self.w_in_d[l][:, c0:c0 + 128].rearrange("(k p) n -> p k n", p=128), [], [sk])
                for kc in range(KC):
                    wdst_ = wbf[:, kc, j * 128:(j + 1) * 128] if j < 2 else wbx[:, kc, (j - 2) * 128:(j - 1) * 128]
                    self.act(wdst_, st[:, kc, :], AF.Copy, reads=[sk, "nw"], writes=[(wk, j)],
                             scale=scale[:, kc:kc + 1])
            chs = [256 * g, 256 * g + 128, 1024 + 128 * g, 1536 + 128 * g]
            acck = self.key("sacc")
            xtk = self.key("sxtok")
            btk = self.key("sBtok")
            for jj, ch in enumerate(chs):
                j = jj + 2
                if jj < 2:
                    dst, dk = xc.next()
                elif jj == 2:
                    dst, dk = BT, "sBT"
                else:
                    dst, dk = CT, "sCT"
                self.conv_silu(l, wbx, (wk, j), jj * 128, self.ssm_conv_w_d, self.ssm_conv_b_d, ch, raw, rawk, acc, acck,
                               cw, dst[:], dk)
                if jj < 2:
                    self.to_tok(dst, dk, lambda tq, jj=jj: x_tok[:, tq * 4:(tq + 1) * 4, jj * 128:(jj + 1) * 128], (xtk, jj))
                elif jj == 2:
                    self.to_tok(dst, dk, lambda tq: B_tok[:, tq * 4:(tq + 1) * 4, :], btk)
            S.barrier()
            A.reset(mA)
            xdt = A.alloc("sxdt", [128, 16, 256], BF16)
            xdtd = A.alloc("sxdtd", [128, 16, 256], BF16)
            xD = A.alloc("sxD", [128, 16, 256], BF16)
            oT = A.alloc("soT", [128, 2, S_LEN], BF16)
            state = A.alloc("sstate", [128, 256], F32)
            state_bf = A.alloc("sstatebf", [128, 256], BF16)
            abc = self.ring("sabc", 2, [128, 128], F32)
            dm = self.ring("sdm", 2, [128, 4, 128], F32)
            MT = self.ring("sMT", 2, [128, 4, 128], BF16)
            szr = self.ring("ssz", 1, [128, 256], F32)
            t1r = self.ring("st1", 1, [128, 256], F32)
            yr = self.ring("sy", 2, [128, 256], F32)
            jr = self.ring("sjunk", 1, [128, 256], F32)
            ssr = self.ring("sssq", 4, [128, 2], F32)
            obr = self.ring("sob", 2, [128, 256], BF16)
            xtks = [(xtk, jj, tq) for jj in range(2) for tq in range(4)]
            x4 = x_tok[:].rearrange("p t (h c) -> p t h c", h=4)
            hs = slice(4 * g, 4 * g + 4)
            self.tt(xdt[:].rearrange("p t (h c) -> p t h c", h=4), x4,
                    dt[:, :, hs].unsqueeze(3).to_broadcast([128, 16, 4, 64]), ALU.mult, reads=xtks + ["dt"], writes=["sxdt"])
            self.tt(xdtd[:].rearrange("p t (h c) -> p t h c", h=4), x4,
                    dtds[:, :, hs].unsqueeze(3).to_broadcast([128, 16, 4, 64]), ALU.mult, reads=xtks + ["dtds"],
                    writes=["sxdtd"])
            for t in range(16):
                self.tt(xD[:, t, :].rearrange("p (h c) -> p h c", h=4), x_tok[:, t, :].rearrange("p (h c) -> p h c", h=4),
                        dbc[:, hs].unsqueeze(2).to_broadcast([128, 4, 64]), ALU.mult, reads=xtks + [dbck],
                        writes=[("sxD", t)], eng="pool")
            stk = self.key("sstate")
            sbk = self.key("sstatebf")
            self.memset(state[:], 0.0, writes=[stk])
            self.memset(state_bf[:], 0.0, writes=[sbk])
            ones_f = cst[:, C_ONES, :]
            for t in range(16):
                tsl = slice(t * 128, (t + 1) * 128)
                psS, psSk = self.psum.next()
                self.mm(psS[:, 0:128], BT[:, tsl], CT[:, tsl], True, True, reads=["sBT", "sCT"], writes=[psSk])
                psD, psDk = self.psum.next()
                for hh in range(4):
                    ab, abk = abc.next()
                    self.ts(ab[:], ones_f, av[:, t, 4 * g + hh:4 * g + hh + 1], ALU.mult, reads=["av", "consts"], writes=[abk])
                    self.mm(psD[:, hh * 128:(hh + 1) * 128], ab[:], cst[:, C_TRI2, :], True, False, reads=[abk, "consts"],
                            writes=[psDk])
                    self.mm(psD[:, hh * 128:(hh + 1) * 128], cst[:, C_TRI2NEG, :], ab[:], False, True,
                            reads=[abk, "consts"], writes=[psDk])
                d_, dk_ = dm.next()
                self.tt(d_[:], psD[:].rearrange("p (h c) -> p h c", h=4),
                        cst[:, C_NEGINCL, :].unsqueeze(1).to_broadcast([128, 4, 128]), ALU.add, reads=[psDk, "consts"],
                        writes=[dk_])
                L_, Lk_ = d_, dk_
                self.act(L_[:], d_[:], AF.Exp, reads=[dk_], writes=[Lk_])
                M_, Mk_ = MT.next()
                self.tt(M_[:], L_[:], psS[:, 0:128].unsqueeze(1).to_broadcast([128, 4, 128]), ALU.mult,
                        reads=[Lk_, psSk], writes=[Mk_])
                psY, psYk = self.psum.next()
                for hh in range(4):
                    cs = slice(hh * 64, (hh + 1) * 64)
                    self.mm(psY[:, cs], M_[:, hh, :], xdt[:, t, cs], True, False, reads=[Mk_, "sxdt"], writes=[psYk])
                    self.mm(psY[:, cs], self.ident_b(), xD[:, t, cs], False, True, reads=["cbf", ("sxD", t)], writes=[psYk])
                psZ, psZk = self.psum.next()
                for kc in range(KC):
                    self.mm(psZ[:, 0:256], self.hT[:, kc, tsl], wbf[:, kc, 0:256], kc == 0, kc == KC - 1,
                            reads=[(wk, 0), (wk, 1), ("hT", kc, t // 4)], writes=[psZk])
                sz, szk = szr.next()
                self.act(sz[:], psZ[:, 0:256], AF.Silu, reads=[psZk], writes=[szk])
                psO, psOk = self.psum_acc.next()
                for c in range(2):
                    j = 2 * t + c
                    self.mm(psO[64 * c:64 * c + 64, 0:256], CT[:, t * 128 + 64 * c:t * 128 + 64 * c + 64], state_bf[:], True, True,
                            reads=["sCT", sbk], writes=[psOk])
                    psT, psTk = self.psum.next()
                    self.mm(psT[:, 0:256], B_tok[64 * c:64 * c + 64, t, :], xdtd[64 * c:64 * c + 64, t, :], True, True,
                            reads=[(btk, t // 4), "sxdtd"], writes=[psTk])
                    self.tt(state[:].rearrange("p (h c) -> p h c", h=4), state[:].rearrange("p (h c) -> p h c", h=4),
                            eatot[:, j, hs].unsqueeze(2).to_broadcast([128, 4, 64]), ALU.mult, reads=[stk, "eatot"],
                            writes=[stk])
                    self.tt(state[:], state[:], psT[:, 0:256], ALU.add, reads=[stk, psTk], writes=[stk])
                    self.copy(state_bf[:], state[:], reads=[stk], writes=[sbk], eng="act")
                t1, t1k = t1r.next()
                self.tt(t1[:].rearrange("p (h c) -> p h c", h=4), psO[:, 0:256].rearrange("p (h c) -> p h c", h=4),
                        eacum[:, t, hs].unsqueeze(2).to_broadcast([128, 4, 64]), ALU.mult, reads=[psOk, "eacum"],
                        writes=[t1k])
                y, yk = yr.next()
                self.tt(y[:], t1[:], psY[:, 0:256], ALU.add, reads=[t1k, psYk], writes=[yk])
                self.tt(y[:], y[:], sz[:], ALU.mult, reads=[yk, szk], writes=[yk])
                jk_, jkk = jr.next()
                ss, ssk = ssr.next()
                self.S.op("act", lambda e, jk_=jk_, y=y, ss=ss: e.activation(out=jk_[:], in_=y[:], func=AF.Square,
                                                                          accum_out=ss[:, 0:1]),
                          reads=[yk], writes=[jkk, ssk])
                self.act(ss[:, 1:2], ss[:, 0:1], AF.Ln, reads=[ssk, "consts"], writes=[ssk], scale=1.0 / 256,
                         bias=self.eps_ap())
                self.act(ss[:, 1:2], ss[:, 1:2], AF.Exp, reads=[ssk], writes=[ssk], scale=-0.5)
                ob, obk = obr.next()
                self.stt(ob[:], y[:], ss[:, 1:2], nwb[:], ALU.mult, ALU.mult, reads=[yk, ssk, nwbk], writes=[obk])
                psX, psXk = self.psum.next()
                pxb = psX[:].bitcast(BF16)
                for ch in range(2):
                    self.tr(pxb[:, ch * 128:(ch + 1) * 128], ob[:, ch * 128:(ch + 1) * 128], self.ident_b(),
                            reads=[obk, "cbf"], writes=[psXk])
                self.copy(oT[:, :, tsl], pxb[:, 0:256].rearrange("p (c t) -> p c t", c=2), reads=[psXk],
                          writes=[("soT", t)], eng="act")
            for ch in range(2):
                oc = 2 * g + ch
                self.dma(self.o_scr[2][oc * 128:(oc + 1) * 128, :], oT[:, ch, :], reads=[("soT", t) for t in range(16)],
                         writes=[("o_scr", 2, oc)])


    def dn_phase(self, l):
        A, S = self.A, self.S
        scale = self.nw[:, 2 * l, :]
        cst = self.consts
        one_col = cst[:, C_ONES, 0:1]
        ones_f = cst[:, C_ONES, :]
        wast = A.alloc("dwast", [128, KC, 16], F32)
        wa = A.alloc("dwa", [128, KC, 16], BF16)
        self.dma(wast[:], self.w_in_d[l][:, O_DNA:O_DNA + 16].rearrange("(k p) n -> p k n", p=128), [], ["dwast"])
        for kc in range(KC):
            self.act(wa[:, kc, :], wast[:, kc, :], AF.Copy, reads=["dwast", "nw"], writes=["dwa"], scale=scale[:, kc:kc + 1])
        dtb, dtbk = self.bload("ddtb", self.dn_dt_bias_d[l], 8)
        alog, alogk = self.bload("dalog", self.dn_a_log_d[l], 8)
        gv = A.alloc("dg", [128, 16, 8], F32)
        beta = A.alloc("dbeta", [128, 16, 8], F32)
        gc = A.alloc("dgc", [128, 16, 8], F32)
        egc = self.dn_egc
        bg = A.alloc("dbg", [128, 16, 8], F32)
        kdec = A.alloc("dkdec", [128, 16, 8], F32)
        eglast = self.dn_eglast
        tmp = A.alloc("dtmp", [128, 16, 8], F32)
        ps, pk = self.psum.next()
        for t in range(16):
            for kc in range(KC):
                self.mm(ps[:, t * 16:(t + 1) * 16], self.hT[:, kc, t * 128:(t + 1) * 128], wa[:, kc, :], kc == 0,
                        kc == KC - 1, reads=["dwa", ("hT", kc, t // 4)], writes=[pk])
        pv = ps[:, 0:256].rearrange("p (t h) -> p t h", t=16)
        self.act(beta[:], pv[:, :, 8:16], AF.Exp, reads=[pk], writes=["dbeta"], scale=-1.0)
        self.ts(beta[:], beta[:], 1.0, ALU.add, reads=["dbeta"], writes=["dbeta"])
        self.S.op("dve", lambda e: e.reciprocal(out=beta[:], in_=beta[:]), reads=["dbeta"], writes=["dbeta"])
        self.tt(gv[:], pv[:, :, 0:8], dtb[:].unsqueeze(1).to_broadcast([128, 16, 8]), ALU.add, reads=[pk, dtbk], writes=["dg"])
        self.act(tmp[:], gv[:], AF.Exp, reads=["dg"], writes=["dtmp"])
        self.act(gv[:], tmp[:], AF.Ln, reads=["dtmp", "consts"], writes=["dg"], bias=one_col, scale=1.0)
        self.act(alog[:], alog[:], AF.Exp, reads=[alogk], writes=[alogk])
        self.S.op("dve", lambda e: e.scalar_tensor_tensor(out=gv[:], in0=gv[:], scalar=-1.0,
                                                          in1=alog[:].unsqueeze(1).to_broadcast([128, 16, 8]),
                                                          op0=ALU.mult, op1=ALU.mult),
                  reads=["dg", alogk], writes=["dg"])
        ps, pk = self.psum.next()
        for t in range(16):
            self.mm(ps[:, t * 8:(t + 1) * 8], cst[:, C_TRI2, :], gv[:, t, :], True, True, reads=["dg", "consts"], writes=[pk])
        self.copy(gc[:], ps[:, 0:128].rearrange("p (t h) -> p t h", t=16), reads=[pk], writes=["dgc"])
        self.act(egc[:], gc[:], AF.Exp, reads=["dgc"], writes=["degc"])
        self.tt(bg[:], egc[:], beta[:], ALU.mult, reads=["degc", "dbeta"], writes=["dbg"])
        ps, pk = self.psum.next()
        for t in range(16):
            self.mm(ps[:, t * 8:(t + 1) * 8], cst[:, C_BD, :], gv[:, t, :], True, True, reads=["dg", "consts"], writes=[pk])
        self.tt(kdec[:], ps[:, 0:128].rearrange("p (t h) -> p t h", t=16), gc[:], ALU.subtract, reads=[pk, "dgc"],
                writes=["dkdec"])
        self.act(kdec[:], kdec[:], AF.Exp, reads=["dkdec"], writes=["dkdec"])
        ps, pk = self.psum.next()
        for t in range(16):
            for c in range(2):
                j = 2 * t + c
                self.mm(ps[:, j * 8:(j + 1) * 8], cst[:, C_IND0 + c, :], gv[:, t, :], True, True,
                        reads=["dg", "consts"], writes=[pk])
        self.act(eglast[:].rearrange("p j h -> p (j h)"), ps[:, 0:256], AF.Exp, reads=[pk], writes=["deglast"])

        mG = A.mark()
        for h in range(8):
            A.reset(mG)
            S.barrier()
            wg = A.alloc("dwg", [128, KC, 128], BF16)
            qTn = A.alloc("dqTn", [128, S_LEN], BF16)
            kTn = A.alloc("dkTn", [128, S_LEN], BF16)
            k_tok = A.alloc("dktok", [128, 16, 128], BF16)
            v_tok = A.alloc("dvtok", [128, 16, 128], BF16)
            mA = A.mark()
            wst = self.ring("dwst", 2, [128, KC, 128], F32)
            wbx = A.alloc("dwbx", [128, KC, 384], BF16)
            raw = A.alloc("draw", [128, S_LEN + 4], F32)
            acc = A.alloc("dacc", [128, S_LEN], F32)
            vT = A.alloc("dvT", [128, S_LEN], BF16)
            cw = self.ring("dcw", 2, [128, 8], F32)
            sqr = self.ring("dsq", 2, [128, 512], BF16)
            rsr = self.ring("drs", 2, [128, 512], F32)
            rawk = self.key("draw")
            self.memset(raw[:, 0:3], 0.0, writes=[(rawk, "pad")])
            cols = [O_DNQKV + 128 * h, O_DNQKV + 1024 + 128 * h, O_DNQKV + 2048 + 128 * h, O_DNGATE + 128 * h]
            wk = self.key("dwb")
            for j, c0 in enumerate(cols):
                st, sk = wst.next()
                self.dma(st[:], self.w_in_d[l][:, c0:c0 + 128].rearrange("(k p) n -> p k n", p=128), [], [sk])
                for kc in range(KC):
                    wd_ = wbx[:, kc, j * 128:(j + 1) * 128] if j < 3 else wg[:, kc, :]
                    self.act(wd_, st[:, kc, :], AF.Copy, reads=[sk, "nw"], writes=[(wk, j)], scale=scale[:, kc:kc + 1])
            acck = self.key("dacc")
            ktk = self.key("dktok")
            vtk = self.key("dvtok")
            for j in range(3):
                ch = j * 1024 + 128 * h
                if j == 2:
                    self.conv_silu(l, wbx, (wk, j), j * 128, self.dn_conv_w_d, None, ch, raw, rawk, acc, acck, cw, vT[:], "dvT")
                    self.to_tok(vT, "dvT", lambda tq: v_tok[:, tq * 4:(tq + 1) * 4, :], vtk)
                    continue
                self.conv_silu(l, wbx, (wk, j), j * 128, self.dn_conv_w_d, None, ch, raw, rawk, acc, acck, cw, acc[:], acck)
                dst, dk = (qTn, "dqTn") if j == 0 else (kTn, "dkTn")
                for g in range(4):
                    sl = slice(g * 512, (g + 1) * 512)
                    q_, qk_ = sqr.next()
                    self.act(q_[:], acc[:, sl], AF.Square, reads=[acck], writes=[qk_])
                    ps, pk = self.psum.next()
                    self.mm(ps[:], self.ones_b(), q_[:], True, True, reads=[qk_, "cbf"], writes=[pk])
                    r_, rk_ = rsr.next()
                    self.act(r_[:], ps[:], AF.Ln, reads=[pk, "consts"], writes=[rk_], scale=1.0, bias=self.eps_ap())
                    self.act(r_[:], r_[:], AF.Exp, reads=[rk_], writes=[rk_], scale=-0.5)
                    if j == 0:
                        self.stt(dst[:, sl], acc[:, sl], 128.0 ** -0.5, r_[:], ALU.mult, ALU.mult, reads=[acck, rk_],
                                 writes=[(dk, g)])
                    else:
                        self.tt(dst[:, sl], acc[:, sl], r_[:], ALU.mult, reads=[acck, rk_], writes=[(dk, g)])
                if j == 1:
                    self.to_tok(kTn, [("dkTn", g) for g in range(4)], lambda tq: k_tok[:, tq * 4:(tq + 1) * 4, :], ktk)
            S.barrier()
            A.reset(mA)
            attnT = A.alloc("dattnT", [128, 16, 128], BF16)
            P = A.alloc("dP", [128, 16, 128], BF16)
            PL = A.alloc("dPL", [128, 16, 128], BF16)
            R32 = A.alloc("dR32", [128, 16, 128], F32)
            Rb = A.alloc("dRb", [128, 16, 128], BF16)
            vb = v_tok
            kbg = k_tok
            kd = A.alloc("dkd", [128, 16, 128], BF16)
            u = A.alloc("du", [128, 16, 128], BF16)
            wT = A.alloc("dwT", [128, 16, 128], BF16)
            sgt = A.alloc("dsgt", [128, 16, 128], BF16)
            abr = self.ring("dab", 4, [128, 128], F32)
            dcr = self.ring("ddec", 1, [128, 4, 128], F32)
            tmr = self.ring("dtm", 1, [128, 4, 128], F32)
            bmr = self.ring("dbm", 1, [128, 4, 128], F32)
            ktks = [(ktk, tq) for tq in range(4)]
            vtks = [(vtk, tq) for tq in range(4)]
            bc3 = lambda t_: t_[:, :, h:h + 1].to_broadcast([128, 16, 128])
            self.tt(vb[:], v_tok[:], bc3(beta), ALU.mult, reads=vtks + ["dbeta"], writes=vtks + ["dvb"])
            self.tt(kd[:], k_tok[:], bc3(kdec), ALU.mult, reads=ktks + ["dkdec"], writes=["dkd"])
            self.tt(kbg[:], k_tok[:], bc3(bg), ALU.mult, reads=ktks + ["dbg", "dkd"], writes=ktks + ["dkbg"])
            kTk = [("dkTn", g) for g in range(4)]
            identf4 = cst[:, C_IDENT, :].unsqueeze(1).to_broadcast([128, 4, 128])
            for q in range(4):
                psK, psKk = self.psum.next()
                psQ, psQk = self.psum.next()
                psD, psDk = self.psum.next()
                psB, psBk = self.psum.next()
                for i4 in range(4):
                    t = 4 * q + i4
                    tsl = slice(t * 128, (t + 1) * 128)
                    cs = slice(i4 * 128, (i4 + 1) * 128)
                    self.mm(psK[:, cs], kTn[:, tsl], kTn[:, tsl], True, True, reads=[("dkTn", q)], writes=[psKk])
                    self.mm(psQ[:, cs], kTn[:, tsl], qTn[:, tsl], True, True, reads=[("dkTn", q), ("dqTn", q)], writes=[psQk])
                    ab, abk = abr.next()
                    self.ts(ab[:], ones_f, gv[:, t, h:h + 1], ALU.mult, reads=["dg", "consts"], writes=[abk])
                    self.mm(psD[:, cs], ab[:], cst[:, C_TRI2, :], True, False, reads=[abk, "consts"], writes=[psDk])
                    self.mm(psD[:, cs], cst[:, C_TRI2NEG, :], ab[:], False, True, reads=[abk, "consts"], writes=[psDk])
                    db, dbk = abr.next()
                    self.ts(db[:], cst[:, C_IDENT, :], beta[:, t, h:h + 1], ALU.mult, reads=["dbeta", "consts"], writes=[dbk])
                    self.mm(psB[:, cs], ones_f, db[:], True, True, reads=[dbk, "consts"], writes=[psBk])
                dc, dck = dcr.next()
                self.tt(dc[:], psD[:].rearrange("p (a b) -> p a b", a=4),
                        cst[:, C_NEGINCL, :].unsqueeze(1).to_broadcast([128, 4, 128]), ALU.add, reads=[psDk, "consts"],
                        writes=[dck])
                self.act(dc[:], dc[:], AF.Exp, reads=[dck], writes=[dck])
                self.tt(attnT[:, 4 * q:4 * q + 4, :], psQ[:].rearrange("p (a b) -> p a b", a=4), dc[:], ALU.mult,
                        reads=[psQk, dck], writes=[("dattnT", q)])
                bm, bmk = bmr.next()
                self.tt(bm[:], psB[:].rearrange("p (a b) -> p a b", a=4),
                        cst[:, C_MSTRICT2, :].unsqueeze(1).to_broadcast([128, 4, 128]), ALU.mult, reads=[psBk, "consts"],
                        writes=[bmk])
                tm, tmk = tmr.next()
                self.tt(tm[:], psK[:].rearrange("p (a b) -> p a b", a=4), dc[:], ALU.mult, reads=[psKk, dck], writes=[tmk])
                self.tt(P[:, 4 * q:4 * q + 4, :], tm[:], bm[:], ALU.mult, reads=[tmk, bmk], writes=[("dP", q)])
                psT, psTk = self.psum.next()
                ptb = psT[:].bitcast(BF16)
                for i4 in range(4):
                    self.tr(ptb[:, i4 * 128:(i4 + 1) * 128], P[:, 4 * q + i4, :], self.ident_b(), reads=[("dP", q), "cbf"],
                            writes=[psTk])
                self.copy(PL[:, 4 * q:4 * q + 4, :], ptb[:, 0:512].rearrange("p (a b) -> p a b", a=4), reads=[psTk],
                          writes=[("dPL", q)], eng="act")
                self.tt(R32[:, 4 * q:4 * q + 4, :], identf4, P[:, 4 * q:4 * q + 4, :], ALU.subtract,
                        reads=[("dP", q), "consts"], writes=[("dR32", q)])
                self.copy(Rb[:, 4 * q:4 * q + 4, :], R32[:, 4 * q:4 * q + 4, :], reads=[("dR32", q)], writes=[("dRb", q)],
                          eng="act")
            for lev in range(5):
                last = lev == 4
                for q in range(4):
                    if not last:
                        psP, psPk = self.psum.next()
                    psL, psLk = self.psum.next()
                    for i4 in range(4):
                        t = 4 * q + i4
                        cs = slice(i4 * 128, (i4 + 1) * 128)
                        if not last:
                            self.mm(psP[:, cs], PL[:, t, :], P[:, t, :], True, True, reads=[("dPL", q), ("dP", q)],
                                    writes=[psPk])
                        self.mm(psL[:, cs], P[:, t, :], PL[:, t, :], True, True, reads=[("dPL", q), ("dP", q)],
                                writes=[psLk])
                    if not last:
                        self.copy(P[:, 4 * q:4 * q + 4, :], psP[:].rearrange("p (a b) -> p a b", a=4), reads=[psPk],
                                  writes=[("dP", q)], eng="dve")
                    self.copy(PL[:, 4 * q:4 * q + 4, :], psL[:].rearrange("p (a b) -> p a b", a=4), reads=[psLk],
                              writes=[("dPL", q)], eng="act")
                for q in range(4):
                    psR, psRk = self.psum.next()
                    for i4 in range(4):
                        t = 4 * q + i4
                        cs = slice(i4 * 128, (i4 + 1) * 128)
                        self.mm(psR[:, cs], PL[:, t, :], Rb[:, t, :], True, True, reads=[("dPL", q), ("dRb", q)],
                                writes=[psRk])
                    self.tt(R32[:, 4 * q:4 * q + 4, :], R32[:, 4 * q:4 * q + 4, :],
                            psR[:].rearrange("p (a b) -> p a b", a=4), ALU.add, reads=[psRk, ("dR32", q)],
                            writes=[("dR32", q)])
                    self.copy(Rb[:, 4 * q:4 * q + 4, :], R32[:, 4 * q:4 * q + 4, :], reads=[("dR32", q)],
                              writes=[("dRb", q)], eng="act")
            for q in range(4):
                psU, psUk = self.psum.next()
                psW, psWk = self.psum.next()
                for i4 in range(4):
                    t = 4 * q + i4
                    cs = slice(i4 * 128, (i4 + 1) * 128)
                    self.mm(psU[:, cs], Rb[:, t, :], vb[:, t, :], True, True, reads=[("dRb", q), "dvb"], writes=[psUk])
                    self.mm(psW[:, cs], kbg[:, t, :], Rb[:, t, :], True, True, reads=[("dRb", q), "dkbg"], writes=[psWk])
                self.copy(u[:, 4 * q:4 * q + 4, :], psU[:].rearrange("p (a b) -> p a b", a=4), reads=[psUk],
                          writes=[("du", q)], eng="dve")
                self.copy(wT[:, 4 * q:4 * q + 4, :], psW[:].rearrange("p (a b) -> p a b", a=4), reads=[psWk],
                          writes=[("dwT", q)], eng="act")
            for q in range(4):
                psG, psGk = self.psum.next()
                for i4 in range(4):
                    t = 4 * q + i4
                    for kc in range(KC):
                        self.mm(psG[:, i4 * 128:(i4 + 1) * 128], self.hT[:, kc, t * 128:(t + 1) * 128], wg[:, kc, :],
                                kc == 0, kc == KC - 1, reads=[(wk, 3), ("hT", kc, q)], writes=[psGk])
                self.act(sgt[:, 4 * q:4 * q + 4, :], psG[:].rearrange("p (a b) -> p a b", a=4), AF.Silu, reads=[psGk],
                         writes=[("dsgt", q)])
            fl = lambda t_: t_[:].rearrange("p a b -> p (a b)")
            for nm, src, keys in (("u", fl(u), [("du", q) for q in range(4)]),
                                  ("wT", fl(wT), [("dwT", q) for q in range(4)]),
                                  ("q", qTn[:], [("dqTn", g) for g in range(4)]),
                                  ("attnT", fl(attnT), [("dattnT", q) for q in range(4)]),
                                  ("kd", fl(kd), ["dkd"]),
                                  ("sg", fl(sgt), [("dsgt", q) for q in range(4)])):
                self.dma(self.dn_scr[nm][h], src, reads=keys, writes=[("dn_scr", nm, h)])

    def final_out(self):
        A, S = self.A, self.S
        m = A.mark()
        sq = self.ring("fsq", 3, [128, 512], BF16)
        rs = self.ring("frs", 2, [128, 512], F32)
        hf = self.ring("hf", 3, [128, 512], F32)
        ost = self.ring("ost", 2, [128, 4, D], F32)
        for g in range(4):
            ps, pk = self.psum.next()
            sl = slice(g * 512, (g + 1) * 512)
            for kc in range(KC):
                q, qk = sq.next()
                self.act(q[:], self.xT[:, kc, sl], AF.Square, reads=[("xT", kc, g)], writes=[qk])
                self.mm(ps[:], self.ones_b(), q[:], kc == 0, kc == KC - 1, reads=[qk, "cbf"], writes=[pk])
            r, rk = rs.next()
            self.act(r[:], ps[:], AF.Ln, reads=[pk], writes=[rk], scale=1.0 / D, bias=self.eps_ap())
            self.act(r[:], r[:], AF.Exp, reads=[rk], writes=[rk], scale=-0.5)
            o, ok = ost.next()
            for kc in range(KC):
                h, hk = hf.next()
                self.stt(h[:], self.xT[:, kc, sl], self.nw[:, 2 * DEPTH, kc:kc + 1], r[:], ALU.mult, ALU.mult,
                         reads=[("xT", kc, g), rk, "nw"], writes=[hk])
                ps2, pk2 = self.psum.next()
                for t in range(4):
                    self.tr(ps2[:, t * 128:(t + 1) * 128], h[:, t * 128:(t + 1) * 128], self.ident_f(),
                            reads=[hk, "consts"], writes=[pk2])
                self.copy(o[:, :, kc * 128:(kc + 1) * 128], ps2[:].rearrange("p (t c) -> p t c", t=4),
                          reads=[pk2], writes=[ok], eng=("act" if kc % 2 else "dve"))
            self.dma(self.out_d[g * 512:(g + 1) * 512, :].rearrange("(t p) d -> p t d", p=128), o[:],
                     reads=[ok], writes=[("out", g)])
        S.barrier()
        A.reset(m)


C_IDENT, C_ONES, C_EPS, C_MSTRICT, C_TINC = 0, 1, 2, 3, 4
C_TRI2, C_TRI2NEG, C_BD, C_IND0, C_IND1, C_NEGINCL, C_NEGSTRICT, C_MINCL2, C_MSTRICT2, C_MSTRICT2T = 5, 6, 7, 8, 9, 10, 11, 12, 13, 14
NCONST = 15
NEG = -30000.0


def make_consts():
    c = np.zeros((128, NCONST, 128), np.float32)
    c[:, C_IDENT, :] = np.eye(128, dtype=np.float32)
    c[:, C_ONES, :] = 1.0
    c[:, C_EPS, :] = EPS
    ii = np.arange(128)
    c[:, C_MSTRICT, :] = (ii[:, None] < ii[None, :]).astype(np.float32)
    c[:, C_TINC, :] = (ii[:, None] >= ii[None, :]).astype(np.float32)
    same = (ii[:, None] // 64) == (ii[None, :] // 64)
    le = ii[:, None] <= ii[None, :]
    lt = ii[:, None] < ii[None, :]
    c[:, C_TRI2, :] = (same & le).astype(np.float32)
    c[:, C_TRI2NEG, :] = -c[:, C_TRI2, :]
    c[:, C_BD, :] = same.astype(np.float32)
    c[:, C_IND0, :] = (ii[:, None] < 64).astype(np.float32) * np.ones((1, 128), np.float32)
    c[:, C_IND1, :] = (ii[:, None] >= 64).astype(np.float32) * np.ones((1, 128), np.float32)
    c[:, C_NEGINCL, :] = np.where(same & le, 0.0, NEG)
    c[:, C_NEGSTRICT, :] = np.where(same & lt, 0.0, NEG)
    c[:, C_MINCL2, :] = (same & le).astype(np.float32)
    c[:, C_MSTRICT2, :] = (same & lt).astype(np.float32)
    c[:, C_MSTRICT2T, :] = (same & lt).T.astype(np.float32)
    return c.reshape(128, NCONST * 128)


_CACHE = {}
_RUN_KW = {}


def get_nc(**kw):
    key = tuple(sorted((k, str(v)) for k, v in kw.items()))
    if key not in _CACHE:
        b = Builder(**kw)
        b.build()
        _CACHE[key] = b
    return _CACHE[key]


def run(inputs, **kw):
    b = get_nc(**kw)
    consts = make_consts()
    common = {
        "consts": consts,
        "w_in": np.ascontiguousarray(inputs["w_in"], dtype=np.float32),
        "norm_mix": np.ascontiguousarray(inputs["norm_mix"], dtype=np.float32),
        "norm_mlp": np.ascontiguousarray(inputs["norm_mlp"], dtype=np.float32),
        "norm_final": np.ascontiguousarray(inputs["norm_final"], dtype=np.float32).reshape(1, D),
        "w_branch": np.ascontiguousarray(inputs["w_branch"], dtype=np.float32),
        "w_out": np.ascontiguousarray(inputs["w_out"], dtype=np.float32),
        "w_up": np.ascontiguousarray(inputs["w_up"], dtype=np.float32),
        "w_down": np.ascontiguousarray(inputs["w_down"], dtype=np.float32),
    }
    for nm in ("ssm_conv_w", "ssm_conv_b", "ssm_a_log", "ssm_dt_bias", "ssm_d", "ssm_norm_w", "dn_conv_w", "dn_a_log",
               "dn_dt_bias", "dn_norm_w"):
        common[nm] = np.ascontiguousarray(inputs[nm], dtype=np.float32)
    x = np.asarray(inputs["x"], dtype=np.float32)
    in_maps = []
    for c in range(NCORES):
        m = dict(common)
        m["x"] = np.ascontiguousarray(x[c])
        in_maps.append(m)
    res = run_bass_kernel_spmd(b.nc, in_maps, core_ids=list(range(NCORES)), **_RUN_KW)
    return res


def kernel(**inputs):
    res = run(inputs)
    out = np.stack([np.asarray(r["out"], dtype=np.float32) for r in res.results], axis=0)
    return out
```

```python
import numpy as np
import concourse.bass as bass
import concourse.mybir as mybir
from concourse.bass_utils import run_bass_kernel_spmd

F32 = mybir.dt.float32
BF16 = mybir.dt.bfloat16
AF = mybir.ActivationFunctionType
ALU = mybir.AluOpType

S_LEN = 2048
D = 1024
KC = 8
NCORES = 8
DEPTH = 2
D_FF = 4096
EPS = 1e-6
IN_SIZES = (3072, 1024, 8, 8, 3072, 1024, 2048, 16, 3072)
IN_DIM = sum(IN_SIZES)
OFF = [0]
for _s in IN_SIZES:
    OFF.append(OFF[-1] + _s)
(O_DNQKV, O_DNGATE, O_DNA, O_DNB, O_SBQKV, O_SSMZ, O_SSMXBC, O_SSMDT, O_GATE, _) = OFF


class _Op:
    __slots__ = ("id", "eng", "fn", "dma", "waits", "idx", "sig", "reuse", "seen")


class Sched:
    ENG = ("pe", "act", "dve", "pool", "sp")
    NSLOT = 12
    SEM_LIMIT = 20000

    def __init__(self, nc):
        self.nc = nc
        self.ops = []
        self.by_eng = {e: [] for e in self.ENG}
        self.kstate = {}
        self.seen = {e: {p: -1 for p in self.ENG} for e in self.ENG}
        self.seen_dma = {e: set() for e in self.ENG}
        self.pending = {e: set() for e in self.ENG}
        self.open_dma = []

    def op(self, eng, fn, reads=(), writes=(), dma=False):
        o = _Op()
        o.id = len(self.ops)
        o.eng = eng
        o.fn = fn
        o.dma = dma
        o.sig = None
        o.reuse = None
        deps = {}
        for k in reads:
            st = self.kstate.get(k)
            if st is not None and st[0] is not None:
                deps[st[0]] = True
        for k in writes:
            st = self.kstate.get(k)
            if st is not None:
                if st[0] is not None:
                    deps.setdefault(st[0], False)
                for r in st[1].values():
                    deps.setdefault(r, False)
                for r in st[2]:
                    deps.setdefault(r, False)
        for d in self.pending[eng]:
            deps[d] = True
        self.pending[eng] = set()
        o.idx = len(self.by_eng[eng])
        seen = self.seen[eng]
        best = {}
        waits = []
        for d, raw in deps.items():
            p = self.ops[d]
            if p.dma:
                if d in self.seen_dma[eng]:
                    continue
                self.seen_dma[eng].add(d)
                waits.append(d)
            else:
                if p.eng == eng and not dma:
                    if eng == "pe" or (not raw and eng != "pool"):
                        continue
                if p.idx <= seen[p.eng]:
                    continue
                if p.eng not in best or self.ops[best[p.eng]].idx < p.idx:
                    best[p.eng] = d
        for pe, d in best.items():
            p = self.ops[d]
            waits.append(d)
            for e2, v in p.seen.items():
                if v > seen[e2]:
                    seen[e2] = v
            if p.idx > seen[pe]:
                seen[pe] = p.idx
        o.waits = waits
        o.seen = dict(seen)
        for k in reads:
            st = self.kstate.get(k)
            if st is None:
                st = [None, {}, []]
                self.kstate[k] = st
            if dma:
                st[2].append(o.id)
            else:
                st[1][eng] = o.id
        for k in writes:
            self.kstate[k] = [o.id, {}, []]
        self.ops.append(o)
        self.by_eng[eng].append(o)
        if dma:
            self.open_dma.append(o.id)
        return o

    def barrier(self):
        last = []
        for e in self.ENG:
            for o in reversed(self.by_eng[e]):
                if o.dma:
                    break
                if o.fn is not None:
                    last.append(o.id)
                    break
        for e in self.ENG:
            self.pending[e] |= set(last) | set(self.open_dma[-self.NSLOT:])
        self.open_dma = self.open_dma[-self.NSLOT:]

    def finalize(self):
        nc = self.nc
        needed = set()
        for o in self.ops:
            needed.update(o.waits)
        for e in self.ENG:
            sem = None
            cnt = 0
            ndma = 0
            slots = None
            for o in self.by_eng[e]:
                if o.dma:
                    if slots is None:
                        slots = [nc.alloc_semaphore(f"dq_{e}_{i}") for i in range(self.NSLOT)]
                    s = ndma % self.NSLOT
                    r = ndma // self.NSLOT
                    o.sig = (slots[s], 16 * (r + 1))
                    if r > 0:
                        o.reuse = (slots[s], 16 * r)
                    ndma += 1
                elif o.id in needed:
                    if sem is None or cnt >= self.SEM_LIMIT:
                        sem = nc.alloc_semaphore(f"pg_{e}_{o.id}")
                        cnt = 0
                    cnt += 1
                    o.sig = (sem, cnt)

    def emit(self, ename, eng):
        ops = self.ops
        for o in self.by_eng[ename]:
            for d in o.waits:
                s, v = ops[d].sig
                eng.wait_ge(s, v)
            if o.reuse is not None:
                eng.wait_ge(o.reuse[0], o.reuse[1])
            if o.fn is None:
                continue
            ins = o.fn(eng)
            if o.sig is not None:
                ins.then_inc(o.sig[0], 16 if o.dma else 1)


class Arena:
    def __init__(self, nc, limit=208 * 1024):
        self.nc = nc
        self.off = 16 * 1024
        self.limit = limit
        self.n = 0
        self.peak = 0

    def alloc(self, name, shape, dtype):
        per = 1
        for s in shape[1:]:
            per *= s
        nbytes = per * (4 if dtype == F32 else 2)
        nbytes = (nbytes + 63) // 64 * 64
        assert self.off + nbytes <= self.limit, f"SBUF arena overflow at {name}: {self.off}+{nbytes}"
        self.n += 1
        t = self.nc.alloc_sbuf_tensor_at(f"{name}_{self.n}", list(shape), dtype, offset=self.off)
        self.off += nbytes
        self.peak = max(self.peak, self.off)
        return t

    def mark(self):
        return self.off

    def reset(self, m):
        self.off = m


class Ring:
    def __init__(self, tiles, name):
        self.tiles = tiles
        self.name = name
        self.i = 0

    def next(self):
        j = self.i % len(self.tiles)
        self.i += 1
        return self.tiles[j], (self.name, j)


class Builder:
    def __init__(self, nlayers=DEPTH, mixers=("dn", "sb", "ssm"), do_mlp=True, debug=()):
        self.nlayers = nlayers
        self.mixers = mixers
        self.do_mlp = do_mlp
        self.debug = debug
        nc = bass.Bass("TRN2", target_bir_lowering=False)
        self.nc = nc
        self.S = Sched(nc)
        self.A = Arena(nc)
        self.uid = 0

    def dram_in(self, name, shape, dtype=F32):
        return self.nc.dram_tensor(name, list(shape), dtype, kind="ExternalInput").ap()

    def dram_out(self, name, shape, dtype=F32):
        return self.nc.dram_tensor(name, list(shape), dtype, kind="ExternalOutput").ap()

    def dram_tmp(self, name, shape, dtype):
        return self.nc.dram_tensor(name, list(shape), dtype, kind="Internal").ap()

    def key(self, base):
        self.uid += 1
        return (base, self.uid)

    def ring(self, name, n, shape, dtype):
        return Ring([self.A.alloc(name, shape, dtype) for _ in range(n)], self.key(name))

    def dma(self, out, in_, reads, writes, slow=False):
        if slow:
            fn = lambda e, out=out, in_=in_: e.dma_start(out=out, in_=in_, allow_slow_non_contiguous=True)
        else:
            fn = lambda e, out=out, in_=in_: e.dma_start(out=out, in_=in_)
        return self.S.op("sp", fn, reads=reads, writes=writes, dma=True)

    def mm(self, out, lhsT, rhs, start, stop, reads, writes, **kw):
        return self.S.op(
            "pe",
            lambda e, out=out, lhsT=lhsT, rhs=rhs, start=start, stop=stop, kw=kw: e.matmul(
                out, lhsT=lhsT, rhs=rhs, start=start, stop=stop, **kw
            ),
            reads=reads,
            writes=writes,
        )

    def tr(self, out, in_, ident, reads, writes):
        return self.S.op(
            "pe",
            lambda e, out=out, in_=in_, ident=ident: e.transpose(out, in_, ident),
            reads=reads,
            writes=writes,
        )

    def act(self, out, in_, func, reads, writes, eng="act", **kw):
        return self.S.op(
            eng,
            lambda e, out=out, in_=in_, func=func, kw=kw: e.activation(out=out, in_=in_, func=func, **kw),
            reads=reads,
            writes=writes,
        )

    def tt(self, out, in0, in1, op, reads, writes, eng="dve"):
        return self.S.op(
            eng,
            lambda e, out=out, in0=in0, in1=in1, op=op: e.tensor_tensor(out=out, in0=in0, in1=in1, op=op),
            reads=reads,
            writes=writes,
        )

    def ts(self, out, in0, s1, op0, reads, writes, s2=None, op1=None, eng="dve"):
        def fn(e, out=out, in0=in0, s1=s1, op0=op0, s2=s2, op1=op1):
            if op1 is None:
                return e.tensor_scalar(out=out, in0=in0, scalar1=s1, scalar2=None, op0=op0)
            return e.tensor_scalar(out=out, in0=in0, scalar1=s1, scalar2=s2, op0=op0, op1=op1)

        return self.S.op(eng, fn, reads=reads, writes=writes)

    def stt(self, out, in0, scalar, in1, op0, op1, reads, writes):
        return self.S.op(
            "dve",
            lambda e, out=out, in0=in0, scalar=scalar, in1=in1, op0=op0, op1=op1: e.scalar_tensor_tensor(
                out=out, in0=in0, scalar=scalar, in1=in1, op0=op0, op1=op1
            ),
            reads=reads,
            writes=writes,
        )

    def copy(self, out, in_, reads, writes, eng="dve"):
        if eng == "act":
            return self.act(out, in_, AF.Copy, reads, writes)
        return self.S.op(
            eng, lambda e, out=out, in_=in_: e.tensor_copy(out=out, in_=in_), reads=reads, writes=writes
        )

    def memset(self, ap, val, writes, eng="dve"):
        return self.S.op(eng, lambda e, ap=ap, val=val: e.memset(ap, val), reads=(), writes=writes)

    def build(self):
        nc, S, A = self.nc, self.S, self.A
        L = self.nlayers
        self.x_d = self.dram_in("x", [S_LEN, D])
        self.consts_d = self.dram_in("consts", [128, NCONST * 128])
        self.w_in_d = self.dram_in("w_in", [DEPTH, D, IN_DIM])
        self.norm_mix_d = self.dram_in("norm_mix", [DEPTH, D])
        self.norm_mlp_d = self.dram_in("norm_mlp", [DEPTH, D])
        self.norm_final_d = self.dram_in("norm_final", [1, D])
        self.w_branch_d = self.dram_in("w_branch", [DEPTH, 3, D, D])
        self.w_out_d = self.dram_in("w_out", [DEPTH, D, D])
        self.w_up_d = self.dram_in("w_up", [DEPTH, D, D_FF])
        self.w_down_d = self.dram_in("w_down", [DEPTH, D_FF, D])
        self.ssm_conv_w_d = self.dram_in("ssm_conv_w", [DEPTH, 4, 2048])
        self.ssm_conv_b_d = self.dram_in("ssm_conv_b", [DEPTH, 2048])
        self.ssm_a_log_d = self.dram_in("ssm_a_log", [DEPTH, 16])
        self.ssm_dt_bias_d = self.dram_in("ssm_dt_bias", [DEPTH, 16])
        self.ssm_d_d = self.dram_in("ssm_d", [DEPTH, 16])
        self.ssm_norm_w_d = self.dram_in("ssm_norm_w", [DEPTH, D])
        self.dn_conv_w_d = self.dram_in("dn_conv_w", [DEPTH, 4, 3072])
        self.dn_a_log_d = self.dram_in("dn_a_log", [DEPTH, 8])
        self.dn_dt_bias_d = self.dram_in("dn_dt_bias", [DEPTH, 8])
        self.dn_norm_w_d = self.dram_in("dn_norm_w", [DEPTH, 128])
        self.out_d = self.dram_out("out", [S_LEN, D])
        self.u_scr = self.dram_tmp("u_scr", [D_FF, S_LEN], BF16)
        if self.debug:
            self.o_scr = self.dram_out("o_scr", [3, D, S_LEN], BF16)
        else:
            self.o_scr = self.dram_tmp("o_scr", [3, D, S_LEN], BF16)
        self.g_scr = self.dram_tmp("g_scr", [3, D, S_LEN], BF16)
        self.dn_scr = {nm: self.dram_tmp("dn_" + nm, [8, 128, S_LEN], BF16) for nm in ("u", "wT", "q", "attnT", "kd", "sg")}

        pst = [nc.alloc_psum_tensor(f"ps{i}", [128, 512], F32) for i in range(8)]
        self.pst = pst
        self.psum = Ring(pst[:6], "psum")
        self.psum_acc = Ring(pst[6:], "psacc")

        self.consts = A.alloc("consts", [128, NCONST, 128], F32)
        self.cbf = A.alloc("cbf", [128, NCONST, 128], BF16)
        self.xT = A.alloc("xT", [128, KC, S_LEN], F32)
        self.nw = A.alloc("nw", [128, 2 * DEPTH + 1, KC], F32)
        KCON = "consts"
        self.dma(self.consts[:].rearrange("p a b -> p (a b)"), self.consts_d, reads=[], writes=[KCON])
        self.copy(self.cbf[:], self.consts[:], reads=[KCON], writes=["cbf"])
        for l in range(DEPTH):
            self.dma(self.nw[:, 2 * l, :], self.norm_mix_d[l].rearrange("(k p) -> p k", p=128), [], ["nw"], slow=True)
            self.dma(self.nw[:, 2 * l + 1, :], self.norm_mlp_d[l].rearrange("(k p) -> p k", p=128), [], ["nw"], slow=True)
        self.dma(self.nw[:, 2 * DEPTH, :], self.norm_final_d[0].rearrange("(k p) -> p k", p=128), [], ["nw"], slow=True)

        self.load_x()
        for l in range(L):
            self.layer(l)
        self.final_out()
        S.barrier()
        S.op("sp", None)
        S.finalize()

        with nc.Block() as block:

            @block.tensor
            def _(e):
                S.emit("pe", e)

            @block.scalar
            def _(e):
                S.emit("act", e)

            @block.vector
            def _(e):
                S.emit("dve", e)

            @block.gpsimd
            def _(e):
                S.emit("pool", e)

            @block.sync
            def _(e):
                S.emit("sp", e)

        return nc

    def ident_f(self):
        return self.consts[:, C_IDENT, :]

    def ident_b(self):
        return self.cbf[:, C_IDENT, :]

    def ones_b(self):
        return self.cbf[:, C_ONES, :]

    def load_x(self):
        A, S = self.A, self.S
        m = A.mark()
        stg = self.ring("xstg", 2, [128, 4, D], F32)
        for tg in range(4):
            st, sk = stg.next()
            self.dma(
                st[:],
                self.x_d[tg * 512:(tg + 1) * 512, :].rearrange("(t p) d -> p t d", p=128),
                reads=[],
                writes=[sk],
            )
            for kc in range(KC):
                ps, pk = self.psum.next()
                for t in range(4):
                    self.tr(ps[:, t * 128:(t + 1) * 128], st[:, t, kc * 128:(kc + 1) * 128], self.ident_f(),
                            reads=[sk, "consts"], writes=[pk])
                self.copy(self.xT[:, kc, tg * 512:(tg + 1) * 512], ps[:], reads=[pk], writes=[("xT", kc, tg)],
                          eng=("act" if kc % 2 else "dve"))
        S.barrier()
        A.reset(m)

    def rmsnorm(self):
        A, S = self.A, self.S
        m = A.mark()
        sq = self.ring("sq", 3, [128, 512], BF16)
        rs = self.ring("rs", 2, [128, 512], F32)
        for g in range(4):
            ps, pk = self.psum.next()
            sl = slice(g * 512, (g + 1) * 512)
            for kc in range(KC):
                q, qk = sq.next()
                self.act(q[:], self.xT[:, kc, sl], AF.Square, reads=[("xT", kc, g)], writes=[qk])
                self.mm(ps[:], self.ones_b(), q[:], kc == 0, kc == KC - 1, reads=[qk, "cbf"], writes=[pk])
            r, rk = rs.next()
            self.act(r[:], ps[:], AF.Ln, reads=[pk], writes=[rk], scale=1.0 / D, bias=self.eps_ap())
            self.act(r[:], r[:], AF.Exp, reads=[rk], writes=[rk], scale=-0.5)
            for kc in range(KC):
                self.tt(self.hT[:, kc, sl], self.xT[:, kc, sl], r[:], ALU.mult,
                        reads=[("xT", kc, g), rk], writes=[("hT", kc, g)])
        self.rstd_ring = rs
        S.barrier()
        A.reset(m)

    def eps_ap(self):
        return self.consts[:, C_EPS, 0:1]

    def wload(self, w_ap, kcn, n, wst_ring, wbf_ring, scale=None):
        st, sk = wst_ring.next()
        wb, wk = wbf_ring.next()
        self.dma(st[:, :kcn, :n], w_ap.rearrange("(k p) n -> p k n", p=128), reads=[], writes=[sk])
        for kc in range(kcn):
            if scale is not None:
                self.act(wb[:, kc, :n], st[:, kc, :n], AF.Copy, reads=[sk, "nw"], writes=[wk],
                         scale=scale[:, kc:kc + 1])
            else:
                self.act(wb[:, kc, :n], st[:, kc, :n], AF.Copy, reads=[sk], writes=[wk])
        return wb, wk

    def hT_keys(self, g):
        return [("hT", kc, g) for kc in range(KC)]

    def linear_T(self, wb, wk, ncols, kcn, rhs_fn, rhs_keys_fn, evac, groups=range(4)):
        for c0 in range(0, ncols, 128):
            n = min(128, ncols - c0)
            for g in groups:
                ps, pk = self.psum.next()
                for kc in range(kcn):
                    self.mm(ps[:n, :], wb[:, kc, c0:c0 + n], rhs_fn(kc, g), kc == 0, kc == kcn - 1,
                            reads=[wk] + rhs_keys_fn(g), writes=[pk])
                evac(c0 // 128, g, ps, pk, n)

    def mlp(self, l):
        A, S = self.A, self.S
        m = A.mark()
        self.hT = A.alloc("hT", [128, KC, S_LEN], BF16)
        self.rmsnorm()
        wst = self.ring("wst", 2, [128, KC, 512], F32)
        wbf = self.ring("wbf", 2, [128, KC, 512], BF16)
        ub = self.ring("ub", 3, [128, S_LEN], BF16)
        rr = self.ring("rr", 3, [128, 512], F32)
        scale = self.nw[:, 2 * l + 1, :]
        nxt = self.wload(self.w_up_d[l][:, 0:512], KC, 512, wst, wbf, scale=scale)
        for cb in range(D_FF // 512):
            wb, wk = nxt
            if cb + 1 < D_FF // 512:
                nxt = self.wload(self.w_up_d[l][:, (cb + 1) * 512:(cb + 2) * 512], KC, 512, wst, wbf, scale=scale)
            cur = {}

            def evac(ci, g, ps, pk, n, cb=cb, cur=cur):
                if g == 0:
                    cur["u"] = ub.next()
                u, uk = cur["u"]
                r, rk = rr.next()
                self.ts(r[:], ps[:], 0.0, ALU.max, reads=[pk], writes=[rk])
                self.act(u[:, g * 512:(g + 1) * 512], r[:], AF.Square, reads=[rk], writes=[uk])
                if g == 3:
                    f = cb * 4 + ci
                    self.dma(self.u_scr[f * 128:(f + 1) * 128, :], u[:], reads=[uk], writes=[("u_scr", f)])

            self.linear_T(wb, wk, 512, KC, lambda kc, g: self.hT[:, kc, g * 512:(g + 1) * 512],
                          self.hT_keys, evac)
        S.barrier()
        A.reset(m)
        wst = self.ring("wdst", 2, [128, 4, 512], F32)
        wbf = self.ring("wdbf", 2, [128, 32, 512], BF16)
        ur = self.ring("ur", 1, [128, 32, 512], BF16)

        def load_down(half):
            wb, wk = wbf.next()
            for q in range(8):
                st, sk = wst.next()
                self.dma(st[:], self.w_down_d[l][q * 512:(q + 1) * 512, half * 512:(half + 1) * 512]
                         .rearrange("(k p) n -> p k n", p=128), reads=[], writes=[sk])
                for k4 in range(4):
                    self.act(wb[:, q * 4 + k4, :], st[:, k4, :], AF.Copy, reads=[sk], writes=[wk])
            return wb, wk

        nxt = load_down(0)
        for half in range(2):
            wb, wk = nxt
            if half == 0:
                nxt = load_down(1)
            for g in range(4):
                u, uk = ur.next()
                self.dma(u[:], self.u_scr[:, g * 512:(g + 1) * 512].rearrange("(k p) t -> p k t", p=128),
                         reads=[("u_scr", f) for f in range(32)], writes=[uk])
                for ci in range(4):
                    ps, pk = self.psum.next()
                    for kc in range(32):
                        self.mm(ps[:], wb[:, kc, ci * 128:(ci + 1) * 128], u[:, kc, :], kc == 0, kc == 31,
                                reads=[wk, uk], writes=[pk])
                    oc = half * 4 + ci
                    sl = slice(g * 512, (g + 1) * 512)
                    self.tt(self.xT[:, oc, sl], self.xT[:, oc, sl], ps[:], ALU.add,
                            reads=[pk, ("xT", oc, g)], writes=[("xT", oc, g)])
        S.barrier()
        A.reset(m)

    def layer(self, l):
        if self.mixers:
            self.mixer(l)
        if self.do_mlp:
            self.mlp(l)


    def mixer(self, l):
        A, S = self.A, self.S
        m00 = A.mark()
        if "dn" in self.mixers:
            self.dn_egc = A.alloc("degc", [128, 16, 8], F32)
            self.dn_eglast = A.alloc("deglast", [128, 32, 8], F32)
            self.dn_nwb, self.dn_nwbk = self.bload("dnwb", self.dn_norm_w_d[l], 128)
        m0 = A.mark()
        self.hT = A.alloc("hT", [128, KC, S_LEN], BF16)
        self.rmsnorm()
        m1 = A.mark()
        if "sb" in self.mixers:
            self.sb_phase(l)
            S.barrier()
            A.reset(m1)
        if "ssm" in self.mixers:
            self.ssm_phase(l)
            S.barrier()
            A.reset(m1)
        if "dn" in self.mixers:
            self.dn_phase(l)
            S.barrier()
            A.reset(m1)
        self.gates_phase(l)
        S.barrier()
        A.reset(m0)
        if "dn" in self.mixers:
            self.dn_phase2(l)
            S.barrier()
            A.reset(m0)
        self.merge_phase(l)
        S.barrier()
        A.reset(m00)

    def dn_phase2(self, l):
        A, S = self.A, self.S
        cst = self.consts
        pst = self.pst
        NH = 8
        names = ["u", "wT", "q", "attnT", "kd", "sg"]
        sets = [[A.alloc(f"d2{n}", [128, NH, 512], BF16) for n in names] for _ in range(1)]
        Sst = A.alloc("d2S", [128, NH, 128], F32)
        Sbf = A.alloc("d2Sbf", [128, NH, 128], BF16)
        vnr = self.ring("d2vn", 2, [128, NH, 128], BF16)
        t1 = A.alloc("d2t1", [128, NH, 128], F32)
        ot = A.alloc("d2ot", [128, NH, 128], F32)
        sq = A.alloc("d2sq", [128, NH, 128], F32)
        ssq = A.alloc("d2ssq", [128, 2, NH], F32)
        obr = self.ring("d2ob", 2, [128, NH, 128], BF16)
        otl = self.ring("d2otl", 2, [128, NH, 128], BF16)
        egc, eglast, nwb, nwbk = self.dn_egc, self.dn_eglast, self.dn_nwb, self.dn_nwbk
        self.memset(Sst[:], 0.0, writes=["d2S"])
        self.memset(Sbf[:], 0.0, writes=["d2Sbf"])
        bO = [(pst[i], ("p2O", i)) for i in range(4)]
        bW = [(pst[4], ("p2W", 0)), (pst[5], ("p2W", 1))]
        bS = [(pst[6], ("p2S", 0)), (pst[7], ("p2S", 1))]
        o_view = self.o_scr[0].rearrange("(h e) s -> e h s", h=NH)
        for q in range(4):
            st = sets[0]
            sk = ("d2set", 0)
            for ni, nm in enumerate(names):
                self.dma(st[ni][:], self.dn_scr[nm][:, :, q * 512:(q + 1) * 512].rearrange("h p t -> p h t"),
                         reads=[("dn_scr", nm, h) for h in range(NH)], writes=[(sk, ni)])
            U, WT, Q, AT, KD, SG = st
            for tl in range(4):
                t = 4 * q + tl
                cs = slice(tl * 128, (tl + 1) * 128)
                vn, vnk = vnr.next()
                for c in range(2):
                    j = 2 * t + c
                    pb = 64 * c
                    prt = slice(pb, pb + 64)
                    ccs = slice(tl * 128 + pb, tl * 128 + pb + 64)
                    for h in range(NH):
                        pw, pwk = bW[h // 4]
                        self.mm(pw[prt, (h % 4) * 128:(h % 4 + 1) * 128], WT[:, h, ccs], Sbf[:, h, :], True, True,
                                reads=[(sk, 1), ("d2Sbf", h // 4)], writes=[pwk])
                    for hb in range(2):
                        pw, pwk = bW[hb]
                        self.tt(vn[prt, 4 * hb:4 * hb + 4, :], U[prt, 4 * hb:4 * hb + 4, cs],
                                pw[prt, :].rearrange("p (a b) -> p a b", a=4), ALU.subtract, reads=[(sk, 0), pwk],
                                writes=[(vnk, c, hb)])
                    for h in range(NH):
                        po, pok = bO[h // 2]
                        co = (h % 2) * 256
                        self.mm(po[prt, co:co + 128], Q[:, h, ccs], Sbf[:, h, :], True, True,
                                reads=[(sk, 2), ("d2Sbf", h // 4)], writes=[(pok, c)])
                        self.mm(po[prt, co + 128:co + 256], AT[prt, h, ccs], vn[prt, h, :], True, True,
                                reads=[(sk, 3), (vnk, c, h // 4)], writes=[(pok, c)])
                        ps_, psk_ = bS[h // 4]
                        self.mm(ps_[:, (h % 4) * 128:(h % 4 + 1) * 128], KD[prt, h, cs], vn[prt, h, :], True, True,
                                reads=[(sk, 4), (vnk, c, h // 4)], writes=[psk_])
                    for hb in range(2):
                        ps_, psk_ = bS[hb]
                        hs = slice(4 * hb, 4 * hb + 4)
                        self.tt(Sst[:, hs, :], Sst[:, hs, :], eglast[:, j, hs].unsqueeze(2).to_broadcast([128, 4, 128]),
                                ALU.mult, reads=[("d2S", hb), "deglast"], writes=[("d2S", hb)])
                        self.tt(Sst[:, hs, :], Sst[:, hs, :], ps_[:].rearrange("p (a b) -> p a b", a=4), ALU.add,
                                reads=[("d2S", hb), psk_], writes=[("d2S", hb)])
                        self.copy(Sbf[:, hs, :], Sst[:, hs, :], reads=[("d2S", hb)], writes=[("d2Sbf", hb)], eng="act")
                for b4 in range(4):
                    po, pok = bO[b4]
                    hs = slice(2 * b4, 2 * b4 + 2)
                    pv = po[:].rearrange("p (a b) -> p a b", a=2)
                    self.tt(t1[:, hs, :], pv[:, :, 0:128], egc[:, t, hs].unsqueeze(2).to_broadcast([128, 2, 128]), ALU.mult,
                            reads=[(pok, 0), (pok, 1), "degc"], writes=[("d2t1", b4)])
                    self.tt(ot[:, hs, :], t1[:, hs, :], pv[:, :, 128:256], ALU.add,
                            reads=[("d2t1", b4), (pok, 0), (pok, 1)], writes=[("d2ot", b4)])
                otk = [("d2ot", b4) for b4 in range(4)]
                self.act(sq[:], ot[:], AF.Square, reads=otk, writes=["d2sq"])
                self.S.op("dve", lambda e: e.tensor_reduce(out=ssq[:, 0, :], in_=sq[:], op=ALU.add,
                                                           axis=mybir.AxisListType.X),
                          reads=["d2sq"], writes=["d2ssq"])
                self.act(ssq[:, 1, :], ssq[:, 0, :], AF.Ln, reads=["d2ssq", "consts"], writes=["d2ssq"], scale=1.0 / 128,
                         bias=self.eps_ap())
                self.act(ssq[:, 1, :], ssq[:, 1, :], AF.Exp, reads=["d2ssq"], writes=["d2ssq"], scale=-0.5)
                self.tt(ot[:], ot[:], ssq[:, 1, :].unsqueeze(2).to_broadcast([128, NH, 128]), ALU.mult,
                        reads=otk + ["d2ssq"], writes=otk)
                self.tt(ot[:], ot[:], nwb[:].unsqueeze(1).to_broadcast([128, NH, 128]), ALU.mult, reads=otk + [nwbk],
                        writes=otk)
                ob, obk = obr.next()
                self.tt(ob[:], ot[:], SG[:, :, cs], ALU.mult, reads=otk + [(sk, 5)], writes=[obk])
                px, pxk = bW[0]
                pxb = px[:].bitcast(BF16)
                for h in range(NH):
                    self.tr(pxb[:, h * 128:(h + 1) * 128], ob[:, h, :], self.ident_b(), reads=[obk, "cbf"], writes=[pxk])
                otile, otlk = otl.next()
                self.copy(otile[:], pxb[:].rearrange("p (a b) -> p a b", a=NH), reads=[pxk], writes=[otlk], eng="act")
                self.dma(o_view[:, :, t * 128:(t + 1) * 128], otile[:], reads=[otlk], writes=[("o_scr", 0, "t", t)])

    def branches(self):
        return [i for i, n in enumerate(("dn", "sb", "ssm")) if n in self.mixers]

    def gates_phase(self, l):
        wst = self.ring("gwst", 2, [128, KC, 512], F32)
        wbf = self.ring("gwbf", 2, [128, KC, 512], BF16)
        gb = self.ring("gb", 3, [128, S_LEN], BF16)
        scale = self.nw[:, 2 * l, :]
        blocks = [(i, cb) for i in self.branches() for cb in range(2)]

        def gload(bi):
            i_, cb_ = blocks[bi]
            c0_ = O_GATE + i_ * D + cb_ * 512
            return self.wload(self.w_in_d[l][:, c0_:c0_ + 512], KC, 512, wst, wbf, scale=scale)

        nxt = gload(0)
        for bi_, (i, cb) in enumerate(blocks):
            if True:
                wb, wk = nxt
                if bi_ + 1 < len(blocks):
                    nxt = gload(bi_ + 1)
                cur = {}

                def evac(ci, g, ps, pk, n, cb=cb, i=i, cur=cur):
                    if g == 0:
                        cur["t"] = gb.next()
                    t, tk = cur["t"]
                    self.act(t[:, g * 512:(g + 1) * 512], ps[:], AF.Sigmoid, reads=[pk], writes=[tk])
                    if g == 3:
                        oc = cb * 4 + ci
                        self.dma(self.g_scr[i][oc * 128:(oc + 1) * 128, :], t[:], reads=[tk],
                                 writes=[("g_scr", i, oc)])

                self.linear_T(wb, wk, 512, KC, lambda kc, g: self.hT[:, kc, g * 512:(g + 1) * 512],
                              self.hT_keys, evac)

    def merge_phase(self, l):
        A, S = self.A, self.S
        wst = self.ring("mwst", 1, [128, KC, 512], F32)
        wbf = self.ring("mwbf", 2, [128, KC, 512], BF16)
        mg = A.alloc("mg", [128, KC, 1024], F32)
        mgb = A.alloc("mgb", [128, KC, 1024], BF16)
        ob = self.ring("ob", 1, [128, KC, 1024], BF16)
        gt = self.ring("gt", 3, [128, 1024], BF16)
        self.mtmp = self.ring("mtmp", 3, [128, 512], F32)
        brs = self.branches()
        mseq = []
        for half_ in range(2):
            for i_ in brs:
                for cb_ in range(2):
                    mseq.append(self.w_branch_d[l][i_][:, cb_ * 512:(cb_ + 1) * 512])
            for cb_ in range(2):
                mseq.append(self.w_out_d[l][:, cb_ * 512:(cb_ + 1) * 512])

        def mload(pos):
            if pos >= len(mseq):
                return None
            return self.wload(mseq[pos], KC, 512, wst, wbf)

        mpos = [0]
        mnxt = [mload(0)]
        for half in range(2):
            tsl = slice(half * 1024, (half + 1) * 1024)
            for bi, i in enumerate(brs):
                o, ok = ob.next()
                self.dma(o[:], self.o_scr[i][:, tsl].rearrange("(k p) t -> p k t", p=128),
                         reads=[("o_scr", i, c) for c in range(8)], writes=[ok])
                for cb in range(2):
                    wb, wk = mnxt[0]
                    mnxt[0] = mload(mpos[0] + 1)
                    mpos[0] += 1
                    cur = {}

                    def evac(ci, g, ps, pk, n, cb=cb, i=i, bi=bi, cur=cur, half=half, tsl=tsl):
                        oc = cb * 4 + ci
                        gl = g - 2 * half
                        if gl == 0:
                            cur["g"] = gt.next()
                            t, tk = cur["g"]
                            self.dma(t[:], self.g_scr[i][oc * 128:(oc + 1) * 128, tsl],
                                     reads=[("g_scr", i, oc)], writes=[tk])
                        t, tk = cur["g"]
                        dst = mg[:, oc, gl * 512:(gl + 1) * 512]
                        mk = ("mg", oc, gl)
                        if bi == 0:
                            self.tt(dst, ps[:], t[:, gl * 512:(gl + 1) * 512], ALU.mult,
                                    reads=[pk, tk], writes=[mk])
                        else:
                            tmp, tmk = self.mtmp.next()
                            self.tt(tmp[:], ps[:], t[:, gl * 512:(gl + 1) * 512], ALU.mult,
                                    reads=[pk, tk], writes=[tmk])
                            self.tt(dst, dst, tmp[:], ALU.add, reads=[tmk, mk], writes=[mk], eng="pool")
                        if bi == len(brs) - 1:
                            self.copy(mgb[:, oc, gl * 512:(gl + 1) * 512], dst, reads=[mk],
                                      writes=[("mgb", oc, gl)], eng="act")

                    self.linear_T(wb, wk, 512, KC, lambda kc, g, o=o, half=half: o[:, kc, (g - 2 * half) * 512:(g - 2 * half + 1) * 512],
                                  lambda g, ok=ok: [ok], evac, groups=(2 * half, 2 * half + 1))
            for cb in range(2):
                wb, wk = mnxt[0]
                mnxt[0] = mload(mpos[0] + 1)
                mpos[0] += 1

                def evac2(ci, g, ps, pk, n, cb=cb):
                    oc = cb * 4 + ci
                    sl = slice(g * 512, (g + 1) * 512)
                    self.tt(self.xT[:, oc, sl], self.xT[:, oc, sl], ps[:], ALU.add,
                            reads=[pk, ("xT", oc, g)], writes=[("xT", oc, g)])

                self.linear_T(wb, wk, 512, KC,
                              lambda kc, g, half=half: mgb[:, kc, (g - 2 * half) * 512:(g - 2 * half + 1) * 512],
                              lambda g, half=half: [("mgb", kc, g - 2 * half) for kc in range(KC)], evac2,
                              groups=(2 * half, 2 * half + 1))

    def sb_phase(self, l):
        A, S = self.A, self.S
        R = {}
        R["wst"] = self.ring("sbwst", 1, [128, KC, 384], F32)
        R["wbf"] = self.ring("sbwbf", 2, [128, KC, 384], BF16)
        R["qT"] = self.ring("sbq", 2, [128, 2, S_LEN], BF16)
        for i_, t_ in enumerate(R["qT"].tiles):
            self.memset(t_[:], 0.0, writes=[(R["qT"].name, i_)], eng="pool")
        R["kT"] = self.ring("sbk", 2, [128, S_LEN], BF16)
        R["v"] = self.ring("sbv", 2, [128, 16, 128], BF16)
        R["osb"] = self.ring("sbo", 1, [128, S_LEN], BF16)
        R["e"] = self.ring("sbe", 4, [128, 512], F32)
        R["spb"] = self.ring("sbsp", 4, [128, 512], BF16)
        R["xa"] = self.ring("sbxa", 2, [128, 512], F32)
        R["w"] = self.ring("sbw", 3, [128, 512], BF16)
        R["pa"] = Ring(self.pst[0:4], "psum")
        for hp in range(8):
            self.sb_unit(l, hp, R)

    def sb_unit(self, l, hp, R):
        scale = self.nw[:, 2 * l, :]
        st, sk = R["wst"].next()
        wb, wk = R["wbf"].next()
        for j in range(3):
            base = O_SBQKV + j * 1024 + 128 * hp
            self.dma(st[:, :, j * 128:(j + 1) * 128],
                     self.w_in_d[l][:, base:base + 128].rearrange("(k p) n -> p k n", p=128),
                     reads=[], writes=[(sk, j)])
        for kc in range(KC):
            self.act(wb[:, kc, :], st[:, kc, :], AF.Copy, reads=[(sk, 0), (sk, 1), (sk, 2), "nw"], writes=[wk],
                     scale=scale[:, kc:kc + 1])
        qT, qk = R["qT"].next()
        kT, kk = R["kT"].next()
        v, vk = R["v"].next()

        def evac_qk(ci, g, ps, pk, n):
            if ci == 0:
                self.copy(qT[0:64, 0, g * 512:(g + 1) * 512], ps[0:64, :], reads=[pk, qk], writes=[(qk, g)], eng="act")
                self.copy(qT[64:128, 1, g * 512:(g + 1) * 512], ps[64:128, :], reads=[pk, qk], writes=[(qk, g)], eng="dve")
            else:
                self.copy(kT[:, g * 512:(g + 1) * 512], ps[:], reads=[pk], writes=[(kk, g)],
                          eng=("act" if g % 2 else "dve"))

        self.linear_T(wb, wk, 256, KC, lambda kc, g: self.hT[:, kc, g * 512:(g + 1) * 512], self.hT_keys, evac_qk)
        for tq in range(4):
            ps, pk = self.psum.next()
            for t in range(4):
                tt_ = tq * 4 + t
                for kc in range(KC):
                    self.mm(ps[:, t * 128:(t + 1) * 128], self.hT[:, kc, tt_ * 128:(tt_ + 1) * 128],
                            wb[:, kc, 256:384], kc == 0, kc == KC - 1, reads=[wk, ("hT", kc, tq)], writes=[pk])
            self.copy(v[:, tq * 4:(tq + 1) * 4, :], ps[:].rearrange("p (t c) -> p t c", t=4), reads=[pk],
                      writes=[(vk, tq)], eng=("act" if tq % 2 else "dve"))
        osb, ok = R["osb"].next()
        mstrict = self.consts[:, C_MSTRICT, :]
        tinc = self.cbf[:, C_TINC, :]
        tlow = self.cbf[:, C_MSTRICT, :]
        one_col = self.consts[:, C_ONES, 0:1]
        pst = self.pst
        pa_ring = R["pa"]
        for g in range(4):
            racc = [(pst[4], ("psum", 4)), (pst[5], ("psum", 5))]
            po = [(pst[6], ("psacc", 0)), (pst[7], ("psacc", 1))]
            tiles = []
            for kb in range(4 * g + 3, -1, -1):
                for e in range(2):
                    t0 = max(kb * 128, g * 512)
                    tiles.append(dict(e=e, kb=kb, pb=64 * e, t0=t0, N=(g + 1) * 512 - t0, c0=t0 - g * 512,
                                      diag=kb * 128 >= g * 512, first=(kb == 4 * g + 3), last=(kb == 0)))

            def stage0(T):
                pa, pak = pa_ring.next()
                pb, N, t0, kb = T["pb"], T["N"], T["t0"], T["kb"]
                self.mm(pa[:, :N], kT[:, kb * 128:(kb + 1) * 128], qT[:, T["e"], t0:t0 + N], True, True,
                        reads=[(kk, kb // 4), (qk, g)], writes=[pak])
                T["pa"], T["pak"] = pa, pak

            def stage1(T):
                N, c0 = T["N"], T["c0"]
                ee, ek = R["e"].next()
                self.act(ee[:, :N], T["pa"][:, :N], AF.Exp, reads=[T["pak"]], writes=[ek], scale=0.125)
                if T["diag"]:
                    self.tt(ee[:, :128], ee[:, :128], mstrict, ALU.mult, reads=[ek, "consts"], writes=[ek])
                spb, spk = R["spb"].next()
                self.act(spb[:, :N], ee[:, :N], AF.Ln, reads=[ek, "consts"], writes=[spk], bias=one_col, scale=1.0)
                ra, rak = racc[T["e"]]
                self.mm(ra[:, c0:512], tinc, spb[:, :N], T["first"], False, reads=[spk, "cbf"], writes=[rak],
                        skip_group_check=True)
                T.update(ee=ee, ek=ek, spb=spb, spk=spk)

            def stage2(T):
                N, c0, pb, kb = T["N"], T["c0"], T["pb"], T["kb"]
                ra, rak = racc[T["e"]]
                xa, xk = R["xa"].next()
                self.act(xa[:, :N], ra[:, c0:512], AF.Exp, reads=[rak], writes=[xk], scale=-1.0)
                if not T["last"]:
                    self.mm(ra[:, c0:512], tlow, T["spb"][:, :N], False, True, reads=[T["spk"], "cbf"], writes=[rak],
                            skip_group_check=True)
                w, wwk = R["w"].next()
                self.tt(w[:, :N], T["ee"][:, :N], xa[:, :N], ALU.mult, reads=[T["ek"], xk], writes=[wwk])
                pp, ppk = po[T["e"]]
                self.mm(pp[:, c0:512], v[:, kb, :], w[:, :N], T["first"], T["last"],
                        reads=[(vk, kb // 4), wwk], writes=[ppk], skip_group_check=True)

            n = len(tiles)
            for i in range(min(2, n)):
                stage0(tiles[i])
            for i in range(n + 2):
                if i - 2 >= 0:
                    stage2(tiles[i - 2])
                if i < n:
                    stage1(tiles[i])
                if i + 2 < n:
                    stage0(tiles[i + 2])
            for e in range(2):
                pp, ppk = po[e]
                pb = 64 * e
                self.copy(osb[pb:pb + 64, g * 512:(g + 1) * 512], pp[pb:pb + 64, :], reads=[ppk],
                          writes=[(ok, e, g)], eng="dve")
        self.dma(self.o_scr[1][hp * 128:(hp + 1) * 128, :], osb[:],
                 reads=[(ok, e, g) for e in range(2) for g in range(4)], writes=[("o_scr", 1, hp)])


    def bload(self, name, dram_row_ap, n):
        t = self.A.alloc(name, [128, n], F32)
        k = self.key(name)
        self.dma(t[:], dram_row_ap.partition_broadcast(128), reads=[], writes=[k])
        return t, k

    def conv_silu(self, l, wb, wk, wcol, conv_w_d, conv_b_d, ch, raw, rawk, acc, acck, cw, dst, dstk, func=AF.Silu):
        c, ck = cw.next()
        self.dma(c[:, 0:4], conv_w_d[l][:, ch:ch + 128].rearrange("k c -> c k"), reads=[], writes=[(ck, 0)], slow=True)
        if conv_b_d is not None:
            self.dma(c[:, 4:5], conv_b_d[l][ch:ch + 128].rearrange("(c o) -> c o", o=1), reads=[], writes=[(ck, 1)],
                     slow=True)
        else:
            self.memset(c[:, 4:5], 0.0, writes=[(ck, 1)])

        def evac(ci, g, ps, pk, n):
            self.copy(raw[:, 3 + g * 512:3 + (g + 1) * 512], ps[:], reads=[pk], writes=[(rawk, g)],
                      eng=("act" if g % 2 else "dve"))

        for g in range(4):
            ps, pk = self.psum.next()
            for kc in range(KC):
                self.mm(ps[:], wb[:, kc, wcol:wcol + 128], self.hT[:, kc, g * 512:(g + 1) * 512], kc == 0, kc == KC - 1,
                        reads=[wk] + self.hT_keys(g), writes=[pk])
            evac(0, g, ps, pk, 128)
        rk = [(rawk, g) for g in range(4)] + [(rawk, "pad")]
        self.ts(acc[:], raw[:, 0:S_LEN], c[:, 0:1], ALU.mult, reads=rk + [(ck, 0), (ck, 1)], writes=[acck],
                s2=c[:, 4:5], op1=ALU.add)
        for k in range(1, 4):
            self.stt(acc[:], raw[:, k:k + S_LEN], c[:, k:k + 1], acc[:], ALU.mult, ALU.add,
                     reads=rk + [(ck, 0), acck], writes=[acck])
        self.act(dst, acc[:], func, reads=[acck], writes=[dstk])

    def to_tok(self, src, srck, dst_fn, dstk):
        for tq in range(4):
            ps, pk = self.psum.next()
            pb = ps[:].bitcast(BF16)
            for t in range(4):
                tt_ = tq * 4 + t
                self.tr(pb[:, t * 128:(t + 1) * 128], src[:, tt_ * 128:(tt_ + 1) * 128], self.ident_b(),
                        reads=(list(srck) if isinstance(srck, list) else [srck]) + ["cbf"], writes=[pk])
            self.copy(dst_fn(tq), pb[:, 0:512].rearrange("p (t c) -> p t c", t=4), reads=[pk], writes=[(dstk, tq)],
                      eng=("act" if tq % 2 else "dve"))

    def ssm_phase(self, l):
        A, S = self.A, self.S
        scale = self.nw[:, 2 * l, :]
        cst = self.consts
        wdst = A.alloc("wdtst", [128, KC, 16], F32)
        wdt = A.alloc("wdt", [128, KC, 16], BF16)
        self.dma(wdst[:], self.w_in_d[l][:, O_SSMDT:O_SSMDT + 16].rearrange("(k p) n -> p k n", p=128), [], ["wdtst"])
        for kc in range(KC):
            self.act(wdt[:, kc, :], wdst[:, kc, :], AF.Copy, reads=["wdtst", "nw"], writes=["wdt"], scale=scale[:, kc:kc + 1])
        dtb, dtbk = self.bload("dtb", self.ssm_dt_bias_d[l], 16)
        alog, alogk = self.bload("alog", self.ssm_a_log_d[l], 16)
        dbc, dbck = self.bload("dbc", self.ssm_d_d[l], 16)
        dt = A.alloc("dt", [128, 16, 16], F32)
        av = A.alloc("av", [128, 16, 16], F32)
        acum = A.alloc("acum", [128, 16, 16], F32)
        eacum = A.alloc("eacum", [128, 16, 16], F32)
        dtds = A.alloc("dtds", [128, 16, 16], F32)
        eatot = A.alloc("eatot", [128, 32, 16], F32)
        tmp = A.alloc("ptmp", [128, 16, 16], F32)
        ps, pk = self.psum.next()
        for t in range(16):
            for kc in range(KC):
                self.mm(ps[:, t * 16:(t + 1) * 16], self.hT[:, kc, t * 128:(t + 1) * 128], wdt[:, kc, :], kc == 0,
                        kc == KC - 1, reads=["wdt", ("hT", kc, t // 4)], writes=[pk])
        self.tt(dt[:], ps[:, 0:256].rearrange("p (t h) -> p t h", t=16),
                dtb[:].unsqueeze(1).to_broadcast([128, 16, 16]), ALU.add, reads=[pk, dtbk], writes=["dt"])
        one_col = cst[:, C_ONES, 0:1]
        self.act(tmp[:], dt[:], AF.Exp, reads=["dt"], writes=["ptmp"])
        self.act(dt[:], tmp[:], AF.Ln, reads=["ptmp", "consts"], writes=["dt"], bias=one_col, scale=1.0)
        self.act(alog[:], alog[:], AF.Exp, reads=[alogk], writes=[alogk])
        self.S.op("dve", lambda e: e.scalar_tensor_tensor(out=av[:], in0=dt[:], scalar=-1.0,
                                                          in1=alog[:].unsqueeze(1).to_broadcast([128, 16, 16]),
                                                          op0=ALU.mult, op1=ALU.mult),
                  reads=["dt", alogk], writes=["av"])
        ps, pk = self.psum.next()
        for t in range(16):
            self.mm(ps[:, t * 16:(t + 1) * 16], cst[:, C_TRI2, :], av[:, t, :], True, True, reads=["av", "consts"], writes=[pk])
        self.copy(acum[:], ps[:, 0:256].rearrange("p (t h) -> p t h", t=16), reads=[pk], writes=["acum"])
        self.act(eacum[:], acum[:], AF.Exp, reads=["acum"], writes=["eacum"])
        ps, pk = self.psum.next()
        for t in range(16):
            self.mm(ps[:, t * 16:(t + 1) * 16], cst[:, C_BD, :], av[:, t, :], True, True, reads=["av", "consts"], writes=[pk])
        self.tt(tmp[:], ps[:, 0:256].rearrange("p (t h) -> p t h", t=16), acum[:], ALU.subtract, reads=[pk, "acum"],
                writes=["ptmp"])
        self.act(tmp[:], tmp[:], AF.Exp, reads=["ptmp"], writes=["ptmp"])
        self.tt(dtds[:], tmp[:], dt[:], ALU.mult, reads=["ptmp", "dt"], writes=["dtds"])
        ps, pk = self.psum.next()
        for t in range(16):
            for c in range(2):
                j = 2 * t + c
                self.mm(ps[:, j * 16:(j + 1) * 16], cst[:, C_IND0 + c, :], av[:, t, :], True, True,
                        reads=["av", "consts"], writes=[pk])
        self.act(eatot[:].rearrange("p j h -> p (j h)"), ps[:], AF.Exp, reads=[pk], writes=["eatot"])

        mG = A.mark()
        for g in range(4):
            A.reset(mG)
            S.barrier()
            wbf = A.alloc("swz", [128, KC, 256], BF16)
            BT = A.alloc("sBT", [128, S_LEN], BF16)
            CT = A.alloc("sCT", [128, S_LEN], BF16)
            x_tok = A.alloc("sxtok", [128, 16, 256], BF16)
            B_tok = A.alloc("sBtok", [128, 16, 128], BF16)
            nwb, nwbk = self.bload("snwb", self.ssm_norm_w_d[l][256 * g:256 * (g + 1)], 256)
            mA = A.mark()
            wst = self.ring("swst", 2, [128, KC, 128], F32)
            wbx = A.alloc("swbx", [128, KC, 512], BF16)
            raw = A.alloc("sraw", [128, S_LEN + 4], F32)
            acc = A.alloc("sacc", [128, S_LEN], F32)
            xc = self.ring("sxc", 2, [128, S_LEN], BF16)
            cw = self.ring("scw", 2, [128, 8], F32)
            rawk = self.key("sraw")
            self.memset(raw[:, 0:3], 0.0, writes=[(rawk, "pad")])
            cols = [O_SSMZ + 256 * g, O_SSMZ + 256 * g + 128, O_SSMXBC + 256 * g, O_SSMXBC + 256 * g + 128,
                    O_SSMXBC + 1024 + 128 * g, O_SSMXBC + 1536 + 128 * g]
            wk = self.key("swbf")
            for j, c0 in enumerate(cols):
                st, sk = wst.next()
                self.dma(st[:], self.w_in_d[l][:, c0:c0 + 128].rearrange("(k p) n -> p k n", p=128), [], [sk])
                for kc in range(KC):
                    wdst_ = wbf[:, kc, j * 128:(j + 1) * 128] if j < 2 else wbx[:, kc, (j - 2) * 128:(j - 1) * 128]
                    self.act(wdst_, st[:, kc, :], AF.Copy, reads=[sk, "nw"], writes=[(wk, j)],
                             scale=scale[:, kc:kc + 1])
            chs = [256 * g, 256 * g + 128, 1024 + 128 * g, 1536 + 128 * g]
            acck = self.key("sacc")
            xtk = self.key("sxtok")
            btk = self.key("sBtok")
            for jj, ch in enumerate(chs):
                j = jj + 2
                if jj < 2:
                    dst, dk = xc.next()
                elif jj == 2:
                    dst, dk = BT, "sBT"
                else:
                    dst, dk = CT, "sCT"
                self.conv_silu(l, wbx, (wk, j), jj * 128, self.ssm_conv_w_d, self.ssm_conv_b_d, ch, raw, rawk, acc, acck,
                               cw, dst[:], dk)
                if jj < 2:
                    self.to_tok(dst, dk, lambda tq, jj=jj: x_tok[:, tq * 4:(tq + 1) * 4, jj * 128:(jj + 1) * 128], (xtk, jj))
                elif jj == 2:
                    self.to_tok(dst, dk, lambda tq: B_tok[:, tq * 4:(tq + 1) * 4, :], btk)
            S.barrier()
            A.reset(mA)
            xdt = A.alloc("sxdt", [128, 16, 256], BF16)
            xdtd = A.alloc("sxdtd", [128, 16, 256], BF16)
            xD = A.alloc("sxD", [128, 16, 256], BF16)
            oT = A.alloc("soT", [128, 2, S_LEN], BF16)
            state = A.alloc("sstate", [128, 256], F32)
            state_bf = A.alloc("sstatebf", [128, 256], BF16)
            abc = self.ring("sabc", 2, [128, 128], F32)
            dm = self.ring("sdm", 2, [128, 4, 128], F32)
            MT = self.ring("sMT", 2, [128, 4, 128], BF16)
            szr = self.ring("ssz", 1, [128, 256], F32)
            t1r = self.ring("st1", 1, [128, 256], F32)
            yr = self.ring("sy", 2, [128, 256], F32)
            jr = self.ring("sjunk", 1, [128, 256], F32)
            ssr = self.ring("sssq", 4, [128, 2], F32)
            obr = self.ring("sob", 2, [128, 256], BF16)
            xtks = [(xtk, jj, tq) for jj in range(2) for tq in range(4)]
            x4 = x_tok[:].rearrange("p t (h c) -> p t h c", h=4)
            hs = slice(4 * g, 4 * g + 4)
            self.tt(xdt[:].rearrange("p t (h c) -> p t h c", h=4), x4,
                    dt[:, :, hs].unsqueeze(3).to_broadcast([128, 16, 4, 64]), ALU.mult, reads=xtks + ["dt"], writes=["sxdt"])
            self.tt(xdtd[:].rearrange("p t (h c) -> p t h c", h=4), x4,
                    dtds[:, :, hs].unsqueeze(3).to_broadcast([128, 16, 4, 64]), ALU.mult, reads=xtks + ["dtds"],
                    writes=["sxdtd"])
            for t in range(16):
                self.tt(xD[:, t, :].rearrange("p (h c) -> p h c", h=4), x_tok[:, t, :].rearrange("p (h c) -> p h c", h=4),
                        dbc[:, hs].unsqueeze(2).to_broadcast([128, 4, 64]), ALU.mult, reads=xtks + [dbck],
                        writes=[("sxD", t)], eng="pool")
            stk = self.key("sstate")
            sbk = self.key("sstatebf")
            self.memset(state[:], 0.0, writes=[stk])
            self.memset(state_bf[:], 0.0, writes=[sbk])
            ones_f = cst[:, C_ONES, :]
            for t in range(16):
                tsl = slice(t * 128, (t + 1) * 128)
                psS, psSk = self.psum.next()
                self.mm(psS[:, 0:128], BT[:, tsl], CT[:, tsl], True, True, reads=["sBT", "sCT"], writes=[psSk])
                psD, psDk = self.psum.next()
                for hh in range(4):
                    ab, abk = abc.next()
                    self.ts(ab[:], ones_f, av[:, t, 4 * g + hh:4 * g + hh + 1], ALU.mult, reads=["av", "consts"], writes=[abk])
                    self.mm(psD[:, hh * 128:(hh + 1) * 128], ab[:], cst[:, C_TRI2, :], True, False, reads=[abk, "consts"],
                            writes=[psDk])
                    self.mm(psD[:, hh * 128:(hh + 1) * 128], cst[:, C_TRI2NEG, :], ab[:], False, True,
                            reads=[abk, "consts"], writes=[psDk])
                d_, dk_ = dm.next()
                self.tt(d_[:], psD[:].rearrange("p (h c) -> p h c", h=4),
                        cst[:, C_NEGINCL, :].unsqueeze(1).to_broadcast([128, 4, 128]), ALU.add, reads=[psDk, "consts"],
                        writes=[dk_])
                L_, Lk_ = d_, dk_
                self.act(L_[:], d_[:], AF.Exp, reads=[dk_], writes=[Lk_])
                M_, Mk_ = MT.next()
                self.tt(M_[:], L_[:], psS[:, 0:128].unsqueeze(1).to_broadcast([128, 4, 128]), ALU.mult,
                        reads=[Lk_, psSk], writes=[Mk_])
                psY, psYk = self.psum.next()
                for hh in range(4):
                    cs = slice(hh * 64, (hh + 1) * 64)
                    self.mm(psY[:, cs], M_[:, hh, :], xdt[:, t, cs], True, False, reads=[Mk_, "sxdt"], writes=[psYk])
                    self.mm(psY[:, cs], self.ident_b(), xD[:, t, cs], False, True, reads=["cbf", ("sxD", t)], writes=[psYk])
                psZ, psZk = self.psum.next()
                for kc in range(KC):
                    self.mm(psZ[:, 0:256], self.hT[:, kc, tsl], wbf[:, kc, 0:256], kc == 0, kc == KC - 1,
                            reads=[(wk, 0), (wk, 1), ("hT", kc, t // 4)], writes=[psZk])
                sz, szk = szr.next()
                self.act(sz[:], psZ[:, 0:256], AF.Silu, reads=[psZk], writes=[szk])
                psO, psOk = self.psum_acc.next()
                for c in range(2):
                    j = 2 * t + c
                    self.mm(psO[64 * c:64 * c + 64, 0:256], CT[:, t * 128 + 64 * c:t * 128 + 64 * c + 64], state_bf[:], True, True,
                            reads=["sCT", sbk], writes=[psOk])
                    psT, psTk = self.psum.next()
                    self.mm(psT[:, 0:256], B_tok[64 * c:64 * c + 64, t, :], xdtd[64 * c:64 * c + 64, t, :], True, True,
                            reads=[(btk, t // 4), "sxdtd"], writes=[psTk])
                    self.tt(state[:].rearrange("p (h c) -> p h c", h=4), state[:].rearrange("p (h c) -> p h c", h=4),
                            eatot[:, j, hs].unsqueeze(2).to_broadcast([128, 4, 64]), ALU.mult, reads=[stk, "eatot"],
                            writes=[stk])
                    self.tt(state[:], state[:], psT[:, 0:256], ALU.add, reads=[stk, psTk], writes=[stk])
                    self.copy(state_bf[:], state[:], reads=[stk], writes=[sbk], eng="act")
                t1, t1k = t1r.next()
                self.tt(t1[:].rearrange("p (h c) -> p h c", h=4), psO[:, 0:256].rearrange("p (h c) -> p h c", h=4),
                        eacum[:, t, hs].unsqueeze(2).to_broadcast([128, 4, 64]), ALU.mult, reads=[psOk, "eacum"],
                        writes=[t1k])
                y, yk = yr.next()
                self.tt(y[:], t1[:], psY[:, 0:256], ALU.add, reads=[t1k, psYk], writes=[yk])
                self.tt(y[:], y[:], sz[:], ALU.mult, reads=[yk, szk], writes=[yk])
                jk_, jkk = jr.next()
                ss, ssk = ssr.next()
                self.S.op("act", lambda e, jk_=jk_, y=y, ss=ss: e.activation(out=jk_[:], in_=y[:], func=AF.Square,
                                                                          accum_out=ss[:, 0:1]),
                          reads=[yk], writes=[jkk, ssk])
                self.act(ss[:, 1:2], ss[:, 0:1], AF.Ln, reads=[ssk, "consts"], writes=[ssk], scale=1.0 / 256,
                         bias=self.eps_ap())
                self.act(ss[:, 1:2], ss[:, 1:2], AF.Exp, reads=[ssk], writes=[ssk], scale=-0.5)
                ob, obk = obr.next()
                self.stt(ob[:], y[:], ss[:, 1:2], nwb[:], ALU.mult, ALU.mult, reads=[yk, ssk, nwbk], writes=[obk])
                psX, psXk = self.psum.next()
                pxb = psX[:].bitcast(BF16)
                for ch in range(2):
                    self.tr(pxb[:, ch * 128:(ch + 1) * 128], ob[:, ch * 128:(ch + 1) * 128], self.ident_b(),
                            reads=[obk, "cbf"], writes=[psXk])
                self.copy(oT[:, :, tsl], pxb[:, 0:256].rearrange("p (c t) -> p c t", c=2), reads=[psXk],
                          writes=[("soT", t)], eng="act")
            for ch in range(2):
                oc = 2 * g + ch
                self.dma(self.o_scr[2][oc * 128:(oc + 1) * 128, :], oT[:, ch, :], reads=[("soT", t) for t in range(16)],
                         writes=[("o_scr", 2, oc)])


    def dn_phase(self, l):
        A, S = self.A, self.S
        scale = self.nw[:, 2 * l, :]
        cst = self.consts
        one_col = cst[:, C_ONES, 0:1]
        ones_f = cst[:, C_ONES, :]
        wast = A.alloc("dwast", [128, KC, 16], F32)
        wa = A.alloc("dwa", [128, KC, 16], BF16)
        self.dma(wast[:], self.w_in_d[l][:, O_DNA:O_DNA + 16].rearrange("(k p) n -> p k n", p=128), [], ["dwast"])
        for kc in range(KC):
            self.act(wa[:, kc, :], wast[:, kc, :], AF.Copy, reads=["dwast", "nw"], writes=["dwa"], scale=scale[:, kc:kc + 1])
        dtb, dtbk = self.bload("ddtb", self.dn_dt_bias_d[l], 8)
        alog, alogk = self.bload("dalog", self.dn_a_log_d[l], 8)
        gv = A.alloc("dg", [128, 16, 8], F32)
        beta = A.alloc("dbeta", [128, 16, 8], F32)
        gc = A.alloc("dgc", [128, 16, 8], F32)
        egc = self.dn_egc
        bg = A.alloc("dbg", [128, 16, 8], F32)
        kdec = A.alloc("dkdec", [128, 16, 8], F32)
        eglast = self.dn_eglast
        tmp = A.alloc("dtmp", [128, 16, 8], F32)
        ps, pk = self.psum.next()
        for t in range(16):
            for kc in range(KC):
                self.mm(ps[:, t * 16:(t + 1) * 16], self.hT[:, kc, t * 128:(t + 1) * 128], wa[:, kc, :], kc == 0,
                        kc == KC - 1, reads=["dwa", ("hT", kc, t // 4)], writes=[pk])
        pv = ps[:, 0:256].rearrange("p (t h) -> p t h", t=16)
        self.act(beta[:], pv[:, :, 8:16], AF.Exp, reads=[pk], writes=["dbeta"], scale=-1.0)
        self.ts(beta[:], beta[:], 1.0, ALU.add, reads=["dbeta"], writes=["dbeta"])
        self.S.op("dve", lambda e: e.reciprocal(out=beta[:], in_=beta[:]), reads=["dbeta"], writes=["dbeta"])
        self.tt(gv[:], pv[:, :, 0:8], dtb[:].unsqueeze(1).to_broadcast([128, 16, 8]), ALU.add, reads=[pk, dtbk], writes=["dg"])
        self.act(tmp[:], gv[:], AF.Exp, reads=["dg"], writes=["dtmp"])
        self.act(gv[:], tmp[:], AF.Ln, reads=["dtmp", "consts"], writes=["dg"], bias=one_col, scale=1.0)
        self.act(alog[:], alog[:], AF.Exp, reads=[alogk], writes=[alogk])
        self.S.op("dve", lambda e: e.scalar_tensor_tensor(out=gv[:], in0=gv[:], scalar=-1.0,
                                                          in1=alog[:].unsqueeze(1).to_broadcast([128, 16, 8]),
                                                          op0=ALU.mult, op1=ALU.mult),
                  reads=["dg", alogk], writes=["dg"])
        ps, pk = self.psum.next()
        for t in range(16):
            self.mm(ps[:, t * 8:(t + 1) * 8], cst[:, C_TRI2, :], gv[:, t, :], True, True, reads=["dg", "consts"], writes=[pk])
        self.copy(gc[:], ps[:, 0:128].rearrange("p (t h) -> p t h", t=16), reads=[pk], writes=["dgc"])
        self.act(egc[:], gc[:], AF.Exp, reads=["dgc"], writes=["degc"])
        self.tt(bg[:], egc[:], beta[:], ALU.mult, reads=["degc", "dbeta"], writes=["dbg"])
        ps, pk = self.psum.next()
        for t in range(16):
            self.mm(ps[:, t * 8:(t + 1) * 8], cst[:, C_BD, :], gv[:, t, :], True, True, reads=["dg", "consts"], writes=[pk])
        self.tt(kdec[:], ps[:, 0:128].rearrange("p (t h) -> p t h", t=16), gc[:], ALU.subtract, reads=[pk, "dgc"],
                writes=["dkdec"])
        self.act(kdec[:], kdec[:], AF.Exp, reads=["dkdec"], writes=["dkdec"])
        ps, pk = self.psum.next()
        for t in range(16):
            for c in range(2):
                j = 2 * t + c
                self.mm(ps[:, j * 8:(j + 1) * 8], cst[:, C_IND0 + c, :], gv[:, t, :], True, True,
                        reads=["dg", "consts"], writes=[pk])
        self.act(eglast[:].rearrange("p j h -> p (j h)"), ps[:, 0:256], AF.Exp, reads=[pk], writes=["deglast"])

        mG = A.mark()
        for h in range(8):
            A.reset(mG)
            S.barrier()
            wg = A.alloc("dwg", [128, KC, 128], BF16)
            qTn = A.alloc("dqTn", [128, S_LEN], BF16)
            kTn = A.alloc("dkTn", [128, S_LEN], BF16)
            k_tok = A.alloc("dktok", [128, 16, 128], BF16)
            v_tok = A.alloc("dvtok", [128, 16, 128], BF16)
            mA = A.mark()
            wst = self.ring("dwst", 2, [128, KC, 128], F32)
            wbx = A.alloc("dwbx", [128, KC, 384], BF16)
            raws = [A.alloc("draw", [128, S_LEN + 4], F32) for _ in range(2)]
            accs = [A.alloc("dacc", [128, S_LEN], F32) for _ in range(2)]
            vT = A.alloc("dvT", [128, S_LEN], BF16)
            cw = self.ring("dcw", 2, [128, 8], F32)
            sqr = self.ring("dsq", 2, [128, 512], BF16)
            rsr = self.ring("drs", 2, [128, 512], F32)
            rawks = [self.key("draw"), self.key("draw")]
            for r_, rk__ in zip(raws, rawks):
                self.memset(r_[:, 0:3], 0.0, writes=[(rk__, "pad")])
            cols = [O_DNQKV + 128 * h, O_DNQKV + 1024 + 128 * h, O_DNQKV + 2048 + 128 * h, O_DNGATE + 128 * h]
            wk = self.key("dwb")
            for j, c0 in enumerate(cols):
                st, sk = wst.next()
                self.dma(st[:], self.w_in_d[l][:, c0:c0 + 128].rearrange("(k p) n -> p k n", p=128), [], [sk])
                for kc in range(KC):
                    wd_ = wbx[:, kc, j * 128:(j + 1) * 128] if j < 3 else wg[:, kc, :]
                    self.act(wd_, st[:, kc, :], AF.Copy, reads=[sk, "nw"], writes=[(wk, j)], scale=scale[:, kc:kc + 1])
            accks = [self.key("dacc"), self.key("dacc")]
            ktk = self.key("dktok")
            vtk = self.key("dvtok")
            for j in range(3):
                ch = j * 1024 + 128 * h
                raw, rawk, acc, acck = raws[j % 2], rawks[j % 2], accs[j % 2], accks[j % 2]
                if j == 2:
                    self.conv_silu(l, wbx, (wk, j), j * 128, self.dn_conv_w_d, None, ch, raw, rawk, acc, acck, cw, vT[:], "dvT")
                    self.to_tok(vT, "dvT", lambda tq: v_tok[:, tq * 4:(tq + 1) * 4, :], vtk)
                    continue
                self.conv_silu(l, wbx, (wk, j), j * 128, self.dn_conv_w_d, None, ch, raw, rawk, acc, acck, cw, acc[:], acck)
                dst, dk = (qTn, "dqTn") if j == 0 else (kTn, "dkTn")
                for g in range(4):
                    sl = slice(g * 512, (g + 1) * 512)
                    q_, qk_ = sqr.next()
                    self.act(q_[:], acc[:, sl], AF.Square, reads=[acck], writes=[qk_])
                    ps, pk = self.psum.next()
                    self.mm(ps[:], self.ones_b(), q_[:], True, True, reads=[qk_, "cbf"], writes=[pk])
                    r_, rk_ = rsr.next()
                    self.act(r_[:], ps[:], AF.Ln, reads=[pk, "consts"], writes=[rk_], scale=1.0, bias=self.eps_ap())
                    self.act(r_[:], r_[:], AF.Exp, reads=[rk_], writes=[rk_], scale=-0.5)
                    if j == 0:
                        self.stt(dst[:, sl], acc[:, sl], 128.0 ** -0.5, r_[:], ALU.mult, ALU.mult, reads=[acck, rk_],
                                 writes=[(dk, g)])
                    else:
                        self.tt(dst[:, sl], acc[:, sl], r_[:], ALU.mult, reads=[acck, rk_], writes=[(dk, g)])
                if j == 1:
                    self.to_tok(kTn, [("dkTn", g) for g in range(4)], lambda tq: k_tok[:, tq * 4:(tq + 1) * 4, :], ktk)
            S.barrier()
            A.reset(mA)
            attnT = A.alloc("dattnT", [128, 16, 128], BF16)
            P = A.alloc("dP", [128, 16, 128], BF16)
            PL = A.alloc("dPL", [128, 16, 128], BF16)
            R32 = A.alloc("dR32", [128, 16, 128], F32)
            Rb = A.alloc("dRb", [128, 16, 128], BF16)
            vb = v_tok
            kbg = k_tok
            kd = A.alloc("dkd", [128, 16, 128], BF16)
            u = A.alloc("du", [128, 16, 128], BF16)
            wT = A.alloc("dwT", [128, 16, 128], BF16)
            sgt = A.alloc("dsgt", [128, 16, 128], BF16)
            abr = self.ring("dab", 4, [128, 128], F32)
            dcr = self.ring("ddec", 1, [128, 4, 128], F32)
            tmr = self.ring("dtm", 1, [128, 4, 128], F32)
            bmr = self.ring("dbm", 1, [128, 4, 128], F32)
            ktks = [(ktk, tq) for tq in range(4)]
            vtks = [(vtk, tq) for tq in range(4)]
            bc3 = lambda t_: t_[:, :, h:h + 1].to_broadcast([128, 16, 128])
            self.tt(vb[:], v_tok[:], bc3(beta), ALU.mult, reads=vtks + ["dbeta"], writes=vtks + ["dvb"])
            self.tt(kd[:], k_tok[:], bc3(kdec), ALU.mult, reads=ktks + ["dkdec"], writes=["dkd"])
            self.tt(kbg[:], k_tok[:], bc3(bg), ALU.mult, reads=ktks + ["dbg", "dkd"], writes=ktks + ["dkbg"])
            kTk = [("dkTn", g) for g in range(4)]
            identf4 = cst[:, C_IDENT, :].unsqueeze(1).to_broadcast([128, 4, 128])
            for q in range(4):
                psK, psKk = self.psum.next()
                psQ, psQk = self.psum.next()
                psD, psDk = self.psum.next()
                psB, psBk = self.psum.next()
                for i4 in range(4):
                    t = 4 * q + i4
                    tsl = slice(t * 128, (t + 1) * 128)
                    cs = slice(i4 * 128, (i4 + 1) * 128)
                    self.mm(psK[:, cs], kTn[:, tsl], kTn[:, tsl], True, True, reads=[("dkTn", q)], writes=[psKk])
                    self.mm(psQ[:, cs], kTn[:, tsl], qTn[:, tsl], True, True, reads=[("dkTn", q), ("dqTn", q)], writes=[psQk])
                    ab, abk = abr.next()
                    self.ts(ab[:], ones_f, gv[:, t, h:h + 1], ALU.mult, reads=["dg", "consts"], writes=[abk])
                    self.mm(psD[:, cs], ab[:], cst[:, C_TRI2, :], True, False, reads=[abk, "consts"], writes=[psDk])
                    self.mm(psD[:, cs], cst[:, C_TRI2NEG, :], ab[:], False, True, reads=[abk, "consts"], writes=[psDk])
                    db, dbk = abr.next()
                    self.ts(db[:], cst[:, C_IDENT, :], beta[:, t, h:h + 1], ALU.mult, reads=["dbeta", "consts"], writes=[dbk])
                    self.mm(psB[:, cs], ones_f, db[:], True, True, reads=[dbk, "consts"], writes=[psBk])
                dc, dck = dcr.next()
                self.tt(dc[:], psD[:].rearrange("p (a b) -> p a b", a=4),
                        cst[:, C_NEGINCL, :].unsqueeze(1).to_broadcast([128, 4, 128]), ALU.add, reads=[psDk, "consts"],
                        writes=[dck])
                self.act(dc[:], dc[:], AF.Exp, reads=[dck], writes=[dck])
                self.tt(attnT[:, 4 * q:4 * q + 4, :], psQ[:].rearrange("p (a b) -> p a b", a=4), dc[:], ALU.mult,
                        reads=[psQk, dck], writes=[("dattnT", q)])
                bm, bmk = bmr.next()
                self.tt(bm[:], psB[:].rearrange("p (a b) -> p a b", a=4),
                        cst[:, C_MSTRICT2, :].unsqueeze(1).to_broadcast([128, 4, 128]), ALU.mult, reads=[psBk, "consts"],
                        writes=[bmk])
                tm, tmk = tmr.next()
                self.tt(tm[:], psK[:].rearrange("p (a b) -> p a b", a=4), dc[:], ALU.mult, reads=[psKk, dck], writes=[tmk])
                self.tt(P[:, 4 * q:4 * q + 4, :], tm[:], bm[:], ALU.mult, reads=[tmk, bmk], writes=[("dP", q)])
                psT, psTk = self.psum.next()
                ptb = psT[:].bitcast(BF16)
                for i4 in range(4):
                    self.tr(ptb[:, i4 * 128:(i4 + 1) * 128], P[:, 4 * q + i4, :], self.ident_b(), reads=[("dP", q), "cbf"],
                            writes=[psTk])
                self.copy(PL[:, 4 * q:4 * q + 4, :], ptb[:, 0:512].rearrange("p (a b) -> p a b", a=4), reads=[psTk],
                          writes=[("dPL", q)], eng="act")
                self.tt(R32[:, 4 * q:4 * q + 4, :], identf4, P[:, 4 * q:4 * q + 4, :], ALU.subtract,
                        reads=[("dP", q), "consts"], writes=[("dR32", q)])
                self.copy(Rb[:, 4 * q:4 * q + 4, :], R32[:, 4 * q:4 * q + 4, :], reads=[("dR32", q)], writes=[("dRb", q)],
                          eng="act")
            for lev in range(5):
                last = lev == 4
                for q in range(4):
                    if not last:
                        psP, psPk = self.psum.next()
                    psL, psLk = self.psum.next()
                    for i4 in range(4):
                        t = 4 * q + i4
                        cs = slice(i4 * 128, (i4 + 1) * 128)
                        if not last:
                            self.mm(psP[:, cs], PL[:, t, :], P[:, t, :], True, True, reads=[("dPL", q), ("dP", q)],
                                    writes=[psPk])
                        self.mm(psL[:, cs], P[:, t, :], PL[:, t, :], True, True, reads=[("dPL", q), ("dP", q)],
                                writes=[psLk])
                    if not last:
                        self.copy(P[:, 4 * q:4 * q + 4, :], psP[:].rearrange("p (a b) -> p a b", a=4), reads=[psPk],
                                  writes=[("dP", q)], eng="dve")
                    self.copy(PL[:, 4 * q:4 * q + 4, :], psL[:].rearrange("p (a b) -> p a b", a=4), reads=[psLk],
                              writes=[("dPL", q)], eng="act")
                for q in range(4):
                    psR, psRk = self.psum.next()
                    for i4 in range(4):
                        t = 4 * q + i4
                        cs = slice(i4 * 128, (i4 + 1) * 128)
                        self.mm(psR[:, cs], PL[:, t, :], Rb[:, t, :], True, True, reads=[("dPL", q), ("dRb", q)],
                                writes=[psRk])
                    self.tt(R32[:, 4 * q:4 * q + 4, :], R32[:, 4 * q:4 * q + 4, :],
                            psR[:].rearrange("p (a b) -> p a b", a=4), ALU.add, reads=[psRk, ("dR32", q)],
                            writes=[("dR32", q)])
                    self.copy(Rb[:, 4 * q:4 * q + 4, :], R32[:, 4 * q:4 * q + 4, :], reads=[("dR32", q)],
                              writes=[("dRb", q)], eng="act")
            for q in range(4):
                psU, psUk = self.psum.next()
                psW, psWk = self.psum.next()
                for i4 in range(4):
                    t = 4 * q + i4
                    cs = slice(i4 * 128, (i4 + 1) * 128)
                    self.mm(psU[:, cs], Rb[:, t, :], vb[:, t, :], True, True, reads=[("dRb", q), "dvb"], writes=[psUk])
                    self.mm(psW[:, cs], kbg[:, t, :], Rb[:, t, :], True, True, reads=[("dRb", q), "dkbg"], writes=[psWk])
                self.copy(u[:, 4 * q:4 * q + 4, :], psU[:].rearrange("p (a b) -> p a b", a=4), reads=[psUk],
                          writes=[("du", q)], eng="dve")
                self.copy(wT[:, 4 * q:4 * q + 4, :], psW[:].rearrange("p (a b) -> p a b", a=4), reads=[psWk],
                          writes=[("dwT", q)], eng="act")
            for q in range(4):
                psG, psGk = self.psum.next()
                for i4 in range(4):
                    t = 4 * q + i4
                    for kc in range(KC):
                        self.mm(psG[:, i4 * 128:(i4 + 1) * 128], self.hT[:, kc, t * 128:(t + 1) * 128], wg[:, kc, :],
                                kc == 0, kc == KC - 1, reads=[(wk, 3), ("hT", kc, q)], writes=[psGk])
                self.act(sgt[:, 4 * q:4 * q + 4, :], psG[:].rearrange("p (a b) -> p a b", a=4), AF.Silu, reads=[psGk],
                         writes=[("dsgt", q)])
            fl = lambda t_: t_[:].rearrange("p a b -> p (a b)")
            for nm, src, keys in (("u", fl(u), [("du", q) for q in range(4)]),
                                  ("wT", fl(wT), [("dwT", q) for q in range(4)]),
                                  ("q", qTn[:], [("dqTn", g) for g in range(4)]),
                                  ("attnT", fl(attnT), [("dattnT", q) for q in range(4)]),
                                  ("kd", fl(kd), ["dkd"]),
                                  ("sg", fl(sgt), [("dsgt", q) for q in range(4)])):
                self.dma(self.dn_scr[nm][h], src, reads=keys, writes=[("dn_scr", nm, h)])

    def final_out(self):
        A, S = self.A, self.S
        m = A.mark()
        sq = self.ring("fsq", 3, [128, 512], BF16)
        rs = self.ring("frs", 2, [128, 512], F32)
        hf = self.ring("hf", 3, [128, 512], F32)
        ost = self.ring("ost", 2, [128, 4, D], F32)
        for g in range(4):
            ps, pk = self.psum.next()
            sl = slice(g * 512, (g + 1) * 512)
            for kc in range(KC):
                q, qk = sq.next()
                self.act(q[:], self.xT[:, kc, sl], AF.Square, reads=[("xT", kc, g)], writes=[qk])
                self.mm(ps[:], self.ones_b(), q[:], kc == 0, kc == KC - 1, reads=[qk, "cbf"], writes=[pk])
            r, rk = rs.next()
            self.act(r[:], ps[:], AF.Ln, reads=[pk], writes=[rk], scale=1.0 / D, bias=self.eps_ap())
            self.act(r[:], r[:], AF.Exp, reads=[rk], writes=[rk], scale=-0.5)
            o, ok = ost.next()
            for kc in range(KC):
                h, hk = hf.next()
                self.stt(h[:], self.xT[:, kc, sl], self.nw[:, 2 * DEPTH, kc:kc + 1], r[:], ALU.mult, ALU.mult,
                         reads=[("xT", kc, g), rk, "nw"], writes=[hk])
                ps2, pk2 = self.psum.next()
                for t in range(4):
                    self.tr(ps2[:, t * 128:(t + 1) * 128], h[:, t * 128:(t + 1) * 128], self.ident_f(),
                            reads=[hk, "consts"], writes=[pk2])
                self.copy(o[:, :, kc * 128:(kc + 1) * 128], ps2[:].rearrange("p (t c) -> p t c", t=4),
                          reads=[pk2], writes=[ok], eng=("act" if kc % 2 else "dve"))
            self.dma(self.out_d[g * 512:(g + 1) * 512, :].rearrange("(t p) d -> p t d", p=128), o[:],
                     reads=[ok], writes=[("out", g)])
        S.barrier()
        A.reset(m)


C_IDENT, C_ONES, C_EPS, C_MSTRICT, C_TINC = 0, 1, 2, 3, 4
C_TRI2, C_TRI2NEG, C_BD, C_IND0, C_IND1, C_NEGINCL, C_NEGSTRICT, C_MINCL2, C_MSTRICT2, C_MSTRICT2T = 5, 6, 7, 8, 9, 10, 11, 12, 13, 14
NCONST = 15
NEG = -30000.0


def make_consts():
    c = np.zeros((128, NCONST, 128), np.float32)
    c[:, C_IDENT, :] = np.eye(128, dtype=np.float32)
    c[:, C_ONES, :] = 1.0
    c[:, C_EPS, :] = EPS
    ii = np.arange(128)
    c[:, C_MSTRICT, :] = (ii[:, None] < ii[None, :]).astype(np.float32)
    c[:, C_TINC, :] = (ii[:, None] >= ii[None, :]).astype(np.float32)
    same = (ii[:, None] // 64) == (ii[None, :] // 64)
    le = ii[:, None] <= ii[None, :]
    lt = ii[:, None] < ii[None, :]
    c[:, C_TRI2, :] = (same & le).astype(np.float32)
    c[:, C_TRI2NEG, :] = -c[:, C_TRI2, :]
    c[:, C_BD, :] = same.astype(np.float32)
    c[:, C_IND0, :] = (ii[:, None] < 64).astype(np.float32) * np.ones((1, 128), np.float32)
    c[:, C_IND1, :] = (ii[:, None] >= 64).astype(np.float32) * np.ones((1, 128), np.float32)
    c[:, C_NEGINCL, :] = np.where(same & le, 0.0, NEG)
    c[:, C_NEGSTRICT, :] = np.where(same & lt, 0.0, NEG)
    c[:, C_MINCL2, :] = (same & le).astype(np.float32)
    c[:, C_MSTRICT2, :] = (same & lt).astype(np.float32)
    c[:, C_MSTRICT2T, :] = (same & lt).T.astype(np.float32)
    return c.reshape(128, NCONST * 128)


_CACHE = {}
_RUN_KW = {}


def get_nc(**kw):
    key = tuple(sorted((k, str(v)) for k, v in kw.items()))
    if key not in _CACHE:
        b = Builder(**kw)
        b.build()
        _CACHE[key] = b
    return _CACHE[key]


def run(inputs, **kw):
    b = get_nc(**kw)
    consts = make_consts()
    common = {
        "consts": consts,
        "w_in": np.ascontiguousarray(inputs["w_in"], dtype=np.float32),
        "norm_mix": np.ascontiguousarray(inputs["norm_mix"], dtype=np.float32),
        "norm_mlp": np.ascontiguousarray(inputs["norm_mlp"], dtype=np.float32),
        "norm_final": np.ascontiguousarray(inputs["norm_final"], dtype=np.float32).reshape(1, D),
        "w_branch": np.ascontiguousarray(inputs["w_branch"], dtype=np.float32),
        "w_out": np.ascontiguousarray(inputs["w_out"], dtype=np.float32),
        "w_up": np.ascontiguousarray(inputs["w_up"], dtype=np.float32),
        "w_down": np.ascontiguousarray(inputs["w_down"], dtype=np.float32),
    }
    for nm in ("ssm_conv_w", "ssm_conv_b", "ssm_a_log", "ssm_dt_bias", "ssm_d", "ssm_norm_w", "dn_conv_w", "dn_a_log",
               "dn_dt_bias", "dn_norm_w"):
        common[nm] = np.ascontiguousarray(inputs[nm], dtype=np.float32)
    x = np.asarray(inputs["x"], dtype=np.float32)
    in_maps = []
    for c in range(NCORES):
        m = dict(common)
        m["x"] = np.ascontiguousarray(x[c])
        in_maps.append(m)
    res = run_bass_kernel_spmd(b.nc, in_maps, core_ids=list(range(NCORES)), **_RUN_KW)
    return res


def kernel(**inputs):
    res = run(inputs)
    out = np.stack([np.asarray(r["out"], dtype=np.float32) for r in res.results], axis=0)
    return out
```

```python
import numpy as np
import concourse.bass as bass
import concourse.mybir as mybir
from concourse.bass_utils import run_bass_kernel_spmd

F32 = mybir.dt.float32
BF16 = mybir.dt.bfloat16
AF = mybir.ActivationFunctionType
ALU = mybir.AluOpType

S_LEN = 2048
D = 1024
KC = 8
NCORES = 8
DEPTH = 2
D_FF = 4096
EPS = 1e-6
IN_SIZES = (3072, 1024, 8, 8, 3072, 1024, 2048, 16, 3072)
IN_DIM = sum(IN_SIZES)
OFF = [0]
for _s in IN_SIZES:
    OFF.append(OFF[-1] + _s)
(O_DNQKV, O_DNGATE, O_DNA, O_DNB, O_SBQKV, O_SSMZ, O_SSMXBC, O_SSMDT, O_GATE, _) = OFF


class _Op:
    __slots__ = ("id", "eng", "fn", "dma", "waits", "idx", "sig", "reuse", "seen")


class Sched:
    ENG = ("pe", "act", "dve", "pool", "sp")
    NSLOT = 12
    SEM_LIMIT = 20000

    def __init__(self, nc):
        self.nc = nc
        self.ops = []
        self.by_eng = {e: [] for e in self.ENG}
        self.kstate = {}
        self.seen = {e: {p: -1 for p in self.ENG} for e in self.ENG}
        self.seen_dma = {e: set() for e in self.ENG}
        self.pending = {e: set() for e in self.ENG}
        self.open_dma = []

    def op(self, eng, fn, reads=(), writes=(), dma=False):
        o = _Op()
        o.id = len(self.ops)
        o.eng = eng
        o.fn = fn
        o.dma = dma
        o.sig = None
        o.reuse = None
        deps = {}
        for k in reads:
            st = self.kstate.get(k)
            if st is not None and st[0] is not None:
                deps[st[0]] = True
        for k in writes:
            st = self.kstate.get(k)
            if st is not None:
                if st[0] is not None:
                    deps.setdefault(st[0], False)
                for r in st[1].values():
                    deps.setdefault(r, False)
                for r in st[2]:
                    deps.setdefault(r, False)
        for d in self.pending[eng]:
            deps[d] = True
        self.pending[eng] = set()
        o.idx = len(self.by_eng[eng])
        seen = self.seen[eng]
        best = {}
        waits = []
        for d, raw in deps.items():
            p = self.ops[d]
            if p.dma:
                if d in self.seen_dma[eng]:
                    continue
                self.seen_dma[eng].add(d)
                waits.append(d)
            else:
                if p.eng == eng and not dma:
                    if eng == "pe" or (not raw and eng != "pool"):
                        continue
                if p.idx <= seen[p.eng]:
                    continue
                if p.eng not in best or self.ops[best[p.eng]].idx < p.idx:
                    best[p.eng] = d
        for pe, d in best.items():
            p = self.ops[d]
            waits.append(d)
            for e2, v in p.seen.items():
                if v > seen[e2]:
                    seen[e2] = v
            if p.idx > seen[pe]:
                seen[pe] = p.idx
        o.waits = waits
        o.seen = dict(seen)
        for k in reads:
            st = self.kstate.get(k)
            if st is None:
                st = [None, {}, []]
                self.kstate[k] = st
            if dma:
                st[2].append(o.id)
            else:
                st[1][eng] = o.id
        for k in writes:
            self.kstate[k] = [o.id, {}, []]
        self.ops.append(o)
        self.by_eng[eng].append(o)
        if dma:
            self.open_dma.append(o.id)
        return o

    def barrier(self):
        last = []
        for e in self.ENG:
            for o in reversed(self.by_eng[e]):
                if o.dma:
                    break
                if o.fn is not None:
                    last.append(o.id)
                    break
        for e in self.ENG:
            self.pending[e] |= set(last) | set(self.open_dma[-self.NSLOT:])
        self.open_dma = self.open_dma[-self.NSLOT:]

    def finalize(self):
        nc = self.nc
        needed = set()
        for o in self.ops:
            needed.update(o.waits)
        for e in self.ENG:
            sem = None
            cnt = 0
            ndma = 0
            slots = None
            for o in self.by_eng[e]:
                if o.dma:
                    if slots is None:
                        slots = [nc.alloc_semaphore(f"dq_{e}_{i}") for i in range(self.NSLOT)]
                    s = ndma % self.NSLOT
                    r = ndma // self.NSLOT
                    o.sig = (slots[s], 16 * (r + 1))
                    if r > 0:
                        o.reuse = (slots[s], 16 * r)
                    ndma += 1
                elif o.id in needed:
                    if sem is None or cnt >= self.SEM_LIMIT:
                        sem = nc.alloc_semaphore(f"pg_{e}_{o.id}")
                        cnt = 0
                    cnt += 1
                    o.sig = (sem, cnt)

    def emit(self, ename, eng):
        ops = self.ops
        for o in self.by_eng[ename]:
            for d in o.waits:
                s, v = ops[d].sig
                eng.wait_ge(s, v)
            if o.reuse is not None:
                eng.wait_ge(o.reuse[0], o.reuse[1])
            if o.fn is None:
                continue
            ins = o.fn(eng)
            if o.sig is not None:
                ins.then_inc(o.sig[0], 16 if o.dma else 1)


class Arena:
    def __init__(self, nc, limit=208 * 1024):
        self.nc = nc
        self.off = 16 * 1024
        self.limit = limit
        self.n = 0
        self.peak = 0

    def alloc(self, name, shape, dtype):
        per = 1
        for s in shape[1:]:
            per *= s
        nbytes = per * (4 if dtype == F32 else 2)
        nbytes = (nbytes + 63) // 64 * 64
        assert self.off + nbytes <= self.limit, f"SBUF arena overflow at {name}: {self.off}+{nbytes}"
        self.n += 1
        t = self.nc.alloc_sbuf_tensor_at(f"{name}_{self.n}", list(shape), dtype, offset=self.off)
        self.off += nbytes
        self.peak = max(self.peak, self.off)
        return t

    def mark(self):
        return self.off

    def reset(self, m):
        self.off = m


class Ring:
    def __init__(self, tiles, name):
        self.tiles = tiles
        self.name = name
        self.i = 0

    def next(self):
        j = self.i % len(self.tiles)
        self.i += 1
        return self.tiles[j], (self.name, j)


class Builder:
    def __init__(self, nlayers=DEPTH, mixers=("dn", "sb", "ssm"), do_mlp=True, debug=()):
        self.nlayers = nlayers
        self.mixers = mixers
        self.do_mlp = do_mlp
        self.debug = debug
        nc = bass.Bass("TRN2", target_bir_lowering=False)
        self.nc = nc
        self.S = Sched(nc)
        self.A = Arena(nc)
        self.uid = 0

    def dram_in(self, name, shape, dtype=F32):
        return self.nc.dram_tensor(name, list(shape), dtype, kind="ExternalInput").ap()

    def dram_out(self, name, shape, dtype=F32):
        return self.nc.dram_tensor(name, list(shape), dtype, kind="ExternalOutput").ap()

    def dram_tmp(self, name, shape, dtype):
        return self.nc.dram_tensor(name, list(shape), dtype, kind="Internal").ap()

    def key(self, base):
        self.uid += 1
        return (base, self.uid)

    def ring(self, name, n, shape, dtype):
        return Ring([self.A.alloc(name, shape, dtype) for _ in range(n)], self.key(name))

    def dma(self, out, in_, reads, writes, slow=False):
        if slow:
            fn = lambda e, out=out, in_=in_: e.dma_start(out=out, in_=in_, allow_slow_non_contiguous=True)
        else:
            fn = lambda e, out=out, in_=in_: e.dma_start(out=out, in_=in_)
        return self.S.op("sp", fn, reads=reads, writes=writes, dma=True)

    def mm(self, out, lhsT, rhs, start, stop, reads, writes, **kw):
        return self.S.op(
            "pe",
            lambda e, out=out, lhsT=lhsT, rhs=rhs, start=start, stop=stop, kw=kw: e.matmul(
                out, lhsT=lhsT, rhs=rhs, start=start, stop=stop, **kw
            ),
            reads=reads,
            writes=writes,
        )

    def tr(self, out, in_, ident, reads, writes):
        return self.S.op(
            "pe",
            lambda e, out=out, in_=in_, ident=ident: e.transpose(out, in_, ident),
            reads=reads,
            writes=writes,
        )

    def act(self, out, in_, func, reads, writes, eng="act", **kw):
        return self.S.op(
            eng,
            lambda e, out=out, in_=in_, func=func, kw=kw: e.activation(out=out, in_=in_, func=func, **kw),
            reads=reads,
            writes=writes,
        )

    def tt(self, out, in0, in1, op, reads, writes, eng="dve"):
        return self.S.op(
            eng,
            lambda e, out=out, in0=in0, in1=in1, op=op: e.tensor_tensor(out=out, in0=in0, in1=in1, op=op),
            reads=reads,
            writes=writes,
        )

    def ts(self, out, in0, s1, op0, reads, writes, s2=None, op1=None, eng="dve"):
        def fn(e, out=out, in0=in0, s1=s1, op0=op0, s2=s2, op1=op1):
            if op1 is None:
                return e.tensor_scalar(out=out, in0=in0, scalar1=s1, scalar2=None, op0=op0)
            return e.tensor_scalar(out=out, in0=in0, scalar1=s1, scalar2=s2, op0=op0, op1=op1)

        return self.S.op(eng, fn, reads=reads, writes=writes)

    def stt(self, out, in0, scalar, in1, op0, op1, reads, writes):
        return self.S.op(
            "dve",
            lambda e, out=out, in0=in0, scalar=scalar, in1=in1, op0=op0, op1=op1: e.scalar_tensor_tensor(
                out=out, in0=in0, scalar=scalar, in1=in1, op0=op0, op1=op1
            ),
            reads=reads,
            writes=writes,
        )

    def copy(self, out, in_, reads, writes, eng="dve"):
        if eng == "act":
            return self.act(out, in_, AF.Copy, reads, writes)
        return self.S.op(
            eng, lambda e, out=out, in_=in_: e.tensor_copy(out=out, in_=in_), reads=reads, writes=writes
        )

    def memset(self, ap, val, writes, eng="dve"):
        return self.S.op(eng, lambda e, ap=ap, val=val: e.memset(ap, val), reads=(), writes=writes)

    def build(self):
        nc, S, A = self.nc, self.S, self.A
        L = self.nlayers
        self.x_d = self.dram_in("x", [S_LEN, D])
        self.consts_d = self.dram_in("consts", [128, NCONST * 128])
        self.w_in_d = self.dram_in("w_in", [DEPTH, D, IN_DIM])
        self.norm_mix_d = self.dram_in("norm_mix", [DEPTH, D])
        self.norm_mlp_d = self.dram_in("norm_mlp", [DEPTH, D])
        self.norm_final_d = self.dram_in("norm_final", [1, D])
        self.w_branch_d = self.dram_in("w_branch", [DEPTH, 3, D, D])
        self.w_out_d = self.dram_in("w_out", [DEPTH, D, D])
        self.w_up_d = self.dram_in("w_up", [DEPTH, D, D_FF])
        self.w_down_d = self.dram_in("w_down", [DEPTH, D_FF, D])
        self.ssm_conv_w_d = self.dram_in("ssm_conv_w", [DEPTH, 4, 2048])
        self.ssm_conv_b_d = self.dram_in("ssm_conv_b", [DEPTH, 2048])
        self.ssm_a_log_d = self.dram_in("ssm_a_log", [DEPTH, 16])
        self.ssm_dt_bias_d = self.dram_in("ssm_dt_bias", [DEPTH, 16])
        self.ssm_d_d = self.dram_in("ssm_d", [DEPTH, 16])
        self.ssm_norm_w_d = self.dram_in("ssm_norm_w", [DEPTH, D])
        self.dn_conv_w_d = self.dram_in("dn_conv_w", [DEPTH, 4, 3072])
        self.dn_a_log_d = self.dram_in("dn_a_log", [DEPTH, 8])
        self.dn_dt_bias_d = self.dram_in("dn_dt_bias", [DEPTH, 8])
        self.dn_norm_w_d = self.dram_in("dn_norm_w", [DEPTH, 128])
        self.out_d = self.dram_out("out", [S_LEN, D])
        self.u_scr = self.dram_tmp("u_scr", [D_FF, S_LEN], BF16)
        if self.debug:
            self.o_scr = self.dram_out("o_scr", [3, D, S_LEN], BF16)
        else:
            self.o_scr = self.dram_tmp("o_scr", [3, D, S_LEN], BF16)
        self.g_scr = self.dram_tmp("g_scr", [3, D, S_LEN], BF16)
        self.dn_scr = {nm: self.dram_tmp("dn_" + nm, [8, 128, S_LEN], BF16) for nm in ("u", "wT", "q", "attnT", "kd", "sg")}

        pst = [nc.alloc_psum_tensor(f"ps{i}", [128, 512], F32) for i in range(8)]
        self.pst = pst
        self.psum = Ring(pst[:6], "psum")
        self.psum_acc = Ring(pst[6:], "psacc")

        self.consts = A.alloc("consts", [128, NCONST, 128], F32)
        self.cbf = A.alloc("cbf", [128, NCONST, 128], BF16)
        self.xT = A.alloc("xT", [128, KC, S_LEN], F32)
        self.nw = A.alloc("nw", [128, 2 * DEPTH + 1, KC], F32)
        KCON = "consts"
        self.dma(self.consts[:].rearrange("p a b -> p (a b)"), self.consts_d, reads=[], writes=[KCON])
        self.copy(self.cbf[:], self.consts[:], reads=[KCON], writes=["cbf"])
        for l in range(DEPTH):
            self.dma(self.nw[:, 2 * l, :], self.norm_mix_d[l].rearrange("(k p) -> p k", p=128), [], ["nw"], slow=True)
            self.dma(self.nw[:, 2 * l + 1, :], self.norm_mlp_d[l].rearrange("(k p) -> p k", p=128), [], ["nw"], slow=True)
        self.dma(self.nw[:, 2 * DEPTH, :], self.norm_final_d[0].rearrange("(k p) -> p k", p=128), [], ["nw"], slow=True)

        self.load_x()
        for l in range(L):
            self.layer(l)
        self.final_out()
        S.barrier()
        S.op("sp", None)
        S.finalize()

        with nc.Block() as block:

            @block.tensor
            def _(e):
                S.emit("pe", e)

            @block.scalar
            def _(e):
                S.emit("act", e)

            @block.vector
            def _(e):
                S.emit("dve", e)

            @block.gpsimd
            def _(e):
                S.emit("pool", e)

            @block.sync
            def _(e):
                S.emit("sp", e)

        return nc

    def ident_f(self):
        return self.consts[:, C_IDENT, :]

    def ident_b(self):
        return self.cbf[:, C_IDENT, :]

    def ones_b(self):
        return self.cbf[:, C_ONES, :]

    def load_x(self):
        A, S = self.A, self.S
        m = A.mark()
        stg = self.ring("xstg", 2, [128, 4, D], F32)
        for tg in range(4):
            st, sk = stg.next()
            self.dma(
                st[:],
                self.x_d[tg * 512:(tg + 1) * 512, :].rearrange("(t p) d -> p t d", p=128),
                reads=[],
                writes=[sk],
            )
            for kc in range(KC):
                ps, pk = self.psum.next()
                for t in range(4):
                    self.tr(ps[:, t * 128:(t + 1) * 128], st[:, t, kc * 128:(kc + 1) * 128], self.ident_f(),
                            reads=[sk, "consts"], writes=[pk])
                self.copy(self.xT[:, kc, tg * 512:(tg + 1) * 512], ps[:], reads=[pk], writes=[("xT", kc, tg)],
                          eng=("act" if kc % 2 else "dve"))
        S.barrier()
        A.reset(m)

    def rmsnorm(self):
        A, S = self.A, self.S
        m = A.mark()
        sq = self.ring("sq", 3, [128, 512], BF16)
        rs = self.ring("rs", 2, [128, 512], F32)
        for g in range(4):
            ps, pk = self.psum.next()
            sl = slice(g * 512, (g + 1) * 512)
            for kc in range(KC):
                q, qk = sq.next()
                self.act(q[:], self.xT[:, kc, sl], AF.Square, reads=[("xT", kc, g)], writes=[qk])
                self.mm(ps[:], self.ones_b(), q[:], kc == 0, kc == KC - 1, reads=[qk, "cbf"], writes=[pk])
            r, rk = rs.next()
            self.act(r[:], ps[:], AF.Ln, reads=[pk], writes=[rk], scale=1.0 / D, bias=EPS)
            self.act(r[:], r[:], AF.Exp, reads=[rk], writes=[rk], scale=-0.5)
            for kc in range(KC):
                self.tt(self.hT[:, kc, sl], self.xT[:, kc, sl], r[:], ALU.mult,
                        reads=[("xT", kc, g), rk], writes=[("hT", kc, g)])
        self.rstd_ring = rs
        S.barrier()
        A.reset(m)

    def eps_ap(self):
        return self.consts[:, C_EPS, 0:1]

    def wload(self, w_ap, kcn, n, wst_ring, wbf_ring, scale=None):
        st, sk = wst_ring.next()
        wb, wk = wbf_ring.next()
        self.dma(st[:, :kcn, :n], w_ap.rearrange("(k p) n -> p k n", p=128), reads=[], writes=[sk])
        for kc in range(kcn):
            if scale is not None:
                self.act(wb[:, kc, :n], st[:, kc, :n], AF.Copy, reads=[sk, "nw"], writes=[wk],
                         scale=scale[:, kc:kc + 1])
            else:
                self.act(wb[:, kc, :n], st[:, kc, :n], AF.Copy, reads=[sk], writes=[wk])
        return wb, wk

    def hT_keys(self, g):
        return [("hT", kc, g) for kc in range(KC)]

    def linear_T(self, wb, wk, ncols, kcn, rhs_fn, rhs_keys_fn, evac, groups=range(4)):
        for c0 in range(0, ncols, 128):
            n = min(128, ncols - c0)
            for g in groups:
                ps, pk = self.psum.next()
                for kc in range(kcn):
                    self.mm(ps[:n, :], wb[:, kc, c0:c0 + n], rhs_fn(kc, g), kc == 0, kc == kcn - 1,
                            reads=[wk] + rhs_keys_fn(g), writes=[pk])
                evac(c0 // 128, g, ps, pk, n)

    def mlp(self, l):
        A, S = self.A, self.S
        m = A.mark()
        self.hT = A.alloc("hT", [128, KC, S_LEN], BF16)
        self.rmsnorm()
        wst = self.ring("wst", 2, [128, KC, 512], F32)
        wbf = self.ring("wbf", 2, [128, KC, 512], BF16)
        ub = self.ring("ub", 3, [128, S_LEN], BF16)
        rr = self.ring("rr", 3, [128, 512], F32)
        scale = self.nw[:, 2 * l + 1, :]
        nxt = self.wload(self.w_up_d[l][:, 0:512], KC, 512, wst, wbf, scale=scale)
        for cb in range(D_FF // 512):
            wb, wk = nxt
            if cb + 1 < D_FF // 512:
                nxt = self.wload(self.w_up_d[l][:, (cb + 1) * 512:(cb + 2) * 512], KC, 512, wst, wbf, scale=scale)
            cur = {}

            def evac(ci, g, ps, pk, n, cb=cb, cur=cur):
                if g == 0:
                    cur["u"] = ub.next()
                u, uk = cur["u"]
                r, rk = rr.next()
                self.ts(r[:], ps[:], 0.0, ALU.max, reads=[pk], writes=[rk])
                self.act(u[:, g * 512:(g + 1) * 512], r[:], AF.Square, reads=[rk], writes=[uk])
                if g == 3:
                    f = cb * 4 + ci
                    self.dma(self.u_scr[f * 128:(f + 1) * 128, :], u[:], reads=[uk], writes=[("u_scr", f)])

            self.linear_T(wb, wk, 512, KC, lambda kc, g: self.hT[:, kc, g * 512:(g + 1) * 512],
                          self.hT_keys, evac)
        S.barrier()
        A.reset(m)
        wst = self.ring("wdst", 2, [128, 4, 512], F32)
        wbf = self.ring("wdbf", 2, [128, 32, 512], BF16)
        ur = self.ring("ur", 1, [128, 32, 512], BF16)

        def load_down(half):
            wb, wk = wbf.next()
            for q in range(8):
                st, sk = wst.next()
                self.dma(st[:], self.w_down_d[l][q * 512:(q + 1) * 512, half * 512:(half + 1) * 512]
                         .rearrange("(k p) n -> p k n", p=128), reads=[], writes=[sk])
                for k4 in range(4):
                    self.act(wb[:, q * 4 + k4, :], st[:, k4, :], AF.Copy, reads=[sk], writes=[wk])
            return wb, wk

        nxt = load_down(0)
        for half in range(2):
            wb, wk = nxt
            if half == 0:
                nxt = load_down(1)
            for g in range(4):
                u, uk = ur.next()
                self.dma(u[:], self.u_scr[:, g * 512:(g + 1) * 512].rearrange("(k p) t -> p k t", p=128),
                         reads=[("u_scr", f) for f in range(32)], writes=[uk])
                for ci in range(4):
                    ps, pk = self.psum.next()
                    for kc in range(32):
                        self.mm(ps[:], wb[:, kc, ci * 128:(ci + 1) * 128], u[:, kc, :], kc == 0, kc == 31,
                                reads=[wk, uk], writes=[pk])
                    oc = half * 4 + ci
                    sl = slice(g * 512, (g + 1) * 512)
                    self.tt(self.xT[:, oc, sl], self.xT[:, oc, sl], ps[:], ALU.add,
                            reads=[pk, ("xT", oc, g)], writes=[("xT", oc, g)])
        S.barrier()
        A.reset(m)

    def layer(self, l):
        if self.mixers:
            self.mixer(l)
        if self.do_mlp:
            self.mlp(l)


    def mixer(self, l):
        A, S = self.A, self.S
        m00 = A.mark()
        if "dn" in self.mixers:
            self.dn_egc = A.alloc("degc", [128, 16, 8], F32)
            self.dn_eglast = A.alloc("deglast", [128, 32, 8], F32)
            self.dn_nwb, self.dn_nwbk = self.bload("dnwb", self.dn_norm_w_d[l], 128)
        m0 = A.mark()
        self.hT = A.alloc("hT", [128, KC, S_LEN], BF16)
        self.rmsnorm()
        m1 = A.mark()
        if "sb" in self.mixers:
            self.sb_phase(l)
            S.barrier()
            A.reset(m1)
        if "ssm" in self.mixers:
            self.ssm_phase(l)
            S.barrier()
            A.reset(m1)
        if "dn" in self.mixers:
            self.dn_phase(l)
            S.barrier()
            A.reset(m1)
        self.gates_phase(l)
        S.barrier()
        A.reset(m0)
        if "dn" in self.mixers:
            self.dn_phase2(l)
            S.barrier()
            A.reset(m0)
        self.merge_phase(l)
        S.barrier()
        A.reset(m00)

    def dn_phase2(self, l):
        A, S = self.A, self.S
        cst = self.consts
        pst = self.pst
        NH = 8
        names = ["u", "wT", "q", "attnT", "kd", "sg"]
        sets = [[A.alloc(f"d2{n}", [128, NH, 512], BF16) for n in names] for _ in range(1)]
        Sst = A.alloc("d2S", [128, NH, 128], F32)
        Sbf = A.alloc("d2Sbf", [128, NH, 128], BF16)
        vnr = self.ring("d2vn", 2, [128, NH, 128], BF16)
        t1 = A.alloc("d2t1", [128, NH, 128], F32)
        ot = A.alloc("d2ot", [128, NH, 128], F32)
        sq = A.alloc("d2sq", [128, NH, 128], F32)
        ssq = A.alloc("d2ssq", [128, 2, NH], F32)
        obr = self.ring("d2ob", 2, [128, NH, 128], BF16)
        otl = self.ring("d2otl", 2, [128, NH, 128], BF16)
        egc, eglast, nwb, nwbk = self.dn_egc, self.dn_eglast, self.dn_nwb, self.dn_nwbk
        self.memset(Sst[:], 0.0, writes=[("d2S", 0), ("d2S", 1)])
        self.memset(Sbf[:], 0.0, writes=[("d2Sbf", 0), ("d2Sbf", 1)])
        bO = [(pst[i], ("p2O", i)) for i in range(4)]
        bW = [(pst[4], ("p2W", 0)), (pst[5], ("p2W", 1))]
        bS = [(pst[6], ("p2S", 0)), (pst[7], ("p2S", 1))]
        o_view = self.o_scr[0].rearrange("(h e) s -> e h s", h=NH)
        for q in range(4):
            st = sets[0]
            sk = ("d2set", 0)
            for ni, nm in enumerate(names):
                self.dma(st[ni][:], self.dn_scr[nm][:, :, q * 512:(q + 1) * 512].rearrange("h p t -> p h t"),
                         reads=[("dn_scr", nm, h) for h in range(NH)], writes=[(sk, ni)])
            U, WT, Q, AT, KD, SG = st
            for tl in range(4):
                t = 4 * q + tl
                cs = slice(tl * 128, (tl + 1) * 128)
                vn, vnk = vnr.next()
                for c in range(2):
                    j = 2 * t + c
                    pb = 64 * c
                    prt = slice(pb, pb + 64)
                    ccs = slice(tl * 128 + pb, tl * 128 + pb + 64)
                    for h in range(NH):
                        pw, pwk = bW[h // 4]
                        self.mm(pw[prt, (h % 4) * 128:(h % 4 + 1) * 128], WT[:, h, ccs], Sbf[:, h, :], True, True,
                                reads=[(sk, 1), ("d2Sbf", h // 4)], writes=[pwk])
                    for hb in range(2):
                        pw, pwk = bW[hb]
                        self.tt(vn[prt, 4 * hb:4 * hb + 4, :], U[prt, 4 * hb:4 * hb + 4, cs],
                                pw[prt, :].rearrange("p (a b) -> p a b", a=4), ALU.subtract, reads=[(sk, 0), pwk],
                                writes=[(vnk, c, hb)])
                    for h in range(NH):
                        po, pok = bO[h // 2]
                        co = (h % 2) * 256
                        self.mm(po[prt, co:co + 128], Q[:, h, ccs], Sbf[:, h, :], True, True,
                                reads=[(sk, 2), ("d2Sbf", h // 4)], writes=[(pok, c)])
                        self.mm(po[prt, co + 128:co + 256], AT[prt, h, ccs], vn[prt, h, :], True, True,
                                reads=[(sk, 3), (vnk, c, h // 4)], writes=[(pok, c)])
                        ps_, psk_ = bS[h // 4]
                        self.mm(ps_[:, (h % 4) * 128:(h % 4 + 1) * 128], KD[prt, h, cs], vn[prt, h, :], True, True,
                                reads=[(sk, 4), (vnk, c, h // 4)], writes=[psk_])
                    for hb in range(2):
                        ps_, psk_ = bS[hb]
                        hs = slice(4 * hb, 4 * hb + 4)
                        self.tt(Sst[:, hs, :], Sst[:, hs, :], eglast[:, j, hs].unsqueeze(2).to_broadcast([128, 4, 128]),
                                ALU.mult, reads=[("d2S", hb), "deglast"], writes=[("d2S", hb)])
                        self.tt(Sst[:, hs, :], Sst[:, hs, :], ps_[:].rearrange("p (a b) -> p a b", a=4), ALU.add,
                                reads=[("d2S", hb), psk_], writes=[("d2S", hb)])
                        self.copy(Sbf[:, hs, :], Sst[:, hs, :], reads=[("d2S", hb)], writes=[("d2Sbf", hb)], eng="act")
                for b4 in range(4):
                    po, pok = bO[b4]
                    hs = slice(2 * b4, 2 * b4 + 2)
                    pv = po[:].rearrange("p (a b) -> p a b", a=2)
                    self.tt(t1[:, hs, :], pv[:, :, 0:128], egc[:, t, hs].unsqueeze(2).to_broadcast([128, 2, 128]), ALU.mult,
                            reads=[(pok, 0), (pok, 1), "degc"], writes=[("d2t1", b4)])
                    self.tt(ot[:, hs, :], t1[:, hs, :], pv[:, :, 128:256], ALU.add,
                            reads=[("d2t1", b4), (pok, 0), (pok, 1)], writes=[("d2ot", b4)])
                otk = [("d2ot", b4) for b4 in range(4)]
                self.act(sq[:], ot[:], AF.Square, reads=otk, writes=["d2sq"])
                self.S.op("dve", lambda e: e.tensor_reduce(out=ssq[:, 0, :], in_=sq[:], op=ALU.add,
                                                           axis=mybir.AxisListType.X),
                          reads=["d2sq"], writes=["d2ssq"])
                self.act(ssq[:, 1, :], ssq[:, 0, :], AF.Ln, reads=["d2ssq", "consts"], writes=["d2ssq"], scale=1.0 / 128,
                         bias=EPS)
                self.act(ssq[:, 1, :], ssq[:, 1, :], AF.Exp, reads=["d2ssq"], writes=["d2ssq"], scale=-0.5)
                self.tt(ot[:], ot[:], ssq[:, 1, :].unsqueeze(2).to_broadcast([128, NH, 128]), ALU.mult,
                        reads=otk + ["d2ssq"], writes=otk)
                self.tt(ot[:], ot[:], nwb[:].unsqueeze(1).to_broadcast([128, NH, 128]), ALU.mult, reads=otk + [nwbk],
                        writes=otk)
                ob, obk = obr.next()
                self.tt(ob[:], ot[:], SG[:, :, cs], ALU.mult, reads=otk + [(sk, 5)], writes=[obk])
                px, pxk = bW[0]
                pxb = px[:].bitcast(BF16)
                for h in range(NH):
                    self.tr(pxb[:, h * 128:(h + 1) * 128], ob[:, h, :], self.ident_b(), reads=[obk, "cbf"], writes=[pxk])
                otile, otlk = otl.next()
                self.copy(otile[:], pxb[:].rearrange("p (a b) -> p a b", a=NH), reads=[pxk], writes=[otlk], eng="act")
                self.dma(o_view[:, :, t * 128:(t + 1) * 128], otile[:], reads=[otlk], writes=[("o_scr", 0, "t", t)])

    def branches(self):
        return [i for i, n in enumerate(("dn", "sb", "ssm")) if n in self.mixers]

    def gates_phase(self, l):
        wst = self.ring("gwst", 2, [128, KC, 512], F32)
        wbf = self.ring("gwbf", 2, [128, KC, 512], BF16)
        gb = self.ring("gb", 3, [128, S_LEN], BF16)
        scale = self.nw[:, 2 * l, :]
        blocks = [(i, cb) for i in self.branches() for cb in range(2)]

        def gload(bi):
            i_, cb_ = blocks[bi]
            c0_ = O_GATE + i_ * D + cb_ * 512
            return self.wload(self.w_in_d[l][:, c0_:c0_ + 512], KC, 512, wst, wbf, scale=scale)

        nxt = gload(0)
        for bi_, (i, cb) in enumerate(blocks):
            if True:
                wb, wk = nxt
                if bi_ + 1 < len(blocks):
                    nxt = gload(bi_ + 1)
                cur = {}

                def evac(ci, g, ps, pk, n, cb=cb, i=i, cur=cur):
                    if g == 0:
                        cur["t"] = gb.next()
                    t, tk = cur["t"]
                    self.act(t[:, g * 512:(g + 1) * 512], ps[:], AF.Sigmoid, reads=[pk], writes=[tk])
                    if g == 3:
                        oc = cb * 4 + ci
                        self.dma(self.g_scr[i][oc * 128:(oc + 1) * 128, :], t[:], reads=[tk],
                                 writes=[("g_scr", i, oc)])

                self.linear_T(wb, wk, 512, KC, lambda kc, g: self.hT[:, kc, g * 512:(g + 1) * 512],
                              self.hT_keys, evac)

    def merge_phase(self, l):
        A, S = self.A, self.S
        wst = self.ring("mwst", 1, [128, KC, 512], F32)
        wbf = self.ring("mwbf", 2, [128, KC, 512], BF16)
        mg = A.alloc("mg", [128, KC, 1024], F32)
        mgb = A.alloc("mgb", [128, KC, 1024], BF16)
        ob = self.ring("ob", 1, [128, KC, 1024], BF16)
        gt = self.ring("gt", 3, [128, 1024], BF16)
        self.mtmp = self.ring("mtmp", 3, [128, 512], F32)
        brs = self.branches()
        mseq = []
        for half_ in range(2):
            for i_ in brs:
                for cb_ in range(2):
                    mseq.append(self.w_branch_d[l][i_][:, cb_ * 512:(cb_ + 1) * 512])
            for cb_ in range(2):
                mseq.append(self.w_out_d[l][:, cb_ * 512:(cb_ + 1) * 512])

        def mload(pos):
            if pos >= len(mseq):
                return None
            return self.wload(mseq[pos], KC, 512, wst, wbf)

        mpos = [0]
        mnxt = [mload(0)]
        for half in range(2):
            tsl = slice(half * 1024, (half + 1) * 1024)
            for bi, i in enumerate(brs):
                o, ok = ob.next()
                self.dma(o[:], self.o_scr[i][:, tsl].rearrange("(k p) t -> p k t", p=128),
                         reads=[("o_scr", i, c) for c in range(8)], writes=[ok])
                for cb in range(2):
                    wb, wk = mnxt[0]
                    mnxt[0] = mload(mpos[0] + 1)
                    mpos[0] += 1
                    cur = {}

                    def evac(ci, g, ps, pk, n, cb=cb, i=i, bi=bi, cur=cur, half=half, tsl=tsl):
                        oc = cb * 4 + ci
                        gl = g - 2 * half
                        if gl == 0:
                            cur["g"] = gt.next()
                            t, tk = cur["g"]
                            self.dma(t[:], self.g_scr[i][oc * 128:(oc + 1) * 128, tsl],
                                     reads=[("g_scr", i, oc)], writes=[tk])
                        t, tk = cur["g"]
                        dst = mg[:, oc, gl * 512:(gl + 1) * 512]
                        mk = ("mg", oc, gl)
                        if bi == 0:
                            self.tt(dst, ps[:], t[:, gl * 512:(gl + 1) * 512], ALU.mult,
                                    reads=[pk, tk], writes=[mk])
                        else:
                            tmp, tmk = self.mtmp.next()
                            self.tt(tmp[:], ps[:], t[:, gl * 512:(gl + 1) * 512], ALU.mult,
                                    reads=[pk, tk], writes=[tmk])
                            self.tt(dst, dst, tmp[:], ALU.add, reads=[tmk, mk], writes=[mk], eng="pool")
                        if bi == len(brs) - 1:
                            self.copy(mgb[:, oc, gl * 512:(gl + 1) * 512], dst, reads=[mk],
                                      writes=[("mgb", oc, gl)], eng="act")

                    self.linear_T(wb, wk, 512, KC, lambda kc, g, o=o, half=half: o[:, kc, (g - 2 * half) * 512:(g - 2 * half + 1) * 512],
                                  lambda g, ok=ok: [ok], evac, groups=(2 * half, 2 * half + 1))
            for cb in range(2):
                wb, wk = mnxt[0]
                mnxt[0] = mload(mpos[0] + 1)
                mpos[0] += 1

                def evac2(ci, g, ps, pk, n, cb=cb):
                    oc = cb * 4 + ci
                    sl = slice(g * 512, (g + 1) * 512)
                    self.tt(self.xT[:, oc, sl], self.xT[:, oc, sl], ps[:], ALU.add,
                            reads=[pk, ("xT", oc, g)], writes=[("xT", oc, g)])

                self.linear_T(wb, wk, 512, KC,
                              lambda kc, g, half=half: mgb[:, kc, (g - 2 * half) * 512:(g - 2 * half + 1) * 512],
                              lambda g, half=half: [("mgb", kc, g - 2 * half) for kc in range(KC)], evac2,
                              groups=(2 * half, 2 * half + 1))

    def sb_phase(self, l):
        A, S = self.A, self.S
        R = {}
        R["wst"] = self.ring("sbwst", 1, [128, KC, 384], F32)
        R["wbf"] = self.ring("sbwbf", 2, [128, KC, 384], BF16)
        R["qT"] = self.ring("sbq", 2, [128, 2, S_LEN], BF16)
        for i_, t_ in enumerate(R["qT"].tiles):
            self.memset(t_[:], 0.0, writes=[(R["qT"].name, i_)], eng="pool")
        R["kT"] = self.ring("sbk", 2, [128, S_LEN], BF16)
        R["v"] = self.ring("sbv", 2, [128, 16, 128], BF16)
        R["osb"] = self.ring("sbo", 1, [128, S_LEN], BF16)
        R["e"] = self.ring("sbe", 4, [128, 512], F32)
        R["spb"] = self.ring("sbsp", 4, [128, 512], BF16)
        R["xa"] = self.ring("sbxa", 2, [128, 512], F32)
        R["w"] = self.ring("sbw", 3, [128, 512], BF16)
        R["pa"] = Ring(self.pst[0:4], "psum")
        for hp in range(8):
            self.sb_unit(l, hp, R)

    def sb_unit(self, l, hp, R):
        scale = self.nw[:, 2 * l, :]
        st, sk = R["wst"].next()
        wb, wk = R["wbf"].next()
        for j in range(3):
            base = O_SBQKV + j * 1024 + 128 * hp
            self.dma(st[:, :, j * 128:(j + 1) * 128],
                     self.w_in_d[l][:, base:base + 128].rearrange("(k p) n -> p k n", p=128),
                     reads=[], writes=[(sk, j)])
        for kc in range(KC):
            self.act(wb[:, kc, :], st[:, kc, :], AF.Copy, reads=[(sk, 0), (sk, 1), (sk, 2), "nw"], writes=[wk],
                     scale=scale[:, kc:kc + 1])
        qT, qk = R["qT"].next()
        kT, kk = R["kT"].next()
        v, vk = R["v"].next()

        def evac_qk(ci, g, ps, pk, n):
            if ci == 0:
                self.copy(qT[0:64, 0, g * 512:(g + 1) * 512], ps[0:64, :], reads=[pk, qk], writes=[(qk, g)], eng="act")
                self.copy(qT[64:128, 1, g * 512:(g + 1) * 512], ps[64:128, :], reads=[pk, qk], writes=[(qk, g)], eng="dve")
            else:
                self.copy(kT[:, g * 512:(g + 1) * 512], ps[:], reads=[pk], writes=[(kk, g)],
                          eng=("act" if g % 2 else "dve"))

        self.linear_T(wb, wk, 256, KC, lambda kc, g: self.hT[:, kc, g * 512:(g + 1) * 512], self.hT_keys, evac_qk)
        for tq in range(4):
            ps, pk = self.psum.next()
            for t in range(4):
                tt_ = tq * 4 + t
                for kc in range(KC):
                    self.mm(ps[:, t * 128:(t + 1) * 128], self.hT[:, kc, tt_ * 128:(tt_ + 1) * 128],
                            wb[:, kc, 256:384], kc == 0, kc == KC - 1, reads=[wk, ("hT", kc, tq)], writes=[pk])
            self.copy(v[:, tq * 4:(tq + 1) * 4, :], ps[:].rearrange("p (t c) -> p t c", t=4), reads=[pk],
                      writes=[(vk, tq)], eng=("act" if tq % 2 else "dve"))
        osb, ok = R["osb"].next()
        mstrict = self.consts[:, C_MSTRICT, :]
        tinc = self.cbf[:, C_TINC, :]
        tlow = self.cbf[:, C_MSTRICT, :]
        one_col = self.consts[:, C_ONES, 0:1]
        pst = self.pst
        pa_ring = R["pa"]
        for g in range(4):
            racc = [(pst[4], ("psum", 4)), (pst[5], ("psum", 5))]
            po = [(pst[6], ("psacc", 0)), (pst[7], ("psacc", 1))]
            tiles = []
            for kb in range(4 * g + 3, -1, -1):
                for e in range(2):
                    t0 = max(kb * 128, g * 512)
                    tiles.append(dict(e=e, kb=kb, pb=64 * e, t0=t0, N=(g + 1) * 512 - t0, c0=t0 - g * 512,
                                      diag=kb * 128 >= g * 512, first=(kb == 4 * g + 3), last=(kb == 0)))

            def stage0(T):
                pa, pak = pa_ring.next()
                pb, N, t0, kb = T["pb"], T["N"], T["t0"], T["kb"]
                self.mm(pa[:, :N], kT[:, kb * 128:(kb + 1) * 128], qT[:, T["e"], t0:t0 + N], True, not T["diag"],
                        reads=[(kk, kb // 4), (qk, g)], writes=[pak])
                if T["diag"]:
                    self.mm(pa[:, 0:128], self.ident_b(), self.cbf[:, C_NEGSB, :], False, True, reads=["cbf"], writes=[pak])
                T["pa"], T["pak"] = pa, pak

            def stage1(T):
                N, c0 = T["N"], T["c0"]
                ee, ek = R["e"].next()
                self.act(ee[:, :N], T["pa"][:, :N], AF.Exp, reads=[T["pak"]], writes=[ek], scale=0.125)
                spb, spk = R["spb"].next()
                self.act(spb[:, :N], ee[:, :N], AF.Ln, reads=[ek], writes=[spk], bias=1.0, scale=1.0)
                ra, rak = racc[T["e"]]
                self.mm(ra[:, c0:512], tinc, spb[:, :N], T["first"], False, reads=[spk, "cbf"], writes=[rak],
                        skip_group_check=True)
                T.update(ee=ee, ek=ek, spb=spb, spk=spk)

            def stage2(T):
                N, c0, pb, kb = T["N"], T["c0"], T["pb"], T["kb"]
                ra, rak = racc[T["e"]]
                xa, xk = R["xa"].next()
                self.act(xa[:, :N], ra[:, c0:512], AF.Exp, reads=[rak], writes=[xk], scale=-1.0)
                if not T["last"]:
                    self.mm(ra[:, c0:512], tlow, T["spb"][:, :N], False, True, reads=[T["spk"], "cbf"], writes=[rak],
                            skip_group_check=True)
                w, wwk = R["w"].next()
                self.tt(w[:, :N], T["ee"][:, :N], xa[:, :N], ALU.mult, reads=[T["ek"], xk], writes=[wwk])
                pp, ppk = po[T["e"]]
                self.mm(pp[:, c0:512], v[:, kb, :], w[:, :N], T["first"], T["last"],
                        reads=[(vk, kb // 4), wwk], writes=[ppk], skip_group_check=True)

            n = len(tiles)
            for i in range(min(2, n)):
                stage0(tiles[i])
            for i in range(n + 2):
                if i - 2 >= 0:
                    stage2(tiles[i - 2])
                if i < n:
                    stage1(tiles[i])
                if i + 2 < n:
                    stage0(tiles[i + 2])
            for e in range(2):
                pp, ppk = po[e]
                pb = 64 * e
                self.copy(osb[pb:pb + 64, g * 512:(g + 1) * 512], pp[pb:pb + 64, :], reads=[ppk],
                          writes=[(ok, e, g)], eng="dve")
        self.dma(self.o_scr[1][hp * 128:(hp + 1) * 128, :], osb[:],
                 reads=[(ok, e, g) for e in range(2) for g in range(4)], writes=[("o_scr", 1, hp)])


    def bload(self, name, dram_row_ap, n):
        t = self.A.alloc(name, [128, n], F32)
        k = self.key(name)
        self.dma(t[:], dram_row_ap.partition_broadcast(128), reads=[], writes=[k])
        return t, k

    def conv_silu(self, l, wb, wk, wcol, conv_w_d, conv_b_d, ch, raw, rawk, acc, acck, cw, dst, dstk, func=AF.Silu):
        c, ck = cw.next()
        self.dma(c[:, 0:4], conv_w_d[l][:, ch:ch + 128].rearrange("k c -> c k"), reads=[], writes=[(ck, 0)], slow=True)
        if conv_b_d is not None:
            self.dma(c[:, 4:5], conv_b_d[l][ch:ch + 128].rearrange("(c o) -> c o", o=1), reads=[], writes=[(ck, 1)],
                     slow=True)
        else:
            self.memset(c[:, 4:5], 0.0, writes=[(ck, 1)])

        def evac(ci, g, ps, pk, n):
            self.copy(raw[:, 3 + g * 512:3 + (g + 1) * 512], ps[:], reads=[pk], writes=[(rawk, g)],
                      eng=("act" if g % 2 else "dve"))

        for g in range(4):
            ps, pk = self.psum.next()
            for kc in range(KC):
                self.mm(ps[:], wb[:, kc, wcol:wcol + 128], self.hT[:, kc, g * 512:(g + 1) * 512], kc == 0, kc == KC - 1,
                        reads=[wk] + self.hT_keys(g), writes=[pk])
            evac(0, g, ps, pk, 128)
        rk = [(rawk, g) for g in range(4)] + [(rawk, "pad")]
        self.ts(acc[:], raw[:, 0:S_LEN], c[:, 0:1], ALU.mult, reads=rk + [(ck, 0), (ck, 1)], writes=[acck],
                s2=c[:, 4:5], op1=ALU.add)
        for k in range(1, 4):
            self.stt(acc[:], raw[:, k:k + S_LEN], c[:, k:k + 1], acc[:], ALU.mult, ALU.add,
                     reads=rk + [(ck, 0), acck], writes=[acck])
        self.act(dst, acc[:], func, reads=[acck], writes=[dstk])

    def to_tok(self, src, srck, dst_fn, dstk):
        for tq in range(4):
            ps, pk = self.psum.next()
            pb = ps[:].bitcast(BF16)
            for t in range(4):
                tt_ = tq * 4 + t
                self.tr(pb[:, t * 128:(t + 1) * 128], src[:, tt_ * 128:(tt_ + 1) * 128], self.ident_b(),
                        reads=(list(srck) if isinstance(srck, list) else [srck]) + ["cbf"], writes=[pk])
            self.copy(dst_fn(tq), pb[:, 0:512].rearrange("p (t c) -> p t c", t=4), reads=[pk], writes=[(dstk, tq)],
                      eng=("act" if tq % 2 else "dve"))

    def ssm_phase(self, l):
        A, S = self.A, self.S
        scale = self.nw[:, 2 * l, :]
        cst = self.consts
        wdst = A.alloc("wdtst", [128, KC, 16], F32)
        wdt = A.alloc("wdt", [128, KC, 16], BF16)
        self.dma(wdst[:], self.w_in_d[l][:, O_SSMDT:O_SSMDT + 16].rearrange("(k p) n -> p k n", p=128), [], ["wdtst"])
        for kc in range(KC):
            self.act(wdt[:, kc, :], wdst[:, kc, :], AF.Copy, reads=["wdtst", "nw"], writes=["wdt"], scale=scale[:, kc:kc + 1])
        dtb, dtbk = self.bload("dtb", self.ssm_dt_bias_d[l], 16)
        alog, alogk = self.bload("alog", self.ssm_a_log_d[l], 16)
        dbc, dbck = self.bload("dbc", self.ssm_d_d[l], 16)
        dt = A.alloc("dt", [128, 16, 16], F32)
        av = A.alloc("av", [128, 16, 16], F32)
        acum = A.alloc("acum", [128, 16, 16], F32)
        eacum = A.alloc("eacum", [128, 16, 16], F32)
        dtds = A.alloc("dtds", [128, 16, 16], F32)
        eatot = A.alloc("eatot", [128, 32, 16], F32)
        tmp = A.alloc("ptmp", [128, 16, 16], F32)
        ps, pk = self.psum.next()
        for t in range(16):
            for kc in range(KC):
                self.mm(ps[:, t * 16:(t + 1) * 16], self.hT[:, kc, t * 128:(t + 1) * 128], wdt[:, kc, :], kc == 0,
                        kc == KC - 1, reads=["wdt", ("hT", kc, t // 4)], writes=[pk])
        self.tt(dt[:], ps[:, 0:256].rearrange("p (t h) -> p t h", t=16),
                dtb[:].unsqueeze(1).to_broadcast([128, 16, 16]), ALU.add, reads=[pk, dtbk], writes=["dt"])
        one_col = cst[:, C_ONES, 0:1]
        self.act(tmp[:], dt[:], AF.Exp, reads=["dt"], writes=["ptmp"])
        self.act(dt[:], tmp[:], AF.Ln, reads=["ptmp", "consts"], writes=["dt"], bias=1.0, scale=1.0)
        self.act(alog[:], alog[:], AF.Exp, reads=[alogk], writes=[alogk])
        self.S.op("dve", lambda e: e.scalar_tensor_tensor(out=av[:], in0=dt[:], scalar=-1.0,
                                                          in1=alog[:].unsqueeze(1).to_broadcast([128, 16, 16]),
                                                          op0=ALU.mult, op1=ALU.mult),
                  reads=["dt", alogk], writes=["av"])
        ps, pk = self.psum.next()
        for t in range(16):
            self.mm(ps[:, t * 16:(t + 1) * 16], cst[:, C_TRI2, :], av[:, t, :], True, True, reads=["av", "consts"], writes=[pk])
        self.copy(acum[:], ps[:, 0:256].rearrange("p (t h) -> p t h", t=16), reads=[pk], writes=["acum"])
        self.act(eacum[:], acum[:], AF.Exp, reads=["acum"], writes=["eacum"])
        ps, pk = self.psum.next()
        for t in range(16):
            self.mm(ps[:, t * 16:(t + 1) * 16], cst[:, C_BD, :], av[:, t, :], True, True, reads=["av", "consts"], writes=[pk])
        self.tt(tmp[:], ps[:, 0:256].rearrange("p (t h) -> p t h", t=16), acum[:], ALU.subtract, reads=[pk, "acum"],
                writes=["ptmp"])
        self.act(tmp[:], tmp[:], AF.Exp, reads=["ptmp"], writes=["ptmp"])
        self.tt(dtds[:], tmp[:], dt[:], ALU.mult, reads=["ptmp", "dt"], writes=["dtds"])
        ps, pk = self.psum.next()
        for t in range(16):
            for c in range(2):
                j = 2 * t + c
                self.mm(ps[:, j * 16:(j + 1) * 16], cst[:, C_IND0 + c, :], av[:, t, :], True, True,
                        reads=["av", "consts"], writes=[pk])
        self.act(eatot[:].rearrange("p j h -> p (j h)"), ps[:], AF.Exp, reads=[pk], writes=["eatot"])

        mG = A.mark()
        for g in range(4):
            A.reset(mG)
            S.barrier()
            wbf = A.alloc("swz", [128, KC, 256], BF16)
            BT = A.alloc("sBT", [128, S_LEN], BF16)
            CT = A.alloc("sCT", [128, S_LEN], BF16)
            x_tok = A.alloc("sxtok", [128, 16, 256], BF16)
            B_tok = A.alloc("sBtok", [128, 16, 128], BF16)
            nwb, nwbk = self.bload("snwb", self.ssm_norm_w_d[l][256 * g:256 * (g + 1)], 256)
            mA = A.mark()
            wst = self.ring("swst", 2, [128, KC, 128], F32)
            wbx = A.alloc("swbx", [128, KC, 512], BF16)
            raw = A.alloc("sraw", [128, S_LEN + 4], F32)
            acc = A.alloc("sacc", [128, S_LEN], F32)
            xc = self.ring("sxc", 2, [128, S_LEN], BF16)
            cw = self.ring("scw", 2, [128, 8], F32)
            rawk = self.key("sraw")
            self.memset(raw[:, 0:3], 0.0, writes=[(rawk, "pad")])
            cols = [O_SSMZ + 256 * g, O_SSMZ + 256 * g + 128, O_SSMXBC + 256 * g, O_SSMXBC + 256 * g + 128,
                    O_SSMXBC + 1024 + 128 * g, O_SSMXBC + 1536 + 128 * g]
            wk = self.key("swbf")
            for j, c0 in enumerate(cols):
                st, sk = wst.next()
                self.dma(st[:], self.w_in_d[l][:, c0:c0 + 128].rearrange("(k p) n -> p k n", p=128), [], [sk])
                for kc in range(KC):
                    wdst_ = wbf[:, kc, j * 128:(j + 1) * 128] if j < 2 else wbx[:, kc, (j - 2) * 128:(j - 1) * 128]
                    self.act(wdst_, st[:, kc, :], AF.Copy, reads=[sk, "nw"], writes=[(wk, j)],
                             scale=scale[:, kc:kc + 1])
            chs = [256 * g, 256 * g + 128, 1024 + 128 * g, 1536 + 128 * g]
            acck = self.key("sacc")
            xtk = self.key("sxtok")
            btk = self.key("sBtok")
            for jj, ch in enumerate(chs):
                j = jj + 2
                if jj < 2:
                    dst, dk = xc.next()
                elif jj == 2:
                    dst, dk = BT, "sBT"
                else:
                    dst, dk = CT, "sCT"
                self.conv_silu(l, wbx, (wk, j), jj * 128, self.ssm_conv_w_d, self.ssm_conv_b_d, ch, raw, rawk, acc, acck,
                               cw, dst[:], dk)
                if jj < 2:
                    self.to_tok(dst, dk, lambda tq, jj=jj: x_tok[:, tq * 4:(tq + 1) * 4, jj * 128:(jj + 1) * 128], (xtk, jj))
                elif jj == 2:
                    self.to_tok(dst, dk, lambda tq: B_tok[:, tq * 4:(tq + 1) * 4, :], btk)
            S.barrier()
            A.reset(mA)
            xdt = A.alloc("sxdt", [128, 16, 256], BF16)
            xdtd = A.alloc("sxdtd", [128, 16, 256], BF16)
            oT = A.alloc("soT", [128, 2, S_LEN], BF16)
            state = A.alloc("sstate", [128, 256], F32)
            state_bf = A.alloc("sstatebf", [128, 256], BF16)
            abc = self.ring("sabc", 2, [128, 128], F32)
            dm = self.ring("sdm", 2, [128, 4, 128], F32)
            MT = self.ring("sMT", 2, [128, 4, 128], BF16)
            xDr = self.ring("sxD", 2, [128, 256], BF16)
            szr = self.ring("ssz", 2, [128, 256], F32)
            ydr = self.ring("syd", 2, [128, 256], F32)
            sttr = self.ring("sstt", 4, [128, 256], F32)
            t1r = self.ring("st1", 1, [128, 256], F32)
            yr = self.ring("sy", 2, [128, 256], F32)
            jr = self.ring("sjunk", 1, [128, 256], F32)
            ssr = self.ring("sssq", 4, [128, 2], F32)
            obr = self.ring("sob", 2, [128, 256], BF16)
            xtks = [(xtk, jj, tq) for jj in range(2) for tq in range(4)]
            x4 = x_tok[:].rearrange("p t (h c) -> p t h c", h=4)
            hs = slice(4 * g, 4 * g + 4)
            self.tt(xdt[:].rearrange("p t (h c) -> p t h c", h=4), x4,
                    dt[:, :, hs].unsqueeze(3).to_broadcast([128, 16, 4, 64]), ALU.mult, reads=xtks + ["dt"], writes=["sxdt"])
            self.tt(xdtd[:].rearrange("p t (h c) -> p t h c", h=4), x4,
                    dtds[:, :, hs].unsqueeze(3).to_broadcast([128, 16, 4, 64]), ALU.mult, reads=xtks + ["dtds"],
                    writes=["sxdtd"])
            stk = self.key("sstate")
            sbk = self.key("sstatebf")
            self.memset(state[:], 0.0, writes=[stk])
            self.memset(state_bf[:], 0.0, writes=[sbk])
            ones_f = cst[:, C_ONES, :]

            def stageP(t, I):
                tsl = slice(t * 128, (t + 1) * 128)
                xD, xDk = xDr.next()
                self.tt(xD[:].rearrange("p (h c) -> p h c", h=4), x_tok[:, t, :].rearrange("p (h c) -> p h c", h=4),
                        dbc[:, hs].unsqueeze(2).to_broadcast([128, 4, 64]), ALU.mult, reads=xtks + [dbck],
                        writes=[xDk], eng="pool")
                yield
                psS, psSk = self.psum.next()
                self.mm(psS[:, 0:128], BT[:, tsl], CT[:, tsl], True, True, reads=["sBT", "sCT"], writes=[psSk])
                yield
                psD, psDk = self.psum.next()
                for hh in range(4):
                    ab, abk = abc.next()
                    self.ts(ab[:], ones_f, av[:, t, 4 * g + hh:4 * g + hh + 1], ALU.mult, reads=["av", "consts"], writes=[abk])
                    yield
                    self.mm(psD[:, hh * 128:(hh + 1) * 128], ab[:], cst[:, C_TRI2, :], True, False, reads=[abk, "consts"],
                            writes=[psDk])
                    yield
                    self.mm(psD[:, hh * 128:(hh + 1) * 128], cst[:, C_TRI2NEG, :], ab[:], False, True,
                            reads=[abk, "consts"], writes=[psDk])
                    yield
                d_, dk_ = dm.next()
                self.tt(d_[:], psD[:].rearrange("p (h c) -> p h c", h=4),
                        cst[:, C_NEGINCL, :].unsqueeze(1).to_broadcast([128, 4, 128]), ALU.add, reads=[psDk, "consts"],
                        writes=[dk_])
                yield
                self.act(d_[:], d_[:], AF.Exp, reads=[dk_], writes=[dk_])
                yield
                M_, Mk_ = MT.next()
                self.tt(M_[:], d_[:], psS[:, 0:128].unsqueeze(1).to_broadcast([128, 4, 128]), ALU.mult,
                        reads=[dk_, psSk], writes=[Mk_])
                yield
                psY, psYk = self.psum.next()
                for hh in range(4):
                    cs = slice(hh * 64, (hh + 1) * 64)
                    self.mm(psY[:, cs], M_[:, hh, :], xdt[:, t, cs], True, False, reads=[Mk_, "sxdt"], writes=[psYk])
                    yield
                    self.mm(psY[:, cs], self.ident_b(), xD[:, cs], False, True, reads=["cbf", xDk], writes=[psYk])
                    yield
                yd, ydk = ydr.next()
                self.copy(yd[:], psY[:, 0:256], reads=[psYk], writes=[ydk], eng="act")
                yield
                psZ, psZk = self.psum.next()
                for kc in range(KC):
                    self.mm(psZ[:, 0:256], self.hT[:, kc, tsl], wbf[:, kc, 0:256], kc == 0, kc == KC - 1,
                            reads=[(wk, 0), (wk, 1), ("hT", kc, t // 4)], writes=[psZk])
                    yield
                sz, szk = szr.next()
                self.act(sz[:], psZ[:, 0:256], AF.Silu, reads=[psZk], writes=[szk])
                yield
                I["stt"] = []
                for c in range(2):
                    psT, psTk = self.psum.next()
                    self.mm(psT[:, 0:256], B_tok[64 * c:64 * c + 64, t, :], xdtd[64 * c:64 * c + 64, t, :], True, True,
                            reads=[(btk, t // 4), "sxdtd"], writes=[psTk])
                    yield
                    sx, sxk = sttr.next()
                    self.copy(sx[:], psT[:, 0:256], reads=[psTk], writes=[sxk], eng=("act" if c else "dve"))
                    yield
                    I["stt"].append((sx, sxk))
                I.update(yd=yd, ydk=ydk, sz=sz, szk=szk)
                yield

            def stageQ(t, I):
                tsl = slice(t * 128, (t + 1) * 128)
                psO, psOk = self.psum_acc.next()
                for c in range(2):
                    j = 2 * t + c
                    sx, sxk = I["stt"][c]
                    self.mm(psO[64 * c:64 * c + 64, 0:256], CT[:, t * 128 + 64 * c:t * 128 + 64 * c + 64], state_bf[:], True, True,
                            reads=["sCT", sbk], writes=[psOk])
                    yield
                    self.tt(state[:].rearrange("p (h c) -> p h c", h=4), state[:].rearrange("p (h c) -> p h c", h=4),
                            eatot[:, j, hs].unsqueeze(2).to_broadcast([128, 4, 64]), ALU.mult, reads=[stk, "eatot"],
                            writes=[stk])
                    yield
                    self.tt(state[:], state[:], sx[:], ALU.add, reads=[stk, sxk], writes=[stk])
                    yield
                    self.copy(state_bf[:], state[:], reads=[stk], writes=[sbk], eng="act")
                    yield
                t1, t1k = t1r.next()
                self.tt(t1[:].rearrange("p (h c) -> p h c", h=4), psO[:, 0:256].rearrange("p (h c) -> p h c", h=4),
                        eacum[:, t, hs].unsqueeze(2).to_broadcast([128, 4, 64]), ALU.mult, reads=[psOk, "eacum"],
                        writes=[t1k])
                yield
                y, yk = yr.next()
                self.tt(y[:], t1[:], I["yd"][:], ALU.add, reads=[t1k, I["ydk"]], writes=[yk])
                yield
                self.tt(y[:], y[:], I["sz"][:], ALU.mult, reads=[yk, I["szk"]], writes=[yk])
                yield
                jk_, jkk = jr.next()
                ss, ssk = ssr.next()
                self.S.op("act", lambda e, jk_=jk_, y=y, ss=ss: e.activation(out=jk_[:], in_=y[:], func=AF.Square,
                                                                          accum_out=ss[:, 0:1]),
                          reads=[yk], writes=[jkk, ssk])
                yield
                self.act(ss[:, 1:2], ss[:, 0:1], AF.Ln, reads=[ssk, "consts"], writes=[ssk], scale=1.0 / 256,
                         bias=EPS)
                yield
                self.act(ss[:, 1:2], ss[:, 1:2], AF.Exp, reads=[ssk], writes=[ssk], scale=-0.5)
                yield
                ob, obk = obr.next()
                self.stt(ob[:], y[:], ss[:, 1:2], nwb[:], ALU.mult, ALU.mult, reads=[yk, ssk, nwbk], writes=[obk])
                yield
                psX, psXk = self.psum.next()
                pxb = psX[:].bitcast(BF16)
                for ch in range(2):
                    self.tr(pxb[:, ch * 128:(ch + 1) * 128], ob[:, ch * 128:(ch + 1) * 128], self.ident_b(),
                            reads=[obk, "cbf"], writes=[psXk])
                    yield
                self.copy(oT[:, :, tsl], pxb[:, 0:256].rearrange("p (c t) -> p c t", c=2), reads=[psXk],
                          writes=[("soT", t)], eng="act")
                yield

            def run2(ga, gb):
                gens = [g_ for g_ in (ga, gb) if g_ is not None]
                while gens:
                    for g_ in list(gens):
                        try:
                            next(g_)
                        except StopIteration:
                            gens.remove(g_)

            infos = {0: {}}
            run2(stageP(0, infos[0]), None)
            for t in range(16):
                gp = None
                if t + 1 < 16:
                    infos[t + 1] = {}
                    gp = stageP(t + 1, infos[t + 1])
                run2(gp, stageQ(t, infos.pop(t)))
            for ch in range(2):
                oc = 2 * g + ch
                self.dma(self.o_scr[2][oc * 128:(oc + 1) * 128, :], oT[:, ch, :], reads=[("soT", t) for t in range(16)],
                         writes=[("o_scr", 2, oc)])


    def dn_phase(self, l):
        A, S = self.A, self.S
        scale = self.nw[:, 2 * l, :]
        cst = self.consts
        one_col = cst[:, C_ONES, 0:1]
        ones_f = cst[:, C_ONES, :]
        wast = A.alloc("dwast", [128, KC, 16], F32)
        wa = A.alloc("dwa", [128, KC, 16], BF16)
        self.dma(wast[:], self.w_in_d[l][:, O_DNA:O_DNA + 16].rearrange("(k p) n -> p k n", p=128), [], ["dwast"])
        for kc in range(KC):
            self.act(wa[:, kc, :], wast[:, kc, :], AF.Copy, reads=["dwast", "nw"], writes=["dwa"], scale=scale[:, kc:kc + 1])
        dtb, dtbk = self.bload("ddtb", self.dn_dt_bias_d[l], 8)
        alog, alogk = self.bload("dalog", self.dn_a_log_d[l], 8)
        gv = A.alloc("dg", [128, 16, 8], F32)
        beta = A.alloc("dbeta", [128, 16, 8], F32)
        gc = A.alloc("dgc", [128, 16, 8], F32)
        egc = self.dn_egc
        bg = A.alloc("dbg", [128, 16, 8], F32)
        kdec = A.alloc("dkdec", [128, 16, 8], F32)
        eglast = self.dn_eglast
        tmp = A.alloc("dtmp", [128, 16, 8], F32)
        ps, pk = self.psum.next()
        for t in range(16):
            for kc in range(KC):
                self.mm(ps[:, t * 16:(t + 1) * 16], self.hT[:, kc, t * 128:(t + 1) * 128], wa[:, kc, :], kc == 0,
                        kc == KC - 1, reads=["dwa", ("hT", kc, t // 4)], writes=[pk])
        pv = ps[:, 0:256].rearrange("p (t h) -> p t h", t=16)
        self.act(beta[:], pv[:, :, 8:16], AF.Exp, reads=[pk], writes=["dbeta"], scale=-1.0)
        self.ts(beta[:], beta[:], 1.0, ALU.add, reads=["dbeta"], writes=["dbeta"])
        self.S.op("dve", lambda e: e.reciprocal(out=beta[:], in_=beta[:]), reads=["dbeta"], writes=["dbeta"])
        self.tt(gv[:], pv[:, :, 0:8], dtb[:].unsqueeze(1).to_broadcast([128, 16, 8]), ALU.add, reads=[pk, dtbk], writes=["dg"])
        self.act(tmp[:], gv[:], AF.Exp, reads=["dg"], writes=["dtmp"])
        self.act(gv[:], tmp[:], AF.Ln, reads=["dtmp", "consts"], writes=["dg"], bias=1.0, scale=1.0)
        self.act(alog[:], alog[:], AF.Exp, reads=[alogk], writes=[alogk])
        self.S.op("dve", lambda e: e.scalar_tensor_tensor(out=gv[:], in0=gv[:], scalar=-1.0,
                                                          in1=alog[:].unsqueeze(1).to_broadcast([128, 16, 8]),
                                                          op0=ALU.mult, op1=ALU.mult),
                  reads=["dg", alogk], writes=["dg"])
        ps, pk = self.psum.next()
        for t in range(16):
            self.mm(ps[:, t * 8:(t + 1) * 8], cst[:, C_TRI2, :], gv[:, t, :], True, True, reads=["dg", "consts"], writes=[pk])
        self.copy(gc[:], ps[:, 0:128].rearrange("p (t h) -> p t h", t=16), reads=[pk], writes=["dgc"])
        self.act(egc[:], gc[:], AF.Exp, reads=["dgc"], writes=["degc"])
        self.tt(bg[:], egc[:], beta[:], ALU.mult, reads=["degc", "dbeta"], writes=["dbg"])
        ps, pk = self.psum.next()
        for t in range(16):
            self.mm(ps[:, t * 8:(t + 1) * 8], cst[:, C_BD, :], gv[:, t, :], True, True, reads=["dg", "consts"], writes=[pk])
        self.tt(kdec[:], ps[:, 0:128].rearrange("p (t h) -> p t h", t=16), gc[:], ALU.subtract, reads=[pk, "dgc"],
                writes=["dkdec"])
        self.act(kdec[:], kdec[:], AF.Exp, reads=["dkdec"], writes=["dkdec"])
        ps, pk = self.psum.next()
        for t in range(16):
            for c in range(2):
                j = 2 * t + c
                self.mm(ps[:, j * 8:(j + 1) * 8], cst[:, C_IND0 + c, :], gv[:, t, :], True, True,
                        reads=["dg", "consts"], writes=[pk])
        self.act(eglast[:].rearrange("p j h -> p (j h)"), ps[:, 0:256], AF.Exp, reads=[pk], writes=["deglast"])

        mG = A.mark()
        for h in range(8):
            A.reset(mG)
            S.barrier()
            wg = A.alloc("dwg", [128, KC, 128], BF16)
            qTn = A.alloc("dqTn", [128, S_LEN], BF16)
            kTn = A.alloc("dkTn", [128, S_LEN], BF16)
            k_tok = A.alloc("dktok", [128, 16, 128], BF16)
            v_tok = A.alloc("dvtok", [128, 16, 128], BF16)
            mA = A.mark()
            wst = self.ring("dwst", 2, [128, KC, 128], F32)
            wbx = A.alloc("dwbx", [128, KC, 384], BF16)
            raws = [A.alloc("draw", [128, S_LEN + 4], F32) for _ in range(2)]
            accs = [A.alloc("dacc", [128, S_LEN], F32) for _ in range(2)]
            vT = A.alloc("dvT", [128, S_LEN], BF16)
            cw = self.ring("dcw", 2, [128, 8], F32)
            sqr = self.ring("dsq", 2, [128, 512], BF16)
            rsr = self.ring("drs", 2, [128, 512], F32)
            rawks = [self.key("draw"), self.key("draw")]
            for r_, rk__ in zip(raws, rawks):
                self.memset(r_[:, 0:3], 0.0, writes=[(rk__, "pad")])
            cols = [O_DNQKV + 128 * h, O_DNQKV + 1024 + 128 * h, O_DNQKV + 2048 + 128 * h, O_DNGATE + 128 * h]
            wk = self.key("dwb")
            for j, c0 in enumerate(cols):
                st, sk = wst.next()
                self.dma(st[:], self.w_in_d[l][:, c0:c0 + 128].rearrange("(k p) n -> p k n", p=128), [], [sk])
                for kc in range(KC):
                    wd_ = wbx[:, kc, j * 128:(j + 1) * 128] if j < 3 else wg[:, kc, :]
                    self.act(wd_, st[:, kc, :], AF.Copy, reads=[sk, "nw"], writes=[(wk, j)], scale=scale[:, kc:kc + 1])
            accks = [self.key("dacc"), self.key("dacc")]
            ktk = self.key("dktok")
            vtk = self.key("dvtok")
            for j in range(3):
                ch = j * 1024 + 128 * h
                raw, rawk, acc, acck = raws[j % 2], rawks[j % 2], accs[j % 2], accks[j % 2]
                if j == 2:
                    self.conv_silu(l, wbx, (wk, j), j * 128, self.dn_conv_w_d, None, ch, raw, rawk, acc, acck, cw, vT[:], "dvT")
                    self.to_tok(vT, "dvT", lambda tq: v_tok[:, tq * 4:(tq + 1) * 4, :], vtk)
                    continue
                self.conv_silu(l, wbx, (wk, j), j * 128, self.dn_conv_w_d, None, ch, raw, rawk, acc, acck, cw, acc[:], acck)
                dst, dk = (qTn, "dqTn") if j == 0 else (kTn, "dkTn")
                for g in range(4):
                    sl = slice(g * 512, (g + 1) * 512)
                    q_, qk_ = sqr.next()
                    self.act(q_[:], acc[:, sl], AF.Square, reads=[acck], writes=[qk_])
                    ps, pk = self.psum.next()
                    self.mm(ps[:], self.ones_b(), q_[:], True, True, reads=[qk_, "cbf"], writes=[pk])
                    r_, rk_ = rsr.next()
                    self.act(r_[:], ps[:], AF.Ln, reads=[pk, "consts"], writes=[rk_], scale=1.0, bias=EPS)
                    self.act(r_[:], r_[:], AF.Exp, reads=[rk_], writes=[rk_], scale=-0.5)
                    if j == 0:
                        self.stt(dst[:, sl], acc[:, sl], 128.0 ** -0.5, r_[:], ALU.mult, ALU.mult, reads=[acck, rk_],
                                 writes=[(dk, g)])
                    else:
                        self.tt(dst[:, sl], acc[:, sl], r_[:], ALU.mult, reads=[acck, rk_], writes=[(dk, g)])
                if j == 1:
                    self.to_tok(kTn, [("dkTn", g) for g in range(4)], lambda tq: k_tok[:, tq * 4:(tq + 1) * 4, :], ktk)
            S.barrier()
            A.reset(mA)
            attnT = A.alloc("dattnT", [128, 16, 128], BF16)
            P = A.alloc("dP", [128, 16, 128], BF16)
            PL = A.alloc("dPL", [128, 16, 128], BF16)
            R32 = A.alloc("dR32", [128, 16, 128], F32)
            Rb = A.alloc("dRb", [128, 16, 128], BF16)
            vb = v_tok
            kbg = k_tok
            kd = A.alloc("dkd", [128, 16, 128], BF16)
            u = A.alloc("du", [128, 16, 128], BF16)
            wT = A.alloc("dwT", [128, 16, 128], BF16)
            sgt = A.alloc("dsgt", [128, 16, 128], BF16)
            abr = self.ring("dab", 4, [128, 128], F32)
            dcr = self.ring("ddec", 1, [128, 4, 128], F32)
            tmr = self.ring("dtm", 1, [128, 4, 128], F32)
            bmr = self.ring("dbm", 1, [128, 4, 128], F32)
            ktks = [(ktk, tq) for tq in range(4)]
            vtks = [(vtk, tq) for tq in range(4)]
            bc3 = lambda t_: t_[:, :, h:h + 1].to_broadcast([128, 16, 128])
            self.tt(vb[:], v_tok[:], bc3(beta), ALU.mult, reads=vtks + ["dbeta"], writes=vtks + ["dvb"])
            self.tt(kd[:], k_tok[:], bc3(kdec), ALU.mult, reads=ktks + ["dkdec"], writes=["dkd"])
            self.tt(kbg[:], k_tok[:], bc3(bg), ALU.mult, reads=ktks + ["dbg", "dkd"], writes=ktks + ["dkbg"])
            kTk = [("dkTn", g) for g in range(4)]
            identf4 = cst[:, C_IDENT, :].unsqueeze(1).to_broadcast([128, 4, 128])
            for q in range(4):
                psK, psKk = self.psum.next()
                psQ, psQk = self.psum.next()
                psD, psDk = self.psum.next()
                psB, psBk = self.psum.next()
                for i4 in range(4):
                    t = 4 * q + i4
                    tsl = slice(t * 128, (t + 1) * 128)
                    cs = slice(i4 * 128, (i4 + 1) * 128)
                    self.mm(psK[:, cs], kTn[:, tsl], kTn[:, tsl], True, True, reads=[("dkTn", q)], writes=[psKk])
                    self.mm(psQ[:, cs], kTn[:, tsl], qTn[:, tsl], True, True, reads=[("dkTn", q), ("dqTn", q)], writes=[psQk])
                    ab, abk = abr.next()
                    self.ts(ab[:], ones_f, gv[:, t, h:h + 1], ALU.mult, reads=["dg", "consts"], writes=[abk])
                    self.mm(psD[:, cs], ab[:], cst[:, C_TRI2, :], True, False, reads=[abk, "consts"], writes=[psDk])
                    self.mm(psD[:, cs], cst[:, C_TRI2NEG, :], ab[:], False, True, reads=[abk, "consts"], writes=[psDk])
                    db, dbk = abr.next()
                    self.ts(db[:], cst[:, C_IDENT, :], beta[:, t, h:h + 1], ALU.mult, reads=["dbeta", "consts"], writes=[dbk])
                    self.mm(psB[:, cs], ones_f, db[:], True, True, reads=[dbk, "consts"], writes=[psBk])
                dc, dck = dcr.next()
                self.tt(dc[:], psD[:].rearrange("p (a b) -> p a b", a=4),
                        cst[:, C_NEGINCL, :].unsqueeze(1).to_broadcast([128, 4, 128]), ALU.add, reads=[psDk, "consts"],
                        writes=[dck])
                self.act(dc[:], dc[:], AF.Exp, reads=[dck], writes=[dck])
                self.tt(attnT[:, 4 * q:4 * q + 4, :], psQ[:].rearrange("p (a b) -> p a b", a=4), dc[:], ALU.mult,
                        reads=[psQk, dck], writes=[("dattnT", q)])
                bm, bmk = bmr.next()
                self.tt(bm[:], psB[:].rearrange("p (a b) -> p a b", a=4),
                        cst[:, C_MSTRICT2, :].unsqueeze(1).to_broadcast([128, 4, 128]), ALU.mult, reads=[psBk, "consts"],
                        writes=[bmk])
                tm, tmk = tmr.next()
                self.tt(tm[:], psK[:].rearrange("p (a b) -> p a b", a=4), dc[:], ALU.mult, reads=[psKk, dck], writes=[tmk])
                self.tt(P[:, 4 * q:4 * q + 4, :], tm[:], bm[:], ALU.mult, reads=[tmk, bmk], writes=[("dP", q)])
                psT, psTk = self.psum.next()
                ptb = psT[:].bitcast(BF16)
                for i4 in range(4):
                    self.tr(ptb[:, i4 * 128:(i4 + 1) * 128], P[:, 4 * q + i4, :], self.ident_b(), reads=[("dP", q), "cbf"],
                            writes=[psTk])
                self.copy(PL[:, 4 * q:4 * q + 4, :], ptb[:, 0:512].rearrange("p (a b) -> p a b", a=4), reads=[psTk],
                          writes=[("dPL", q)], eng="act")
                self.tt(R32[:, 4 * q:4 * q + 4, :], identf4, P[:, 4 * q:4 * q + 4, :], ALU.subtract,
                        reads=[("dP", q), "consts"], writes=[("dR32", q)])
                self.copy(Rb[:, 4 * q:4 * q + 4, :], R32[:, 4 * q:4 * q + 4, :], reads=[("dR32", q)], writes=[("dRb", q)],
                          eng="act")
            for lev in range(5):
                last = lev == 4
                for q in range(4):
                    if not last:
                        psP, psPk = self.psum.next()
                    psL, psLk = self.psum.next()
                    for i4 in range(4):
                        t = 4 * q + i4
                        cs = slice(i4 * 128, (i4 + 1) * 128)
                        if not last:
                            self.mm(psP[:, cs], PL[:, t, :], P[:, t, :], True, True, reads=[("dPL", q), ("dP", q)],
                                    writes=[psPk])
                        self.mm(psL[:, cs], P[:, t, :], PL[:, t, :], True, True, reads=[("dPL", q), ("dP", q)],
                                writes=[psLk])
                    if not last:
                        self.copy(P[:, 4 * q:4 * q + 4, :], psP[:].rearrange("p (a b) -> p a b", a=4), reads=[psPk],
                                  writes=[("dP", q)], eng="dve")
                    self.copy(PL[:, 4 * q:4 * q + 4, :], psL[:].rearrange("p (a b) -> p a b", a=4), reads=[psLk],
                              writes=[("dPL", q)], eng="act")
                for q in range(4):
                    psR, psRk = self.psum.next()
                    for i4 in range(4):
                        t = 4 * q + i4
                        cs = slice(i4 * 128, (i4 + 1) * 128)
                        self.mm(psR[:, cs], PL[:, t, :], Rb[:, t, :], True, True, reads=[("dPL", q), ("dRb", q)],
                                writes=[psRk])
                    self.tt(R32[:, 4 * q:4 * q + 4, :], R32[:, 4 * q:4 * q + 4, :],
                            psR[:].rearrange("p (a b) -> p a b", a=4), ALU.add, reads=[psRk, ("dR32", q)],
                            writes=[("dR32", q)])
                    self.copy(Rb[:, 4 * q:4 * q + 4, :], R32[:, 4 * q:4 * q + 4, :], reads=[("dR32", q)],
                              writes=[("dRb", q)], eng="act")
            for q in range(4):
                psU, psUk = self.psum.next()
                psW, psWk = self.psum.next()
                for i4 in range(4):
                    t = 4 * q + i4
                    cs = slice(i4 * 128, (i4 + 1) * 128)
                    self.mm(psU[:, cs], Rb[:, t, :], vb[:, t, :], True, True, reads=[("dRb", q), "dvb"], writes=[psUk])
                    self.mm(psW[:, cs], kbg[:, t, :], Rb[:, t, :], True, True, reads=[("dRb", q), "dkbg"], writes=[psWk])
                self.copy(u[:, 4 * q:4 * q + 4, :], psU[:].rearrange("p (a b) -> p a b", a=4), reads=[psUk],
                          writes=[("du", q)], eng="dve")
                self.copy(wT[:, 4 * q:4 * q + 4, :], psW[:].rearrange("p (a b) -> p a b", a=4), reads=[psWk],
                          writes=[("dwT", q)], eng="act")
            for q in range(4):
                psG, psGk = self.psum.next()
                for i4 in range(4):
                    t = 4 * q + i4
                    for kc in range(KC):
                        self.mm(psG[:, i4 * 128:(i4 + 1) * 128], self.hT[:, kc, t * 128:(t + 1) * 128], wg[:, kc, :],
                                kc == 0, kc == KC - 1, reads=[(wk, 3), ("hT", kc, q)], writes=[psGk])
                self.act(sgt[:, 4 * q:4 * q + 4, :], psG[:].rearrange("p (a b) -> p a b", a=4), AF.Silu, reads=[psGk],
                         writes=[("dsgt", q)])
            fl = lambda t_: t_[:].rearrange("p a b -> p (a b)")
            for nm, src, keys in (("u", fl(u), [("du", q) for q in range(4)]),
                                  ("wT", fl(wT), [("dwT", q) for q in range(4)]),
                                  ("q", qTn[:], [("dqTn", g) for g in range(4)]),
                                  ("attnT", fl(attnT), [("dattnT", q) for q in range(4)]),
                                  ("kd", fl(kd), ["dkd"]),
                                  ("sg", fl(sgt), [("dsgt", q) for q in range(4)])):
                self.dma(self.dn_scr[nm][h], src, reads=keys, writes=[("dn_scr", nm, h)])

    def final_out(self):
        A, S = self.A, self.S
        m = A.mark()
        sq = self.ring("fsq", 3, [128, 512], BF16)
        rs = self.ring("frs", 2, [128, 512], F32)
        hf = self.ring("hf", 3, [128, 512], F32)
        ost = self.ring("ost", 2, [128, 4, D], F32)
        for g in range(4):
            ps, pk = self.psum.next()
            sl = slice(g * 512, (g + 1) * 512)
            for kc in range(KC):
                q, qk = sq.next()
                self.act(q[:], self.xT[:, kc, sl], AF.Square, reads=[("xT", kc, g)], writes=[qk])
                self.mm(ps[:], self.ones_b(), q[:], kc == 0, kc == KC - 1, reads=[qk, "cbf"], writes=[pk])
            r, rk = rs.next()
            self.act(r[:], ps[:], AF.Ln, reads=[pk], writes=[rk], scale=1.0 / D, bias=EPS)
            self.act(r[:], r[:], AF.Exp, reads=[rk], writes=[rk], scale=-0.5)
            o, ok = ost.next()
            for kc in range(KC):
                h, hk = hf.next()
                self.stt(h[:], self.xT[:, kc, sl], self.nw[:, 2 * DEPTH, kc:kc + 1], r[:], ALU.mult, ALU.mult,
                         reads=[("xT", kc, g), rk, "nw"], writes=[hk])
                ps2, pk2 = self.psum.next()
                for t in range(4):
                    self.tr(ps2[:, t * 128:(t + 1) * 128], h[:, t * 128:(t + 1) * 128], self.ident_f(),
                            reads=[hk, "consts"], writes=[pk2])
                self.copy(o[:, :, kc * 128:(kc + 1) * 128], ps2[:].rearrange("p (t c) -> p t c", t=4),
                          reads=[pk2], writes=[ok], eng=("act" if kc % 2 else "dve"))
            self.dma(self.out_d[g * 512:(g + 1) * 512, :].rearrange("(t p) d -> p t d", p=128), o[:],
                     reads=[ok], writes=[("out", g)])
        S.barrier()
        A.reset(m)


C_IDENT, C_ONES, C_EPS, C_MSTRICT, C_TINC = 0, 1, 2, 3, 4
C_TRI2, C_TRI2NEG, C_BD, C_IND0, C_IND1, C_NEGINCL, C_NEGSB, C_MINCL2, C_MSTRICT2, C_MSTRICT2T = 5, 6, 7, 8, 9, 10, 11, 12, 13, 14
NCONST = 15
NEG = -30000.0


def make_consts():
    c = np.zeros((128, NCONST, 128), np.float32)
    c[:, C_IDENT, :] = np.eye(128, dtype=np.float32)
    c[:, C_ONES, :] = 1.0
    c[:, C_EPS, :] = EPS
    ii = np.arange(128)
    c[:, C_MSTRICT, :] = (ii[:, None] < ii[None, :]).astype(np.float32)
    c[:, C_TINC, :] = (ii[:, None] >= ii[None, :]).astype(np.float32)
    same = (ii[:, None] // 64) == (ii[None, :] // 64)
    le = ii[:, None] <= ii[None, :]
    lt = ii[:, None] < ii[None, :]
    c[:, C_TRI2, :] = (same & le).astype(np.float32)
    c[:, C_TRI2NEG, :] = -c[:, C_TRI2, :]
    c[:, C_BD, :] = same.astype(np.float32)
    c[:, C_IND0, :] = (ii[:, None] < 64).astype(np.float32) * np.ones((1, 128), np.float32)
    c[:, C_IND1, :] = (ii[:, None] >= 64).astype(np.float32) * np.ones((1, 128), np.float32)
    c[:, C_NEGINCL, :] = np.where(same & le, 0.0, NEG)
    c[:, C_NEGSB, :] = np.where(lt, 0.0, 8.0 * NEG)
    c[:, C_MINCL2, :] = (same & le).astype(np.float32)
    c[:, C_MSTRICT2, :] = (same & lt).astype(np.float32)
    c[:, C_MSTRICT2T, :] = (same & lt).T.astype(np.float32)
    return c.reshape(128, NCONST * 128)


_CACHE = {}
_RUN_KW = {}


def get_nc(**kw):
    key = tuple(sorted((k, str(v)) for k, v in kw.items()))
    if key not in _CACHE:
        b = Builder(**kw)
        b.build()
        _CACHE[key] = b
    return _CACHE[key]


def run(inputs, **kw):
    b = get_nc(**kw)
    consts = make_consts()
    common = {
        "consts": consts,
        "w_in": np.ascontiguousarray(inputs["w_in"], dtype=np.float32),
        "norm_mix": np.ascontiguousarray(inputs["norm_mix"], dtype=np.float32),
        "norm_mlp": np.ascontiguousarray(inputs["norm_mlp"], dtype=np.float32),
        "norm_final": np.ascontiguousarray(inputs["norm_final"], dtype=np.float32).reshape(1, D),
        "w_branch": np.ascontiguousarray(inputs["w_branch"], dtype=np.float32),
        "w_out": np.ascontiguousarray(inputs["w_out"], dtype=np.float32),
        "w_up": np.ascontiguousarray(inputs["w_up"], dtype=np.float32),
        "w_down": np.ascontiguousarray(inputs["w_down"], dtype=np.float32),
    }
    for nm in ("ssm_conv_w", "ssm_conv_b", "ssm_a_log", "ssm_dt_bias", "ssm_d", "ssm_norm_w", "dn_conv_w", "dn_a_log",
               "dn_dt_bias", "dn_norm_w"):
        common[nm] = np.ascontiguousarray(inputs[nm], dtype=np.float32)
    x = np.asarray(inputs["x"], dtype=np.float32)
    in_maps = []
    for c in range(NCORES):
        m = dict(common)
        m["x"] = np.ascontiguousarray(x[c])
        in_maps.append(m)
    res = run_bass_kernel_spmd(b.nc, in_maps, core_ids=list(range(NCORES)), **_RUN_KW)
    return res


def kernel(**inputs):
    res = run(inputs)
    out = np.stack([np.asarray(r["out"], dtype=np.float32) for r in res.results], axis=0)
    return out
```

```python
import numpy as np
import concourse.bass as bass
import concourse.mybir as mybir
from concourse.bass_utils import run_bass_kernel_spmd

F32 = mybir.dt.float32
BF16 = mybir.dt.bfloat16
AF = mybir.ActivationFunctionType
ALU = mybir.AluOpType

S_LEN = 2048
D = 1024
KC = 8
NCORES = 8
DEPTH = 2
D_FF = 4096
EPS = 1e-6
IN_SIZES = (3072, 1024, 8, 8, 3072, 1024, 2048, 16, 3072)
IN_DIM = sum(IN_SIZES)
OFF = [0]
for _s in IN_SIZES:
    OFF.append(OFF[-1] + _s)
(O_DNQKV, O_DNGATE, O_DNA, O_DNB, O_SBQKV, O_SSMZ, O_SSMXBC, O_SSMDT, O_GATE, _) = OFF


class _Op:
    __slots__ = ("id", "eng", "fn", "dma", "waits", "idx", "sig", "reuse", "seen")


class Sched:
    ENG = ("pe", "act", "dve", "pool", "sp")
    NSLOT = 12
    SEM_LIMIT = 20000

    def __init__(self, nc):
        self.nc = nc
        self.ops = []
        self.by_eng = {e: [] for e in self.ENG}
        self.kstate = {}
        self.seen = {e: {p: -1 for p in self.ENG} for e in self.ENG}
        self.seen_dma = {e: set() for e in self.ENG}
        self.pending = {e: set() for e in self.ENG}
        self.open_dma = []

    def op(self, eng, fn, reads=(), writes=(), dma=False):
        o = _Op()
        o.id = len(self.ops)
        o.eng = eng
        o.fn = fn
        o.dma = dma
        o.sig = None
        o.reuse = None
        deps = {}
        for k in reads:
            st = self.kstate.get(k)
            if st is not None and st[0] is not None:
                deps[st[0]] = True
        for k in writes:
            st = self.kstate.get(k)
            if st is not None:
                if st[0] is not None:
                    deps.setdefault(st[0], False)
                for r in st[1].values():
                    deps.setdefault(r, False)
                for r in st[2]:
                    deps.setdefault(r, False)
        for d in self.pending[eng]:
            deps[d] = True
        self.pending[eng] = set()
        o.idx = len(self.by_eng[eng])
        seen = self.seen[eng]
        best = {}
        waits = []
        for d, raw in deps.items():
            p = self.ops[d]
            if p.dma:
                if d in self.seen_dma[eng]:
                    continue
                self.seen_dma[eng].add(d)
                waits.append(d)
            else:
                if p.eng == eng and not dma:
                    if eng == "pe" or (not raw and eng != "pool"):
                        continue
                if p.idx <= seen[p.eng]:
                    continue
                if p.eng not in best or self.ops[best[p.eng]].idx < p.idx:
                    best[p.eng] = d
        for pe, d in best.items():
            p = self.ops[d]
            waits.append(d)
            for e2, v in p.seen.items():
                if v > seen[e2]:
                    seen[e2] = v
            if p.idx > seen[pe]:
                seen[pe] = p.idx
        o.waits = waits
        o.seen = dict(seen)
        for k in reads:
            st = self.kstate.get(k)
            if st is None:
                st = [None, {}, []]
                self.kstate[k] = st
            if dma:
                st[2].append(o.id)
            else:
                st[1][eng] = o.id
        for k in writes:
            self.kstate[k] = [o.id, {}, []]
        self.ops.append(o)
        self.by_eng[eng].append(o)
        if dma:
            self.open_dma.append(o.id)
        return o

    def barrier(self):
        last = []
        for e in self.ENG:
            for o in reversed(self.by_eng[e]):
                if o.dma:
                    break
                if o.fn is not None:
                    last.append(o.id)
                    break
        for e in self.ENG:
            self.pending[e] |= set(last) | set(self.open_dma[-self.NSLOT:])
        self.open_dma = self.open_dma[-self.NSLOT:]

    def finalize(self):
        nc = self.nc
        needed = set()
        for o in self.ops:
            needed.update(o.waits)
        for e in self.ENG:
            sem = None
            cnt = 0
            ndma = 0
            slots = None
            for o in self.by_eng[e]:
                if o.dma:
                    if slots is None:
                        slots = [nc.alloc_semaphore(f"dq_{e}_{i}") for i in range(self.NSLOT)]
                    s = ndma % self.NSLOT
                    r = ndma // self.NSLOT
                    o.sig = (slots[s], 16 * (r + 1))
                    if r > 0:
                        o.reuse = (slots[s], 16 * r)
                    ndma += 1
                elif o.id in needed:
                    if sem is None or cnt >= self.SEM_LIMIT:
                        sem = nc.alloc_semaphore(f"pg_{e}_{o.id}")
                        cnt = 0
                    cnt += 1
                    o.sig = (sem, cnt)

    def emit(self, ename, eng):
        ops = self.ops
        for o in self.by_eng[ename]:
            for d in o.waits:
                s, v = ops[d].sig
                eng.wait_ge(s, v)
            if o.reuse is not None:
                eng.wait_ge(o.reuse[0], o.reuse[1])
            if o.fn is None:
                continue
            ins = o.fn(eng)
            if o.sig is not None:
                ins.then_inc(o.sig[0], 16 if o.dma else 1)


class Arena:
    def __init__(self, nc, limit=208 * 1024):
        self.nc = nc
        self.off = 16 * 1024
        self.limit = limit
        self.n = 0
        self.peak = 0

    def alloc(self, name, shape, dtype):
        per = 1
        for s in shape[1:]:
            per *= s
        nbytes = per * (4 if dtype == F32 else 2)
        nbytes = (nbytes + 63) // 64 * 64
        assert self.off + nbytes <= self.limit, f"SBUF arena overflow at {name}: {self.off}+{nbytes}"
        self.n += 1
        t = self.nc.alloc_sbuf_tensor_at(f"{name}_{self.n}", list(shape), dtype, offset=self.off)
        self.off += nbytes
        self.peak = max(self.peak, self.off)
        return t

    def mark(self):
        return self.off

    def reset(self, m):
        self.off = m


class Ring:
    def __init__(self, tiles, name):
        self.tiles = tiles
        self.name = name
        self.i = 0

    def next(self):
        j = self.i % len(self.tiles)
        self.i += 1
        return self.tiles[j], (self.name, j)


class Builder:
    def __init__(self, nlayers=DEPTH, mixers=("dn", "sb", "ssm"), do_mlp=True, debug=()):
        self.nlayers = nlayers
        self.mixers = mixers
        self.do_mlp = do_mlp
        self.debug = debug
        nc = bass.Bass("TRN2", target_bir_lowering=False)
        self.nc = nc
        self.S = Sched(nc)
        self.A = Arena(nc)
        self.uid = 0

    def dram_in(self, name, shape, dtype=F32):
        return self.nc.dram_tensor(name, list(shape), dtype, kind="ExternalInput").ap()

    def dram_out(self, name, shape, dtype=F32):
        return self.nc.dram_tensor(name, list(shape), dtype, kind="ExternalOutput").ap()

    def dram_tmp(self, name, shape, dtype):
        return self.nc.dram_tensor(name, list(shape), dtype, kind="Internal").ap()

    def key(self, base):
        self.uid += 1
        return (base, self.uid)

    def ring(self, name, n, shape, dtype):
        return Ring([self.A.alloc(name, shape, dtype) for _ in range(n)], self.key(name))

    def dma(self, out, in_, reads, writes, slow=False):
        if slow:
            fn = lambda e, out=out, in_=in_: e.dma_start(out=out, in_=in_, allow_slow_non_contiguous=True)
        else:
            fn = lambda e, out=out, in_=in_: e.dma_start(out=out, in_=in_)
        return self.S.op("sp", fn, reads=reads, writes=writes, dma=True)

    def mm(self, out, lhsT, rhs, start, stop, reads, writes, **kw):
        return self.S.op(
            "pe",
            lambda e, out=out, lhsT=lhsT, rhs=rhs, start=start, stop=stop, kw=kw: e.matmul(
                out, lhsT=lhsT, rhs=rhs, start=start, stop=stop, **kw
            ),
            reads=reads,
            writes=writes,
        )

    def tr(self, out, in_, ident, reads, writes):
        return self.S.op(
            "pe",
            lambda e, out=out, in_=in_, ident=ident: e.transpose(out, in_, ident),
            reads=reads,
            writes=writes,
        )

    def act(self, out, in_, func, reads, writes, eng="act", **kw):
        return self.S.op(
            eng,
            lambda e, out=out, in_=in_, func=func, kw=kw: e.activation(out=out, in_=in_, func=func, **kw),
            reads=reads,
            writes=writes,
        )

    def tt(self, out, in0, in1, op, reads, writes, eng="dve"):
        return self.S.op(
            eng,
            lambda e, out=out, in0=in0, in1=in1, op=op: e.tensor_tensor(out=out, in0=in0, in1=in1, op=op),
            reads=reads,
            writes=writes,
        )

    def ts(self, out, in0, s1, op0, reads, writes, s2=None, op1=None, eng="dve"):
        def fn(e, out=out, in0=in0, s1=s1, op0=op0, s2=s2, op1=op1):
            if op1 is None:
                return e.tensor_scalar(out=out, in0=in0, scalar1=s1, scalar2=None, op0=op0)
            return e.tensor_scalar(out=out, in0=in0, scalar1=s1, scalar2=s2, op0=op0, op1=op1)

        return self.S.op(eng, fn, reads=reads, writes=writes)

    def stt(self, out, in0, scalar, in1, op0, op1, reads, writes):
        return self.S.op(
            "dve",
            lambda e, out=out, in0=in0, scalar=scalar, in1=in1, op0=op0, op1=op1: e.scalar_tensor_tensor(
                out=out, in0=in0, scalar=scalar, in1=in1, op0=op0, op1=op1
            ),
            reads=reads,
            writes=writes,
        )

    def copy(self, out, in_, reads, writes, eng="dve"):
        if eng == "act":
            return self.act(out, in_, AF.Copy, reads, writes)
        return self.S.op(
            eng, lambda e, out=out, in_=in_: e.tensor_copy(out=out, in_=in_), reads=reads, writes=writes
        )

    def memset(self, ap, val, writes, eng="dve"):
        return self.S.op(eng, lambda e, ap=ap, val=val: e.memset(ap, val), reads=(), writes=writes)

    def build(self):
        nc, S, A = self.nc, self.S, self.A
        L = self.nlayers
        self.x_d = self.dram_in("x", [S_LEN, D])
        self.consts_d = self.dram_in("consts", [128, NCONST * 128])
        self.w_in_d = self.dram_in("w_in", [DEPTH, D, IN_DIM])
        self.norm_mix_d = self.dram_in("norm_mix", [DEPTH, D])
        self.norm_mlp_d = self.dram_in("norm_mlp", [DEPTH, D])
        self.norm_final_d = self.dram_in("norm_final", [1, D])
        self.w_branch_d = self.dram_in("w_branch", [DEPTH, 3, D, D])
        self.w_out_d = self.dram_in("w_out", [DEPTH, D, D])
        self.w_up_d = self.dram_in("w_up", [DEPTH, D, D_FF])
        self.w_down_d = self.dram_in("w_down", [DEPTH, D_FF, D])
        self.ssm_conv_w_d = self.dram_in("ssm_conv_w", [DEPTH, 4, 2048])
        self.ssm_conv_b_d = self.dram_in("ssm_conv_b", [DEPTH, 2048])
        self.ssm_a_log_d = self.dram_in("ssm_a_log", [DEPTH, 16])
        self.ssm_dt_bias_d = self.dram_in("ssm_dt_bias", [DEPTH, 16])
        self.ssm_d_d = self.dram_in("ssm_d", [DEPTH, 16])
        self.ssm_norm_w_d = self.dram_in("ssm_norm_w", [DEPTH, D])
        self.dn_conv_w_d = self.dram_in("dn_conv_w", [DEPTH, 4, 3072])
        self.dn_a_log_d = self.dram_in("dn_a_log", [DEPTH, 8])
        self.dn_dt_bias_d = self.dram_in("dn_dt_bias", [DEPTH, 8])
        self.dn_norm_w_d = self.dram_in("dn_norm_w", [DEPTH, 128])
        self.out_d = self.dram_out("out", [S_LEN, D])
        self.u_scr = self.dram_tmp("u_scr", [D_FF, S_LEN], BF16)
        if self.debug:
            self.o_scr = self.dram_out("o_scr", [3, D, S_LEN], BF16)
        else:
            self.o_scr = self.dram_tmp("o_scr", [3, D, S_LEN], BF16)
        self.g_scr = self.dram_tmp("g_scr", [3, D, S_LEN], BF16)
        self.dn_scr = {nm: self.dram_tmp("dn_" + nm, [8, 128, S_LEN], BF16) for nm in ("u", "wT", "q", "attnT", "kd", "sg")}

        pst = [nc.alloc_psum_tensor(f"ps{i}", [128, 512], F32) for i in range(8)]
        self.pst = pst
        self.psum = Ring(pst[:6], "psum")
        self.psum_acc = Ring(pst[6:], "psacc")

        self.consts = A.alloc("consts", [128, NCONST, 128], F32)
        self.cbf = A.alloc("cbf", [128, NCONST, 128], BF16)
        self.xT = A.alloc("xT", [128, KC, S_LEN], F32)
        self.nw = A.alloc("nw", [128, 2 * DEPTH + 1, KC], F32)
        KCON = "consts"
        self.dma(self.consts[:].rearrange("p a b -> p (a b)"), self.consts_d, reads=[], writes=[KCON])
        self.copy(self.cbf[:], self.consts[:], reads=[KCON], writes=["cbf"])
        for l in range(DEPTH):
            self.dma(self.nw[:, 2 * l, :], self.norm_mix_d[l].rearrange("(k p) -> p k", p=128), [], ["nw"], slow=True)
            self.dma(self.nw[:, 2 * l + 1, :], self.norm_mlp_d[l].rearrange("(k p) -> p k", p=128), [], ["nw"], slow=True)
        self.dma(self.nw[:, 2 * DEPTH, :], self.norm_final_d[0].rearrange("(k p) -> p k", p=128), [], ["nw"], slow=True)

        self.load_x()
        for l in range(L):
            self.layer(l)
        self.final_out()
        S.barrier()
        S.op("sp", None)
        S.finalize()

        with nc.Block() as block:

            @block.tensor
            def _(e):
                S.emit("pe", e)

            @block.scalar
            def _(e):
                S.emit("act", e)

            @block.vector
            def _(e):
                S.emit("dve", e)

            @block.gpsimd
            def _(e):
                S.emit("pool", e)

            @block.sync
            def _(e):
                S.emit("sp", e)

        return nc

    def ident_f(self):
        return self.consts[:, C_IDENT, :]

    def ident_b(self):
        return self.cbf[:, C_IDENT, :]

    def ones_b(self):
        return self.cbf[:, C_ONES, :]

    def load_x(self):
        A, S = self.A, self.S
        m = A.mark()
        stg = self.ring("xstg", 2, [128, 4, D], F32)
        for tg in range(4):
            st, sk = stg.next()
            self.dma(
                st[:],
                self.x_d[tg * 512:(tg + 1) * 512, :].rearrange("(t p) d -> p t d", p=128),
                reads=[],
                writes=[sk],
            )
            for kc in range(KC):
                ps, pk = self.psum.next()
                for t in range(4):
                    self.tr(ps[:, t * 128:(t + 1) * 128], st[:, t, kc * 128:(kc + 1) * 128], self.ident_f(),
                            reads=[sk, "consts"], writes=[pk])
                self.copy(self.xT[:, kc, tg * 512:(tg + 1) * 512], ps[:], reads=[pk], writes=[("xT", kc, tg)],
                          eng=("act" if kc % 2 else "dve"))
        S.barrier()
        A.reset(m)

    def rmsnorm(self):
        A, S = self.A, self.S
        m = A.mark()
        sq = self.ring("sq", 3, [128, 512], BF16)
        rs = self.ring("rs", 2, [128, 512], F32)
        for g in range(4):
            ps, pk = self.psum.next()
            sl = slice(g * 512, (g + 1) * 512)
            for kc in range(KC):
                q, qk = sq.next()
                self.act(q[:], self.xT[:, kc, sl], AF.Square, reads=[("xT", kc, g)], writes=[qk])
                self.mm(ps[:], self.ones_b(), q[:], kc == 0, kc == KC - 1, reads=[qk, "cbf"], writes=[pk])
            r, rk = rs.next()
            self.act(r[:], ps[:], AF.Ln, reads=[pk], writes=[rk], scale=1.0 / D, bias=EPS)
            self.act(r[:], r[:], AF.Exp, reads=[rk], writes=[rk], scale=-0.5)
            for kc in range(KC):
                self.tt(self.hT[:, kc, sl], self.xT[:, kc, sl], r[:], ALU.mult,
                        reads=[("xT", kc, g), rk], writes=[("hT", kc, g)])
        self.rstd_ring = rs
        S.barrier()
        A.reset(m)

    def eps_ap(self):
        return self.consts[:, C_EPS, 0:1]

    def wload(self, w_ap, kcn, n, wst_ring, wbf_ring, scale=None):
        st, sk = wst_ring.next()
        wb, wk = wbf_ring.next()
        self.dma(st[:, :kcn, :n], w_ap.rearrange("(k p) n -> p k n", p=128), reads=[], writes=[sk])
        for kc in range(kcn):
            if scale is not None:
                self.act(wb[:, kc, :n], st[:, kc, :n], AF.Copy, reads=[sk, "nw"], writes=[wk],
                         scale=scale[:, kc:kc + 1])
            else:
                self.act(wb[:, kc, :n], st[:, kc, :n], AF.Copy, reads=[sk], writes=[wk])
        return wb, wk

    def hT_keys(self, g):
        return [("hT", kc, g) for kc in range(KC)]

    def linear_T(self, wb, wk, ncols, kcn, rhs_fn, rhs_keys_fn, evac, groups=range(4)):
        for c0 in range(0, ncols, 128):
            n = min(128, ncols - c0)
            for g in groups:
                ps, pk = self.psum.next()
                for kc in range(kcn):
                    self.mm(ps[:n, :], wb[:, kc, c0:c0 + n], rhs_fn(kc, g), kc == 0, kc == kcn - 1,
                            reads=[wk] + rhs_keys_fn(g), writes=[pk])
                evac(c0 // 128, g, ps, pk, n)

    def mlp(self, l):
        A, S = self.A, self.S
        m = A.mark()
        self.hT = A.alloc("hT", [128, KC, S_LEN], BF16)
        self.rmsnorm()
        wst = self.ring("wst", 2, [128, KC, 512], F32)
        wbf = self.ring("wbf", 2, [128, KC, 512], BF16)
        ub = self.ring("ub", 3, [128, S_LEN], BF16)
        rr = self.ring("rr", 3, [128, 512], F32)
        scale = self.nw[:, 2 * l + 1, :]
        nxt = self.wload(self.w_up_d[l][:, 0:512], KC, 512, wst, wbf, scale=scale)
        for cb in range(D_FF // 512):
            wb, wk = nxt
            if cb + 1 < D_FF // 512:
                nxt = self.wload(self.w_up_d[l][:, (cb + 1) * 512:(cb + 2) * 512], KC, 512, wst, wbf, scale=scale)
            cur = {}

            def evac(ci, g, ps, pk, n, cb=cb, cur=cur):
                if g == 0:
                    cur["u"] = ub.next()
                u, uk = cur["u"]
                r, rk = rr.next()
                self.ts(r[:], ps[:], 0.0, ALU.max, reads=[pk], writes=[rk])
                self.act(u[:, g * 512:(g + 1) * 512], r[:], AF.Square, reads=[rk], writes=[uk])
                if g == 3:
                    f = cb * 4 + ci
                    self.dma(self.u_scr[f * 128:(f + 1) * 128, :], u[:], reads=[uk], writes=[("u_scr", f)])

            self.linear_T(wb, wk, 512, KC, lambda kc, g: self.hT[:, kc, g * 512:(g + 1) * 512],
                          self.hT_keys, evac)
        S.barrier()
        A.reset(m)
        wst = self.ring("wdst", 2, [128, 4, 512], F32)
        wbf = self.ring("wdbf", 2, [128, 32, 512], BF16)
        ur = self.ring("ur", 1, [128, 32, 512], BF16)

        def load_down(half):
            wb, wk = wbf.next()
            for q in range(8):
                st, sk = wst.next()
                self.dma(st[:], self.w_down_d[l][q * 512:(q + 1) * 512, half * 512:(half + 1) * 512]
                         .rearrange("(k p) n -> p k n", p=128), reads=[], writes=[sk])
                for k4 in range(4):
                    self.act(wb[:, q * 4 + k4, :], st[:, k4, :], AF.Copy, reads=[sk], writes=[wk])
            return wb, wk

        nxt = load_down(0)
        for half in range(2):
            wb, wk = nxt
            if half == 0:
                nxt = load_down(1)
            for g in range(4):
                u, uk = ur.next()
                self.dma(u[:], self.u_scr[:, g * 512:(g + 1) * 512].rearrange("(k p) t -> p k t", p=128),
                         reads=[("u_scr", f) for f in range(32)], writes=[uk])
                for ci in range(4):
                    ps, pk = self.psum.next()
                    for kc in range(32):
                        self.mm(ps[:], wb[:, kc, ci * 128:(ci + 1) * 128], u[:, kc, :], kc == 0, kc == 31,
                                reads=[wk, uk], writes=[pk])
                    oc = half * 4 + ci
                    sl = slice(g * 512, (g + 1) * 512)
                    self.tt(self.xT[:, oc, sl], self.xT[:, oc, sl], ps[:], ALU.add,
                            reads=[pk, ("xT", oc, g)], writes=[("xT", oc, g)])
        S.barrier()
        A.reset(m)

    def layer(self, l):
        if self.mixers:
            self.mixer(l)
        if self.do_mlp:
            self.mlp(l)


    def mixer(self, l):
        A, S = self.A, self.S
        m00 = A.mark()
        if "dn" in self.mixers:
            self.dn_egc = A.alloc("degc", [128, 16, 8], F32)
            self.dn_eglast = A.alloc("deglast", [128, 32, 8], F32)
            self.dn_nwb, self.dn_nwbk = self.bload("dnwb", self.dn_norm_w_d[l], 128)
        m0 = A.mark()
        self.hT = A.alloc("hT", [128, KC, S_LEN], BF16)
        self.rmsnorm()
        m1 = A.mark()
        if "sb" in self.mixers:
            self.sb_phase(l)
            S.barrier()
            A.reset(m1)
        if "ssm" in self.mixers:
            self.ssm_phase(l)
            S.barrier()
            A.reset(m1)
        if "dn" in self.mixers:
            self.dn_phase(l)
            S.barrier()
            A.reset(m1)
        self.gates_phase(l)
        S.barrier()
        A.reset(m0)
        if "dn" in self.mixers:
            self.dn_phase2(l)
            S.barrier()
            A.reset(m0)
        self.merge_phase(l)
        S.barrier()
        A.reset(m00)

    def dn_phase2(self, l):
        A, S = self.A, self.S
        cst = self.consts
        pst = self.pst
        NH = 8
        names = ["u", "wT", "q", "attnT", "kd", "sg"]
        sets = [[A.alloc(f"d2{n}", [128, NH, 512], BF16) for n in names] for _ in range(1)]
        Sst = A.alloc("d2S", [128, NH, 128], F32)
        Sbf = A.alloc("d2Sbf", [128, NH, 128], BF16)
        vnr = self.ring("d2vn", 2, [128, NH, 128], BF16)
        t1 = A.alloc("d2t1", [128, NH, 128], F32)
        ot = A.alloc("d2ot", [128, NH, 128], F32)
        sq = A.alloc("d2sq", [128, NH, 128], F32)
        ssq = A.alloc("d2ssq", [128, 2, NH], F32)
        obr = self.ring("d2ob", 2, [128, NH, 128], BF16)
        otl = self.ring("d2otl", 2, [128, NH, 128], BF16)
        egc, eglast, nwb, nwbk = self.dn_egc, self.dn_eglast, self.dn_nwb, self.dn_nwbk
        self.memset(Sst[:], 0.0, writes=[("d2S", 0), ("d2S", 1)])
        self.memset(Sbf[:], 0.0, writes=[("d2Sbf", 0), ("d2Sbf", 1)])
        bO = [(pst[i], ("p2O", i)) for i in range(4)]
        bW = [(pst[4], ("p2W", 0)), (pst[5], ("p2W", 1))]
        bS = [(pst[6], ("p2S", 0)), (pst[7], ("p2S", 1))]
        o_view = self.o_scr[0].rearrange("(h e) s -> e h s", h=NH)
        for q in range(4):
            st = sets[0]
            sk = ("d2set", 0)
            for ni, nm in enumerate(names):
                self.dma(st[ni][:], self.dn_scr[nm][:, :, q * 512:(q + 1) * 512].rearrange("h p t -> p h t"),
                         reads=[("dn_scr", nm, h) for h in range(NH)], writes=[(sk, ni)])
            U, WT, Q, AT, KD, SG = st
            for tl in range(4):
                t = 4 * q + tl
                cs = slice(tl * 128, (tl + 1) * 128)
                vn, vnk = vnr.next()
                for c in range(2):
                    j = 2 * t + c
                    pb = 64 * c
                    prt = slice(pb, pb + 64)
                    ccs = slice(tl * 128 + pb, tl * 128 + pb + 64)
                    for h in range(NH):
                        pw, pwk = bW[h // 4]
                        self.mm(pw[prt, (h % 4) * 128:(h % 4 + 1) * 128], WT[:, h, ccs], Sbf[:, h, :], True, True,
                                reads=[(sk, 1), ("d2Sbf", h // 4)], writes=[pwk])
                    for hb in range(2):
                        pw, pwk = bW[hb]
                        self.tt(vn[prt, 4 * hb:4 * hb + 4, :], U[prt, 4 * hb:4 * hb + 4, cs],
                                pw[prt, :].rearrange("p (a b) -> p a b", a=4), ALU.subtract, reads=[(sk, 0), pwk],
                                writes=[(vnk, c, hb)])
                    for h in range(NH):
                        po, pok = bO[h // 2]
                        co = (h % 2) * 256
                        self.mm(po[prt, co:co + 128], Q[:, h, ccs], Sbf[:, h, :], True, True,
                                reads=[(sk, 2), ("d2Sbf", h // 4)], writes=[(pok, c)])
                        self.mm(po[prt, co + 128:co + 256], AT[prt, h, ccs], vn[prt, h, :], True, True,
                                reads=[(sk, 3), (vnk, c, h // 4)], writes=[(pok, c)])
                        ps_, psk_ = bS[h // 4]
                        self.mm(ps_[:, (h % 4) * 128:(h % 4 + 1) * 128], KD[prt, h, cs], vn[prt, h, :], True, True,
                                reads=[(sk, 4), (vnk, c, h // 4)], writes=[psk_])
                    for hb in range(2):
                        ps_, psk_ = bS[hb]
                        hs = slice(4 * hb, 4 * hb + 4)
                        self.tt(Sst[:, hs, :], Sst[:, hs, :], eglast[:, j, hs].unsqueeze(2).to_broadcast([128, 4, 128]),
                                ALU.mult, reads=[("d2S", hb), "deglast"], writes=[("d2S", hb)])
                        self.tt(Sst[:, hs, :], Sst[:, hs, :], ps_[:].rearrange("p (a b) -> p a b", a=4), ALU.add,
                                reads=[("d2S", hb), psk_], writes=[("d2S", hb)])
                        self.copy(Sbf[:, hs, :], Sst[:, hs, :], reads=[("d2S", hb)], writes=[("d2Sbf", hb)], eng="act")
                for b4 in range(4):
                    po, pok = bO[b4]
                    hs = slice(2 * b4, 2 * b4 + 2)
                    pv = po[:].rearrange("p (a b) -> p a b", a=2)
                    self.tt(t1[:, hs, :], pv[:, :, 0:128], egc[:, t, hs].unsqueeze(2).to_broadcast([128, 2, 128]), ALU.mult,
                            reads=[(pok, 0), (pok, 1), "degc"], writes=[("d2t1", b4)])
                    self.tt(ot[:, hs, :], t1[:, hs, :], pv[:, :, 128:256], ALU.add,
                            reads=[("d2t1", b4), (pok, 0), (pok, 1)], writes=[("d2ot", b4)])
                otk = [("d2ot", b4) for b4 in range(4)]
                self.act(sq[:], ot[:], AF.Square, reads=otk, writes=["d2sq"])
                self.S.op("dve", lambda e: e.tensor_reduce(out=ssq[:, 0, :], in_=sq[:], op=ALU.add,
                                                           axis=mybir.AxisListType.X),
                          reads=["d2sq"], writes=["d2ssq"])
                self.act(ssq[:, 1, :], ssq[:, 0, :], AF.Ln, reads=["d2ssq", "consts"], writes=["d2ssq"], scale=1.0 / 128,
                         bias=EPS)
                self.act(ssq[:, 1, :], ssq[:, 1, :], AF.Exp, reads=["d2ssq"], writes=["d2ssq"], scale=-0.5)
                self.tt(ot[:], ot[:], ssq[:, 1, :].unsqueeze(2).to_broadcast([128, NH, 128]), ALU.mult,
                        reads=otk + ["d2ssq"], writes=otk)
                self.tt(ot[:], ot[:], nwb[:].unsqueeze(1).to_broadcast([128, NH, 128]), ALU.mult, reads=otk + [nwbk],
                        writes=otk)
                ob, obk = obr.next()
                self.tt(ob[:], ot[:], SG[:, :, cs], ALU.mult, reads=otk + [(sk, 5)], writes=[obk])
                px, pxk = bW[0]
                pxb = px[:].bitcast(BF16)
                for h in range(NH):
                    self.tr(pxb[:, h * 128:(h + 1) * 128], ob[:, h, :], self.ident_b(), reads=[obk, "cbf"], writes=[pxk])
                otile, otlk = otl.next()
                self.copy(otile[:], pxb[:].rearrange("p (a b) -> p a b", a=NH), reads=[pxk], writes=[otlk], eng="act")
                self.dma(o_view[:, :, t * 128:(t + 1) * 128], otile[:], reads=[otlk], writes=[("o_scr", 0, "t", t)])

    def branches(self):
        return [i for i, n in enumerate(("dn", "sb", "ssm")) if n in self.mixers]

    def gates_phase(self, l):
        wst = self.ring("gwst", 2, [128, KC, 512], F32)
        wbf = self.ring("gwbf", 2, [128, KC, 512], BF16)
        gb = self.ring("gb", 3, [128, S_LEN], BF16)
        scale = self.nw[:, 2 * l, :]
        blocks = [(i, cb) for i in self.branches() for cb in range(2)]

        def gload(bi):
            i_, cb_ = blocks[bi]
            c0_ = O_GATE + i_ * D + cb_ * 512
            return self.wload(self.w_in_d[l][:, c0_:c0_ + 512], KC, 512, wst, wbf, scale=scale)

        nxt = gload(0)
        for bi_, (i, cb) in enumerate(blocks):
            if True:
                wb, wk = nxt
                if bi_ + 1 < len(blocks):
                    nxt = gload(bi_ + 1)
                cur = {}

                def evac(ci, g, ps, pk, n, cb=cb, i=i, cur=cur):
                    if g == 0:
                        cur["t"] = gb.next()
                    t, tk = cur["t"]
                    self.act(t[:, g * 512:(g + 1) * 512], ps[:], AF.Sigmoid, reads=[pk], writes=[tk])
                    if g == 3:
                        oc = cb * 4 + ci
                        self.dma(self.g_scr[i][oc * 128:(oc + 1) * 128, :], t[:], reads=[tk],
                                 writes=[("g_scr", i, oc)])

                self.linear_T(wb, wk, 512, KC, lambda kc, g: self.hT[:, kc, g * 512:(g + 1) * 512],
                              self.hT_keys, evac)

    def merge_phase(self, l):
        A, S = self.A, self.S
        wst = self.ring("mwst", 1, [128, KC, 512], F32)
        wbf = self.ring("mwbf", 2, [128, KC, 512], BF16)
        mg = A.alloc("mg", [128, KC, 1024], F32)
        mgb = A.alloc("mgb", [128, KC, 1024], BF16)
        ob = self.ring("ob", 1, [128, KC, 1024], BF16)
        gt = self.ring("gt", 3, [128, 1024], BF16)
        self.mtmp = self.ring("mtmp", 3, [128, 512], F32)
        brs = self.branches()
        mseq = []
        for half_ in range(2):
            for i_ in brs:
                for cb_ in range(2):
                    mseq.append(self.w_branch_d[l][i_][:, cb_ * 512:(cb_ + 1) * 512])
            for cb_ in range(2):
                mseq.append(self.w_out_d[l][:, cb_ * 512:(cb_ + 1) * 512])

        def mload(pos):
            if pos >= len(mseq):
                return None
            return self.wload(mseq[pos], KC, 512, wst, wbf)

        mpos = [0]
        mnxt = [mload(0)]
        for half in range(2):
            tsl = slice(half * 1024, (half + 1) * 1024)
            for bi, i in enumerate(brs):
                o, ok = ob.next()
                self.dma(o[:], self.o_scr[i][:, tsl].rearrange("(k p) t -> p k t", p=128),
                         reads=[("o_scr", i, c) for c in range(8)], writes=[ok])
                for cb in range(2):
                    wb, wk = mnxt[0]
                    mnxt[0] = mload(mpos[0] + 1)
                    mpos[0] += 1
                    cur = {}

                    def evac(ci, g, ps, pk, n, cb=cb, i=i, bi=bi, cur=cur, half=half, tsl=tsl):
                        oc = cb * 4 + ci
                        gl = g - 2 * half
                        if gl == 0:
                            cur["g"] = gt.next()
                            t, tk = cur["g"]
                            self.dma(t[:], self.g_scr[i][oc * 128:(oc + 1) * 128, tsl],
                                     reads=[("g_scr", i, oc)], writes=[tk])
                        t, tk = cur["g"]
                        dst = mg[:, oc, gl * 512:(gl + 1) * 512]
                        mk = ("mg", oc, gl)
                        if bi == 0:
                            self.tt(dst, ps[:], t[:, gl * 512:(gl + 1) * 512], ALU.mult,
                                    reads=[pk, tk], writes=[mk])
                        else:
                            tmp, tmk = self.mtmp.next()
                            self.tt(tmp[:], ps[:], t[:, gl * 512:(gl + 1) * 512], ALU.mult,
                                    reads=[pk, tk], writes=[tmk])
                            self.tt(dst, dst, tmp[:], ALU.add, reads=[tmk, mk], writes=[mk], eng="pool")
                        if bi == len(brs) - 1:
                            self.copy(mgb[:, oc, gl * 512:(gl + 1) * 512], dst, reads=[mk],
                                      writes=[("mgb", oc, gl)], eng="act")

                    self.linear_T(wb, wk, 512, KC, lambda kc, g, o=o, half=half: o[:, kc, (g - 2 * half) * 512:(g - 2 * half + 1) * 512],
                                  lambda g, ok=ok: [ok], evac, groups=(2 * half, 2 * half + 1))
            for cb in range(2):
                wb, wk = mnxt[0]
                mnxt[0] = mload(mpos[0] + 1)
                mpos[0] += 1

                def evac2(ci, g, ps, pk, n, cb=cb):
                    oc = cb * 4 + ci
                    sl = slice(g * 512, (g + 1) * 512)
                    self.tt(self.xT[:, oc, sl], self.xT[:, oc, sl], ps[:], ALU.add,
                            reads=[pk, ("xT", oc, g)], writes=[("xT", oc, g)])

                self.linear_T(wb, wk, 512, KC,
                              lambda kc, g, half=half: mgb[:, kc, (g - 2 * half) * 512:(g - 2 * half + 1) * 512],
                              lambda g, half=half: [("mgb", kc, g - 2 * half) for kc in range(KC)], evac2,
                              groups=(2 * half, 2 * half + 1))

    def sb_phase(self, l):
        A, S = self.A, self.S
        R = {}
        R["wst"] = self.ring("sbwst", 1, [128, KC, 384], F32)
        R["wbf"] = self.ring("sbwbf", 2, [128, KC, 384], BF16)
        R["qT"] = self.ring("sbq", 2, [128, 2, S_LEN], BF16)
        for i_, t_ in enumerate(R["qT"].tiles):
            self.memset(t_[:], 0.0, writes=[(R["qT"].name, i_)], eng="pool")
        R["kT"] = self.ring("sbk", 2, [128, S_LEN], BF16)
        R["v"] = self.ring("sbv", 2, [128, 16, 128], BF16)
        R["osb"] = self.ring("sbo", 1, [128, S_LEN], BF16)
        R["e"] = self.ring("sbe", 4, [128, 512], F32)
        R["spb"] = self.ring("sbsp", 4, [128, 512], BF16)
        R["xa"] = self.ring("sbxa", 2, [128, 512], F32)
        R["w"] = self.ring("sbw", 3, [128, 512], BF16)
        R["pa"] = Ring(self.pst[0:4], "psum")
        for hp in range(8):
            self.sb_unit(l, hp, R)

    def sb_unit(self, l, hp, R):
        scale = self.nw[:, 2 * l, :]
        st, sk = R["wst"].next()
        wb, wk = R["wbf"].next()
        for j in range(3):
            base = O_SBQKV + j * 1024 + 128 * hp
            self.dma(st[:, :, j * 128:(j + 1) * 128],
                     self.w_in_d[l][:, base:base + 128].rearrange("(k p) n -> p k n", p=128),
                     reads=[], writes=[(sk, j)])
        for kc in range(KC):
            self.act(wb[:, kc, :], st[:, kc, :], AF.Copy, reads=[(sk, 0), (sk, 1), (sk, 2), "nw"], writes=[wk],
                     scale=scale[:, kc:kc + 1])
        qT, qk = R["qT"].next()
        kT, kk = R["kT"].next()
        v, vk = R["v"].next()

        def evac_qk(ci, g, ps, pk, n):
            if ci == 0:
                self.copy(qT[0:64, 0, g * 512:(g + 1) * 512], ps[0:64, :], reads=[pk, qk], writes=[(qk, g)], eng="act")
                self.copy(qT[64:128, 1, g * 512:(g + 1) * 512], ps[64:128, :], reads=[pk, qk], writes=[(qk, g)], eng="dve")
            else:
                self.copy(kT[:, g * 512:(g + 1) * 512], ps[:], reads=[pk], writes=[(kk, g)],
                          eng=("act" if g % 2 else "dve"))

        self.linear_T(wb, wk, 256, KC, lambda kc, g: self.hT[:, kc, g * 512:(g + 1) * 512], self.hT_keys, evac_qk)
        for tq in range(4):
            ps, pk = self.psum.next()
            for t in range(4):
                tt_ = tq * 4 + t
                for kc in range(KC):
                    self.mm(ps[:, t * 128:(t + 1) * 128], self.hT[:, kc, tt_ * 128:(tt_ + 1) * 128],
                            wb[:, kc, 256:384], kc == 0, kc == KC - 1, reads=[wk, ("hT", kc, tq)], writes=[pk])
            self.copy(v[:, tq * 4:(tq + 1) * 4, :], ps[:].rearrange("p (t c) -> p t c", t=4), reads=[pk],
                      writes=[(vk, tq)], eng=("act" if tq % 2 else "dve"))
        osb, ok = R["osb"].next()
        mstrict = self.consts[:, C_MSTRICT, :]
        tinc = self.cbf[:, C_TINC, :]
        tlow = self.cbf[:, C_MSTRICT, :]
        one_col = self.consts[:, C_ONES, 0:1]
        pst = self.pst
        pa_ring = R["pa"]
        for g in range(4):
            racc = [(pst[4], ("psum", 4)), (pst[5], ("psum", 5))]
            po = [(pst[6], ("psacc", 0)), (pst[7], ("psacc", 1))]
            tiles = []
            for kb in range(4 * g + 3, -1, -1):
                for e in range(2):
                    t0 = max(kb * 128, g * 512)
                    tiles.append(dict(e=e, kb=kb, pb=64 * e, t0=t0, N=(g + 1) * 512 - t0, c0=t0 - g * 512,
                                      diag=kb * 128 >= g * 512, first=(kb == 4 * g + 3), last=(kb == 0)))

            def stage0(T):
                pa, pak = pa_ring.next()
                pb, N, t0, kb = T["pb"], T["N"], T["t0"], T["kb"]
                self.mm(pa[:, :N], kT[:, kb * 128:(kb + 1) * 128], qT[:, T["e"], t0:t0 + N], True, not T["diag"],
                        reads=[(kk, kb // 4), (qk, g)], writes=[pak])
                if T["diag"]:
                    self.mm(pa[:, 0:128], self.ident_b(), self.cbf[:, C_NEGSB, :], False, True, reads=["cbf"], writes=[pak])
                T["pa"], T["pak"] = pa, pak

            def stage1(T):
                N, c0 = T["N"], T["c0"]
                ee, ek = R["e"].next()
                self.act(ee[:, :N], T["pa"][:, :N], AF.Exp, reads=[T["pak"]], writes=[ek], scale=0.125)
                spb, spk = R["spb"].next()
                self.act(spb[:, :N], ee[:, :N], AF.Ln, reads=[ek], writes=[spk], bias=1.0, scale=1.0)
                ra, rak = racc[T["e"]]
                self.mm(ra[:, c0:512], tinc, spb[:, :N], T["first"], False, reads=[spk, "cbf"], writes=[rak],
                        skip_group_check=True)
                T.update(ee=ee, ek=ek, spb=spb, spk=spk)

            def stage2(T):
                N, c0, pb, kb = T["N"], T["c0"], T["pb"], T["kb"]
                ra, rak = racc[T["e"]]
                xa, xk = R["xa"].next()
                self.act(xa[:, :N], ra[:, c0:512], AF.Exp, reads=[rak], writes=[xk], scale=-1.0)
                if not T["last"]:
                    self.mm(ra[:, c0:512], tlow, T["spb"][:, :N], False, True, reads=[T["spk"], "cbf"], writes=[rak],
                            skip_group_check=True)
                w, wwk = R["w"].next()
                self.tt(w[:, :N], T["ee"][:, :N], xa[:, :N], ALU.mult, reads=[T["ek"], xk], writes=[wwk])
                pp, ppk = po[T["e"]]
                self.mm(pp[:, c0:512], v[:, kb, :], w[:, :N], T["first"], T["last"],
                        reads=[(vk, kb // 4), wwk], writes=[ppk], skip_group_check=True)

            n = len(tiles)
            for i in range(min(2, n)):
                stage0(tiles[i])
            for i in range(n + 2):
                if i - 2 >= 0:
                    stage2(tiles[i - 2])
                if i < n:
                    stage1(tiles[i])
                if i + 2 < n:
                    stage0(tiles[i + 2])
            for e in range(2):
                pp, ppk = po[e]
                pb = 64 * e
                self.copy(osb[pb:pb + 64, g * 512:(g + 1) * 512], pp[pb:pb + 64, :], reads=[ppk],
                          writes=[(ok, e, g)], eng="dve")
        self.dma(self.o_scr[1][hp * 128:(hp + 1) * 128, :], osb[:],
                 reads=[(ok, e, g) for e in range(2) for g in range(4)], writes=[("o_scr", 1, hp)])


    def bload(self, name, dram_row_ap, n):
        t = self.A.alloc(name, [128, n], F32)
        k = self.key(name)
        self.dma(t[:], dram_row_ap.partition_broadcast(128), reads=[], writes=[k])
        return t, k

    def conv_silu(self, l, wb, wk, wcol, conv_w_d, conv_b_d, ch, raw, rawk, acc, acck, cw, dst, dstk, func=AF.Silu):
        c, ck = cw.next()
        self.dma(c[:, 0:4], conv_w_d[l][:, ch:ch + 128].rearrange("k c -> c k"), reads=[], writes=[(ck, 0)], slow=True)
        if conv_b_d is not None:
            self.dma(c[:, 4:5], conv_b_d[l][ch:ch + 128].rearrange("(c o) -> c o", o=1), reads=[], writes=[(ck, 1)],
                     slow=True)
        else:
            self.memset(c[:, 4:5], 0.0, writes=[(ck, 1)])

        def evac(ci, g, ps, pk, n):
            self.copy(raw[:, 3 + g * 512:3 + (g + 1) * 512], ps[:], reads=[pk], writes=[(rawk, g)],
                      eng=("act" if g % 2 else "dve"))

        for g in range(4):
            ps, pk = self.psum.next()
            for kc in range(KC):
                self.mm(ps[:], wb[:, kc, wcol:wcol + 128], self.hT[:, kc, g * 512:(g + 1) * 512], kc == 0, kc == KC - 1,
                        reads=[wk] + self.hT_keys(g), writes=[pk])
            evac(0, g, ps, pk, 128)
        rk = [(rawk, g) for g in range(4)] + [(rawk, "pad")]
        self.ts(acc[:], raw[:, 0:S_LEN], c[:, 0:1], ALU.mult, reads=rk + [(ck, 0), (ck, 1)], writes=[acck],
                s2=c[:, 4:5], op1=ALU.add)
        for k in range(1, 4):
            self.stt(acc[:], raw[:, k:k + S_LEN], c[:, k:k + 1], acc[:], ALU.mult, ALU.add,
                     reads=rk + [(ck, 0), acck], writes=[acck])
        self.act(dst, acc[:], func, reads=[acck], writes=[dstk])

    def conv_P(self, l, wb, wk, wcol, conv_w_d, conv_b_d, ch, cw):
        c, ck = cw.next()
        self.dma(c[:, 0:4], conv_w_d[l][:, ch:ch + 128].rearrange("k c -> c k"), reads=[], writes=[(ck, 0)], slow=True)
        if conv_b_d is not None:
            self.dma(c[:, 4:5], conv_b_d[l][ch:ch + 128].rearrange("(c o) -> c o", o=1), reads=[], writes=[(ck, 1)],
                     slow=True)
        else:
            self.memset(c[:, 4:5], 0.0, writes=[(ck, 1)])
        banks = []
        for g in range(4):
            ps, pk = self.psum.next()
            for kc in range(KC):
                self.mm(ps[:], wb[:, kc, wcol:wcol + 128], self.hT[:, kc, g * 512:(g + 1) * 512], kc == 0, kc == KC - 1,
                        reads=[wk] + self.hT_keys(g), writes=[pk])
            banks.append((ps, pk))
        return dict(c=c, ck=ck, banks=banks)

    def conv_E(self, H, raw, rawk):
        for g, (ps, pk) in enumerate(H["banks"]):
            self.copy(raw[:, 3 + g * 512:3 + (g + 1) * 512], ps[:], reads=[pk], writes=[(rawk, g)],
                      eng=("act" if g % 2 else "dve"))

    def conv_C(self, H, raw, rawk, acc, acck, dst, dstk, func=AF.Silu):
        c, ck = H["c"], H["ck"]
        rk = [(rawk, g) for g in range(4)] + [(rawk, "pad")]
        self.ts(acc[:], raw[:, 0:S_LEN], c[:, 0:1], ALU.mult, reads=rk + [(ck, 0), (ck, 1)], writes=[acck],
                s2=c[:, 4:5], op1=ALU.add)
        for k in range(1, 4):
            self.stt(acc[:], raw[:, k:k + S_LEN], c[:, k:k + 1], acc[:], ALU.mult, ALU.add,
                     reads=rk + [(ck, 0), acck], writes=[acck])
        self.act(dst, acc[:], func, reads=[acck], writes=[dstk])

    def to_tok(self, src, srck, dst_fn, dstk):
        for tq in range(4):
            ps, pk = self.psum.next()
            pb = ps[:].bitcast(BF16)
            for t in range(4):
                tt_ = tq * 4 + t
                self.tr(pb[:, t * 128:(t + 1) * 128], src[:, tt_ * 128:(tt_ + 1) * 128], self.ident_b(),
                        reads=(list(srck) if isinstance(srck, list) else [srck]) + ["cbf"], writes=[pk])
            self.copy(dst_fn(tq), pb[:, 0:512].rearrange("p (t c) -> p t c", t=4), reads=[pk], writes=[(dstk, tq)],
                      eng=("act" if tq % 2 else "dve"))

    def ssm_phase(self, l):
        A, S = self.A, self.S
        scale = self.nw[:, 2 * l, :]
        cst = self.consts
        wdst = A.alloc("wdtst", [128, KC, 16], F32)
        wdt = A.alloc("wdt", [128, KC, 16], BF16)
        self.dma(wdst[:], self.w_in_d[l][:, O_SSMDT:O_SSMDT + 16].rearrange("(k p) n -> p k n", p=128), [], ["wdtst"])
        for kc in range(KC):
            self.act(wdt[:, kc, :], wdst[:, kc, :], AF.Copy, reads=["wdtst", "nw"], writes=["wdt"], scale=scale[:, kc:kc + 1])
        dtb, dtbk = self.bload("dtb", self.ssm_dt_bias_d[l], 16)
        alog, alogk = self.bload("alog", self.ssm_a_log_d[l], 16)
        dbc, dbck = self.bload("dbc", self.ssm_d_d[l], 16)
        dt = A.alloc("dt", [128, 16, 16], F32)
        av = A.alloc("av", [128, 16, 16], F32)
        acum = A.alloc("acum", [128, 16, 16], F32)
        eacum = A.alloc("eacum", [128, 16, 16], F32)
        dtds = A.alloc("dtds", [128, 16, 16], F32)
        eatot = A.alloc("eatot", [128, 32, 16], F32)
        tmp = A.alloc("ptmp", [128, 16, 16], F32)
        ps, pk = self.psum.next()
        for t in range(16):
            for kc in range(KC):
                self.mm(ps[:, t * 16:(t + 1) * 16], self.hT[:, kc, t * 128:(t + 1) * 128], wdt[:, kc, :], kc == 0,
                        kc == KC - 1, reads=["wdt", ("hT", kc, t // 4)], writes=[pk])
        self.tt(dt[:], ps[:, 0:256].rearrange("p (t h) -> p t h", t=16),
                dtb[:].unsqueeze(1).to_broadcast([128, 16, 16]), ALU.add, reads=[pk, dtbk], writes=["dt"])
        one_col = cst[:, C_ONES, 0:1]
        self.act(tmp[:], dt[:], AF.Exp, reads=["dt"], writes=["ptmp"])
        self.act(dt[:], tmp[:], AF.Ln, reads=["ptmp", "consts"], writes=["dt"], bias=1.0, scale=1.0)
        self.act(alog[:], alog[:], AF.Exp, reads=[alogk], writes=[alogk])
        self.S.op("dve", lambda e: e.scalar_tensor_tensor(out=av[:], in0=dt[:], scalar=-1.0,
                                                          in1=alog[:].unsqueeze(1).to_broadcast([128, 16, 16]),
                                                          op0=ALU.mult, op1=ALU.mult),
                  reads=["dt", alogk], writes=["av"])
        ps, pk = self.psum.next()
        for t in range(16):
            self.mm(ps[:, t * 16:(t + 1) * 16], cst[:, C_TRI2, :], av[:, t, :], True, True, reads=["av", "consts"], writes=[pk])
        self.copy(acum[:], ps[:, 0:256].rearrange("p (t h) -> p t h", t=16), reads=[pk], writes=["acum"])
        self.act(eacum[:], acum[:], AF.Exp, reads=["acum"], writes=["eacum"])
        ps, pk = self.psum.next()
        for t in range(16):
            self.mm(ps[:, t * 16:(t + 1) * 16], cst[:, C_BD, :], av[:, t, :], True, True, reads=["av", "consts"], writes=[pk])
        self.tt(tmp[:], ps[:, 0:256].rearrange("p (t h) -> p t h", t=16), acum[:], ALU.subtract, reads=[pk, "acum"],
                writes=["ptmp"])
        self.act(tmp[:], tmp[:], AF.Exp, reads=["ptmp"], writes=["ptmp"])
        self.tt(dtds[:], tmp[:], dt[:], ALU.mult, reads=["ptmp", "dt"], writes=["dtds"])
        ps, pk = self.psum.next()
        for t in range(16):
            for c in range(2):
                j = 2 * t + c
                self.mm(ps[:, j * 16:(j + 1) * 16], cst[:, C_IND0 + c, :], av[:, t, :], True, True,
                        reads=["av", "consts"], writes=[pk])
        self.act(eatot[:].rearrange("p j h -> p (j h)"), ps[:], AF.Exp, reads=[pk], writes=["eatot"])

        mG = A.mark()
        for g in range(4):
            A.reset(mG)
            S.barrier()
            wbf = A.alloc("swz", [128, KC, 256], BF16)
            BT = A.alloc("sBT", [128, S_LEN], BF16)
            CT = A.alloc("sCT", [128, S_LEN], BF16)
            x_tok = A.alloc("sxtok", [128, 16, 256], BF16)
            B_tok = A.alloc("sBtok", [128, 16, 128], BF16)
            nwb, nwbk = self.bload("snwb", self.ssm_norm_w_d[l][256 * g:256 * (g + 1)], 256)
            mA = A.mark()
            wst = self.ring("swst", 2, [128, KC, 128], F32)
            wbx = A.alloc("swbx", [128, KC, 512], BF16)
            raw = A.alloc("sraw", [128, S_LEN + 4], F32)
            acc = A.alloc("sacc", [128, S_LEN], F32)
            xc = self.ring("sxc", 2, [128, S_LEN], BF16)
            cw = self.ring("scw", 2, [128, 8], F32)
            rawk = self.key("sraw")
            self.memset(raw[:, 0:3], 0.0, writes=[(rawk, "pad")])
            cols = [O_SSMZ + 256 * g, O_SSMZ + 256 * g + 128, O_SSMXBC + 256 * g, O_SSMXBC + 256 * g + 128,
                    O_SSMXBC + 1024 + 128 * g, O_SSMXBC + 1536 + 128 * g]
            wk = self.key("swbf")
            for j, c0 in enumerate(cols):
                st, sk = wst.next()
                self.dma(st[:], self.w_in_d[l][:, c0:c0 + 128].rearrange("(k p) n -> p k n", p=128), [], [sk])
                for kc in range(KC):
                    wdst_ = wbf[:, kc, j * 128:(j + 1) * 128] if j < 2 else wbx[:, kc, (j - 2) * 128:(j - 1) * 128]
                    self.act(wdst_, st[:, kc, :], AF.Copy, reads=[sk, "nw"], writes=[(wk, j)],
                             scale=scale[:, kc:kc + 1])
            chs = [256 * g, 256 * g + 128, 1024 + 128 * g, 1536 + 128 * g]
            acck = self.key("sacc")
            xtk = self.key("sxtok")
            btk = self.key("sBtok")
            def s_P(jj):
                return self.conv_P(l, wbx, (wk, jj + 2), jj * 128, self.ssm_conv_w_d, self.ssm_conv_b_d, chs[jj], cw)

            dsts = {}

            def s_C(jj, H_):
                if jj < 2:
                    dst, dk = xc.next()
                elif jj == 2:
                    dst, dk = BT, "sBT"
                else:
                    dst, dk = CT, "sCT"
                self.conv_C(H_, raw, rawk, acc, acck, dst[:], dk)
                dsts[jj] = (dst, dk)

            def s_N(jj):
                dst, dk = dsts[jj]
                if jj < 2:
                    self.to_tok(dst, dk, lambda tq, jj=jj: x_tok[:, tq * 4:(tq + 1) * 4, jj * 128:(jj + 1) * 128], (xtk, jj))
                elif jj == 2:
                    self.to_tok(dst, dk, lambda tq: B_tok[:, tq * 4:(tq + 1) * 4, :], btk)

            Hs = {0: s_P(0)}
            self.conv_E(Hs[0], raw, rawk)
            for jj in range(4):
                if jj + 1 < 4:
                    Hs[jj + 1] = s_P(jj + 1)
                s_C(jj, Hs[jj])
                if jj + 1 < 4:
                    self.conv_E(Hs[jj + 1], raw, rawk)
                s_N(jj)
            S.barrier()
            A.reset(mA)
            xdt = A.alloc("sxdt", [128, 16, 256], BF16)
            xdtd = A.alloc("sxdtd", [128, 16, 256], BF16)
            oT = A.alloc("soT", [128, 2, S_LEN], BF16)
            state = A.alloc("sstate", [128, 256], F32)
            state_bf = A.alloc("sstatebf", [128, 256], BF16)
            abc = self.ring("sabc", 2, [128, 128], F32)
            dm = self.ring("sdm", 2, [128, 4, 128], F32)
            MT = self.ring("sMT", 2, [128, 4, 128], BF16)
            xDr = self.ring("sxD", 2, [128, 256], BF16)
            szr = self.ring("ssz", 2, [128, 256], F32)
            ydr = self.ring("syd", 2, [128, 256], F32)
            sttr = self.ring("sstt", 4, [128, 256], F32)
            t1r = self.ring("st1", 1, [128, 256], F32)
            yr = self.ring("sy", 2, [128, 256], F32)
            jr = self.ring("sjunk", 1, [128, 256], F32)
            ssr = self.ring("sssq", 4, [128, 2], F32)
            obr = self.ring("sob", 2, [128, 256], BF16)
            xtks = [(xtk, jj, tq) for jj in range(2) for tq in range(4)]
            x4 = x_tok[:].rearrange("p t (h c) -> p t h c", h=4)
            hs = slice(4 * g, 4 * g + 4)
            self.tt(xdt[:].rearrange("p t (h c) -> p t h c", h=4), x4,
                    dt[:, :, hs].unsqueeze(3).to_broadcast([128, 16, 4, 64]), ALU.mult, reads=xtks + ["dt"], writes=["sxdt"])
            self.tt(xdtd[:].rearrange("p t (h c) -> p t h c", h=4), x4,
                    dtds[:, :, hs].unsqueeze(3).to_broadcast([128, 16, 4, 64]), ALU.mult, reads=xtks + ["dtds"],
                    writes=["sxdtd"])
            stk = self.key("sstate")
            sbk = self.key("sstatebf")
            self.memset(state[:], 0.0, writes=[stk])
            self.memset(state_bf[:], 0.0, writes=[sbk])
            ones_f = cst[:, C_ONES, :]

            def stageP(t, I):
                tsl = slice(t * 128, (t + 1) * 128)
                xD, xDk = xDr.next()
                self.tt(xD[:].rearrange("p (h c) -> p h c", h=4), x_tok[:, t, :].rearrange("p (h c) -> p h c", h=4),
                        dbc[:, hs].unsqueeze(2).to_broadcast([128, 4, 64]), ALU.mult, reads=xtks + [dbck],
                        writes=[xDk], eng="pool")
                yield
                psS, psSk = self.psum.next()
                self.mm(psS[:, 0:128], BT[:, tsl], CT[:, tsl], True, True, reads=["sBT", "sCT"], writes=[psSk])
                yield
                psD, psDk = self.psum.next()
                for hh in range(4):
                    ab, abk = abc.next()
                    self.ts(ab[:], ones_f, av[:, t, 4 * g + hh:4 * g + hh + 1], ALU.mult, reads=["av", "consts"], writes=[abk])
                    yield
                    self.mm(psD[:, hh * 128:(hh + 1) * 128], ab[:], cst[:, C_TRI2, :], True, False, reads=[abk, "consts"],
                            writes=[psDk])
                    yield
                    self.mm(psD[:, hh * 128:(hh + 1) * 128], cst[:, C_TRI2NEG, :], ab[:], False, True,
                            reads=[abk, "consts"], writes=[psDk])
                    yield
                d_, dk_ = dm.next()
                self.tt(d_[:], psD[:].rearrange("p (h c) -> p h c", h=4),
                        cst[:, C_NEGINCL, :].unsqueeze(1).to_broadcast([128, 4, 128]), ALU.add, reads=[psDk, "consts"],
                        writes=[dk_])
                yield
                self.act(d_[:], d_[:], AF.Exp, reads=[dk_], writes=[dk_])
                yield
                M_, Mk_ = MT.next()
                self.tt(M_[:], d_[:], psS[:, 0:128].unsqueeze(1).to_broadcast([128, 4, 128]), ALU.mult,
                        reads=[dk_, psSk], writes=[Mk_])
                yield
                psY, psYk = self.psum.next()
                for hh in range(4):
                    cs = slice(hh * 64, (hh + 1) * 64)
                    self.mm(psY[:, cs], M_[:, hh, :], xdt[:, t, cs], True, False, reads=[Mk_, "sxdt"], writes=[psYk])
                    yield
                    self.mm(psY[:, cs], self.ident_b(), xD[:, cs], False, True, reads=["cbf", xDk], writes=[psYk])
                    yield
                yd, ydk = ydr.next()
                self.copy(yd[:], psY[:, 0:256], reads=[psYk], writes=[ydk], eng="act")
                yield
                psZ, psZk = self.psum.next()
                for kc in range(KC):
                    self.mm(psZ[:, 0:256], self.hT[:, kc, tsl], wbf[:, kc, 0:256], kc == 0, kc == KC - 1,
                            reads=[(wk, 0), (wk, 1), ("hT", kc, t // 4)], writes=[psZk])
                    yield
                sz, szk = szr.next()
                self.act(sz[:], psZ[:, 0:256], AF.Silu, reads=[psZk], writes=[szk])
                yield
                I["stt"] = []
                for c in range(2):
                    psT, psTk = self.psum.next()
                    self.mm(psT[:, 0:256], B_tok[64 * c:64 * c + 64, t, :], xdtd[64 * c:64 * c + 64, t, :], True, True,
                            reads=[(btk, t // 4), "sxdtd"], writes=[psTk])
                    yield
                    sx, sxk = sttr.next()
                    self.copy(sx[:], psT[:, 0:256], reads=[psTk], writes=[sxk], eng=("act" if c else "dve"))
                    yield
                    I["stt"].append((sx, sxk))
                I.update(yd=yd, ydk=ydk, sz=sz, szk=szk)
                yield

            def stageQ(t, I):
                tsl = slice(t * 128, (t + 1) * 128)
                psO, psOk = self.psum_acc.next()
                for c in range(2):
                    j = 2 * t + c
                    sx, sxk = I["stt"][c]
                    self.mm(psO[64 * c:64 * c + 64, 0:256], CT[:, t * 128 + 64 * c:t * 128 + 64 * c + 64], state_bf[:], True, True,
                            reads=["sCT", sbk], writes=[psOk])
                    yield
                    self.tt(state[:].rearrange("p (h c) -> p h c", h=4), state[:].rearrange("p (h c) -> p h c", h=4),
                            eatot[:, j, hs].unsqueeze(2).to_broadcast([128, 4, 64]), ALU.mult, reads=[stk, "eatot"],
                            writes=[stk])
                    yield
                    self.tt(state[:], state[:], sx[:], ALU.add, reads=[stk, sxk], writes=[stk])
                    yield
                    self.copy(state_bf[:], state[:], reads=[stk], writes=[sbk], eng="act")
                    yield
                t1, t1k = t1r.next()
                self.tt(t1[:].rearrange("p (h c) -> p h c", h=4), psO[:, 0:256].rearrange("p (h c) -> p h c", h=4),
                        eacum[:, t, hs].unsqueeze(2).to_broadcast([128, 4, 64]), ALU.mult, reads=[psOk, "eacum"],
                        writes=[t1k])
                yield
                y, yk = yr.next()
                self.tt(y[:], t1[:], I["yd"][:], ALU.add, reads=[t1k, I["ydk"]], writes=[yk])
                yield
                self.tt(y[:], y[:], I["sz"][:], ALU.mult, reads=[yk, I["szk"]], writes=[yk])
                yield
                jk_, jkk = jr.next()
                ss, ssk = ssr.next()
                self.S.op("act", lambda e, jk_=jk_, y=y, ss=ss: e.activation(out=jk_[:], in_=y[:], func=AF.Square,
                                                                          accum_out=ss[:, 0:1]),
                          reads=[yk], writes=[jkk, ssk])
                yield
                self.act(ss[:, 1:2], ss[:, 0:1], AF.Ln, reads=[ssk, "consts"], writes=[ssk], scale=1.0 / 256,
                         bias=EPS)
                yield
                self.act(ss[:, 1:2], ss[:, 1:2], AF.Exp, reads=[ssk], writes=[ssk], scale=-0.5)
                yield
                ob, obk = obr.next()
                self.stt(ob[:], y[:], ss[:, 1:2], nwb[:], ALU.mult, ALU.mult, reads=[yk, ssk, nwbk], writes=[obk])
                yield
                psX, psXk = self.psum.next()
                pxb = psX[:].bitcast(BF16)
                for ch in range(2):
                    self.tr(pxb[:, ch * 128:(ch + 1) * 128], ob[:, ch * 128:(ch + 1) * 128], self.ident_b(),
                            reads=[obk, "cbf"], writes=[psXk])
                    yield
                self.copy(oT[:, :, tsl], pxb[:, 0:256].rearrange("p (c t) -> p c t", c=2), reads=[psXk],
                          writes=[("soT", t)], eng="act")
                yield

            def run2(ga, gb):
                gens = [g_ for g_ in (ga, gb) if g_ is not None]
                while gens:
                    for g_ in list(gens):
                        try:
                            next(g_)
                        except StopIteration:
                            gens.remove(g_)

            infos = {0: {}}
            run2(stageP(0, infos[0]), None)
            for t in range(16):
                gp = None
                if t + 1 < 16:
                    infos[t + 1] = {}
                    gp = stageP(t + 1, infos[t + 1])
                run2(gp, stageQ(t, infos.pop(t)))
            for ch in range(2):
                oc = 2 * g + ch
                self.dma(self.o_scr[2][oc * 128:(oc + 1) * 128, :], oT[:, ch, :], reads=[("soT", t) for t in range(16)],
                         writes=[("o_scr", 2, oc)])


    def dn_phase(self, l):
        A, S = self.A, self.S
        scale = self.nw[:, 2 * l, :]
        cst = self.consts
        one_col = cst[:, C_ONES, 0:1]
        ones_f = cst[:, C_ONES, :]
        wast = A.alloc("dwast", [128, KC, 16], F32)
        wa = A.alloc("dwa", [128, KC, 16], BF16)
        self.dma(wast[:], self.w_in_d[l][:, O_DNA:O_DNA + 16].rearrange("(k p) n -> p k n", p=128), [], ["dwast"])
        for kc in range(KC):
            self.act(wa[:, kc, :], wast[:, kc, :], AF.Copy, reads=["dwast", "nw"], writes=["dwa"], scale=scale[:, kc:kc + 1])
        dtb, dtbk = self.bload("ddtb", self.dn_dt_bias_d[l], 8)
        alog, alogk = self.bload("dalog", self.dn_a_log_d[l], 8)
        gv = A.alloc("dg", [128, 16, 8], F32)
        beta = A.alloc("dbeta", [128, 16, 8], F32)
        gc = A.alloc("dgc", [128, 16, 8], F32)
        egc = self.dn_egc
        bg = A.alloc("dbg", [128, 16, 8], F32)
        kdec = A.alloc("dkdec", [128, 16, 8], F32)
        eglast = self.dn_eglast
        tmp = A.alloc("dtmp", [128, 16, 8], F32)
        ps, pk = self.psum.next()
        for t in range(16):
            for kc in range(KC):
                self.mm(ps[:, t * 16:(t + 1) * 16], self.hT[:, kc, t * 128:(t + 1) * 128], wa[:, kc, :], kc == 0,
                        kc == KC - 1, reads=["dwa", ("hT", kc, t // 4)], writes=[pk])
        pv = ps[:, 0:256].rearrange("p (t h) -> p t h", t=16)
        self.act(beta[:], pv[:, :, 8:16], AF.Exp, reads=[pk], writes=["dbeta"], scale=-1.0)
        self.ts(beta[:], beta[:], 1.0, ALU.add, reads=["dbeta"], writes=["dbeta"])
        self.S.op("dve", lambda e: e.reciprocal(out=beta[:], in_=beta[:]), reads=["dbeta"], writes=["dbeta"])
        self.tt(gv[:], pv[:, :, 0:8], dtb[:].unsqueeze(1).to_broadcast([128, 16, 8]), ALU.add, reads=[pk, dtbk], writes=["dg"])
        self.act(tmp[:], gv[:], AF.Exp, reads=["dg"], writes=["dtmp"])
        self.act(gv[:], tmp[:], AF.Ln, reads=["dtmp", "consts"], writes=["dg"], bias=1.0, scale=1.0)
        self.act(alog[:], alog[:], AF.Exp, reads=[alogk], writes=[alogk])
        self.S.op("dve", lambda e: e.scalar_tensor_tensor(out=gv[:], in0=gv[:], scalar=-1.0,
                                                          in1=alog[:].unsqueeze(1).to_broadcast([128, 16, 8]),
                                                          op0=ALU.mult, op1=ALU.mult),
                  reads=["dg", alogk], writes=["dg"])
        ps, pk = self.psum.next()
        for t in range(16):
            self.mm(ps[:, t * 8:(t + 1) * 8], cst[:, C_TRI2, :], gv[:, t, :], True, True, reads=["dg", "consts"], writes=[pk])
        self.copy(gc[:], ps[:, 0:128].rearrange("p (t h) -> p t h", t=16), reads=[pk], writes=["dgc"])
        self.act(egc[:], gc[:], AF.Exp, reads=["dgc"], writes=["degc"])
        self.tt(bg[:], egc[:], beta[:], ALU.mult, reads=["degc", "dbeta"], writes=["dbg"])
        ps, pk = self.psum.next()
        for t in range(16):
            self.mm(ps[:, t * 8:(t + 1) * 8], cst[:, C_BD, :], gv[:, t, :], True, True, reads=["dg", "consts"], writes=[pk])
        self.tt(kdec[:], ps[:, 0:128].rearrange("p (t h) -> p t h", t=16), gc[:], ALU.subtract, reads=[pk, "dgc"],
                writes=["dkdec"])
        self.act(kdec[:], kdec[:], AF.Exp, reads=["dkdec"], writes=["dkdec"])
        ps, pk = self.psum.next()
        for t in range(16):
            for c in range(2):
                j = 2 * t + c
                self.mm(ps[:, j * 8:(j + 1) * 8], cst[:, C_IND0 + c, :], gv[:, t, :], True, True,
                        reads=["dg", "consts"], writes=[pk])
        self.act(eglast[:].rearrange("p j h -> p (j h)"), ps[:, 0:256], AF.Exp, reads=[pk], writes=["deglast"])

        mG = A.mark()
        for h in range(8):
            A.reset(mG)
            S.barrier()
            wg = A.alloc("dwg", [128, KC, 128], BF16)
            qTn = A.alloc("dqTn", [128, S_LEN], BF16)
            kTn = A.alloc("dkTn", [128, S_LEN], BF16)
            k_tok = A.alloc("dktok", [128, 16, 128], BF16)
            v_tok = A.alloc("dvtok", [128, 16, 128], BF16)
            mA = A.mark()
            wst = self.ring("dwst", 2, [128, KC, 128], F32)
            wbx = A.alloc("dwbx", [128, KC, 384], BF16)
            raws = [A.alloc("draw", [128, S_LEN + 4], F32) for _ in range(2)]
            accs = [A.alloc("dacc", [128, S_LEN], F32) for _ in range(2)]
            vT = A.alloc("dvT", [128, S_LEN], BF16)
            cw = self.ring("dcw", 2, [128, 8], F32)
            sqr = self.ring("dsq", 2, [128, 512], BF16)
            rsr = self.ring("drs", 2, [128, 512], F32)
            rawks = [self.key("draw"), self.key("draw")]
            for r_, rk__ in zip(raws, rawks):
                self.memset(r_[:, 0:3], 0.0, writes=[(rk__, "pad")])
            cols = [O_DNQKV + 128 * h, O_DNQKV + 1024 + 128 * h, O_DNQKV + 2048 + 128 * h, O_DNGATE + 128 * h]
            wk = self.key("dwb")
            for j, c0 in enumerate(cols):
                st, sk = wst.next()
                self.dma(st[:], self.w_in_d[l][:, c0:c0 + 128].rearrange("(k p) n -> p k n", p=128), [], [sk])
                for kc in range(KC):
                    wd_ = wbx[:, kc, j * 128:(j + 1) * 128] if j < 3 else wg[:, kc, :]
                    self.act(wd_, st[:, kc, :], AF.Copy, reads=[sk, "nw"], writes=[(wk, j)], scale=scale[:, kc:kc + 1])
            accks = [self.key("dacc"), self.key("dacc")]
            ktk = self.key("dktok")
            vtk = self.key("dvtok")
            def dn_P(j):
                return self.conv_P(l, wbx, (wk, j), j * 128, self.dn_conv_w_d, None, j * 1024 + 128 * h, cw)

            def dn_C(j, H):
                raw, rawk, acc, acck = raws[j % 2], rawks[j % 2], accs[j % 2], accks[j % 2]
                if j == 2:
                    self.conv_C(H, raw, rawk, acc, acck, vT[:], "dvT")
                else:
                    self.conv_C(H, raw, rawk, acc, acck, acc[:], acck)

            def dn_N(j):
                acc, acck = accs[j % 2], accks[j % 2]
                if j == 2:
                    self.to_tok(vT, "dvT", lambda tq: v_tok[:, tq * 4:(tq + 1) * 4, :], vtk)
                    return
                dst, dk = (qTn, "dqTn") if j == 0 else (kTn, "dkTn")
                for g in range(4):
                    sl = slice(g * 512, (g + 1) * 512)
                    q_, qk_ = sqr.next()
                    self.act(q_[:], acc[:, sl], AF.Square, reads=[acck], writes=[qk_])
                    ps, pk = self.psum.next()
                    self.mm(ps[:], self.ones_b(), q_[:], True, True, reads=[qk_, "cbf"], writes=[pk])
                    r_, rk_ = rsr.next()
                    self.act(r_[:], ps[:], AF.Ln, reads=[pk], writes=[rk_], scale=1.0, bias=EPS)
                    self.act(r_[:], r_[:], AF.Exp, reads=[rk_], writes=[rk_], scale=-0.5)
                    if j == 0:
                        self.stt(dst[:, sl], acc[:, sl], 128.0 ** -0.5, r_[:], ALU.mult, ALU.mult, reads=[acck, rk_],
                                 writes=[(dk, g)])
                    else:
                        self.tt(dst[:, sl], acc[:, sl], r_[:], ALU.mult, reads=[acck, rk_], writes=[(dk, g)])
                if j == 1:
                    self.to_tok(kTn, [("dkTn", g) for g in range(4)], lambda tq: k_tok[:, tq * 4:(tq + 1) * 4, :], ktk)

            H = {0: dn_P(0)}
            self.conv_E(H[0], raws[0], rawks[0])
            for j in range(3):
                if j + 1 < 3:
                    H[j + 1] = dn_P(j + 1)
                dn_C(j, H[j])
                if j + 1 < 3:
                    self.conv_E(H[j + 1], raws[(j + 1) % 2], rawks[(j + 1) % 2])
                dn_N(j)
            S.barrier()
            A.reset(mA)
            attnT = A.alloc("dattnT", [128, 16, 128], BF16)
            P = A.alloc("dP", [128, 16, 128], BF16)
            PL = A.alloc("dPL", [128, 16, 128], BF16)
            R32 = A.alloc("dR32", [128, 16, 128], F32)
            Rb = A.alloc("dRb", [128, 16, 128], BF16)
            vb = v_tok
            kbg = k_tok
            kd = A.alloc("dkd", [128, 16, 128], BF16)
            u = A.alloc("du", [128, 16, 128], BF16)
            wT = A.alloc("dwT", [128, 16, 128], BF16)
            sgt = A.alloc("dsgt", [128, 16, 128], BF16)
            abr = self.ring("dab", 4, [128, 128], F32)
            dcr = self.ring("ddec", 1, [128, 4, 128], F32)
            tmr = self.ring("dtm", 1, [128, 4, 128], F32)
            bmr = self.ring("dbm", 1, [128, 4, 128], F32)
            ktks = [(ktk, tq) for tq in range(4)]
            vtks = [(vtk, tq) for tq in range(4)]
            bc3 = lambda t_: t_[:, :, h:h + 1].to_broadcast([128, 16, 128])
            self.tt(vb[:], v_tok[:], bc3(beta), ALU.mult, reads=vtks + ["dbeta"], writes=vtks + ["dvb"])
            self.tt(kd[:], k_tok[:], bc3(kdec), ALU.mult, reads=ktks + ["dkdec"], writes=["dkd"])
            self.tt(kbg[:], k_tok[:], bc3(bg), ALU.mult, reads=ktks + ["dbg", "dkd"], writes=ktks + ["dkbg"])
            kTk = [("dkTn", g) for g in range(4)]
            identf4 = cst[:, C_IDENT, :].unsqueeze(1).to_broadcast([128, 4, 128])
            for q in range(4):
                psK, psKk = self.psum.next()
                psQ, psQk = self.psum.next()
                psD, psDk = self.psum.next()
                psB, psBk = self.psum.next()
                for i4 in range(4):
                    t = 4 * q + i4
                    tsl = slice(t * 128, (t + 1) * 128)
                    cs = slice(i4 * 128, (i4 + 1) * 128)
                    self.mm(psK[:, cs], kTn[:, tsl], kTn[:, tsl], True, True, reads=[("dkTn", q)], writes=[psKk])
                    self.mm(psQ[:, cs], kTn[:, tsl], qTn[:, tsl], True, True, reads=[("dkTn", q), ("dqTn", q)], writes=[psQk])
                    ab, abk = abr.next()
                    self.ts(ab[:], ones_f, gv[:, t, h:h + 1], ALU.mult, reads=["dg", "consts"], writes=[abk])
                    self.mm(psD[:, cs], ab[:], cst[:, C_TRI2, :], True, False, reads=[abk, "consts"], writes=[psDk])
                    self.mm(psD[:, cs], cst[:, C_TRI2NEG, :], ab[:], False, True, reads=[abk, "consts"], writes=[psDk])
                    db, dbk = abr.next()
                    self.ts(db[:], cst[:, C_IDENT, :], beta[:, t, h:h + 1], ALU.mult, reads=["dbeta", "consts"], writes=[dbk])
                    self.mm(psB[:, cs], ones_f, db[:], True, True, reads=[dbk, "consts"], writes=[psBk])
                dc, dck = dcr.next()
                self.tt(dc[:], psD[:].rearrange("p (a b) -> p a b", a=4),
                        cst[:, C_NEGINCL, :].unsqueeze(1).to_broadcast([128, 4, 128]), ALU.add, reads=[psDk, "consts"],
                        writes=[dck])
                self.act(dc[:], dc[:], AF.Exp, reads=[dck], writes=[dck])
                self.tt(attnT[:, 4 * q:4 * q + 4, :], psQ[:].rearrange("p (a b) -> p a b", a=4), dc[:], ALU.mult,
                        reads=[psQk, dck], writes=[("dattnT", q)])
                bm, bmk = bmr.next()
                self.tt(bm[:], psB[:].rearrange("p (a b) -> p a b", a=4),
                        cst[:, C_MSTRICT2, :].unsqueeze(1).to_broadcast([128, 4, 128]), ALU.mult, reads=[psBk, "consts"],
                        writes=[bmk])
                tm, tmk = tmr.next()
                self.tt(tm[:], psK[:].rearrange("p (a b) -> p a b", a=4), dc[:], ALU.mult, reads=[psKk, dck], writes=[tmk])
                self.tt(P[:, 4 * q:4 * q + 4, :], tm[:], bm[:], ALU.mult, reads=[tmk, bmk], writes=[("dP", q)])
                psT, psTk = self.psum.next()
                ptb = psT[:].bitcast(BF16)
                for i4 in range(4):
                    self.tr(ptb[:, i4 * 128:(i4 + 1) * 128], P[:, 4 * q + i4, :], self.ident_b(), reads=[("dP", q), "cbf"],
                            writes=[psTk])
                self.copy(PL[:, 4 * q:4 * q + 4, :], ptb[:, 0:512].rearrange("p (a b) -> p a b", a=4), reads=[psTk],
                          writes=[("dPL", q)], eng="act")
                self.tt(R32[:, 4 * q:4 * q + 4, :], identf4, P[:, 4 * q:4 * q + 4, :], ALU.subtract,
                        reads=[("dP", q), "consts"], writes=[("dR32", q)])
                self.copy(Rb[:, 4 * q:4 * q + 4, :], R32[:, 4 * q:4 * q + 4, :], reads=[("dR32", q)], writes=[("dRb", q)],
                          eng="act")
            for lev in range(5):
                last = lev == 4
                for q in range(4):
                    if not last:
                        psP, psPk = self.psum.next()
                    psL, psLk = self.psum.next()
                    for i4 in range(4):
                        t = 4 * q + i4
                        cs = slice(i4 * 128, (i4 + 1) * 128)
                        if not last:
                            self.mm(psP[:, cs], PL[:, t, :], P[:, t, :], True, True, reads=[("dPL", q), ("dP", q)],
                                    writes=[psPk])
                        self.mm(psL[:, cs], P[:, t, :], PL[:, t, :], True, True, reads=[("dPL", q), ("dP", q)],
                                writes=[psLk])
                    if not last:
                        self.copy(P[:, 4 * q:4 * q + 4, :], psP[:].rearrange("p (a b) -> p a b", a=4), reads=[psPk],
                                  writes=[("dP", q)], eng="dve")
                    self.copy(PL[:, 4 * q:4 * q + 4, :], psL[:].rearrange("p (a b) -> p a b", a=4), reads=[psLk],
                              writes=[("dPL", q)], eng="act")
                for q in range(4):
                    psR, psRk = self.psum.next()
                    for i4 in range(4):
                        t = 4 * q + i4
                        cs = slice(i4 * 128, (i4 + 1) * 128)
                        self.mm(psR[:, cs], PL[:, t, :], Rb[:, t, :], True, True, reads=[("dPL", q), ("dRb", q)],
                                writes=[psRk])
                    self.tt(R32[:, 4 * q:4 * q + 4, :], R32[:, 4 * q:4 * q + 4, :],
                            psR[:].rearrange("p (a b) -> p a b", a=4), ALU.add, reads=[psRk, ("dR32", q)],
                            writes=[("dR32", q)])
                    self.copy(Rb[:, 4 * q:4 * q + 4, :], R32[:, 4 * q:4 * q + 4, :], reads=[("dR32", q)],
                              writes=[("dRb", q)], eng="act")
            for q in range(4):
                psU, psUk = self.psum.next()
                psW, psWk = self.psum.next()
                for i4 in range(4):
                    t = 4 * q + i4
                    cs = slice(i4 * 128, (i4 + 1) * 128)
                    self.mm(psU[:, cs], Rb[:, t, :], vb[:, t, :], True, True, reads=[("dRb", q), "dvb"], writes=[psUk])
                    self.mm(psW[:, cs], kbg[:, t, :], Rb[:, t, :], True, True, reads=[("dRb", q), "dkbg"], writes=[psWk])
                self.copy(u[:, 4 * q:4 * q + 4, :], psU[:].rearrange("p (a b) -> p a b", a=4), reads=[psUk],
                          writes=[("du", q)], eng="dve")
                self.copy(wT[:, 4 * q:4 * q + 4, :], psW[:].rearrange("p (a b) -> p a b", a=4), reads=[psWk],
                          writes=[("dwT", q)], eng="act")
            for q in range(4):
                psG, psGk = self.psum.next()
                for i4 in range(4):
                    t = 4 * q + i4
                    for kc in range(KC):
                        self.mm(psG[:, i4 * 128:(i4 + 1) * 128], self.hT[:, kc, t * 128:(t + 1) * 128], wg[:, kc, :],
                                kc == 0, kc == KC - 1, reads=[(wk, 3), ("hT", kc, q)], writes=[psGk])
                self.act(sgt[:, 4 * q:4 * q + 4, :], psG[:].rearrange("p (a b) -> p a b", a=4), AF.Silu, reads=[psGk],
                         writes=[("dsgt", q)])
            fl = lambda t_: t_[:].rearrange("p a b -> p (a b)")
            for nm, src, keys in (("u", fl(u), [("du", q) for q in range(4)]),
                                  ("wT", fl(wT), [("dwT", q) for q in range(4)]),
                                  ("q", qTn[:], [("dqTn", g) for g in range(4)]),
                                  ("attnT", fl(attnT), [("dattnT", q) for q in range(4)]),
                                  ("kd", fl(kd), ["dkd"]),
                                  ("sg", fl(sgt), [("dsgt", q) for q in range(4)])):
                self.dma(self.dn_scr[nm][h], src, reads=keys, writes=[("dn_scr", nm, h)])

    def final_out(self):
        A, S = self.A, self.S
        m = A.mark()
        sq = self.ring("fsq", 3, [128, 512], BF16)
        rs = self.ring("frs", 2, [128, 512], F32)
        hf = self.ring("hf", 3, [128, 512], F32)
        ost = self.ring("ost", 2, [128, 4, D], F32)
        for g in range(4):
            ps, pk = self.psum.next()
            sl = slice(g * 512, (g + 1) * 512)
            for kc in range(KC):
                q, qk = sq.next()
                self.act(q[:], self.xT[:, kc, sl], AF.Square, reads=[("xT", kc, g)], writes=[qk])
                self.mm(ps[:], self.ones_b(), q[:], kc == 0, kc == KC - 1, reads=[qk, "cbf"], writes=[pk])
            r, rk = rs.next()
            self.act(r[:], ps[:], AF.Ln, reads=[pk], writes=[rk], scale=1.0 / D, bias=EPS)
            self.act(r[:], r[:], AF.Exp, reads=[rk], writes=[rk], scale=-0.5)
            o, ok = ost.next()
            for kc in range(KC):
                h, hk = hf.next()
                self.stt(h[:], self.xT[:, kc, sl], self.nw[:, 2 * DEPTH, kc:kc + 1], r[:], ALU.mult, ALU.mult,
                         reads=[("xT", kc, g), rk, "nw"], writes=[hk])
                ps2, pk2 = self.psum.next()
                for t in range(4):
                    self.tr(ps2[:, t * 128:(t + 1) * 128], h[:, t * 128:(t + 1) * 128], self.ident_f(),
                            reads=[hk, "consts"], writes=[pk2])
                self.copy(o[:, :, kc * 128:(kc + 1) * 128], ps2[:].rearrange("p (t c) -> p t c", t=4),
                          reads=[pk2], writes=[ok], eng=("act" if kc % 2 else "dve"))
            self.dma(self.out_d[g * 512:(g + 1) * 512, :].rearrange("(t p) d -> p t d", p=128), o[:],
                     reads=[ok], writes=[("out", g)])
        S.barrier()
        A.reset(m)


C_IDENT, C_ONES, C_EPS, C_MSTRICT, C_TINC = 0, 1, 2, 3, 4
C_TRI2, C_TRI2NEG, C_BD, C_IND0, C_IND1, C_NEGINCL, C_NEGSB, C_MINCL2, C_MSTRICT2, C_MSTRICT2T = 5, 6, 7, 8, 9, 10, 11, 12, 13, 14
NCONST = 15
NEG = -30000.0


def make_consts():
    c = np.zeros((128, NCONST, 128), np.float32)
    c[:, C_IDENT, :] = np.eye(128, dtype=np.float32)
    c[:, C_ONES, :] = 1.0
    c[:, C_EPS, :] = EPS
    ii = np.arange(128)
    c[:, C_MSTRICT, :] = (ii[:, None] < ii[None, :]).astype(np.float32)
    c[:, C_TINC, :] = (ii[:, None] >= ii[None, :]).astype(np.float32)
    same = (ii[:, None] // 64) == (ii[None, :] // 64)
    le = ii[:, None] <= ii[None, :]
    lt = ii[:, None] < ii[None, :]
    c[:, C_TRI2, :] = (same & le).astype(np.float32)
    c[:, C_TRI2NEG, :] = -c[:, C_TRI2, :]
    c[:, C_BD, :] = same.astype(np.float32)
    c[:, C_IND0, :] = (ii[:, None] < 64).astype(np.float32) * np.ones((1, 128), np.float32)
    c[:, C_IND1, :] = (ii[:, None] >= 64).astype(np.float32) * np.ones((1, 128), np.float32)
    c[:, C_NEGINCL, :] = np.where(same & le, 0.0, NEG)
    c[:, C_NEGSB, :] = np.where(lt, 0.0, 8.0 * NEG)
    c[:, C_MINCL2, :] = (same & le).astype(np.float32)
    c[:, C_MSTRICT2, :] = (same & lt).astype(np.float32)
    c[:, C_MSTRICT2T, :] = (same & lt).T.astype(np.float32)
    return c.reshape(128, NCONST * 128)


_CACHE = {}
_RUN_KW = {}


def get_nc(**kw):
    key = tuple(sorted((k, str(v)) for k, v in kw.items()))
    if key not in _CACHE:
        b = Builder(**kw)
        b.build()
        _CACHE[key] = b
    return _CACHE[key]


def run(inputs, **kw):
    b = get_nc(**kw)
    consts = make_consts()
    common = {
        "consts": consts,
        "w_in": np.ascontiguousarray(inputs["w_in"], dtype=np.float32),
        "norm_mix": np.ascontiguousarray(inputs["norm_mix"], dtype=np.float32),
        "norm_mlp": np.ascontiguousarray(inputs["norm_mlp"], dtype=np.float32),
        "norm_final": np.ascontiguousarray(inputs["norm_final"], dtype=np.float32).reshape(1, D),
        "w_branch": np.ascontiguousarray(inputs["w_branch"], dtype=np.float32),
        "w_out": np.ascontiguousarray(inputs["w_out"], dtype=np.float32),
        "w_up": np.ascontiguousarray(inputs["w_up"], dtype=np.float32),
        "w_down": np.ascontiguousarray(inputs["w_down"], dtype=np.float32),
    }
    for nm in ("ssm_conv_w", "ssm_conv_b", "ssm_a_log", "ssm_dt_bias", "ssm_d", "ssm_norm_w", "dn_conv_w", "dn_a_log",
               "dn_dt_bias", "dn_norm_w"):
        common[nm] = np.ascontiguousarray(inputs[nm], dtype=np.float32)
    x = np.asarray(inputs["x"], dtype=np.float32)
    in_maps = []
    for c in range(NCORES):
        m = dict(common)
        m["x"] = np.ascontiguousarray(x[c])
        in_maps.append(m)
    res = run_bass_kernel_spmd(b.nc, in_maps, core_ids=list(range(NCORES)), **_RUN_KW)
    return res


def kernel(**inputs):
    res = run(inputs)
    out = np.stack([np.asarray(r["out"], dtype=np.float32) for r in res.results], axis=0)
    return out
```

```python
import numpy as np
import concourse.bass as bass
import concourse.mybir as mybir
from concourse.bass_utils import run_bass_kernel_spmd

F32 = mybir.dt.float32
BF16 = mybir.dt.bfloat16
AF = mybir.ActivationFunctionType
ALU = mybir.AluOpType

S_LEN = 2048
D = 1024
KC = 8
NCORES = 8
DEPTH = 2
D_FF = 4096
EPS = 1e-6
IN_SIZES = (3072, 1024, 8, 8, 3072, 1024, 2048, 16, 3072)
IN_DIM = sum(IN_SIZES)
OFF = [0]
for _s in IN_SIZES:
    OFF.append(OFF[-1] + _s)
(O_DNQKV, O_DNGATE, O_DNA, O_DNB, O_SBQKV, O_SSMZ, O_SSMXBC, O_SSMDT, O_GATE, _) = OFF


class _Op:
    __slots__ = ("id", "eng", "fn", "dma", "waits", "idx", "sig", "reuse", "seen")


class Sched:
    ENG = ("pe", "act", "dve", "pool", "sp")
    NSLOT = 12
    SEM_LIMIT = 20000

    def __init__(self, nc):
        self.nc = nc
        self.ops = []
        self.by_eng = {e: [] for e in self.ENG}
        self.kstate = {}
        self.seen = {e: {p: -1 for p in self.ENG} for e in self.ENG}
        self.seen_dma = {e: set() for e in self.ENG}
        self.pending = {e: set() for e in self.ENG}
        self.open_dma = []

    def op(self, eng, fn, reads=(), writes=(), dma=False):
        o = _Op()
        o.id = len(self.ops)
        o.eng = eng
        o.fn = fn
        o.dma = dma
        o.sig = None
        o.reuse = None
        deps = {}
        for k in reads:
            st = self.kstate.get(k)
            if st is not None and st[0] is not None:
                deps[st[0]] = True
        for k in writes:
            st = self.kstate.get(k)
            if st is not None:
                if st[0] is not None:
                    deps.setdefault(st[0], False)
                for r in st[1].values():
                    deps.setdefault(r, False)
                for r in st[2]:
                    deps.setdefault(r, False)
        for d in self.pending[eng]:
            deps[d] = True
        self.pending[eng] = set()
        o.idx = len(self.by_eng[eng])
        seen = self.seen[eng]
        best = {}
        waits = []
        for d, raw in deps.items():
            p = self.ops[d]
            if p.dma:
                if d in self.seen_dma[eng]:
                    continue
                self.seen_dma[eng].add(d)
                waits.append(d)
            else:
                if p.eng == eng and not dma:
                    if eng == "pe" or (not raw and eng != "pool"):
                        continue
                if p.idx <= seen[p.eng]:
                    continue
                if p.eng not in best or self.ops[best[p.eng]].idx < p.idx:
                    best[p.eng] = d
        for pe, d in best.items():
            p = self.ops[d]
            waits.append(d)
            for e2, v in p.seen.items():
                if v > seen[e2]:
                    seen[e2] = v
            if p.idx > seen[pe]:
                seen[pe] = p.idx
        o.waits = waits
        o.seen = dict(seen)
        for k in reads:
            st = self.kstate.get(k)
            if st is None:
                st = [None, {}, []]
                self.kstate[k] = st
            if dma:
                st[2].append(o.id)
            else:
                st[1][eng] = o.id
        for k in writes:
            self.kstate[k] = [o.id, {}, []]
        self.ops.append(o)
        self.by_eng[eng].append(o)
        if dma:
            self.open_dma.append(o.id)
        return o

    def barrier(self):
        last = []
        for e in self.ENG:
            for o in reversed(self.by_eng[e]):
                if o.dma:
                    break
                if o.fn is not None:
                    last.append(o.id)
                    break
        for e in self.ENG:
            self.pending[e] |= set(last) | set(self.open_dma[-self.NSLOT:])
        self.open_dma = self.open_dma[-self.NSLOT:]

    def finalize(self):
        nc = self.nc
        needed = set()
        for o in self.ops:
            needed.update(o.waits)
        for e in self.ENG:
            sem = None
            cnt = 0
            ndma = 0
            slots = None
            for o in self.by_eng[e]:
                if o.dma:
                    if slots is None:
                        slots = [nc.alloc_semaphore(f"dq_{e}_{i}") for i in range(self.NSLOT)]
                    s = ndma % self.NSLOT
                    r = ndma // self.NSLOT
                    o.sig = (slots[s], 16 * (r + 1))
                    if r > 0:
                        o.reuse = (slots[s], 16 * r)
                    ndma += 1
                elif o.id in needed:
                    if sem is None or cnt >= self.SEM_LIMIT:
                        sem = nc.alloc_semaphore(f"pg_{e}_{o.id}")
                        cnt = 0
                    cnt += 1
                    o.sig = (sem, cnt)

    def emit(self, ename, eng):
        ops = self.ops
        for o in self.by_eng[ename]:
            for d in o.waits:
                s, v = ops[d].sig
                eng.wait_ge(s, v)
            if o.reuse is not None:
                eng.wait_ge(o.reuse[0], o.reuse[1])
            if o.fn is None:
                continue
            ins = o.fn(eng)
            if o.sig is not None:
                ins.then_inc(o.sig[0], 16 if o.dma else 1)


class Arena:
    def __init__(self, nc, limit=208 * 1024):
        self.nc = nc
        self.off = 16 * 1024
        self.limit = limit
        self.n = 0
        self.peak = 0

    def alloc(self, name, shape, dtype):
        per = 1
        for s in shape[1:]:
            per *= s
        nbytes = per * (4 if dtype == F32 else 2)
        nbytes = (nbytes + 63) // 64 * 64
        assert self.off + nbytes <= self.limit, f"SBUF arena overflow at {name}: {self.off}+{nbytes}"
        self.n += 1
        t = self.nc.alloc_sbuf_tensor_at(f"{name}_{self.n}", list(shape), dtype, offset=self.off)
        self.off += nbytes
        self.peak = max(self.peak, self.off)
        return t

    def mark(self):
        return self.off

    def reset(self, m):
        self.off = m


class Ring:
    def __init__(self, tiles, name):
        self.tiles = tiles
        self.name = name
        self.i = 0

    def next(self):
        j = self.i % len(self.tiles)
        self.i += 1
        return self.tiles[j], (self.name, j)


class Builder:
    def __init__(self, nlayers=DEPTH, mixers=("dn", "sb", "ssm"), do_mlp=True, debug=()):
        self.nlayers = nlayers
        self.mixers = mixers
        self.do_mlp = do_mlp
        self.debug = debug
        nc = bass.Bass("TRN2", target_bir_lowering=False)
        self.nc = nc
        self.S = Sched(nc)
        self.A = Arena(nc)
        self.uid = 0

    def dram_in(self, name, shape, dtype=F32):
        return self.nc.dram_tensor(name, list(shape), dtype, kind="ExternalInput").ap()

    def dram_out(self, name, shape, dtype=F32):
        return self.nc.dram_tensor(name, list(shape), dtype, kind="ExternalOutput").ap()

    def dram_tmp(self, name, shape, dtype):
        return self.nc.dram_tensor(name, list(shape), dtype, kind="Internal").ap()

    def key(self, base):
        self.uid += 1
        return (base, self.uid)

    def ring(self, name, n, shape, dtype):
        return Ring([self.A.alloc(name, shape, dtype) for _ in range(n)], self.key(name))

    def dma(self, out, in_, reads, writes, slow=False):
        if slow:
            fn = lambda e, out=out, in_=in_: e.dma_start(out=out, in_=in_, allow_slow_non_contiguous=True)
        else:
            fn = lambda e, out=out, in_=in_: e.dma_start(out=out, in_=in_)
        return self.S.op("sp", fn, reads=reads, writes=writes, dma=True)

    def mm(self, out, lhsT, rhs, start, stop, reads, writes, **kw):
        return self.S.op(
            "pe",
            lambda e, out=out, lhsT=lhsT, rhs=rhs, start=start, stop=stop, kw=kw: e.matmul(
                out, lhsT=lhsT, rhs=rhs, start=start, stop=stop, **kw
            ),
            reads=reads,
            writes=writes,
        )

    def tr(self, out, in_, ident, reads, writes):
        return self.S.op(
            "pe",
            lambda e, out=out, in_=in_, ident=ident: e.transpose(out, in_, ident),
            reads=reads,
            writes=writes,
        )

    def act(self, out, in_, func, reads, writes, eng="act", **kw):
        return self.S.op(
            eng,
            lambda e, out=out, in_=in_, func=func, kw=kw: e.activation(out=out, in_=in_, func=func, **kw),
            reads=reads,
            writes=writes,
        )

    def tt(self, out, in0, in1, op, reads, writes, eng="dve"):
        return self.S.op(
            eng,
            lambda e, out=out, in0=in0, in1=in1, op=op: e.tensor_tensor(out=out, in0=in0, in1=in1, op=op),
            reads=reads,
            writes=writes,
        )

    def ts(self, out, in0, s1, op0, reads, writes, s2=None, op1=None, eng="dve"):
        def fn(e, out=out, in0=in0, s1=s1, op0=op0, s2=s2, op1=op1):
            if op1 is None:
                return e.tensor_scalar(out=out, in0=in0, scalar1=s1, scalar2=None, op0=op0)
            return e.tensor_scalar(out=out, in0=in0, scalar1=s1, scalar2=s2, op0=op0, op1=op1)

        return self.S.op(eng, fn, reads=reads, writes=writes)

    def stt(self, out, in0, scalar, in1, op0, op1, reads, writes):
        return self.S.op(
            "dve",
            lambda e, out=out, in0=in0, scalar=scalar, in1=in1, op0=op0, op1=op1: e.scalar_tensor_tensor(
                out=out, in0=in0, scalar=scalar, in1=in1, op0=op0, op1=op1
            ),
            reads=reads,
            writes=writes,
        )

    def copy(self, out, in_, reads, writes, eng="dve"):
        if eng == "act":
            return self.act(out, in_, AF.Copy, reads, writes)
        return self.S.op(
            eng, lambda e, out=out, in_=in_: e.tensor_copy(out=out, in_=in_), reads=reads, writes=writes
        )

    def memset(self, ap, val, writes, eng="dve"):
        return self.S.op(eng, lambda e, ap=ap, val=val: e.memset(ap, val), reads=(), writes=writes)

    def build(self):
        nc, S, A = self.nc, self.S, self.A
        L = self.nlayers
        self.x_d = self.dram_in("x", [S_LEN, D])
        self.consts_d = self.dram_in("consts", [128, NCONST * 128])
        self.w_in_d = self.dram_in("w_in", [DEPTH, D, IN_DIM])
        self.norm_mix_d = self.dram_in("norm_mix", [DEPTH, D])
        self.norm_mlp_d = self.dram_in("norm_mlp", [DEPTH, D])
        self.norm_final_d = self.dram_in("norm_final", [1, D])
        self.w_branch_d = self.dram_in("w_branch", [DEPTH, 3, D, D])
        self.w_out_d = self.dram_in("w_out", [DEPTH, D, D])
        self.w_up_d = self.dram_in("w_up", [DEPTH, D, D_FF])
        self.w_down_d = self.dram_in("w_down", [DEPTH, D_FF, D])
        self.ssm_conv_w_d = self.dram_in("ssm_conv_w", [DEPTH, 4, 2048])
        self.ssm_conv_b_d = self.dram_in("ssm_conv_b", [DEPTH, 2048])
        self.ssm_a_log_d = self.dram_in("ssm_a_log", [DEPTH, 16])
        self.ssm_dt_bias_d = self.dram_in("ssm_dt_bias", [DEPTH, 16])
        self.ssm_d_d = self.dram_in("ssm_d", [DEPTH, 16])
        self.ssm_norm_w_d = self.dram_in("ssm_norm_w", [DEPTH, D])
        self.dn_conv_w_d = self.dram_in("dn_conv_w", [DEPTH, 4, 3072])
        self.dn_a_log_d = self.dram_in("dn_a_log", [DEPTH, 8])
        self.dn_dt_bias_d = self.dram_in("dn_dt_bias", [DEPTH, 8])
        self.dn_norm_w_d = self.dram_in("dn_norm_w", [DEPTH, 128])
        self.out_d = self.dram_out("out", [S_LEN, D])
        self.u_scr = self.dram_tmp("u_scr", [D_FF, S_LEN], BF16)
        if self.debug:
            self.o_scr = self.dram_out("o_scr", [3, D, S_LEN], BF16)
        else:
            self.o_scr = self.dram_tmp("o_scr", [3, D, S_LEN], BF16)
        self.g_scr = self.dram_tmp("g_scr", [3, D, S_LEN], BF16)
        self.dn_scr = {nm: self.dram_tmp("dn_" + nm, [8, 128, S_LEN], BF16) for nm in ("u", "wT", "q", "attnT", "kd", "sg")}

        pst = [nc.alloc_psum_tensor(f"ps{i}", [128, 512], F32) for i in range(8)]
        self.pst = pst
        self.psum = Ring(pst[:6], "psum")
        self.psum_acc = Ring(pst[6:], "psacc")

        self.consts = A.alloc("consts", [128, NCONST, 128], F32)
        self.cbf = A.alloc("cbf", [128, NCONST, 128], BF16)
        self.xT = A.alloc("xT", [128, KC, S_LEN], F32)
        self.nw = A.alloc("nw", [128, 2 * DEPTH + 1, KC], F32)
        KCON = "consts"
        self.dma(self.consts[:].rearrange("p a b -> p (a b)"), self.consts_d, reads=[], writes=[KCON])
        self.copy(self.cbf[:], self.consts[:], reads=[KCON], writes=["cbf"])
        for l in range(DEPTH):
            self.dma(self.nw[:, 2 * l, :], self.norm_mix_d[l].rearrange("(k p) -> p k", p=128), [], ["nw"], slow=True)
            self.dma(self.nw[:, 2 * l + 1, :], self.norm_mlp_d[l].rearrange("(k p) -> p k", p=128), [], ["nw"], slow=True)
        self.dma(self.nw[:, 2 * DEPTH, :], self.norm_final_d[0].rearrange("(k p) -> p k", p=128), [], ["nw"], slow=True)

        self.load_x()
        for l in range(L):
            self.layer(l)
        self.final_out()
        S.barrier()
        S.op("sp", None)
        S.finalize()

        with nc.Block() as block:

            @block.tensor
            def _(e):
                S.emit("pe", e)

            @block.scalar
            def _(e):
                S.emit("act", e)

            @block.vector
            def _(e):
                S.emit("dve", e)

            @block.gpsimd
            def _(e):
                S.emit("pool", e)

            @block.sync
            def _(e):
                S.emit("sp", e)

        return nc

    def ident_f(self):
        return self.consts[:, C_IDENT, :]

    def ident_b(self):
        return self.cbf[:, C_IDENT, :]

    def ones_b(self):
        return self.cbf[:, C_ONES, :]

    def load_x(self):
        A, S = self.A, self.S
        m = A.mark()
        stg = self.ring("xstg", 2, [128, 4, D], F32)
        for tg in range(4):
            st, sk = stg.next()
            self.dma(
                st[:],
                self.x_d[tg * 512:(tg + 1) * 512, :].rearrange("(t p) d -> p t d", p=128),
                reads=[],
                writes=[sk],
            )
            for kc in range(KC):
                ps, pk = self.psum.next()
                for t in range(4):
                    self.tr(ps[:, t * 128:(t + 1) * 128], st[:, t, kc * 128:(kc + 1) * 128], self.ident_f(),
                            reads=[sk, "consts"], writes=[pk])
                self.copy(self.xT[:, kc, tg * 512:(tg + 1) * 512], ps[:], reads=[pk], writes=[("xT", kc, tg)],
                          eng=("act" if kc % 2 else "dve"))
        S.barrier()
        A.reset(m)

    def rmsnorm(self):
        A, S = self.A, self.S
        m = A.mark()
        sq = self.ring("sq", 3, [128, 512], BF16)
        rs = self.ring("rs", 2, [128, 512], F32)
        for g in range(4):
            ps, pk = self.psum.next()
            sl = slice(g * 512, (g + 1) * 512)
            for kc in range(KC):
                q, qk = sq.next()
                self.act(q[:], self.xT[:, kc, sl], AF.Square, reads=[("xT", kc, g)], writes=[qk])
                self.mm(ps[:], self.ones_b(), q[:], kc == 0, kc == KC - 1, reads=[qk, "cbf"], writes=[pk])
            r, rk = rs.next()
            self.act(r[:], ps[:], AF.Ln, reads=[pk], writes=[rk], scale=1.0 / D, bias=EPS)
            self.act(r[:], r[:], AF.Exp, reads=[rk], writes=[rk], scale=-0.5)
            for kc in range(KC):
                self.tt(self.hT[:, kc, sl], self.xT[:, kc, sl], r[:], ALU.mult,
                        reads=[("xT", kc, g), rk], writes=[("hT", kc, g)])
        self.rstd_ring = rs
        S.barrier()
        A.reset(m)

    def eps_ap(self):
        return self.consts[:, C_EPS, 0:1]

    def wload(self, w_ap, kcn, n, wst_ring, wbf_ring, scale=None):
        st, sk = wst_ring.next()
        wb, wk = wbf_ring.next()
        self.dma(st[:, :kcn, :n], w_ap.rearrange("(k p) n -> p k n", p=128), reads=[], writes=[sk])
        for kc in range(kcn):
            if scale is not None:
                self.act(wb[:, kc, :n], st[:, kc, :n], AF.Copy, reads=[sk, "nw"], writes=[wk],
                         scale=scale[:, kc:kc + 1])
            else:
                self.act(wb[:, kc, :n], st[:, kc, :n], AF.Copy, reads=[sk], writes=[wk])
        return wb, wk

    def hT_keys(self, g):
        return [("hT", kc, g) for kc in range(KC)]

    def linear_T(self, wb, wk, ncols, kcn, rhs_fn, rhs_keys_fn, evac, groups=range(4)):
        for c0 in range(0, ncols, 128):
            n = min(128, ncols - c0)
            for g in groups:
                ps, pk = self.psum.next()
                for kc in range(kcn):
                    self.mm(ps[:n, :], wb[:, kc, c0:c0 + n], rhs_fn(kc, g), kc == 0, kc == kcn - 1,
                            reads=[wk] + rhs_keys_fn(g), writes=[pk])
                evac(c0 // 128, g, ps, pk, n)

    def mlp(self, l):
        A, S = self.A, self.S
        m = A.mark()
        self.hT = A.alloc("hT", [128, KC, S_LEN], BF16)
        self.rmsnorm()
        wst = self.ring("wst", 2, [128, KC, 512], F32)
        wbf = self.ring("wbf", 2, [128, KC, 512], BF16)
        ub = self.ring("ub", 3, [128, S_LEN], BF16)
        rr = self.ring("rr", 3, [128, 512], F32)
        scale = self.nw[:, 2 * l + 1, :]
        nxt = self.wload(self.w_up_d[l][:, 0:512], KC, 512, wst, wbf, scale=scale)
        for cb in range(D_FF // 512):
            wb, wk = nxt
            if cb + 1 < D_FF // 512:
                nxt = self.wload(self.w_up_d[l][:, (cb + 1) * 512:(cb + 2) * 512], KC, 512, wst, wbf, scale=scale)
            cur = {}

            def evac(ci, g, ps, pk, n, cb=cb, cur=cur):
                if g == 0:
                    cur["u"] = ub.next()
                u, uk = cur["u"]
                r, rk = rr.next()
                self.ts(r[:], ps[:], 0.0, ALU.max, reads=[pk], writes=[rk])
                self.act(u[:, g * 512:(g + 1) * 512], r[:], AF.Square, reads=[rk], writes=[uk])
                if g == 3:
                    f = cb * 4 + ci
                    self.dma(self.u_scr[f * 128:(f + 1) * 128, :], u[:], reads=[uk], writes=[("u_scr", f)])

            self.linear_T(wb, wk, 512, KC, lambda kc, g: self.hT[:, kc, g * 512:(g + 1) * 512],
                          self.hT_keys, evac)
        S.barrier()
        A.reset(m)
        wst = self.ring("wdst", 2, [128, 4, 512], F32)
        wbf = self.ring("wdbf", 2, [128, 32, 512], BF16)
        ur = self.ring("ur", 1, [128, 32, 512], BF16)

        def load_down(half):
            wb, wk = wbf.next()
            for q in range(8):
                st, sk = wst.next()
                self.dma(st[:], self.w_down_d[l][q * 512:(q + 1) * 512, half * 512:(half + 1) * 512]
                         .rearrange("(k p) n -> p k n", p=128), reads=[], writes=[sk])
                for k4 in range(4):
                    self.act(wb[:, q * 4 + k4, :], st[:, k4, :], AF.Copy, reads=[sk], writes=[wk])
            return wb, wk

        nxt = load_down(0)
        for half in range(2):
            wb, wk = nxt
            if half == 0:
                nxt = load_down(1)
            for g in range(4):
                u, uk = ur.next()
                self.dma(u[:], self.u_scr[:, g * 512:(g + 1) * 512].rearrange("(k p) t -> p k t", p=128),
                         reads=[("u_scr", f) for f in range(32)], writes=[uk])
                for ci in range(4):
                    ps, pk = self.psum.next()
                    for kc in range(32):
                        self.mm(ps[:], wb[:, kc, ci * 128:(ci + 1) * 128], u[:, kc, :], kc == 0, kc == 31,
                                reads=[wk, uk], writes=[pk])
                    oc = half * 4 + ci
                    sl = slice(g * 512, (g + 1) * 512)
                    self.tt(self.xT[:, oc, sl], self.xT[:, oc, sl], ps[:], ALU.add,
                            reads=[pk, ("xT", oc, g)], writes=[("xT", oc, g)])
        S.barrier()
        A.reset(m)

    def layer(self, l):
        if self.mixers:
            self.mixer(l)
        if self.do_mlp:
            self.mlp(l)


    def mixer(self, l):
        A, S = self.A, self.S
        m00 = A.mark()
        if "dn" in self.mixers:
            self.dn_egc = A.alloc("degc", [128, 16, 8], F32)
            self.dn_eglast = A.alloc("deglast", [128, 32, 8], F32)
            self.dn_nwb, self.dn_nwbk = self.bload("dnwb", self.dn_norm_w_d[l], 128)
        m0 = A.mark()
        self.hT = A.alloc("hT", [128, KC, S_LEN], BF16)
        self.rmsnorm()
        m1 = A.mark()
        if "sb" in self.mixers:
            self.sb_phase(l)
            S.barrier()
            A.reset(m1)
        if "ssm" in self.mixers:
            self.ssm_phase(l)
            S.barrier()
            A.reset(m1)
        if "dn" in self.mixers:
            self.dn_phase(l)
            S.barrier()
            A.reset(m1)
        self.gates_phase(l)
        S.barrier()
        A.reset(m0)
        if "dn" in self.mixers:
            self.dn_phase2(l)
            S.barrier()
            A.reset(m0)
        self.merge_phase(l)
        S.barrier()
        A.reset(m00)

    def dn_phase2(self, l):
        A, S = self.A, self.S
        cst = self.consts
        pst = self.pst
        NH = 8
        names = ["u", "wT", "q", "attnT", "kd", "sg"]
        sets = [[A.alloc(f"d2{n}", [128, NH, 512], BF16) for n in names] for _ in range(1)]
        Sst = A.alloc("d2S", [128, NH, 128], F32)
        Sbf = A.alloc("d2Sbf", [128, NH, 128], BF16)
        vnr = self.ring("d2vn", 2, [128, NH, 128], BF16)
        t1 = A.alloc("d2t1", [128, NH, 128], F32)
        ot = A.alloc("d2ot", [128, NH, 128], F32)
        sq = A.alloc("d2sq", [128, NH, 128], F32)
        ssq = A.alloc("d2ssq", [128, 2, NH], F32)
        obr = self.ring("d2ob", 2, [128, NH, 128], BF16)
        otl = self.ring("d2otl", 2, [128, NH, 128], BF16)
        egc, eglast, nwb, nwbk = self.dn_egc, self.dn_eglast, self.dn_nwb, self.dn_nwbk
        self.memset(Sst[:], 0.0, writes=[("d2S", 0), ("d2S", 1)])
        self.memset(Sbf[:], 0.0, writes=[("d2Sbf", 0), ("d2Sbf", 1)])
        bO = [(pst[i], ("p2O", i)) for i in range(4)]
        bW = [(pst[4], ("p2W", 0)), (pst[5], ("p2W", 1))]
        bS = [(pst[6], ("p2S", 0)), (pst[7], ("p2S", 1))]
        o_view = self.o_scr[0].rearrange("(h e) s -> e h s", h=NH)
        for q in range(4):
            st = sets[0]
            sk = ("d2set", 0)
            for ni, nm in enumerate(names):
                self.dma(st[ni][:], self.dn_scr[nm][:, :, q * 512:(q + 1) * 512].rearrange("h p t -> p h t"),
                         reads=[("dn_scr", nm, h) for h in range(NH)], writes=[(sk, ni)])
            U, WT, Q, AT, KD, SG = st
            for tl in range(4):
                t = 4 * q + tl
                cs = slice(tl * 128, (tl + 1) * 128)
                vn, vnk = vnr.next()
                for c in range(2):
                    j = 2 * t + c
                    pb = 64 * c
                    prt = slice(pb, pb + 64)
                    ccs = slice(tl * 128 + pb, tl * 128 + pb + 64)
                    for h in range(NH):
                        pw, pwk = bW[h // 4]
                        self.mm(pw[prt, (h % 4) * 128:(h % 4 + 1) * 128], WT[:, h, ccs], Sbf[:, h, :], True, True,
                                reads=[(sk, 1), ("d2Sbf", h // 4)], writes=[pwk])
                    for hb in range(2):
                        pw, pwk = bW[hb]
                        self.tt(vn[prt, 4 * hb:4 * hb + 4, :], U[prt, 4 * hb:4 * hb + 4, cs],
                                pw[prt, :].rearrange("p (a b) -> p a b", a=4), ALU.subtract, reads=[(sk, 0), pwk],
                                writes=[(vnk, c, hb)])
                    for h in range(NH):
                        po, pok = bO[h // 2]
                        co = (h % 2) * 256
                        self.mm(po[prt, co:co + 128], Q[:, h, ccs], Sbf[:, h, :], True, True,
                                reads=[(sk, 2), ("d2Sbf", h // 4)], writes=[(pok, c)])
                        self.mm(po[prt, co + 128:co + 256], AT[prt, h, ccs], vn[prt, h, :], True, True,
                                reads=[(sk, 3), (vnk, c, h // 4)], writes=[(pok, c)])
                        ps_, psk_ = bS[h // 4]
                        self.mm(ps_[:, (h % 4) * 128:(h % 4 + 1) * 128], KD[prt, h, cs], vn[prt, h, :], True, True,
                                reads=[(sk, 4), (vnk, c, h // 4)], writes=[psk_])
                    for hb in range(2):
                        ps_, psk_ = bS[hb]
                        hs = slice(4 * hb, 4 * hb + 4)
                        self.tt(Sst[:, hs, :], Sst[:, hs, :], eglast[:, j, hs].unsqueeze(2).to_broadcast([128, 4, 128]),
                                ALU.mult, reads=[("d2S", hb), "deglast"], writes=[("d2S", hb)])
                        self.tt(Sst[:, hs, :], Sst[:, hs, :], ps_[:].rearrange("p (a b) -> p a b", a=4), ALU.add,
                                reads=[("d2S", hb), psk_], writes=[("d2S", hb)])
                        self.copy(Sbf[:, hs, :], Sst[:, hs, :], reads=[("d2S", hb)], writes=[("d2Sbf", hb)], eng="act")
                for b4 in range(4):
                    po, pok = bO[b4]
                    hs = slice(2 * b4, 2 * b4 + 2)
                    pv = po[:].rearrange("p (a b) -> p a b", a=2)
                    self.tt(t1[:, hs, :], pv[:, :, 0:128], egc[:, t, hs].unsqueeze(2).to_broadcast([128, 2, 128]), ALU.mult,
                            reads=[(pok, 0), (pok, 1), "degc"], writes=[("d2t1", b4)])
                    self.tt(ot[:, hs, :], t1[:, hs, :], pv[:, :, 128:256], ALU.add,
                            reads=[("d2t1", b4), (pok, 0), (pok, 1)], writes=[("d2ot", b4)])
                otk = [("d2ot", b4) for b4 in range(4)]
                self.act(sq[:], ot[:], AF.Square, reads=otk, writes=["d2sq"])
                self.S.op("dve", lambda e: e.tensor_reduce(out=ssq[:, 0, :], in_=sq[:], op=ALU.add,
                                                           axis=mybir.AxisListType.X),
                          reads=["d2sq"], writes=["d2ssq"])
                self.act(ssq[:, 1, :], ssq[:, 0, :], AF.Ln, reads=["d2ssq", "consts"], writes=["d2ssq"], scale=1.0 / 128,
                         bias=EPS)
                self.act(ssq[:, 1, :], ssq[:, 1, :], AF.Exp, reads=["d2ssq"], writes=["d2ssq"], scale=-0.5)
                self.tt(ot[:], ot[:], ssq[:, 1, :].unsqueeze(2).to_broadcast([128, NH, 128]), ALU.mult,
                        reads=otk + ["d2ssq"], writes=otk)
                self.tt(ot[:], ot[:], nwb[:].unsqueeze(1).to_broadcast([128, NH, 128]), ALU.mult, reads=otk + [nwbk],
                        writes=otk)
                ob, obk = obr.next()
                self.tt(ob[:], ot[:], SG[:, :, cs], ALU.mult, reads=otk + [(sk, 5)], writes=[obk])
                px, pxk = bW[0]
                pxb = px[:].bitcast(BF16)
                for h in range(NH):
                    self.tr(pxb[:, h * 128:(h + 1) * 128], ob[:, h, :], self.ident_b(), reads=[obk, "cbf"], writes=[pxk])
                otile, otlk = otl.next()
                self.copy(otile[:], pxb[:].rearrange("p (a b) -> p a b", a=NH), reads=[pxk], writes=[otlk], eng="act")
                self.dma(o_view[:, :, t * 128:(t + 1) * 128], otile[:], reads=[otlk], writes=[("o_scr", 0, "t", t)])

    def branches(self):
        return [i for i, n in enumerate(("dn", "sb", "ssm")) if n in self.mixers]

    def gates_phase(self, l):
        wst = self.ring("gwst", 2, [128, KC, 512], F32)
        wbf = self.ring("gwbf", 2, [128, KC, 512], BF16)
        gb = self.ring("gb", 3, [128, S_LEN], BF16)
        scale = self.nw[:, 2 * l, :]
        blocks = [(i, cb) for i in self.branches() for cb in range(2)]

        def gload(bi):
            i_, cb_ = blocks[bi]
            c0_ = O_GATE + i_ * D + cb_ * 512
            return self.wload(self.w_in_d[l][:, c0_:c0_ + 512], KC, 512, wst, wbf, scale=scale)

        nxt = gload(0)
        for bi_, (i, cb) in enumerate(blocks):
            if True:
                wb, wk = nxt
                if bi_ + 1 < len(blocks):
                    nxt = gload(bi_ + 1)
                cur = {}

                def evac(ci, g, ps, pk, n, cb=cb, i=i, cur=cur):
                    if g == 0:
                        cur["t"] = gb.next()
                    t, tk = cur["t"]
                    self.act(t[:, g * 512:(g + 1) * 512], ps[:], AF.Sigmoid, reads=[pk], writes=[tk])
                    if g == 3:
                        oc = cb * 4 + ci
                        self.dma(self.g_scr[i][oc * 128:(oc + 1) * 128, :], t[:], reads=[tk],
                                 writes=[("g_scr", i, oc)])

                self.linear_T(wb, wk, 512, KC, lambda kc, g: self.hT[:, kc, g * 512:(g + 1) * 512],
                              self.hT_keys, evac)

    def merge_phase(self, l):
        A, S = self.A, self.S
        wst = self.ring("mwst", 1, [128, KC, 512], F32)
        wbf = self.ring("mwbf", 2, [128, KC, 512], BF16)
        mg = A.alloc("mg", [128, KC, 1024], F32)
        mgb = A.alloc("mgb", [128, KC, 1024], BF16)
        ob = self.ring("ob", 1, [128, KC, 1024], BF16)
        gt = self.ring("gt", 3, [128, 1024], BF16)
        self.mtmp = self.ring("mtmp", 3, [128, 512], F32)
        brs = self.branches()
        mseq = []
        for half_ in range(2):
            for i_ in brs:
                for cb_ in range(2):
                    mseq.append(self.w_branch_d[l][i_][:, cb_ * 512:(cb_ + 1) * 512])
            for cb_ in range(2):
                mseq.append(self.w_out_d[l][:, cb_ * 512:(cb_ + 1) * 512])

        def mload(pos):
            if pos >= len(mseq):
                return None
            return self.wload(mseq[pos], KC, 512, wst, wbf)

        mpos = [0]
        mnxt = [mload(0)]
        for half in range(2):
            tsl = slice(half * 1024, (half + 1) * 1024)
            for bi, i in enumerate(brs):
                o, ok = ob.next()
                self.dma(o[:], self.o_scr[i][:, tsl].rearrange("(k p) t -> p k t", p=128),
                         reads=[("o_scr", i, c) for c in range(8)], writes=[ok])
                for cb in range(2):
                    wb, wk = mnxt[0]
                    mnxt[0] = mload(mpos[0] + 1)
                    mpos[0] += 1
                    cur = {}

                    def evac(ci, g, ps, pk, n, cb=cb, i=i, bi=bi, cur=cur, half=half, tsl=tsl):
                        oc = cb * 4 + ci
                        gl = g - 2 * half
                        if gl == 0:
                            cur["g"] = gt.next()
                            t, tk = cur["g"]
                            self.dma(t[:], self.g_scr[i][oc * 128:(oc + 1) * 128, tsl],
                                     reads=[("g_scr", i, oc)], writes=[tk])
                        t, tk = cur["g"]
                        dst = mg[:, oc, gl * 512:(gl + 1) * 512]
                        mk = ("mg", oc, gl)
                        if bi == 0:
                            self.tt(dst, ps[:], t[:, gl * 512:(gl + 1) * 512], ALU.mult,
                                    reads=[pk, tk], writes=[mk])
                        else:
                            tmp, tmk = self.mtmp.next()
                            self.tt(tmp[:], ps[:], t[:, gl * 512:(gl + 1) * 512], ALU.mult,
                                    reads=[pk, tk], writes=[tmk])
                            self.tt(dst, dst, tmp[:], ALU.add, reads=[tmk, mk], writes=[mk], eng="pool")
                        if bi == len(brs) - 1:
                            self.copy(mgb[:, oc, gl * 512:(gl + 1) * 512], dst, reads=[mk],
                                      writes=[("mgb", oc, gl)], eng="act")

                    self.linear_T(wb, wk, 512, KC, lambda kc, g, o=o, half=half: o[:, kc, (g - 2 * half) * 512:(g - 2 * half + 1) * 512],
                                  lambda g, ok=ok: [ok], evac, groups=(2 * half, 2 * half + 1))
            for cb in range(2):
                wb, wk = mnxt[0]
                mnxt[0] = mload(mpos[0] + 1)
                mpos[0] += 1

                def evac2(ci, g, ps, pk, n, cb=cb):
                    oc = cb * 4 + ci
                    sl = slice(g * 512, (g + 1) * 512)
                    self.tt(self.xT[:, oc, sl], self.xT[:, oc, sl], ps[:], ALU.add,
                            reads=[pk, ("xT", oc, g)], writes=[("xT", oc, g)])

                self.linear_T(wb, wk, 512, KC,
                              lambda kc, g, half=half: mgb[:, kc, (g - 2 * half) * 512:(g - 2 * half + 1) * 512],
                              lambda g, half=half: [("mgb", kc, g - 2 * half) for kc in range(KC)], evac2,
                              groups=(2 * half, 2 * half + 1))

    def sb_phase(self, l):
        A, S = self.A, self.S
        R = {}
        R["wst"] = self.ring("sbwst", 1, [128, KC, 384], F32)
        R["wbf"] = self.ring("sbwbf", 2, [128, KC, 384], BF16)
        R["qT"] = self.ring("sbq", 2, [128, 2, S_LEN], BF16)
        for i_, t_ in enumerate(R["qT"].tiles):
            self.memset(t_[:], 0.0, writes=[(R["qT"].name, i_)], eng="pool")
        R["kT"] = self.ring("sbk", 2, [128, S_LEN], BF16)
        R["v"] = self.ring("sbv", 2, [128, 16, 128], BF16)
        R["osb"] = self.ring("sbo", 1, [128, S_LEN], BF16)
        R["e"] = self.ring("sbe", 4, [128, 512], F32)
        R["spb"] = self.ring("sbsp", 4, [128, 512], BF16)
        R["xa"] = self.ring("sbxa", 2, [128, 512], F32)
        R["w"] = self.ring("sbw", 3, [128, 512], BF16)
        R["pa"] = Ring(self.pst[0:4], "psum")
        for hp in range(8):
            self.sb_unit(l, hp, R)

    def sb_unit(self, l, hp, R):
        scale = self.nw[:, 2 * l, :]
        st, sk = R["wst"].next()
        wb, wk = R["wbf"].next()
        for j in range(3):
            base = O_SBQKV + j * 1024 + 128 * hp
            self.dma(st[:, :, j * 128:(j + 1) * 128],
                     self.w_in_d[l][:, base:base + 128].rearrange("(k p) n -> p k n", p=128),
                     reads=[], writes=[(sk, j)])
        for kc in range(KC):
            self.act(wb[:, kc, :], st[:, kc, :], AF.Copy, reads=[(sk, 0), (sk, 1), (sk, 2), "nw"], writes=[wk],
                     scale=scale[:, kc:kc + 1])
        qT, qk = R["qT"].next()
        kT, kk = R["kT"].next()
        v, vk = R["v"].next()

        def evac_qk(ci, g, ps, pk, n):
            if ci == 0:
                self.copy(qT[0:64, 0, g * 512:(g + 1) * 512], ps[0:64, :], reads=[pk, qk], writes=[(qk, g)], eng="act")
                self.copy(qT[64:128, 1, g * 512:(g + 1) * 512], ps[64:128, :], reads=[pk, qk], writes=[(qk, g)], eng="dve")
            else:
                self.copy(kT[:, g * 512:(g + 1) * 512], ps[:], reads=[pk], writes=[(kk, g)],
                          eng=("act" if g % 2 else "dve"))

        self.linear_T(wb, wk, 256, KC, lambda kc, g: self.hT[:, kc, g * 512:(g + 1) * 512], self.hT_keys, evac_qk)
        for tq in range(4):
            ps, pk = self.psum.next()
            for t in range(4):
                tt_ = tq * 4 + t
                for kc in range(KC):
                    self.mm(ps[:, t * 128:(t + 1) * 128], self.hT[:, kc, tt_ * 128:(tt_ + 1) * 128],
                            wb[:, kc, 256:384], kc == 0, kc == KC - 1, reads=[wk, ("hT", kc, tq)], writes=[pk])
            self.copy(v[:, tq * 4:(tq + 1) * 4, :], ps[:].rearrange("p (t c) -> p t c", t=4), reads=[pk],
                      writes=[(vk, tq)], eng=("act" if tq % 2 else "dve"))
        osb, ok = R["osb"].next()
        mstrict = self.consts[:, C_MSTRICT, :]
        tinc = self.cbf[:, C_TINC, :]
        tlow = self.cbf[:, C_MSTRICT, :]
        one_col = self.consts[:, C_ONES, 0:1]
        pst = self.pst
        pa_ring = R["pa"]
        for g in range(4):
            racc = [(pst[4], ("psum", 4)), (pst[5], ("psum", 5))]
            po = [(pst[6], ("psacc", 0)), (pst[7], ("psacc", 1))]
            tiles = []
            for kb in range(4 * g + 3, -1, -1):
                for e in range(2):
                    t0 = max(kb * 128, g * 512)
                    tiles.append(dict(e=e, kb=kb, pb=64 * e, t0=t0, N=(g + 1) * 512 - t0, c0=t0 - g * 512,
                                      diag=kb * 128 >= g * 512, first=(kb == 4 * g + 3), last=(kb == 0)))

            def stage0(T):
                pa, pak = pa_ring.next()
                pb, N, t0, kb = T["pb"], T["N"], T["t0"], T["kb"]
                self.mm(pa[:, :N], kT[:, kb * 128:(kb + 1) * 128], qT[:, T["e"], t0:t0 + N], True, not T["diag"],
                        reads=[(kk, kb // 4), (qk, g)], writes=[pak])
                if T["diag"]:
                    self.mm(pa[:, 0:128], self.ident_b(), self.cbf[:, C_NEGSB, :], False, True, reads=["cbf"], writes=[pak])
                T["pa"], T["pak"] = pa, pak

            def stage1(T):
                N, c0 = T["N"], T["c0"]
                ee, ek = R["e"].next()
                self.act(ee[:, :N], T["pa"][:, :N], AF.Exp, reads=[T["pak"]], writes=[ek], scale=0.125)
                spb, spk = R["spb"].next()
                self.act(spb[:, :N], ee[:, :N], AF.Ln, reads=[ek], writes=[spk], bias=1.0, scale=1.0)
                ra, rak = racc[T["e"]]
                self.mm(ra[:, c0:512], tinc, spb[:, :N], T["first"], False, reads=[spk, "cbf"], writes=[rak],
                        skip_group_check=True)
                T.update(ee=ee, ek=ek, spb=spb, spk=spk)

            def stage2(T):
                N, c0, pb, kb = T["N"], T["c0"], T["pb"], T["kb"]
                ra, rak = racc[T["e"]]
                xa, xk = R["xa"].next()
                self.act(xa[:, :N], ra[:, c0:512], AF.Exp, reads=[rak], writes=[xk], scale=-1.0)
                if not T["last"]:
                    self.mm(ra[:, c0:512], tlow, T["spb"][:, :N], False, True, reads=[T["spk"], "cbf"], writes=[rak],
                            skip_group_check=True)
                w, wwk = R["w"].next()
                self.tt(w[:, :N], T["ee"][:, :N], xa[:, :N], ALU.mult, reads=[T["ek"], xk], writes=[wwk])
                pp, ppk = po[T["e"]]
                self.mm(pp[:, c0:512], v[:, kb, :], w[:, :N], T["first"], T["last"],
                        reads=[(vk, kb // 4), wwk], writes=[ppk], skip_group_check=True)

            n = len(tiles)
            for i in range(min(2, n)):
                stage0(tiles[i])
            for i in range(n + 2):
                if i - 2 >= 0:
                    stage2(tiles[i - 2])
                if i < n:
                    stage1(tiles[i])
                if i + 2 < n:
                    stage0(tiles[i + 2])
            for e in range(2):
                pp, ppk = po[e]
                pb = 64 * e
                self.copy(osb[pb:pb + 64, g * 512:(g + 1) * 512], pp[pb:pb + 64, :], reads=[ppk],
                          writes=[(ok, e, g)], eng="dve")
        self.dma(self.o_scr[1][hp * 128:(hp + 1) * 128, :], osb[:],
                 reads=[(ok, e, g) for e in range(2) for g in range(4)], writes=[("o_scr", 1, hp)])


    def bload(self, name, dram_row_ap, n):
        t = self.A.alloc(name, [128, n], F32)
        k = self.key(name)
        self.dma(t[:], dram_row_ap.partition_broadcast(128), reads=[], writes=[k])
        return t, k

    def conv_silu(self, l, wb, wk, wcol, conv_w_d, conv_b_d, ch, raw, rawk, acc, acck, cw, dst, dstk, func=AF.Silu):
        c, ck = cw.next()
        self.dma(c[:, 0:4], conv_w_d[l][:, ch:ch + 128].rearrange("k c -> c k"), reads=[], writes=[(ck, 0)], slow=True)
        if conv_b_d is not None:
            self.dma(c[:, 4:5], conv_b_d[l][ch:ch + 128].rearrange("(c o) -> c o", o=1), reads=[], writes=[(ck, 1)],
                     slow=True)
        else:
            self.memset(c[:, 4:5], 0.0, writes=[(ck, 1)])

        def evac(ci, g, ps, pk, n):
            self.copy(raw[:, 3 + g * 512:3 + (g + 1) * 512], ps[:], reads=[pk], writes=[(rawk, g)],
                      eng=("act" if g % 2 else "dve"))

        for g in range(4):
            ps, pk = self.psum.next()
            for kc in range(KC):
                self.mm(ps[:], wb[:, kc, wcol:wcol + 128], self.hT[:, kc, g * 512:(g + 1) * 512], kc == 0, kc == KC - 1,
                        reads=[wk] + self.hT_keys(g), writes=[pk])
            evac(0, g, ps, pk, 128)
        rk = [(rawk, g) for g in range(4)] + [(rawk, "pad")]
        self.ts(acc[:], raw[:, 0:S_LEN], c[:, 0:1], ALU.mult, reads=rk + [(ck, 0), (ck, 1)], writes=[acck],
                s2=c[:, 4:5], op1=ALU.add)
        for k in range(1, 4):
            self.stt(acc[:], raw[:, k:k + S_LEN], c[:, k:k + 1], acc[:], ALU.mult, ALU.add,
                     reads=rk + [(ck, 0), acck], writes=[acck])
        self.act(dst, acc[:], func, reads=[acck], writes=[dstk])

    def conv_P(self, l, wb, wk, wcol, conv_w_d, conv_b_d, ch, cw):
        c, ck = cw.next()
        self.dma(c[:, 0:4], conv_w_d[l][:, ch:ch + 128].rearrange("k c -> c k"), reads=[], writes=[(ck, 0)], slow=True)
        if conv_b_d is not None:
            self.dma(c[:, 4:5], conv_b_d[l][ch:ch + 128].rearrange("(c o) -> c o", o=1), reads=[], writes=[(ck, 1)],
                     slow=True)
        else:
            self.memset(c[:, 4:5], 0.0, writes=[(ck, 1)])
        banks = []
        for g in range(4):
            ps, pk = self.psum.next()
            for kc in range(KC):
                self.mm(ps[:], wb[:, kc, wcol:wcol + 128], self.hT[:, kc, g * 512:(g + 1) * 512], kc == 0, kc == KC - 1,
                        reads=[wk] + self.hT_keys(g), writes=[pk])
            banks.append((ps, pk))
        return dict(c=c, ck=ck, banks=banks)

    def conv_E(self, H, raw, rawk):
        for g, (ps, pk) in enumerate(H["banks"]):
            self.copy(raw[:, 3 + g * 512:3 + (g + 1) * 512], ps[:], reads=[pk], writes=[(rawk, g)],
                      eng=("act" if g % 2 else "dve"))

    def conv_C(self, H, raw, rawk, acc, acck, dst, dstk, func=AF.Silu):
        c, ck = H["c"], H["ck"]
        rk = [(rawk, g) for g in range(4)] + [(rawk, "pad")]
        self.ts(acc[:], raw[:, 0:S_LEN], c[:, 0:1], ALU.mult, reads=rk + [(ck, 0), (ck, 1)], writes=[acck],
                s2=c[:, 4:5], op1=ALU.add)
        for k in range(1, 4):
            self.stt(acc[:], raw[:, k:k + S_LEN], c[:, k:k + 1], acc[:], ALU.mult, ALU.add,
                     reads=rk + [(ck, 0), acck], writes=[acck])
        self.act(dst, acc[:], func, reads=[acck], writes=[dstk])

    def to_tok(self, src, srck, dst_fn, dstk):
        for tq in range(4):
            ps, pk = self.psum.next()
            pb = ps[:].bitcast(BF16)
            for t in range(4):
                tt_ = tq * 4 + t
                self.tr(pb[:, t * 128:(t + 1) * 128], src[:, tt_ * 128:(tt_ + 1) * 128], self.ident_b(),
                        reads=(list(srck) if isinstance(srck, list) else [srck]) + ["cbf"], writes=[pk])
            self.copy(dst_fn(tq), pb[:, 0:512].rearrange("p (t c) -> p t c", t=4), reads=[pk], writes=[(dstk, tq)],
                      eng=("act" if tq % 2 else "dve"))

    def ssm_phase(self, l):
        A, S = self.A, self.S
        scale = self.nw[:, 2 * l, :]
        cst = self.consts
        wdst = A.alloc("wdtst", [128, KC, 16], F32)
        wdt = A.alloc("wdt", [128, KC, 16], BF16)
        self.dma(wdst[:], self.w_in_d[l][:, O_SSMDT:O_SSMDT + 16].rearrange("(k p) n -> p k n", p=128), [], ["wdtst"])
        for kc in range(KC):
            self.act(wdt[:, kc, :], wdst[:, kc, :], AF.Copy, reads=["wdtst", "nw"], writes=["wdt"], scale=scale[:, kc:kc + 1])
        dtb, dtbk = self.bload("dtb", self.ssm_dt_bias_d[l], 16)
        alog, alogk = self.bload("alog", self.ssm_a_log_d[l], 16)
        dbc, dbck = self.bload("dbc", self.ssm_d_d[l], 16)
        dt = A.alloc("dt", [128, 16, 16], F32)
        av = A.alloc("av", [128, 16, 16], F32)
        acum = A.alloc("acum", [128, 16, 16], F32)
        eacum = A.alloc("eacum", [128, 16, 16], F32)
        dtds = A.alloc("dtds", [128, 16, 16], F32)
        eatot = A.alloc("eatot", [128, 32, 16], F32)
        tmp = A.alloc("ptmp", [128, 16, 16], F32)
        ps, pk = self.psum.next()
        for t in range(16):
            for kc in range(KC):
                self.mm(ps[:, t * 16:(t + 1) * 16], self.hT[:, kc, t * 128:(t + 1) * 128], wdt[:, kc, :], kc == 0,
                        kc == KC - 1, reads=["wdt", ("hT", kc, t // 4)], writes=[pk])
        self.tt(dt[:], ps[:, 0:256].rearrange("p (t h) -> p t h", t=16),
                dtb[:].unsqueeze(1).to_broadcast([128, 16, 16]), ALU.add, reads=[pk, dtbk], writes=["dt"])
        one_col = cst[:, C_ONES, 0:1]
        self.act(tmp[:], dt[:], AF.Exp, reads=["dt"], writes=["ptmp"])
        self.act(dt[:], tmp[:], AF.Ln, reads=["ptmp", "consts"], writes=["dt"], bias=1.0, scale=1.0)
        self.act(alog[:], alog[:], AF.Exp, reads=[alogk], writes=[alogk])
        self.S.op("dve", lambda e: e.scalar_tensor_tensor(out=av[:], in0=dt[:], scalar=-1.0,
                                                          in1=alog[:].unsqueeze(1).to_broadcast([128, 16, 16]),
                                                          op0=ALU.mult, op1=ALU.mult),
                  reads=["dt", alogk], writes=["av"])
        ps, pk = self.psum.next()
        for t in range(16):
            self.mm(ps[:, t * 16:(t + 1) * 16], cst[:, C_TRI2, :], av[:, t, :], True, True, reads=["av", "consts"], writes=[pk])
        self.copy(acum[:], ps[:, 0:256].rearrange("p (t h) -> p t h", t=16), reads=[pk], writes=["acum"])
        self.act(eacum[:], acum[:], AF.Exp, reads=["acum"], writes=["eacum"])
        ps, pk = self.psum.next()
        for t in range(16):
            self.mm(ps[:, t * 16:(t + 1) * 16], cst[:, C_BD, :], av[:, t, :], True, True, reads=["av", "consts"], writes=[pk])
        self.tt(tmp[:], ps[:, 0:256].rearrange("p (t h) -> p t h", t=16), acum[:], ALU.subtract, reads=[pk, "acum"],
                writes=["ptmp"])
        self.act(tmp[:], tmp[:], AF.Exp, reads=["ptmp"], writes=["ptmp"])
        self.tt(dtds[:], tmp[:], dt[:], ALU.mult, reads=["ptmp", "dt"], writes=["dtds"])
        ps, pk = self.psum.next()
        for t in range(16):
            for c in range(2):
                j = 2 * t + c
                self.mm(ps[:, j * 16:(j + 1) * 16], cst[:, C_IND0 + c, :], av[:, t, :], True, True,
                        reads=["av", "consts"], writes=[pk])
        self.act(eatot[:].rearrange("p j h -> p (j h)"), ps[:], AF.Exp, reads=[pk], writes=["eatot"])

        mG = A.mark()
        for g in range(4):
            A.reset(mG)
            S.barrier()
            wbf = A.alloc("swz", [128, KC, 256], BF16)
            BT = A.alloc("sBT", [128, S_LEN], BF16)
            CT = A.alloc("sCT", [128, S_LEN], BF16)
            x_tok = A.alloc("sxtok", [128, 16, 256], BF16)
            B_tok = A.alloc("sBtok", [128, 16, 128], BF16)
            nwb, nwbk = self.bload("snwb", self.ssm_norm_w_d[l][256 * g:256 * (g + 1)], 256)
            mA = A.mark()
            wst = self.ring("swst", 2, [128, KC, 128], F32)
            wbx = A.alloc("swbx", [128, KC, 512], BF16)
            raw = A.alloc("sraw", [128, S_LEN + 4], F32)
            acc = A.alloc("sacc", [128, S_LEN], F32)
            xc = self.ring("sxc", 2, [128, S_LEN], BF16)
            cw = self.ring("scw", 2, [128, 8], F32)
            rawk = self.key("sraw")
            self.memset(raw[:, 0:3], 0.0, writes=[(rawk, "pad")])
            cols = [O_SSMZ + 256 * g, O_SSMZ + 256 * g + 128, O_SSMXBC + 256 * g, O_SSMXBC + 256 * g + 128,
                    O_SSMXBC + 1024 + 128 * g, O_SSMXBC + 1536 + 128 * g]
            wk = self.key("swbf")
            for j, c0 in enumerate(cols):
                st, sk = wst.next()
                self.dma(st[:], self.w_in_d[l][:, c0:c0 + 128].rearrange("(k p) n -> p k n", p=128), [], [sk])
                for kc in range(KC):
                    wdst_ = wbf[:, kc, j * 128:(j + 1) * 128] if j < 2 else wbx[:, kc, (j - 2) * 128:(j - 1) * 128]
                    self.act(wdst_, st[:, kc, :], AF.Copy, reads=[sk, "nw"], writes=[(wk, j)],
                             scale=scale[:, kc:kc + 1])
            chs = [256 * g, 256 * g + 128, 1024 + 128 * g, 1536 + 128 * g]
            acck = self.key("sacc")
            xtk = self.key("sxtok")
            btk = self.key("sBtok")
            def s_P(jj):
                return self.conv_P(l, wbx, (wk, jj + 2), jj * 128, self.ssm_conv_w_d, self.ssm_conv_b_d, chs[jj], cw)

            dsts = {}

            def s_C(jj, H_):
                if jj < 2:
                    dst, dk = xc.next()
                elif jj == 2:
                    dst, dk = BT, "sBT"
                else:
                    dst, dk = CT, "sCT"
                self.conv_C(H_, raw, rawk, acc, acck, dst[:], dk)
                dsts[jj] = (dst, dk)

            def s_N(jj):
                dst, dk = dsts[jj]
                if jj < 2:
                    self.to_tok(dst, dk, lambda tq, jj=jj: x_tok[:, tq * 4:(tq + 1) * 4, jj * 128:(jj + 1) * 128], (xtk, jj))
                elif jj == 2:
                    self.to_tok(dst, dk, lambda tq: B_tok[:, tq * 4:(tq + 1) * 4, :], btk)

            Hs = {0: s_P(0)}
            self.conv_E(Hs[0], raw, rawk)
            for jj in range(4):
                if jj + 1 < 4:
                    Hs[jj + 1] = s_P(jj + 1)
                s_C(jj, Hs[jj])
                if jj + 1 < 4:
                    self.conv_E(Hs[jj + 1], raw, rawk)
                s_N(jj)
            S.barrier()
            A.reset(mA)
            xdt = A.alloc("sxdt", [128, 16, 256], BF16)
            xdtd = A.alloc("sxdtd", [128, 16, 256], BF16)
            oT = A.alloc("soT", [128, 2, S_LEN], BF16)
            state = A.alloc("sstate", [128, 256], F32)
            state_bf = A.alloc("sstatebf", [128, 256], BF16)
            abc = self.ring("sabc", 2, [128, 128], F32)
            dm = self.ring("sdm", 2, [128, 4, 128], F32)
            MT = self.ring("sMT", 2, [128, 4, 128], BF16)
            xDr = self.ring("sxD", 2, [128, 256], BF16)
            szr = self.ring("ssz", 2, [128, 256], F32)
            ydr = self.ring("syd", 2, [128, 256], F32)
            sttr = self.ring("sstt", 4, [128, 256], F32)
            t1r = self.ring("st1", 1, [128, 256], F32)
            yr = self.ring("sy", 2, [128, 256], F32)
            jr = self.ring("sjunk", 1, [128, 256], F32)
            ssr = self.ring("sssq", 4, [128, 2], F32)
            obr = self.ring("sob", 2, [128, 256], BF16)
            xtks = [(xtk, jj, tq) for jj in range(2) for tq in range(4)]
            x4 = x_tok[:].rearrange("p t (h c) -> p t h c", h=4)
            hs = slice(4 * g, 4 * g + 4)
            self.tt(xdt[:].rearrange("p t (h c) -> p t h c", h=4), x4,
                    dt[:, :, hs].unsqueeze(3).to_broadcast([128, 16, 4, 64]), ALU.mult, reads=xtks + ["dt"], writes=["sxdt"])
            self.tt(xdtd[:].rearrange("p t (h c) -> p t h c", h=4), x4,
                    dtds[:, :, hs].unsqueeze(3).to_broadcast([128, 16, 4, 64]), ALU.mult, reads=xtks + ["dtds"],
                    writes=["sxdtd"])
            stk = self.key("sstate")
            sbk = self.key("sstatebf")
            self.memset(state[:], 0.0, writes=[stk])
            self.memset(state_bf[:], 0.0, writes=[sbk])
            ones_f = cst[:, C_ONES, :]

            def stageP(t, I):
                tsl = slice(t * 128, (t + 1) * 128)
                xD, xDk = xDr.next()
                self.tt(xD[:].rearrange("p (h c) -> p h c", h=4), x_tok[:, t, :].rearrange("p (h c) -> p h c", h=4),
                        dbc[:, hs].unsqueeze(2).to_broadcast([128, 4, 64]), ALU.mult, reads=xtks + [dbck],
                        writes=[xDk], eng="pool")
                yield
                psS, psSk = self.psum.next()
                self.mm(psS[:, 0:128], BT[:, tsl], CT[:, tsl], True, True, reads=["sBT", "sCT"], writes=[psSk])
                yield
                psD, psDk = self.psum.next()
                for hh in range(4):
                    ab, abk = abc.next()
                    self.ts(ab[:], ones_f, av[:, t, 4 * g + hh:4 * g + hh + 1], ALU.mult, reads=["av", "consts"], writes=[abk])
                    yield
                    self.mm(psD[:, hh * 128:(hh + 1) * 128], ab[:], cst[:, C_TRI2, :], True, False, reads=[abk, "consts"],
                            writes=[psDk])
                    yield
                    self.mm(psD[:, hh * 128:(hh + 1) * 128], cst[:, C_TRI2NEG, :], ab[:], False, True,
                            reads=[abk, "consts"], writes=[psDk])
                    yield
                d_, dk_ = dm.next()
                self.tt(d_[:], psD[:].rearrange("p (h c) -> p h c", h=4),
                        cst[:, C_NEGINCL, :].unsqueeze(1).to_broadcast([128, 4, 128]), ALU.add, reads=[psDk, "consts"],
                        writes=[dk_])
                yield
                self.act(d_[:], d_[:], AF.Exp, reads=[dk_], writes=[dk_])
                yield
                M_, Mk_ = MT.next()
                self.tt(M_[:], d_[:], psS[:, 0:128].unsqueeze(1).to_broadcast([128, 4, 128]), ALU.mult,
                        reads=[dk_, psSk], writes=[Mk_])
                yield
                psY, psYk = self.psum.next()
                for hh in range(4):
                    cs = slice(hh * 64, (hh + 1) * 64)
                    self.mm(psY[:, cs], M_[:, hh, :], xdt[:, t, cs], True, False, reads=[Mk_, "sxdt"], writes=[psYk])
                    yield
                    self.mm(psY[:, cs], self.ident_b(), xD[:, cs], False, True, reads=["cbf", xDk], writes=[psYk])
                    yield
                yd, ydk = ydr.next()
                self.copy(yd[:], psY[:, 0:256], reads=[psYk], writes=[ydk], eng="act")
                yield
                psZ, psZk = self.psum.next()
                for kc in range(KC):
                    self.mm(psZ[:, 0:256], self.hT[:, kc, tsl], wbf[:, kc, 0:256], kc == 0, kc == KC - 1,
                            reads=[(wk, 0), (wk, 1), ("hT", kc, t // 4)], writes=[psZk])
                    yield
                sz, szk = szr.next()
                self.act(sz[:], psZ[:, 0:256], AF.Silu, reads=[psZk], writes=[szk])
                yield
                I["stt"] = []
                for c in range(2):
                    psT, psTk = self.psum.next()
                    self.mm(psT[:, 0:256], B_tok[64 * c:64 * c + 64, t, :], xdtd[64 * c:64 * c + 64, t, :], True, True,
                            reads=[(btk, t // 4), "sxdtd"], writes=[psTk])
                    yield
                    sx, sxk = sttr.next()
                    self.copy(sx[:], psT[:, 0:256], reads=[psTk], writes=[sxk], eng=("act" if c else "dve"))
                    yield
                    I["stt"].append((sx, sxk))
                I.update(yd=yd, ydk=ydk, sz=sz, szk=szk)
                yield

            def stageQ(t, I):
                tsl = slice(t * 128, (t + 1) * 128)
                psO, psOk = self.psum_acc.next()
                for c in range(2):
                    j = 2 * t + c
                    sx, sxk = I["stt"][c]
                    self.mm(psO[64 * c:64 * c + 64, 0:256], CT[:, t * 128 + 64 * c:t * 128 + 64 * c + 64], state_bf[:], True, True,
                            reads=["sCT", sbk], writes=[psOk])
                    yield
                    self.tt(state[:].rearrange("p (h c) -> p h c", h=4), state[:].rearrange("p (h c) -> p h c", h=4),
                            eatot[:, j, hs].unsqueeze(2).to_broadcast([128, 4, 64]), ALU.mult, reads=[stk, "eatot"],
                            writes=[stk])
                    yield
                    self.tt(state[:], state[:], sx[:], ALU.add, reads=[stk, sxk], writes=[stk])
                    yield
                    self.copy(state_bf[:], state[:], reads=[stk], writes=[sbk], eng="act")
                    yield
                t1, t1k = t1r.next()
                self.tt(t1[:].rearrange("p (h c) -> p h c", h=4), psO[:, 0:256].rearrange("p (h c) -> p h c", h=4),
                        eacum[:, t, hs].unsqueeze(2).to_broadcast([128, 4, 64]), ALU.mult, reads=[psOk, "eacum"],
                        writes=[t1k])
                yield
                y, yk = yr.next()
                self.tt(y[:], t1[:], I["yd"][:], ALU.add, reads=[t1k, I["ydk"]], writes=[yk])
                yield
                self.tt(y[:], y[:], I["sz"][:], ALU.mult, reads=[yk, I["szk"]], writes=[yk])
                yield
                jk_, jkk = jr.next()
                ss, ssk = ssr.next()
                self.S.op("act", lambda e, jk_=jk_, y=y, ss=ss: e.activation(out=jk_[:], in_=y[:], func=AF.Square,
                                                                          accum_out=ss[:, 0:1]),
                          reads=[yk], writes=[jkk, ssk])
                yield
                self.act(ss[:, 1:2], ss[:, 0:1], AF.Ln, reads=[ssk, "consts"], writes=[ssk], scale=1.0 / 256,
                         bias=EPS)
                yield
                self.act(ss[:, 1:2], ss[:, 1:2], AF.Exp, reads=[ssk], writes=[ssk], scale=-0.5)
                yield
                ob, obk = obr.next()
                self.stt(ob[:], y[:], ss[:, 1:2], nwb[:], ALU.mult, ALU.mult, reads=[yk, ssk, nwbk], writes=[obk])
                yield
                psX, psXk = self.psum.next()
                pxb = psX[:].bitcast(BF16)
                for ch in range(2):
                    self.tr(pxb[:, ch * 128:(ch + 1) * 128], ob[:, ch * 128:(ch + 1) * 128], self.ident_b(),
                            reads=[obk, "cbf"], writes=[psXk])
                    yield
                self.copy(oT[:, :, tsl], pxb[:, 0:256].rearrange("p (c t) -> p c t", c=2), reads=[psXk],
                          writes=[("soT", t)], eng="act")
                yield

            def run2(ga, gb):
                gens = [g_ for g_ in (ga, gb) if g_ is not None]
                while gens:
                    for g_ in list(gens):
                        try:
                            next(g_)
                        except StopIteration:
                            gens.remove(g_)

            infos = {0: {}}
            run2(stageP(0, infos[0]), None)
            for t in range(16):
                gp = None
                if t + 1 < 16:
                    infos[t + 1] = {}
                    gp = stageP(t + 1, infos[t + 1])
                run2(gp, stageQ(t, infos.pop(t)))
            for ch in range(2):
                oc = 2 * g + ch
                self.dma(self.o_scr[2][oc * 128:(oc + 1) * 128, :], oT[:, ch, :], reads=[("soT", t) for t in range(16)],
                         writes=[("o_scr", 2, oc)])


    def dn_phase(self, l):
        A, S = self.A, self.S
        scale = self.nw[:, 2 * l, :]
        cst = self.consts
        one_col = cst[:, C_ONES, 0:1]
        ones_f = cst[:, C_ONES, :]
        wast = A.alloc("dwast", [128, KC, 16], F32)
        wa = A.alloc("dwa", [128, KC, 16], BF16)
        self.dma(wast[:], self.w_in_d[l][:, O_DNA:O_DNA + 16].rearrange("(k p) n -> p k n", p=128), [], ["dwast"])
        for kc in range(KC):
            self.act(wa[:, kc, :], wast[:, kc, :], AF.Copy, reads=["dwast", "nw"], writes=["dwa"], scale=scale[:, kc:kc + 1])
        dtb, dtbk = self.bload("ddtb", self.dn_dt_bias_d[l], 8)
        alog, alogk = self.bload("dalog", self.dn_a_log_d[l], 8)
        gv = A.alloc("dg", [128, 16, 8], F32)
        beta = A.alloc("dbeta", [128, 16, 8], F32)
        gc = A.alloc("dgc", [128, 16, 8], F32)
        egc = self.dn_egc
        bg = A.alloc("dbg", [128, 16, 8], F32)
        kdec = A.alloc("dkdec", [128, 16, 8], F32)
        eglast = self.dn_eglast
        tmp = A.alloc("dtmp", [128, 16, 8], F32)
        ps, pk = self.psum.next()
        for t in range(16):
            for kc in range(KC):
                self.mm(ps[:, t * 16:(t + 1) * 16], self.hT[:, kc, t * 128:(t + 1) * 128], wa[:, kc, :], kc == 0,
                        kc == KC - 1, reads=["dwa", ("hT", kc, t // 4)], writes=[pk])
        pv = ps[:, 0:256].rearrange("p (t h) -> p t h", t=16)
        self.act(beta[:], pv[:, :, 8:16], AF.Exp, reads=[pk], writes=["dbeta"], scale=-1.0)
        self.ts(beta[:], beta[:], 1.0, ALU.add, reads=["dbeta"], writes=["dbeta"])
        self.S.op("dve", lambda e: e.reciprocal(out=beta[:], in_=beta[:]), reads=["dbeta"], writes=["dbeta"])
        self.tt(gv[:], pv[:, :, 0:8], dtb[:].unsqueeze(1).to_broadcast([128, 16, 8]), ALU.add, reads=[pk, dtbk], writes=["dg"])
        self.act(tmp[:], gv[:], AF.Exp, reads=["dg"], writes=["dtmp"])
        self.act(gv[:], tmp[:], AF.Ln, reads=["dtmp", "consts"], writes=["dg"], bias=1.0, scale=1.0)
        self.act(alog[:], alog[:], AF.Exp, reads=[alogk], writes=[alogk])
        self.S.op("dve", lambda e: e.scalar_tensor_tensor(out=gv[:], in0=gv[:], scalar=-1.0,
                                                          in1=alog[:].unsqueeze(1).to_broadcast([128, 16, 8]),
                                                          op0=ALU.mult, op1=ALU.mult),
                  reads=["dg", alogk], writes=["dg"])
        ps, pk = self.psum.next()
        for t in range(16):
            self.mm(ps[:, t * 8:(t + 1) * 8], cst[:, C_TRI2, :], gv[:, t, :], True, True, reads=["dg", "consts"], writes=[pk])
        self.copy(gc[:], ps[:, 0:128].rearrange("p (t h) -> p t h", t=16), reads=[pk], writes=["dgc"])
        self.act(egc[:], gc[:], AF.Exp, reads=["dgc"], writes=["degc"])
        self.tt(bg[:], egc[:], beta[:], ALU.mult, reads=["degc", "dbeta"], writes=["dbg"])
        ps, pk = self.psum.next()
        for t in range(16):
            self.mm(ps[:, t * 8:(t + 1) * 8], cst[:, C_BD, :], gv[:, t, :], True, True, reads=["dg", "consts"], writes=[pk])
        self.tt(kdec[:], ps[:, 0:128].rearrange("p (t h) -> p t h", t=16), gc[:], ALU.subtract, reads=[pk, "dgc"],
                writes=["dkdec"])
        self.act(kdec[:], kdec[:], AF.Exp, reads=["dkdec"], writes=["dkdec"])
        ps, pk = self.psum.next()
        for t in range(16):
            for c in range(2):
                j = 2 * t + c
                self.mm(ps[:, j * 8:(j + 1) * 8], cst[:, C_IND0 + c, :], gv[:, t, :], True, True,
                        reads=["dg", "consts"], writes=[pk])
        self.act(eglast[:].rearrange("p j h -> p (j h)"), ps[:, 0:256], AF.Exp, reads=[pk], writes=["deglast"])

        mG = A.mark()
        for h in range(8):
            A.reset(mG)
            S.barrier()
            wg = A.alloc("dwg", [128, KC, 128], BF16)
            qTn = A.alloc("dqTn", [128, S_LEN], BF16)
            kTn = A.alloc("dkTn", [128, S_LEN], BF16)
            k_tok = A.alloc("dktok", [128, 16, 128], BF16)
            v_tok = A.alloc("dvtok", [128, 16, 128], BF16)
            mA = A.mark()
            wst = self.ring("dwst", 2, [128, KC, 128], F32)
            wbx = A.alloc("dwbx", [128, KC, 384], BF16)
            raws = [A.alloc("draw", [128, S_LEN + 4], F32) for _ in range(2)]
            accs = [A.alloc("dacc", [128, S_LEN], F32) for _ in range(2)]
            vT = A.alloc("dvT", [128, S_LEN], BF16)
            cw = self.ring("dcw", 2, [128, 8], F32)
            sqr = self.ring("dsq", 2, [128, 512], BF16)
            rsr = self.ring("drs", 2, [128, 512], F32)
            rawks = [self.key("draw"), self.key("draw")]
            for r_, rk__ in zip(raws, rawks):
                self.memset(r_[:, 0:3], 0.0, writes=[(rk__, "pad")])
            cols = [O_DNQKV + 128 * h, O_DNQKV + 1024 + 128 * h, O_DNQKV + 2048 + 128 * h, O_DNGATE + 128 * h]
            wk = self.key("dwb")
            for j, c0 in enumerate(cols):
                st, sk = wst.next()
                self.dma(st[:], self.w_in_d[l][:, c0:c0 + 128].rearrange("(k p) n -> p k n", p=128), [], [sk])
                for kc in range(KC):
                    wd_ = wbx[:, kc, j * 128:(j + 1) * 128] if j < 3 else wg[:, kc, :]
                    self.act(wd_, st[:, kc, :], AF.Copy, reads=[sk, "nw"], writes=[(wk, j)], scale=scale[:, kc:kc + 1])
            accks = [self.key("dacc"), self.key("dacc")]
            ktk = self.key("dktok")
            vtk = self.key("dvtok")
            def dn_P(j):
                return self.conv_P(l, wbx, (wk, j), j * 128, self.dn_conv_w_d, None, j * 1024 + 128 * h, cw)

            def dn_C(j, H):
                raw, rawk, acc, acck = raws[j % 2], rawks[j % 2], accs[j % 2], accks[j % 2]
                if j == 2:
                    self.conv_C(H, raw, rawk, acc, acck, vT[:], "dvT")
                else:
                    self.conv_C(H, raw, rawk, acc, acck, acc[:], acck)

            def dn_N(j):
                acc, acck = accs[j % 2], accks[j % 2]
                if j == 2:
                    self.to_tok(vT, "dvT", lambda tq: v_tok[:, tq * 4:(tq + 1) * 4, :], vtk)
                    return
                dst, dk = (qTn, "dqTn") if j == 0 else (kTn, "dkTn")
                for g in range(4):
                    sl = slice(g * 512, (g + 1) * 512)
                    q_, qk_ = sqr.next()
                    self.act(q_[:], acc[:, sl], AF.Square, reads=[acck], writes=[qk_])
                    ps, pk = self.psum.next()
                    self.mm(ps[:], self.ones_b(), q_[:], True, True, reads=[qk_, "cbf"], writes=[pk])
                    r_, rk_ = rsr.next()
                    self.act(r_[:], ps[:], AF.Ln, reads=[pk], writes=[rk_], scale=1.0, bias=EPS)
                    self.act(r_[:], r_[:], AF.Exp, reads=[rk_], writes=[rk_], scale=-0.5)
                    if j == 0:
                        self.stt(dst[:, sl], acc[:, sl], 128.0 ** -0.5, r_[:], ALU.mult, ALU.mult, reads=[acck, rk_],
                                 writes=[(dk, g)])
                    else:
                        self.tt(dst[:, sl], acc[:, sl], r_[:], ALU.mult, reads=[acck, rk_], writes=[(dk, g)])
                if j == 1:
                    self.to_tok(kTn, [("dkTn", g) for g in range(4)], lambda tq: k_tok[:, tq * 4:(tq + 1) * 4, :], ktk)

            H = {0: dn_P(0)}
            self.conv_E(H[0], raws[0], rawks[0])
            for j in range(3):
                if j + 1 < 3:
                    H[j + 1] = dn_P(j + 1)
                dn_C(j, H[j])
                if j + 1 < 3:
                    self.conv_E(H[j + 1], raws[(j + 1) % 2], rawks[(j + 1) % 2])
                dn_N(j)
            S.barrier()
            A.reset(mA)
            attnT = A.alloc("dattnT", [128, 16, 128], BF16)
            P = A.alloc("dP", [128, 16, 128], BF16)
            PL = A.alloc("dPL", [128, 16, 128], BF16)
            R32 = A.alloc("dR32", [128, 16, 128], F32)
            Rb = A.alloc("dRb", [128, 16, 128], BF16)
            vb = v_tok
            kbg = k_tok
            kd = A.alloc("dkd", [128, 16, 128], BF16)
            u = A.alloc("du", [128, 16, 128], BF16)
            wT = A.alloc("dwT", [128, 16, 128], BF16)
            abrs = [self.ring("dab", 2, [128, 128], F32) for _ in range(4)]
            dcs = [A.alloc("ddec", [128, 4, 128], F32) for _ in range(4)]
            bms = [A.alloc("dbm", [128, 4, 128], F32) for _ in range(4)]
            ktks = [(ktk, tq) for tq in range(4)]
            vtks = [(vtk, tq) for tq in range(4)]
            bc3 = lambda t_: t_[:, :, h:h + 1].to_broadcast([128, 16, 128])
            self.tt(vb[:], v_tok[:], bc3(beta), ALU.mult, reads=vtks + ["dbeta"], writes=vtks + ["dvb"])
            self.tt(kd[:], k_tok[:], bc3(kdec), ALU.mult, reads=ktks + ["dkdec"], writes=["dkd"])
            self.tt(kbg[:], k_tok[:], bc3(bg), ALU.mult, reads=ktks + ["dbg", "dkd"], writes=ktks + ["dkbg"])
            kTk = [("dkTn", g) for g in range(4)]
            identf4 = cst[:, C_IDENT, :].unsqueeze(1).to_broadcast([128, 4, 128])
            def chain(q):
                pr = Ring([self.pst[2 * q], self.pst[2 * q + 1]], ("dnps", q))
                tq = slice(4 * q, 4 * q + 4)
                abr = abrs[q]
                dc, dck = dcs[q], ("ddec", q)
                bm, bmk = bms[q], ("dbm", q)
                v4 = lambda ps_: ps_[:].rearrange("p (a b) -> p a b", a=4)
                tiles = [(4 * q + i4, slice((4 * q + i4) * 128, (4 * q + i4 + 1) * 128), slice(i4 * 128, (i4 + 1) * 128))
                         for i4 in range(4)]
                psD, psDk = pr.next()
                for t, tsl, cs in tiles:
                    ab, abk = abr.next()
                    self.ts(ab[:], ones_f, gv[:, t, h:h + 1], ALU.mult, reads=["dg", "consts"], writes=[abk])
                    yield
                    self.mm(psD[:, cs], ab[:], cst[:, C_TRI2, :], True, False, reads=[abk, "consts"], writes=[psDk])
                    self.mm(psD[:, cs], cst[:, C_TRI2NEG, :], ab[:], False, True, reads=[abk, "consts"], writes=[psDk])
                    yield
                self.tt(dc[:], v4(psD), cst[:, C_NEGINCL, :].unsqueeze(1).to_broadcast([128, 4, 128]), ALU.add,
                        reads=[psDk, "consts"], writes=[dck])
                yield
                self.act(dc[:], dc[:], AF.Exp, reads=[dck], writes=[dck])
                yield
                psQ, psQk = pr.next()
                for t, tsl, cs in tiles:
                    self.mm(psQ[:, cs], kTn[:, tsl], qTn[:, tsl], True, True, reads=[("dkTn", q), ("dqTn", q)], writes=[psQk])
                    yield
                self.tt(attnT[:, tq, :], v4(psQ), dc[:], ALU.mult, reads=[psQk, dck], writes=[("dattnT", q)])
                yield
                psB, psBk = pr.next()
                for t, tsl, cs in tiles:
                    db, dbk = abr.next()
                    self.ts(db[:], cst[:, C_IDENT, :], beta[:, t, h:h + 1], ALU.mult, reads=["dbeta", "consts"], writes=[dbk])
                    yield
                    self.mm(psB[:, cs], ones_f, db[:], True, True, reads=[dbk, "consts"], writes=[psBk])
                    yield
                self.tt(bm[:], v4(psB), cst[:, C_MSTRICT2, :].unsqueeze(1).to_broadcast([128, 4, 128]), ALU.mult,
                        reads=[psBk, "consts"], writes=[bmk])
                yield
                psK, psKk = pr.next()
                for t, tsl, cs in tiles:
                    self.mm(psK[:, cs], kTn[:, tsl], kTn[:, tsl], True, True, reads=[("dkTn", q)], writes=[psKk])
                    yield
                self.tt(dc[:], v4(psK), dc[:], ALU.mult, reads=[psKk, dck], writes=[dck])
                yield
                self.tt(P[:, tq, :], dc[:], bm[:], ALU.mult, reads=[dck, bmk], writes=[("dP", q)])
                yield
                psT, psTk = pr.next()
                ptb = psT[:].bitcast(BF16)
                for t, tsl, cs in tiles:
                    self.tr(ptb[:, cs], P[:, t, :], self.ident_b(), reads=[("dP", q), "cbf"], writes=[psTk])
                yield
                self.copy(PL[:, tq, :], ptb[:, 0:512].rearrange("p (a b) -> p a b", a=4), reads=[psTk],
                          writes=[("dPL", q)], eng="act")
                yield
                self.tt(R32[:, tq, :], identf4, P[:, tq, :], ALU.subtract, reads=[("dP", q), "consts"], writes=[("dR32", q)])
                yield
                self.copy(Rb[:, tq, :], R32[:, tq, :], reads=[("dR32", q)], writes=[("dRb", q)], eng="act")
                yield

            S.barrier()
            for q0 in (0, 2):
                gens = [chain(q0), chain(q0 + 1)]
                while gens:
                    for g_ in list(gens):
                        try:
                            next(g_)
                        except StopIteration:
                            gens.remove(g_)
            S.barrier()
            for lev in range(5):
                last = lev == 4
                for q in range(4):
                    if not last:
                        psP, psPk = self.psum.next()
                    psL, psLk = self.psum.next()
                    for i4 in range(4):
                        t = 4 * q + i4
                        cs = slice(i4 * 128, (i4 + 1) * 128)
                        if not last:
                            self.mm(psP[:, cs], PL[:, t, :], P[:, t, :], True, True, reads=[("dPL", q), ("dP", q)],
                                    writes=[psPk])
                        self.mm(psL[:, cs], P[:, t, :], PL[:, t, :], True, True, reads=[("dPL", q), ("dP", q)],
                                writes=[psLk])
                    if not last:
                        self.copy(P[:, 4 * q:4 * q + 4, :], psP[:].rearrange("p (a b) -> p a b", a=4), reads=[psPk],
                                  writes=[("dP", q)], eng="dve")
                    self.copy(PL[:, 4 * q:4 * q + 4, :], psL[:].rearrange("p (a b) -> p a b", a=4), reads=[psLk],
                              writes=[("dPL", q)], eng="act")
                for q in range(4):
                    psR, psRk = self.psum.next()
                    for i4 in range(4):
                        t = 4 * q + i4
                        cs = slice(i4 * 128, (i4 + 1) * 128)
                        self.mm(psR[:, cs], PL[:, t, :], Rb[:, t, :], True, True, reads=[("dPL", q), ("dRb", q)],
                                writes=[psRk])
                    self.tt(R32[:, 4 * q:4 * q + 4, :], R32[:, 4 * q:4 * q + 4, :],
                            psR[:].rearrange("p (a b) -> p a b", a=4), ALU.add, reads=[psRk, ("dR32", q)],
                            writes=[("dR32", q)])
                    self.copy(Rb[:, 4 * q:4 * q + 4, :], R32[:, 4 * q:4 * q + 4, :], reads=[("dR32", q)],
                              writes=[("dRb", q)], eng="act")
            for q in range(4):
                psU, psUk = self.psum.next()
                psW, psWk = self.psum.next()
                for i4 in range(4):
                    t = 4 * q + i4
                    cs = slice(i4 * 128, (i4 + 1) * 128)
                    self.mm(psU[:, cs], Rb[:, t, :], vb[:, t, :], True, True, reads=[("dRb", q), "dvb"], writes=[psUk])
                    self.mm(psW[:, cs], kbg[:, t, :], Rb[:, t, :], True, True, reads=[("dRb", q), "dkbg"], writes=[psWk])
                self.copy(u[:, 4 * q:4 * q + 4, :], psU[:].rearrange("p (a b) -> p a b", a=4), reads=[psUk],
                          writes=[("du", q)], eng="dve")
                self.copy(wT[:, 4 * q:4 * q + 4, :], psW[:].rearrange("p (a b) -> p a b", a=4), reads=[psWk],
                          writes=[("dwT", q)], eng="act")
            sgt = P
            for q in range(4):
                psG, psGk = self.psum.next()
                for i4 in range(4):
                    t = 4 * q + i4
                    for kc in range(KC):
                        self.mm(psG[:, i4 * 128:(i4 + 1) * 128], self.hT[:, kc, t * 128:(t + 1) * 128], wg[:, kc, :],
                                kc == 0, kc == KC - 1, reads=[(wk, 3), ("hT", kc, q)], writes=[psGk])
                self.act(sgt[:, 4 * q:4 * q + 4, :], psG[:].rearrange("p (a b) -> p a b", a=4), AF.Silu, reads=[psGk],
                         writes=[("dP", q)])
            fl = lambda t_: t_[:].rearrange("p a b -> p (a b)")
            for nm, src, keys in (("u", fl(u), [("du", q) for q in range(4)]),
                                  ("wT", fl(wT), [("dwT", q) for q in range(4)]),
                                  ("q", qTn[:], [("dqTn", g) for g in range(4)]),
                                  ("attnT", fl(attnT), [("dattnT", q) for q in range(4)]),
                                  ("kd", fl(kd), ["dkd"]),
                                  ("sg", fl(sgt), [("dP", q) for q in range(4)])):
                self.dma(self.dn_scr[nm][h], src, reads=keys, writes=[("dn_scr", nm, h)])

    def final_out(self):
        A, S = self.A, self.S
        m = A.mark()
        sq = self.ring("fsq", 3, [128, 512], BF16)
        rs = self.ring("frs", 2, [128, 512], F32)
        hf = self.ring("hf", 3, [128, 512], F32)
        ost = self.ring("ost", 2, [128, 4, D], F32)
        for g in range(4):
            ps, pk = self.psum.next()
            sl = slice(g * 512, (g + 1) * 512)
            for kc in range(KC):
                q, qk = sq.next()
                self.act(q[:], self.xT[:, kc, sl], AF.Square, reads=[("xT", kc, g)], writes=[qk])
                self.mm(ps[:], self.ones_b(), q[:], kc == 0, kc == KC - 1, reads=[qk, "cbf"], writes=[pk])
            r, rk = rs.next()
            self.act(r[:], ps[:], AF.Ln, reads=[pk], writes=[rk], scale=1.0 / D, bias=EPS)
            self.act(r[:], r[:], AF.Exp, reads=[rk], writes=[rk], scale=-0.5)
            o, ok = ost.next()
            for kc in range(KC):
                h, hk = hf.next()
                self.stt(h[:], self.xT[:, kc, sl], self.nw[:, 2 * DEPTH, kc:kc + 1], r[:], ALU.mult, ALU.mult,
                         reads=[("xT", kc, g), rk, "nw"], writes=[hk])
                ps2, pk2 = self.psum.next()
                for t in range(4):
                    self.tr(ps2[:, t * 128:(t + 1) * 128], h[:, t * 128:(t + 1) * 128], self.ident_f(),
                            reads=[hk, "consts"], writes=[pk2])
                self.copy(o[:, :, kc * 128:(kc + 1) * 128], ps2[:].rearrange("p (t c) -> p t c", t=4),
                          reads=[pk2], writes=[ok], eng=("act" if kc % 2 else "dve"))
            self.dma(self.out_d[g * 512:(g + 1) * 512, :].rearrange("(t p) d -> p t d", p=128), o[:],
                     reads=[ok], writes=[("out", g)])
        S.barrier()
        A.reset(m)


C_IDENT, C_ONES, C_EPS, C_MSTRICT, C_TINC = 0, 1, 2, 3, 4
C_TRI2, C_TRI2NEG, C_BD, C_IND0, C_IND1, C_NEGINCL, C_NEGSB, C_MINCL2, C_MSTRICT2, C_MSTRICT2T = 5, 6, 7, 8, 9, 10, 11, 12, 13, 14
NCONST = 15
NEG = -30000.0


def make_consts():
    c = np.zeros((128, NCONST, 128), np.float32)
    c[:, C_IDENT, :] = np.eye(128, dtype=np.float32)
    c[:, C_ONES, :] = 1.0
    c[:, C_EPS, :] = EPS
    ii = np.arange(128)
    c[:, C_MSTRICT, :] = (ii[:, None] < ii[None, :]).astype(np.float32)
    c[:, C_TINC, :] = (ii[:, None] >= ii[None, :]).astype(np.float32)
    same = (ii[:, None] // 64) == (ii[None, :] // 64)
    le = ii[:, None] <= ii[None, :]
    lt = ii[:, None] < ii[None, :]
    c[:, C_TRI2, :] = (same & le).astype(np.float32)
    c[:, C_TRI2NEG, :] = -c[:, C_TRI2, :]
    c[:, C_BD, :] = same.astype(np.float32)
    c[:, C_IND0, :] = (ii[:, None] < 64).astype(np.float32) * np.ones((1, 128), np.float32)
    c[:, C_IND1, :] = (ii[:, None] >= 64).astype(np.float32) * np.ones((1, 128), np.float32)
    c[:, C_NEGINCL, :] = np.where(same & le, 0.0, NEG)
    c[:, C_NEGSB, :] = np.where(lt, 0.0, 8.0 * NEG)
    c[:, C_MINCL2, :] = (same & le).astype(np.float32)
    c[:, C_MSTRICT2, :] = (same & lt).astype(np.float32)
    c[:, C_MSTRICT2T, :] = (same & lt).T.astype(np.float32)
    return c.reshape(128, NCONST * 128)


_CACHE = {}
_RUN_KW = {}


def get_nc(**kw):
    key = tuple(sorted((k, str(v)) for k, v in kw.items()))
    if key not in _CACHE:
        b = Builder(**kw)
        b.build()
        _CACHE[key] = b
    return _CACHE[key]


def run(inputs, **kw):
    b = get_nc(**kw)
    consts = make_consts()
    common = {
        "consts": consts,
        "w_in": np.ascontiguousarray(inputs["w_in"], dtype=np.float32),
        "norm_mix": np.ascontiguousarray(inputs["norm_mix"], dtype=np.float32),
        "norm_mlp": np.ascontiguousarray(inputs["norm_mlp"], dtype=np.float32),
        "norm_final": np.ascontiguousarray(inputs["norm_final"], dtype=np.float32).reshape(1, D),
        "w_branch": np.ascontiguousarray(inputs["w_branch"], dtype=np.float32),
        "w_out": np.ascontiguousarray(inputs["w_out"], dtype=np.float32),
        "w_up": np.ascontiguousarray(inputs["w_up"], dtype=np.float32),
        "w_down": np.ascontiguousarray(inputs["w_down"], dtype=np.float32),
    }
    for nm in ("ssm_conv_w", "ssm_conv_b", "ssm_a_log", "ssm_dt_bias", "ssm_d", "ssm_norm_w", "dn_conv_w", "dn_a_log",
               "dn_dt_bias", "dn_norm_w"):
        common[nm] = np.ascontiguousarray(inputs[nm], dtype=np.float32)
    x = np.asarray(inputs["x"], dtype=np.float32)
    in_maps = []
    for c in range(NCORES):
        m = dict(common)
        m["x"] = np.ascontiguousarray(x[c])
        in_maps.append(m)
    res = run_bass_kernel_spmd(b.nc, in_maps, core_ids=list(range(NCORES)), **_RUN_KW)
    return res


def kernel(**inputs):
    res = run(inputs)
    out = np.stack([np.asarray(r["out"], dtype=np.float32) for r in res.results], axis=0)
    return out
```

```python
import numpy as np
import concourse.bass as bass
import concourse.mybir as mybir
from concourse.bass_utils import run_bass_kernel_spmd

F32 = mybir.dt.float32
BF16 = mybir.dt.bfloat16
AF = mybir.ActivationFunctionType
ALU = mybir.AluOpType

S_LEN = 2048
D = 1024
KC = 8
NCORES = 8
DEPTH = 2
D_FF = 4096
EPS = 1e-6
IN_SIZES = (3072, 1024, 8, 8, 3072, 1024, 2048, 16, 3072)
IN_DIM = sum(IN_SIZES)
OFF = [0]
for _s in IN_SIZES:
    OFF.append(OFF[-1] + _s)
(O_DNQKV, O_DNGATE, O_DNA, O_DNB, O_SBQKV, O_SSMZ, O_SSMXBC, O_SSMDT, O_GATE, _) = OFF


class _Op:
    __slots__ = ("id", "eng", "fn", "dma", "waits", "idx", "sig", "reuse", "seen")


class Sched:
    ENG = ("pe", "act", "dve", "pool", "sp")
    NSLOT = 12
    SEM_LIMIT = 20000

    def __init__(self, nc):
        self.nc = nc
        self.ops = []
        self.by_eng = {e: [] for e in self.ENG}
        self.kstate = {}
        self.seen = {e: {p: -1 for p in self.ENG} for e in self.ENG}
        self.seen_dma = {e: set() for e in self.ENG}
        self.pending = {e: set() for e in self.ENG}
        self.open_dma = []

    def op(self, eng, fn, reads=(), writes=(), dma=False):
        o = _Op()
        o.id = len(self.ops)
        o.eng = eng
        o.fn = fn
        o.dma = dma
        o.sig = None
        o.reuse = None
        deps = {}
        for k in reads:
            st = self.kstate.get(k)
            if st is not None and st[0] is not None:
                deps[st[0]] = True
        for k in writes:
            st = self.kstate.get(k)
            if st is not None:
                if st[0] is not None:
                    deps.setdefault(st[0], False)
                for r in st[1].values():
                    deps.setdefault(r, False)
                for r in st[2]:
                    deps.setdefault(r, False)
        for d in self.pending[eng]:
            deps[d] = True
        self.pending[eng] = set()
        o.idx = len(self.by_eng[eng])
        seen = self.seen[eng]
        best = {}
        waits = []
        for d, raw in deps.items():
            p = self.ops[d]
            if p.dma:
                if d in self.seen_dma[eng]:
                    continue
                self.seen_dma[eng].add(d)
                waits.append(d)
            else:
                if p.eng == eng and not dma:
                    if eng == "pe" or (not raw and eng != "pool"):
                        continue
                if p.idx <= seen[p.eng]:
                    continue
                if p.eng not in best or self.ops[best[p.eng]].idx < p.idx:
                    best[p.eng] = d
        for pe, d in best.items():
            p = self.ops[d]
            waits.append(d)
            for e2, v in p.seen.items():
                if v > seen[e2]:
                    seen[e2] = v
            if p.idx > seen[pe]:
                seen[pe] = p.idx
        o.waits = waits
        o.seen = dict(seen)
        for k in reads:
            st = self.kstate.get(k)
            if st is None:
                st = [None, {}, []]
                self.kstate[k] = st
            if dma:
                st[2].append(o.id)
            else:
                st[1][eng] = o.id
        for k in writes:
            self.kstate[k] = [o.id, {}, []]
        self.ops.append(o)
        self.by_eng[eng].append(o)
        if dma:
            self.open_dma.append(o.id)
        return o

    def barrier(self):
        last = []
        for e in self.ENG:
            for o in reversed(self.by_eng[e]):
                if o.dma:
                    break
                if o.fn is not None:
                    last.append(o.id)
                    break
        for e in self.ENG:
            self.pending[e] |= set(last) | set(self.open_dma[-self.NSLOT:])
        self.open_dma = self.open_dma[-self.NSLOT:]

    def finalize(self):
        nc = self.nc
        needed = set()
        for o in self.ops:
            needed.update(o.waits)
        for e in self.ENG:
            sem = None
            cnt = 0
            ndma = 0
            slots = None
            for o in self.by_eng[e]:
                if o.dma:
                    if slots is None:
                        slots = [nc.alloc_semaphore(f"dq_{e}_{i}") for i in range(self.NSLOT)]
                    s = ndma % self.NSLOT
                    r = ndma // self.NSLOT
                    o.sig = (slots[s], 16 * (r + 1))
                    if r > 0:
                        o.reuse = (slots[s], 16 * r)
                    ndma += 1
                elif o.id in needed:
                    if sem is None or cnt >= self.SEM_LIMIT:
                        sem = nc.alloc_semaphore(f"pg_{e}_{o.id}")
                        cnt = 0
                    cnt += 1
                    o.sig = (sem, cnt)

    def emit(self, ename, eng):
        ops = self.ops
        for o in self.by_eng[ename]:
            for d in o.waits:
                s, v = ops[d].sig
                eng.wait_ge(s, v)
            if o.reuse is not None:
                eng.wait_ge(o.reuse[0], o.reuse[1])
            if o.fn is None:
                continue
            ins = o.fn(eng)
            if o.sig is not None:
                ins.then_inc(o.sig[0], 16 if o.dma else 1)


class Arena:
    def __init__(self, nc, limit=208 * 1024):
        self.nc = nc
        self.off = 16 * 1024
        self.limit = limit
        self.n = 0
        self.peak = 0

    def alloc(self, name, shape, dtype):
        per = 1
        for s in shape[1:]:
            per *= s
        nbytes = per * (4 if dtype == F32 else 2)
        nbytes = (nbytes + 63) // 64 * 64
        assert self.off + nbytes <= self.limit, f"SBUF arena overflow at {name}: {self.off}+{nbytes}"
        self.n += 1
        t = self.nc.alloc_sbuf_tensor_at(f"{name}_{self.n}", list(shape), dtype, offset=self.off)
        self.off += nbytes
        self.peak = max(self.peak, self.off)
        return t

    def mark(self):
        return self.off

    def reset(self, m):
        self.off = m


class Ring:
    def __init__(self, tiles, name):
        self.tiles = tiles
        self.name = name
        self.i = 0

    def next(self):
        j = self.i % len(self.tiles)
        self.i += 1
        return self.tiles[j], (self.name, j)


class Builder:
    def __init__(self, nlayers=DEPTH, mixers=("dn", "sb", "ssm"), do_mlp=True, debug=()):
        self.nlayers = nlayers
        self.mixers = mixers
        self.do_mlp = do_mlp
        self.debug = debug
        nc = bass.Bass("TRN2", target_bir_lowering=False)
        self.nc = nc
        self.S = Sched(nc)
        self.A = Arena(nc)
        self.uid = 0

    def dram_in(self, name, shape, dtype=F32):
        return self.nc.dram_tensor(name, list(shape), dtype, kind="ExternalInput").ap()

    def dram_out(self, name, shape, dtype=F32):
        return self.nc.dram_tensor(name, list(shape), dtype, kind="ExternalOutput").ap()

    def dram_tmp(self, name, shape, dtype):
        return self.nc.dram_tensor(name, list(shape), dtype, kind="Internal").ap()

    def key(self, base):
        self.uid += 1
        return (base, self.uid)

    def ring(self, name, n, shape, dtype):
        return Ring([self.A.alloc(name, shape, dtype) for _ in range(n)], self.key(name))

    def dma(self, out, in_, reads, writes, slow=False):
        if slow:
            fn = lambda e, out=out, in_=in_: e.dma_start(out=out, in_=in_, allow_slow_non_contiguous=True)
        else:
            fn = lambda e, out=out, in_=in_: e.dma_start(out=out, in_=in_)
        return self.S.op("sp", fn, reads=reads, writes=writes, dma=True)

    def mm(self, out, lhsT, rhs, start, stop, reads, writes, **kw):
        return self.S.op(
            "pe",
            lambda e, out=out, lhsT=lhsT, rhs=rhs, start=start, stop=stop, kw=kw: e.matmul(
                out, lhsT=lhsT, rhs=rhs, start=start, stop=stop, **kw
            ),
            reads=reads,
            writes=writes,
        )

    def tr(self, out, in_, ident, reads, writes):
        return self.S.op(
            "pe",
            lambda e, out=out, in_=in_, ident=ident: e.transpose(out, in_, ident),
            reads=reads,
            writes=writes,
        )

    def act(self, out, in_, func, reads, writes, eng="act", **kw):
        return self.S.op(
            eng,
            lambda e, out=out, in_=in_, func=func, kw=kw: e.activation(out=out, in_=in_, func=func, **kw),
            reads=reads,
            writes=writes,
        )

    def tt(self, out, in0, in1, op, reads, writes, eng="dve"):
        return self.S.op(
            eng,
            lambda e, out=out, in0=in0, in1=in1, op=op: e.tensor_tensor(out=out, in0=in0, in1=in1, op=op),
            reads=reads,
            writes=writes,
        )

    def ts(self, out, in0, s1, op0, reads, writes, s2=None, op1=None, eng="dve"):
        def fn(e, out=out, in0=in0, s1=s1, op0=op0, s2=s2, op1=op1):
            if op1 is None:
                return e.tensor_scalar(out=out, in0=in0, scalar1=s1, scalar2=None, op0=op0)
            return e.tensor_scalar(out=out, in0=in0, scalar1=s1, scalar2=s2, op0=op0, op1=op1)

        return self.S.op(eng, fn, reads=reads, writes=writes)

    def stt(self, out, in0, scalar, in1, op0, op1, reads, writes):
        return self.S.op(
            "dve",
            lambda e, out=out, in0=in0, scalar=scalar, in1=in1, op0=op0, op1=op1: e.scalar_tensor_tensor(
                out=out, in0=in0, scalar=scalar, in1=in1, op0=op0, op1=op1
            ),
            reads=reads,
            writes=writes,
        )

    def copy(self, out, in_, reads, writes, eng="dve"):
        if eng == "act":
            return self.act(out, in_, AF.Copy, reads, writes)
        return self.S.op(
            eng, lambda e, out=out, in_=in_: e.tensor_copy(out=out, in_=in_), reads=reads, writes=writes
        )

    def memset(self, ap, val, writes, eng="dve"):
        return self.S.op(eng, lambda e, ap=ap, val=val: e.memset(ap, val), reads=(), writes=writes)

    def build(self):
        nc, S, A = self.nc, self.S, self.A
        L = self.nlayers
        self.x_d = self.dram_in("x", [S_LEN, D])
        self.consts_d = self.dram_in("consts", [128, NCONST * 128])
        self.w_in_d = self.dram_in("w_in", [DEPTH, D, IN_DIM])
        self.norm_mix_d = self.dram_in("norm_mix", [DEPTH, D])
        self.norm_mlp_d = self.dram_in("norm_mlp", [DEPTH, D])
        self.norm_final_d = self.dram_in("norm_final", [1, D])
        self.w_branch_d = self.dram_in("w_branch", [DEPTH, 3, D, D])
        self.w_out_d = self.dram_in("w_out", [DEPTH, D, D])
        self.w_up_d = self.dram_in("w_up", [DEPTH, D, D_FF])
        self.w_down_d = self.dram_in("w_down", [DEPTH, D_FF, D])
        self.ssm_conv_w_d = self.dram_in("ssm_conv_w", [DEPTH, 4, 2048])
        self.ssm_conv_b_d = self.dram_in("ssm_conv_b", [DEPTH, 2048])
        self.ssm_a_log_d = self.dram_in("ssm_a_log", [DEPTH, 16])
        self.ssm_dt_bias_d = self.dram_in("ssm_dt_bias", [DEPTH, 16])
        self.ssm_d_d = self.dram_in("ssm_d", [DEPTH, 16])
        self.ssm_norm_w_d = self.dram_in("ssm_norm_w", [DEPTH, D])
        self.dn_conv_w_d = self.dram_in("dn_conv_w", [DEPTH, 4, 3072])
        self.dn_a_log_d = self.dram_in("dn_a_log", [DEPTH, 8])
        self.dn_dt_bias_d = self.dram_in("dn_dt_bias", [DEPTH, 8])
        self.dn_norm_w_d = self.dram_in("dn_norm_w", [DEPTH, 128])
        self.out_d = self.dram_out("out", [S_LEN, D])
        self.u_scr = self.dram_tmp("u_scr", [D_FF, S_LEN], BF16)
        if self.debug:
            self.o_scr = self.dram_out("o_scr", [3, D, S_LEN], BF16)
        else:
            self.o_scr = self.dram_tmp("o_scr", [3, D, S_LEN], BF16)
        self.g_scr = self.dram_tmp("g_scr", [3, D, S_LEN], BF16)
        self.dn_scr = {nm: self.dram_tmp("dn_" + nm, [8, 128, S_LEN], BF16) for nm in ("u", "wT", "q", "attnT", "kd", "sg")}

        pst = [nc.alloc_psum_tensor(f"ps{i}", [128, 512], F32) for i in range(8)]
        self.pst = pst
        self.psum = Ring(pst[:6], "psum")
        self.psum_acc = Ring(pst[6:], "psacc")

        self.consts = A.alloc("consts", [128, NCONST, 128], F32)
        self.cbf = A.alloc("cbf", [128, NCONST, 128], BF16)
        self.xT = A.alloc("xT", [128, KC, S_LEN], F32)
        self.nw = A.alloc("nw", [128, 2 * DEPTH + 1, KC], F32)
        KCON = "consts"
        self.dma(self.consts[:].rearrange("p a b -> p (a b)"), self.consts_d, reads=[], writes=[KCON])
        self.copy(self.cbf[:], self.consts[:], reads=[KCON], writes=["cbf"])
        for l in range(DEPTH):
            self.dma(self.nw[:, 2 * l, :], self.norm_mix_d[l].rearrange("(k p) -> p k", p=128), [], ["nw"], slow=True)
            self.dma(self.nw[:, 2 * l + 1, :], self.norm_mlp_d[l].rearrange("(k p) -> p k", p=128), [], ["nw"], slow=True)
        self.dma(self.nw[:, 2 * DEPTH, :], self.norm_final_d[0].rearrange("(k p) -> p k", p=128), [], ["nw"], slow=True)

        self.load_x()
        for l in range(L):
            self.layer(l)
        self.final_out()
        S.barrier()
        S.op("sp", None)
        S.finalize()

        with nc.Block() as block:

            @block.tensor
            def _(e):
                S.emit("pe", e)

            @block.scalar
            def _(e):
                S.emit("act", e)

            @block.vector
            def _(e):
                S.emit("dve", e)

            @block.gpsimd
            def _(e):
                S.emit("pool", e)

            @block.sync
            def _(e):
                S.emit("sp", e)

        return nc

    def ident_f(self):
        return self.consts[:, C_IDENT, :]

    def ident_b(self):
        return self.cbf[:, C_IDENT, :]

    def ones_b(self):
        return self.cbf[:, C_ONES, :]

    def load_x(self):
        A, S = self.A, self.S
        m = A.mark()
        stg = self.ring("xstg", 2, [128, 4, D], F32)
        for tg in range(4):
            st, sk = stg.next()
            self.dma(
                st[:],
                self.x_d[tg * 512:(tg + 1) * 512, :].rearrange("(t p) d -> p t d", p=128),
                reads=[],
                writes=[sk],
            )
            for kc in range(KC):
                ps, pk = self.psum.next()
                for t in range(4):
                    self.tr(ps[:, t * 128:(t + 1) * 128], st[:, t, kc * 128:(kc + 1) * 128], self.ident_f(),
                            reads=[sk, "consts"], writes=[pk])
                self.copy(self.xT[:, kc, tg * 512:(tg + 1) * 512], ps[:], reads=[pk], writes=[("xT", kc, tg)],
                          eng=("act" if kc % 2 else "dve"))
        S.barrier()
        A.reset(m)

    def rmsnorm(self):
        A, S = self.A, self.S
        m = A.mark()
        sq = self.ring("sq", 3, [128, 512], BF16)
        rs = self.ring("rs", 2, [128, 512], F32)
        for g in range(4):
            ps, pk = self.psum.next()
            sl = slice(g * 512, (g + 1) * 512)
            for kc in range(KC):
                q, qk = sq.next()
                self.act(q[:], self.xT[:, kc, sl], AF.Square, reads=[("xT", kc, g)], writes=[qk])
                self.mm(ps[:], self.ones_b(), q[:], kc == 0, kc == KC - 1, reads=[qk, "cbf"], writes=[pk])
            r, rk = rs.next()
            self.act(r[:], ps[:], AF.Ln, reads=[pk], writes=[rk], scale=1.0 / D, bias=EPS)
            self.act(r[:], r[:], AF.Exp, reads=[rk], writes=[rk], scale=-0.5)
            for kc in range(KC):
                self.tt(self.hT[:, kc, sl], self.xT[:, kc, sl], r[:], ALU.mult,
                        reads=[("xT", kc, g), rk], writes=[("hT", kc, g)])
        self.rstd_ring = rs
        S.barrier()
        A.reset(m)

    def eps_ap(self):
        return self.consts[:, C_EPS, 0:1]

    def wload(self, w_ap, kcn, n, wst_ring, wbf_ring, scale=None):
        st, sk = wst_ring.next()
        wb, wk = wbf_ring.next()
        self.dma(st[:, :kcn, :n], w_ap.rearrange("(k p) n -> p k n", p=128), reads=[], writes=[sk])
        for kc in range(kcn):
            if scale is not None:
                self.act(wb[:, kc, :n], st[:, kc, :n], AF.Copy, reads=[sk, "nw"], writes=[wk],
                         scale=scale[:, kc:kc + 1])
            else:
                self.act(wb[:, kc, :n], st[:, kc, :n], AF.Copy, reads=[sk], writes=[wk])
        return wb, wk

    def hT_keys(self, g):
        return [("hT", kc, g) for kc in range(KC)]

    def linear_T(self, wb, wk, ncols, kcn, rhs_fn, rhs_keys_fn, evac, groups=range(4)):
        for c0 in range(0, ncols, 128):
            n = min(128, ncols - c0)
            for g in groups:
                ps, pk = self.psum.next()
                for kc in range(kcn):
                    self.mm(ps[:n, :], wb[:, kc, c0:c0 + n], rhs_fn(kc, g), kc == 0, kc == kcn - 1,
                            reads=[wk] + rhs_keys_fn(g), writes=[pk])
                evac(c0 // 128, g, ps, pk, n)

    def mlp(self, l):
        A, S = self.A, self.S
        m = A.mark()
        self.hT = A.alloc("hT", [128, KC, S_LEN], BF16)
        self.rmsnorm()
        wst = self.ring("wst", 2, [128, KC, 512], F32)
        wbf = self.ring("wbf", 2, [128, KC, 512], BF16)
        ub = self.ring("ub", 3, [128, S_LEN], BF16)
        rr = self.ring("rr", 3, [128, 512], F32)
        scale = self.nw[:, 2 * l + 1, :]
        nxt = self.wload(self.w_up_d[l][:, 0:512], KC, 512, wst, wbf, scale=scale)
        for cb in range(D_FF // 512):
            wb, wk = nxt
            if cb + 1 < D_FF // 512:
                nxt = self.wload(self.w_up_d[l][:, (cb + 1) * 512:(cb + 2) * 512], KC, 512, wst, wbf, scale=scale)
            cur = {}

            def evac(ci, g, ps, pk, n, cb=cb, cur=cur):
                if g == 0:
                    cur["u"] = ub.next()
                u, uk = cur["u"]
                r, rk = rr.next()
                self.ts(r[:], ps[:], 0.0, ALU.max, reads=[pk], writes=[rk])
                self.act(u[:, g * 512:(g + 1) * 512], r[:], AF.Square, reads=[rk], writes=[uk])
                if g == 3:
                    f = cb * 4 + ci
                    self.dma(self.u_scr[f * 128:(f + 1) * 128, :], u[:], reads=[uk], writes=[("u_scr", f)])

            self.linear_T(wb, wk, 512, KC, lambda kc, g: self.hT[:, kc, g * 512:(g + 1) * 512],
                          self.hT_keys, evac)
        S.barrier()
        A.reset(m)
        wst = self.ring("wdst", 2, [128, 4, 512], F32)
        wbf = self.ring("wdbf", 2, [128, 32, 512], BF16)
        ur = self.ring("ur", 1, [128, 32, 512], BF16)

        def load_down(half):
            wb, wk = wbf.next()
            for q in range(8):
                st, sk = wst.next()
                self.dma(st[:], self.w_down_d[l][q * 512:(q + 1) * 512, half * 512:(half + 1) * 512]
                         .rearrange("(k p) n -> p k n", p=128), reads=[], writes=[sk])
                for k4 in range(4):
                    self.act(wb[:, q * 4 + k4, :], st[:, k4, :], AF.Copy, reads=[sk], writes=[wk])
            return wb, wk

        nxt = load_down(0)
        for half in range(2):
            wb, wk = nxt
            if half == 0:
                nxt = load_down(1)
            for g in range(4):
                u, uk = ur.next()
                self.dma(u[:], self.u_scr[:, g * 512:(g + 1) * 512].rearrange("(k p) t -> p k t", p=128),
                         reads=[("u_scr", f) for f in range(32)], writes=[uk])
                for ci in range(4):
                    ps, pk = self.psum.next()
                    for kc in range(32):
                        self.mm(ps[:], wb[:, kc, ci * 128:(ci + 1) * 128], u[:, kc, :], kc == 0, kc == 31,
                                reads=[wk, uk], writes=[pk])
                    oc = half * 4 + ci
                    sl = slice(g * 512, (g + 1) * 512)
                    self.tt(self.xT[:, oc, sl], self.xT[:, oc, sl], ps[:], ALU.add,
                            reads=[pk, ("xT", oc, g)], writes=[("xT", oc, g)])
        S.barrier()
        A.reset(m)

    def layer(self, l):
        if self.mixers:
            self.mixer(l)
        if self.do_mlp:
            self.mlp(l)


    def mixer(self, l):
        A, S = self.A, self.S
        m00 = A.mark()
        if "dn" in self.mixers:
            self.dn_egc = A.alloc("degc", [128, 16, 8], F32)
            self.dn_eglast = A.alloc("deglast", [128, 32, 8], F32)
            self.dn_nwb, self.dn_nwbk = self.bload("dnwb", self.dn_norm_w_d[l], 128)
        m0 = A.mark()
        self.hT = A.alloc("hT", [128, KC, S_LEN], BF16)
        self.rmsnorm()
        m1 = A.mark()
        if "sb" in self.mixers:
            self.sb_phase(l)
            S.barrier()
            A.reset(m1)
        if "ssm" in self.mixers:
            self.ssm_phase(l)
            S.barrier()
            A.reset(m1)
        if "dn" in self.mixers:
            self.dn_phase(l)
            S.barrier()
            A.reset(m1)
        self.gates_phase(l)
        S.barrier()
        A.reset(m0)
        if "dn" in self.mixers:
            self.dn_phase2(l)
            S.barrier()
            A.reset(m0)
        self.merge_phase(l)
        S.barrier()
        A.reset(m00)

    def dn_phase2(self, l):
        A, S = self.A, self.S
        cst = self.consts
        pst = self.pst
        NH = 8
        names = ["u", "wT", "q", "attnT", "kd", "sg"]
        sets = [[A.alloc(f"d2{n}", [128, NH, 512], BF16) for n in names] for _ in range(1)]
        Sst = A.alloc("d2S", [128, NH, 128], F32)
        Sbf = A.alloc("d2Sbf", [128, NH, 128], BF16)
        vnr = self.ring("d2vn", 2, [128, NH, 128], BF16)
        t1 = A.alloc("d2t1", [128, NH, 128], F32)
        ot = A.alloc("d2ot", [128, NH, 128], F32)
        sq = A.alloc("d2sq", [128, NH, 128], F32)
        ssq = A.alloc("d2ssq", [128, 2, NH], F32)
        obr = self.ring("d2ob", 2, [128, NH, 128], BF16)
        otl = self.ring("d2otl", 2, [128, NH, 128], BF16)
        egc, eglast, nwb, nwbk = self.dn_egc, self.dn_eglast, self.dn_nwb, self.dn_nwbk
        self.memset(Sst[:], 0.0, writes=[("d2S", 0), ("d2S", 1)])
        self.memset(Sbf[:], 0.0, writes=[("d2Sbf", 0), ("d2Sbf", 1)])
        bO = [(pst[i], ("p2O", i)) for i in range(4)]
        bW = [(pst[4], ("p2W", 0)), (pst[5], ("p2W", 1))]
        bS = [(pst[6], ("p2S", 0)), (pst[7], ("p2S", 1))]
        o_view = self.o_scr[0].rearrange("(h e) s -> e h s", h=NH)
        for q in range(4):
            st = sets[0]
            sk = ("d2set", 0)
            for ni, nm in enumerate(names):
                self.dma(st[ni][:], self.dn_scr[nm][:, :, q * 512:(q + 1) * 512].rearrange("h p t -> p h t"),
                         reads=[("dn_scr", nm, h) for h in range(NH)], writes=[(sk, ni)])
            U, WT, Q, AT, KD, SG = st
            for tl in range(4):
                t = 4 * q + tl
                cs = slice(tl * 128, (tl + 1) * 128)
                vn, vnk = vnr.next()
                for c in range(2):
                    j = 2 * t + c
                    pb = 64 * c
                    prt = slice(pb, pb + 64)
                    ccs = slice(tl * 128 + pb, tl * 128 + pb + 64)
                    for h in range(NH):
                        pw, pwk = bW[h // 4]
                        self.mm(pw[prt, (h % 4) * 128:(h % 4 + 1) * 128], WT[:, h, ccs], Sbf[:, h, :], True, True,
                                reads=[(sk, 1), ("d2Sbf", h // 4)], writes=[pwk])
                    for hb in range(2):
                        pw, pwk = bW[hb]
                        self.tt(vn[prt, 4 * hb:4 * hb + 4, :], U[prt, 4 * hb:4 * hb + 4, cs],
                                pw[prt, :].rearrange("p (a b) -> p a b", a=4), ALU.subtract, reads=[(sk, 0), pwk],
                                writes=[(vnk, c, hb)])
                    for h in range(NH):
                        po, pok = bO[h // 2]
                        co = (h % 2) * 256
                        self.mm(po[prt, co:co + 128], Q[:, h, ccs], Sbf[:, h, :], True, True,
                                reads=[(sk, 2), ("d2Sbf", h // 4)], writes=[(pok, c)])
                        self.mm(po[prt, co + 128:co + 256], AT[prt, h, ccs], vn[prt, h, :], True, True,
                                reads=[(sk, 3), (vnk, c, h // 4)], writes=[(pok, c)])
                        ps_, psk_ = bS[h // 4]
                        self.mm(ps_[:, (h % 4) * 128:(h % 4 + 1) * 128], KD[prt, h, cs], vn[prt, h, :], True, True,
                                reads=[(sk, 4), (vnk, c, h // 4)], writes=[psk_])
                    for hb in range(2):
                        ps_, psk_ = bS[hb]
                        hs = slice(4 * hb, 4 * hb + 4)
                        self.tt(Sst[:, hs, :], Sst[:, hs, :], eglast[:, j, hs].unsqueeze(2).to_broadcast([128, 4, 128]),
                                ALU.mult, reads=[("d2S", hb), "deglast"], writes=[("d2S", hb)])
                        self.tt(Sst[:, hs, :], Sst[:, hs, :], ps_[:].rearrange("p (a b) -> p a b", a=4), ALU.add,
                                reads=[("d2S", hb), psk_], writes=[("d2S", hb)])
                        self.copy(Sbf[:, hs, :], Sst[:, hs, :], reads=[("d2S", hb)], writes=[("d2Sbf", hb)], eng="act")
                for b4 in range(4):
                    po, pok = bO[b4]
                    hs = slice(2 * b4, 2 * b4 + 2)
                    pv = po[:].rearrange("p (a b) -> p a b", a=2)
                    self.tt(t1[:, hs, :], pv[:, :, 0:128], egc[:, t, hs].unsqueeze(2).to_broadcast([128, 2, 128]), ALU.mult,
                            reads=[(pok, 0), (pok, 1), "degc"], writes=[("d2t1", b4)])
                    self.tt(ot[:, hs, :], t1[:, hs, :], pv[:, :, 128:256], ALU.add,
                            reads=[("d2t1", b4), (pok, 0), (pok, 1)], writes=[("d2ot", b4)])
                otk = [("d2ot", b4) for b4 in range(4)]
                self.act(sq[:], ot[:], AF.Square, reads=otk, writes=["d2sq"])
                self.S.op("dve", lambda e: e.tensor_reduce(out=ssq[:, 0, :], in_=sq[:], op=ALU.add,
                                                           axis=mybir.AxisListType.X),
                          reads=["d2sq"], writes=["d2ssq"])
                self.act(ssq[:, 1, :], ssq[:, 0, :], AF.Ln, reads=["d2ssq", "consts"], writes=["d2ssq"], scale=1.0 / 128,
                         bias=EPS)
                self.act(ssq[:, 1, :], ssq[:, 1, :], AF.Exp, reads=["d2ssq"], writes=["d2ssq"], scale=-0.5)
                self.tt(ot[:], ot[:], ssq[:, 1, :].unsqueeze(2).to_broadcast([128, NH, 128]), ALU.mult,
                        reads=otk + ["d2ssq"], writes=otk)
                self.tt(ot[:], ot[:], nwb[:].unsqueeze(1).to_broadcast([128, NH, 128]), ALU.mult, reads=otk + [nwbk],
                        writes=otk)
                ob, obk = obr.next()
                self.tt(ob[:], ot[:], SG[:, :, cs], ALU.mult, reads=otk + [(sk, 5)], writes=[obk])
                px, pxk = bW[0]
                pxb = px[:].bitcast(BF16)
                for h in range(NH):
                    self.tr(pxb[:, h * 128:(h + 1) * 128], ob[:, h, :], self.ident_b(), reads=[obk, "cbf"], writes=[pxk])
                otile, otlk = otl.next()
                self.copy(otile[:], pxb[:].rearrange("p (a b) -> p a b", a=NH), reads=[pxk], writes=[otlk], eng="act")
                self.dma(o_view[:, :, t * 128:(t + 1) * 128], otile[:], reads=[otlk], writes=[("o_scr", 0, "t", t)])

    def branches(self):
        return [i for i, n in enumerate(("dn", "sb", "ssm")) if n in self.mixers]

    def gates_phase(self, l):
        wst = self.ring("gwst", 2, [128, KC, 512], F32)
        wbf = self.ring("gwbf", 2, [128, KC, 512], BF16)
        gb = self.ring("gb", 3, [128, S_LEN], BF16)
        scale = self.nw[:, 2 * l, :]
        blocks = [(i, cb) for i in self.branches() for cb in range(2)]

        def gload(bi):
            i_, cb_ = blocks[bi]
            c0_ = O_GATE + i_ * D + cb_ * 512
            return self.wload(self.w_in_d[l][:, c0_:c0_ + 512], KC, 512, wst, wbf, scale=scale)

        nxt = gload(0)
        for bi_, (i, cb) in enumerate(blocks):
            if True:
                wb, wk = nxt
                if bi_ + 1 < len(blocks):
                    nxt = gload(bi_ + 1)
                cur = {}

                def evac(ci, g, ps, pk, n, cb=cb, i=i, cur=cur):
                    if g == 0:
                        cur["t"] = gb.next()
                    t, tk = cur["t"]
                    self.act(t[:, g * 512:(g + 1) * 512], ps[:], AF.Sigmoid, reads=[pk], writes=[tk])
                    if g == 3:
                        oc = cb * 4 + ci
                        self.dma(self.g_scr[i][oc * 128:(oc + 1) * 128, :], t[:], reads=[tk],
                                 writes=[("g_scr", i, oc)])

                self.linear_T(wb, wk, 512, KC, lambda kc, g: self.hT[:, kc, g * 512:(g + 1) * 512],
                              self.hT_keys, evac)

    def merge_phase(self, l):
        A, S = self.A, self.S
        wst = self.ring("mwst", 1, [128, KC, 512], F32)
        wbf = self.ring("mwbf", 2, [128, KC, 512], BF16)
        mg = A.alloc("mg", [128, KC, 1024], F32)
        mgb = A.alloc("mgb", [128, KC, 1024], BF16)
        ob = self.ring("ob", 1, [128, KC, 1024], BF16)
        gt = self.ring("gt", 3, [128, 1024], BF16)
        self.mtmp = self.ring("mtmp", 3, [128, 512], F32)
        brs = self.branches()
        mseq = []
        for half_ in range(2):
            for i_ in brs:
                for cb_ in range(2):
                    mseq.append(self.w_branch_d[l][i_][:, cb_ * 512:(cb_ + 1) * 512])
            for cb_ in range(2):
                mseq.append(self.w_out_d[l][:, cb_ * 512:(cb_ + 1) * 512])

        def mload(pos):
            if pos >= len(mseq):
                return None
            return self.wload(mseq[pos], KC, 512, wst, wbf)

        mpos = [0]
        mnxt = [mload(0)]
        for half in range(2):
            tsl = slice(half * 1024, (half + 1) * 1024)
            for bi, i in enumerate(brs):
                o, ok = ob.next()
                self.dma(o[:], self.o_scr[i][:, tsl].rearrange("(k p) t -> p k t", p=128),
                         reads=[("o_scr", i, c) for c in range(8)], writes=[ok])
                for cb in range(2):
                    wb, wk = mnxt[0]
                    mnxt[0] = mload(mpos[0] + 1)
                    mpos[0] += 1
                    cur = {}

                    def evac(ci, g, ps, pk, n, cb=cb, i=i, bi=bi, cur=cur, half=half, tsl=tsl):
                        oc = cb * 4 + ci
                        gl = g - 2 * half
                        if gl == 0:
                            cur["g"] = gt.next()
                            t, tk = cur["g"]
                            self.dma(t[:], self.g_scr[i][oc * 128:(oc + 1) * 128, tsl],
                                     reads=[("g_scr", i, oc)], writes=[tk])
                        t, tk = cur["g"]
                        dst = mg[:, oc, gl * 512:(gl + 1) * 512]
                        mk = ("mg", oc, gl)
                        if bi == 0:
                            self.tt(dst, ps[:], t[:, gl * 512:(gl + 1) * 512], ALU.mult,
                                    reads=[pk, tk], writes=[mk])
                        else:
                            tmp, tmk = self.mtmp.next()
                            self.tt(tmp[:], ps[:], t[:, gl * 512:(gl + 1) * 512], ALU.mult,
                                    reads=[pk, tk], writes=[tmk])
                            self.tt(dst, dst, tmp[:], ALU.add, reads=[tmk, mk], writes=[mk], eng="pool")
                        if bi == len(brs) - 1:
                            self.copy(mgb[:, oc, gl * 512:(gl + 1) * 512], dst, reads=[mk],
                                      writes=[("mgb", oc, gl)], eng="act")

                    self.linear_T(wb, wk, 512, KC, lambda kc, g, o=o, half=half: o[:, kc, (g - 2 * half) * 512:(g - 2 * half + 1) * 512],
                                  lambda g, ok=ok: [ok], evac, groups=(2 * half, 2 * half + 1))
            for cb in range(2):
                wb, wk = mnxt[0]
                mnxt[0] = mload(mpos[0] + 1)
                mpos[0] += 1

                def evac2(ci, g, ps, pk, n, cb=cb):
                    oc = cb * 4 + ci
                    sl = slice(g * 512, (g + 1) * 512)
                    self.tt(self.xT[:, oc, sl], self.xT[:, oc, sl], ps[:], ALU.add,
                            reads=[pk, ("xT", oc, g)], writes=[("xT", oc, g)])

                self.linear_T(wb, wk, 512, KC,
                              lambda kc, g, half=half: mgb[:, kc, (g - 2 * half) * 512:(g - 2 * half + 1) * 512],
                              lambda g, half=half: [("mgb", kc, g - 2 * half) for kc in range(KC)], evac2,
                              groups=(2 * half, 2 * half + 1))

    def sb_phase(self, l):
        A, S = self.A, self.S
        R = {}
        R["wst"] = self.ring("sbwst", 1, [128, KC, 384], F32)
        R["wbf"] = self.ring("sbwbf", 2, [128, KC, 384], BF16)
        R["qT"] = self.ring("sbq", 2, [128, 2, S_LEN], BF16)
        for i_, t_ in enumerate(R["qT"].tiles):
            self.memset(t_[:], 0.0, writes=[(R["qT"].name, i_)], eng="pool")
        R["kT"] = self.ring("sbk", 2, [128, S_LEN], BF16)
        R["v"] = self.ring("sbv", 2, [128, 16, 128], BF16)
        R["osb"] = self.ring("sbo", 1, [128, S_LEN], BF16)
        R["e"] = self.ring("sbe", 4, [128, 512], F32)
        R["spb"] = self.ring("sbsp", 4, [128, 512], BF16)
        R["xa"] = self.ring("sbxa", 2, [128, 512], F32)
        R["w"] = self.ring("sbw", 3, [128, 512], BF16)
        R["pa"] = Ring(self.pst[0:4], "psum")
        wnext = self.sb_wload(l, 0, R)
        for hp in range(8):
            wcur = wnext
            wnext = self.sb_wload(l, hp + 1, R) if hp + 1 < 8 else None
            self.sb_unit(l, hp, R, wcur)

    def sb_wload(self, l, hp, R):
        scale = self.nw[:, 2 * l, :]
        st, sk = R["wst"].next()
        wb, wk = R["wbf"].next()
        for j in range(3):
            base = O_SBQKV + j * 1024 + 128 * hp
            self.dma(st[:, :, j * 128:(j + 1) * 128],
                     self.w_in_d[l][:, base:base + 128].rearrange("(k p) n -> p k n", p=128),
                     reads=[], writes=[(sk, j)])
        for kc in range(KC):
            self.ts(wb[:, kc, :], st[:, kc, :], scale[:, kc:kc + 1], ALU.mult, reads=[(sk, 0), (sk, 1), (sk, 2), "nw"],
                    writes=[wk], s2=1.0, op1=ALU.mult, eng="pool")
        return wb, wk

    def sb_unit(self, l, hp, R, wcur):
        wb, wk = wcur
        qT, qk = R["qT"].next()
        kT, kk = R["kT"].next()
        v, vk = R["v"].next()

        def evac_qk(ci, g, ps, pk, n):
            if ci == 0:
                self.copy(qT[0:64, 0, g * 512:(g + 1) * 512], ps[0:64, :], reads=[pk, qk], writes=[(qk, g)], eng="dve")
                self.copy(qT[64:128, 1, g * 512:(g + 1) * 512], ps[64:128, :], reads=[pk, qk], writes=[(qk, g)], eng="dve")
            else:
                self.copy(kT[:, g * 512:(g + 1) * 512], ps[:], reads=[pk], writes=[(kk, g)], eng="dve")

        self.linear_T(wb, wk, 256, KC, lambda kc, g: self.hT[:, kc, g * 512:(g + 1) * 512], self.hT_keys, evac_qk)
        for tq in range(4):
            ps, pk = self.psum.next()
            for t in range(4):
                tt_ = tq * 4 + t
                for kc in range(KC):
                    self.mm(ps[:, t * 128:(t + 1) * 128], self.hT[:, kc, tt_ * 128:(tt_ + 1) * 128],
                            wb[:, kc, 256:384], kc == 0, kc == KC - 1, reads=[wk, ("hT", kc, tq)], writes=[pk])
            self.copy(v[:, tq * 4:(tq + 1) * 4, :], ps[:].rearrange("p (t c) -> p t c", t=4), reads=[pk],
                      writes=[(vk, tq)], eng="dve")
        osb, ok = R["osb"].next()
        mstrict = self.consts[:, C_MSTRICT, :]
        tinc = self.cbf[:, C_TINC, :]
        tlow = self.cbf[:, C_MSTRICT, :]
        one_col = self.consts[:, C_ONES, 0:1]
        pst = self.pst
        pa_ring = R["pa"]
        for g in range(4):
            racc = [(pst[4], ("psum", 4)), (pst[5], ("psum", 5))]
            po = [(pst[6], ("psacc", 0)), (pst[7], ("psacc", 1))]
            tiles = []
            for kb in range(4 * g + 3, -1, -1):
                for e in range(2):
                    t0 = max(kb * 128, g * 512)
                    tiles.append(dict(e=e, kb=kb, pb=64 * e, t0=t0, N=(g + 1) * 512 - t0, c0=t0 - g * 512,
                                      diag=kb * 128 >= g * 512, first=(kb == 4 * g + 3), last=(kb == 0)))

            def stage0(T):
                pa, pak = pa_ring.next()
                pb, N, t0, kb = T["pb"], T["N"], T["t0"], T["kb"]
                self.mm(pa[:, :N], kT[:, kb * 128:(kb + 1) * 128], qT[:, T["e"], t0:t0 + N], True, not T["diag"],
                        reads=[(kk, kb // 4), (qk, g)], writes=[pak])
                if T["diag"]:
                    self.mm(pa[:, 0:128], self.ident_b(), self.cbf[:, C_NEGSB, :], False, True, reads=["cbf"], writes=[pak])
                T["pa"], T["pak"] = pa, pak

            def stage1(T):
                N, c0 = T["N"], T["c0"]
                ee, ek = R["e"].next()
                self.act(ee[:, :N], T["pa"][:, :N], AF.Exp, reads=[T["pak"]], writes=[ek], scale=0.125)
                spb, spk = R["spb"].next()
                self.act(spb[:, :N], ee[:, :N], AF.Ln, reads=[ek], writes=[spk], bias=1.0, scale=1.0)
                ra, rak = racc[T["e"]]
                self.mm(ra[:, c0:512], tinc, spb[:, :N], T["first"], False, reads=[spk, "cbf"], writes=[rak],
                        skip_group_check=True)
                T.update(ee=ee, ek=ek, spb=spb, spk=spk)

            def stage2(T):
                N, c0, pb, kb = T["N"], T["c0"], T["pb"], T["kb"]
                ra, rak = racc[T["e"]]
                xa, xk = R["xa"].next()
                self.act(xa[:, :N], ra[:, c0:512], AF.Exp, reads=[rak], writes=[xk], scale=-1.0)
                if not T["last"]:
                    self.mm(ra[:, c0:512], tlow, T["spb"][:, :N], False, True, reads=[T["spk"], "cbf"], writes=[rak],
                            skip_group_check=True)
                w, wwk = R["w"].next()
                self.tt(w[:, :N], T["ee"][:, :N], xa[:, :N], ALU.mult, reads=[T["ek"], xk], writes=[wwk])
                pp, ppk = po[T["e"]]
                self.mm(pp[:, c0:512], v[:, kb, :], w[:, :N], T["first"], T["last"],
                        reads=[(vk, kb // 4), wwk], writes=[ppk], skip_group_check=True)

            n = len(tiles)
            for i in range(min(2, n)):
                stage0(tiles[i])
            for i in range(n + 2):
                if i - 2 >= 0:
                    stage2(tiles[i - 2])
                if i < n:
                    stage1(tiles[i])
                if i + 2 < n:
                    stage0(tiles[i + 2])
            for e in range(2):
                pp, ppk = po[e]
                pb = 64 * e
                self.copy(osb[pb:pb + 64, g * 512:(g + 1) * 512], pp[pb:pb + 64, :], reads=[ppk],
                          writes=[(ok, e, g)], eng="dve")
        self.dma(self.o_scr[1][hp * 128:(hp + 1) * 128, :], osb[:],
                 reads=[(ok, e, g) for e in range(2) for g in range(4)], writes=[("o_scr", 1, hp)])


    def bload(self, name, dram_row_ap, n):
        t = self.A.alloc(name, [128, n], F32)
        k = self.key(name)
        self.dma(t[:], dram_row_ap.partition_broadcast(128), reads=[], writes=[k])
        return t, k

    def conv_silu(self, l, wb, wk, wcol, conv_w_d, conv_b_d, ch, raw, rawk, acc, acck, cw, dst, dstk, func=AF.Silu):
        c, ck = cw.next()
        self.dma(c[:, 0:4], conv_w_d[l][:, ch:ch + 128].rearrange("k c -> c k"), reads=[], writes=[(ck, 0)], slow=True)
        if conv_b_d is not None:
            self.dma(c[:, 4:5], conv_b_d[l][ch:ch + 128].rearrange("(c o) -> c o", o=1), reads=[], writes=[(ck, 1)],
                     slow=True)
        else:
            self.memset(c[:, 4:5], 0.0, writes=[(ck, 1)])

        def evac(ci, g, ps, pk, n):
            self.copy(raw[:, 3 + g * 512:3 + (g + 1) * 512], ps[:], reads=[pk], writes=[(rawk, g)],
                      eng=("act" if g % 2 else "dve"))

        for g in range(4):
            ps, pk = self.psum.next()
            for kc in range(KC):
                self.mm(ps[:], wb[:, kc, wcol:wcol + 128], self.hT[:, kc, g * 512:(g + 1) * 512], kc == 0, kc == KC - 1,
                        reads=[wk] + self.hT_keys(g), writes=[pk])
            evac(0, g, ps, pk, 128)
        rk = [(rawk, g) for g in range(4)] + [(rawk, "pad")]
        self.ts(acc[:], raw[:, 0:S_LEN], c[:, 0:1], ALU.mult, reads=rk + [(ck, 0), (ck, 1)], writes=[acck],
                s2=c[:, 4:5], op1=ALU.add)
        for k in range(1, 4):
            self.stt(acc[:], raw[:, k:k + S_LEN], c[:, k:k + 1], acc[:], ALU.mult, ALU.add,
                     reads=rk + [(ck, 0), acck], writes=[acck])
        self.act(dst, acc[:], func, reads=[acck], writes=[dstk])

    def conv_P(self, l, wb, wk, wcol, conv_w_d, conv_b_d, ch, cw):
        c, ck = cw.next()
        self.dma(c[:, 0:4], conv_w_d[l][:, ch:ch + 128].rearrange("k c -> c k"), reads=[], writes=[(ck, 0)], slow=True)
        if conv_b_d is not None:
            self.dma(c[:, 4:5], conv_b_d[l][ch:ch + 128].rearrange("(c o) -> c o", o=1), reads=[], writes=[(ck, 1)],
                     slow=True)
        else:
            self.memset(c[:, 4:5], 0.0, writes=[(ck, 1)])
        banks = []
        for g in range(4):
            ps, pk = self.psum.next()
            for kc in range(KC):
                self.mm(ps[:], wb[:, kc, wcol:wcol + 128], self.hT[:, kc, g * 512:(g + 1) * 512], kc == 0, kc == KC - 1,
                        reads=[wk] + self.hT_keys(g), writes=[pk])
            banks.append((ps, pk))
        return dict(c=c, ck=ck, banks=banks)

    def conv_E(self, H, raw, rawk):
        for g, (ps, pk) in enumerate(H["banks"]):
            self.copy(raw[:, 3 + g * 512:3 + (g + 1) * 512], ps[:], reads=[pk], writes=[(rawk, g)],
                      eng=("act" if g % 2 else "dve"))

    def conv_C(self, H, raw, rawk, acc, acck, dst, dstk, func=AF.Silu):
        c, ck = H["c"], H["ck"]
        rk = [(rawk, g) for g in range(4)] + [(rawk, "pad")]
        self.ts(acc[:], raw[:, 0:S_LEN], c[:, 0:1], ALU.mult, reads=rk + [(ck, 0), (ck, 1)], writes=[acck],
                s2=c[:, 4:5], op1=ALU.add)
        for k in range(1, 4):
            self.stt(acc[:], raw[:, k:k + S_LEN], c[:, k:k + 1], acc[:], ALU.mult, ALU.add,
                     reads=rk + [(ck, 0), acck], writes=[acck])
        self.act(dst, acc[:], func, reads=[acck], writes=[dstk])

    def to_tok(self, src, srck, dst_fn, dstk):
        for tq in range(4):
            ps, pk = self.psum.next()
            pb = ps[:].bitcast(BF16)
            for t in range(4):
                tt_ = tq * 4 + t
                self.tr(pb[:, t * 128:(t + 1) * 128], src[:, tt_ * 128:(tt_ + 1) * 128], self.ident_b(),
                        reads=(list(srck) if isinstance(srck, list) else [srck]) + ["cbf"], writes=[pk])
            self.copy(dst_fn(tq), pb[:, 0:512].rearrange("p (t c) -> p t c", t=4), reads=[pk], writes=[(dstk, tq)],
                      eng=("act" if tq % 2 else "dve"))

    def ssm_phase(self, l):
        A, S = self.A, self.S
        scale = self.nw[:, 2 * l, :]
        cst = self.consts
        wdst = A.alloc("wdtst", [128, KC, 16], F32)
        wdt = A.alloc("wdt", [128, KC, 16], BF16)
        self.dma(wdst[:], self.w_in_d[l][:, O_SSMDT:O_SSMDT + 16].rearrange("(k p) n -> p k n", p=128), [], ["wdtst"])
        for kc in range(KC):
            self.act(wdt[:, kc, :], wdst[:, kc, :], AF.Copy, reads=["wdtst", "nw"], writes=["wdt"], scale=scale[:, kc:kc + 1])
        dtb, dtbk = self.bload("dtb", self.ssm_dt_bias_d[l], 16)
        alog, alogk = self.bload("alog", self.ssm_a_log_d[l], 16)
        dbc, dbck = self.bload("dbc", self.ssm_d_d[l], 16)
        dt = A.alloc("dt", [128, 16, 16], F32)
        av = A.alloc("av", [128, 16, 16], F32)
        acum = A.alloc("acum", [128, 16, 16], F32)
        eacum = A.alloc("eacum", [128, 16, 16], F32)
        dtds = A.alloc("dtds", [128, 16, 16], F32)
        eatot = A.alloc("eatot", [128, 32, 16], F32)
        tmp = A.alloc("ptmp", [128, 16, 16], F32)
        ps, pk = self.psum.next()
        for t in range(16):
            for kc in range(KC):
                self.mm(ps[:, t * 16:(t + 1) * 16], self.hT[:, kc, t * 128:(t + 1) * 128], wdt[:, kc, :], kc == 0,
                        kc == KC - 1, reads=["wdt", ("hT", kc, t // 4)], writes=[pk])
        self.tt(dt[:], ps[:, 0:256].rearrange("p (t h) -> p t h", t=16),
                dtb[:].unsqueeze(1).to_broadcast([128, 16, 16]), ALU.add, reads=[pk, dtbk], writes=["dt"])
        one_col = cst[:, C_ONES, 0:1]
        self.act(tmp[:], dt[:], AF.Exp, reads=["dt"], writes=["ptmp"])
        self.act(dt[:], tmp[:], AF.Ln, reads=["ptmp", "consts"], writes=["dt"], bias=1.0, scale=1.0)
        self.act(alog[:], alog[:], AF.Exp, reads=[alogk], writes=[alogk])
        self.S.op("dve", lambda e: e.scalar_tensor_tensor(out=av[:], in0=dt[:], scalar=-1.0,
                                                          in1=alog[:].unsqueeze(1).to_broadcast([128, 16, 16]),
                                                          op0=ALU.mult, op1=ALU.mult),
                  reads=["dt", alogk], writes=["av"])
        ps, pk = self.psum.next()
        for t in range(16):
            self.mm(ps[:, t * 16:(t + 1) * 16], cst[:, C_TRI2, :], av[:, t, :], True, True, reads=["av", "consts"], writes=[pk])
        self.copy(acum[:], ps[:, 0:256].rearrange("p (t h) -> p t h", t=16), reads=[pk], writes=["acum"])
        self.act(eacum[:], acum[:], AF.Exp, reads=["acum"], writes=["eacum"])
        ps, pk = self.psum.next()
        for t in range(16):
            self.mm(ps[:, t * 16:(t + 1) * 16], cst[:, C_BD, :], av[:, t, :], True, True, reads=["av", "consts"], writes=[pk])
        self.tt(tmp[:], ps[:, 0:256].rearrange("p (t h) -> p t h", t=16), acum[:], ALU.subtract, reads=[pk, "acum"],
                writes=["ptmp"])
        self.act(tmp[:], tmp[:], AF.Exp, reads=["ptmp"], writes=["ptmp"])
        self.tt(dtds[:], tmp[:], dt[:], ALU.mult, reads=["ptmp", "dt"], writes=["dtds"])
        ps, pk = self.psum.next()
        for t in range(16):
            for c in range(2):
                j = 2 * t + c
                self.mm(ps[:, j * 16:(j + 1) * 16], cst[:, C_IND0 + c, :], av[:, t, :], True, True,
                        reads=["av", "consts"], writes=[pk])
        self.act(eatot[:].rearrange("p j h -> p (j h)"), ps[:], AF.Exp, reads=[pk], writes=["eatot"])

        mG = A.mark()
        for g in range(4):
            A.reset(mG)
            S.barrier()
            wbf = A.alloc("swz", [128, KC, 256], BF16)
            BT = A.alloc("sBT", [128, S_LEN], BF16)
            CT = A.alloc("sCT", [128, S_LEN], BF16)
            x_tok = A.alloc("sxtok", [128, 16, 256], BF16)
            B_tok = A.alloc("sBtok", [128, 16, 128], BF16)
            nwb, nwbk = self.bload("snwb", self.ssm_norm_w_d[l][256 * g:256 * (g + 1)], 256)
            mA = A.mark()
            wst = self.ring("swst", 2, [128, KC, 128], F32)
            wbx = A.alloc("swbx", [128, KC, 512], BF16)
            raw = A.alloc("sraw", [128, S_LEN + 4], F32)
            acc = A.alloc("sacc", [128, S_LEN], F32)
            xc = self.ring("sxc", 2, [128, S_LEN], BF16)
            cw = self.ring("scw", 2, [128, 8], F32)
            rawk = self.key("sraw")
            self.memset(raw[:, 0:3], 0.0, writes=[(rawk, "pad")])
            cols = [O_SSMZ + 256 * g, O_SSMZ + 256 * g + 128, O_SSMXBC + 256 * g, O_SSMXBC + 256 * g + 128,
                    O_SSMXBC + 1024 + 128 * g, O_SSMXBC + 1536 + 128 * g]
            wk = self.key("swbf")
            for j, c0 in enumerate(cols):
                st, sk = wst.next()
                self.dma(st[:], self.w_in_d[l][:, c0:c0 + 128].rearrange("(k p) n -> p k n", p=128), [], [sk])
                for kc in range(KC):
                    wdst_ = wbf[:, kc, j * 128:(j + 1) * 128] if j < 2 else wbx[:, kc, (j - 2) * 128:(j - 1) * 128]
                    self.act(wdst_, st[:, kc, :], AF.Copy, reads=[sk, "nw"], writes=[(wk, j)],
                             scale=scale[:, kc:kc + 1])
            chs = [256 * g, 256 * g + 128, 1024 + 128 * g, 1536 + 128 * g]
            acck = self.key("sacc")
            xtk = self.key("sxtok")
            btk = self.key("sBtok")
            def s_P(jj):
                return self.conv_P(l, wbx, (wk, jj + 2), jj * 128, self.ssm_conv_w_d, self.ssm_conv_b_d, chs[jj], cw)

            dsts = {}

            def s_C(jj, H_):
                if jj < 2:
                    dst, dk = xc.next()
                elif jj == 2:
                    dst, dk = BT, "sBT"
                else:
                    dst, dk = CT, "sCT"
                self.conv_C(H_, raw, rawk, acc, acck, dst[:], dk)
                dsts[jj] = (dst, dk)

            def s_N(jj):
                dst, dk = dsts[jj]
                if jj < 2:
                    self.to_tok(dst, dk, lambda tq, jj=jj: x_tok[:, tq * 4:(tq + 1) * 4, jj * 128:(jj + 1) * 128], (xtk, jj))
                elif jj == 2:
                    self.to_tok(dst, dk, lambda tq: B_tok[:, tq * 4:(tq + 1) * 4, :], btk)

            Hs = {0: s_P(0)}
            self.conv_E(Hs[0], raw, rawk)
            for jj in range(4):
                if jj + 1 < 4:
                    Hs[jj + 1] = s_P(jj + 1)
                s_C(jj, Hs[jj])
                if jj + 1 < 4:
                    self.conv_E(Hs[jj + 1], raw, rawk)
                s_N(jj)
            S.barrier()
            A.reset(mA)
            xdt = A.alloc("sxdt", [128, 16, 256], BF16)
            xdtd = A.alloc("sxdtd", [128, 16, 256], BF16)
            oT = A.alloc("soT", [128, 2, S_LEN], BF16)
            state = A.alloc("sstate", [128, 256], F32)
            state_bf = A.alloc("sstatebf", [128, 256], BF16)
            abc = self.ring("sabc", 2, [128, 128], F32)
            dm = self.ring("sdm", 2, [128, 4, 128], F32)
            MT = self.ring("sMT", 2, [128, 4, 128], BF16)
            xDr = self.ring("sxD", 2, [128, 256], BF16)
            szr = self.ring("ssz", 2, [128, 256], F32)
            ydr = self.ring("syd", 2, [128, 256], F32)
            sttr = self.ring("sstt", 4, [128, 256], F32)
            t1r = self.ring("st1", 1, [128, 256], F32)
            yr = self.ring("sy", 2, [128, 256], F32)
            jr = self.ring("sjunk", 1, [128, 256], F32)
            ssr = self.ring("sssq", 4, [128, 2], F32)
            obr = self.ring("sob", 2, [128, 256], BF16)
            xtks = [(xtk, jj, tq) for jj in range(2) for tq in range(4)]
            x4 = x_tok[:].rearrange("p t (h c) -> p t h c", h=4)
            hs = slice(4 * g, 4 * g + 4)
            self.tt(xdt[:].rearrange("p t (h c) -> p t h c", h=4), x4,
                    dt[:, :, hs].unsqueeze(3).to_broadcast([128, 16, 4, 64]), ALU.mult, reads=xtks + ["dt"], writes=["sxdt"])
            self.tt(xdtd[:].rearrange("p t (h c) -> p t h c", h=4), x4,
                    dtds[:, :, hs].unsqueeze(3).to_broadcast([128, 16, 4, 64]), ALU.mult, reads=xtks + ["dtds"],
                    writes=["sxdtd"])
            stk = self.key("sstate")
            sbk = self.key("sstatebf")
            self.memset(state[:], 0.0, writes=[stk])
            self.memset(state_bf[:], 0.0, writes=[sbk])
            ones_f = cst[:, C_ONES, :]

            def stageP(t, I):
                tsl = slice(t * 128, (t + 1) * 128)
                xD, xDk = xDr.next()
                self.tt(xD[:].rearrange("p (h c) -> p h c", h=4), x_tok[:, t, :].rearrange("p (h c) -> p h c", h=4),
                        dbc[:, hs].unsqueeze(2).to_broadcast([128, 4, 64]), ALU.mult, reads=xtks + [dbck],
                        writes=[xDk], eng="pool")
                yield
                psS, psSk = self.psum.next()
                self.mm(psS[:, 0:128], BT[:, tsl], CT[:, tsl], True, True, reads=["sBT", "sCT"], writes=[psSk])
                yield
                psD, psDk = self.psum.next()
                for hh in range(4):
                    ab, abk = abc.next()
                    self.ts(ab[:], ones_f, av[:, t, 4 * g + hh:4 * g + hh + 1], ALU.mult, reads=["av", "consts"], writes=[abk])
                    yield
                    self.mm(psD[:, hh * 128:(hh + 1) * 128], ab[:], cst[:, C_TRI2, :], True, False, reads=[abk, "consts"],
                            writes=[psDk])
                    yield
                    self.mm(psD[:, hh * 128:(hh + 1) * 128], cst[:, C_TRI2NEG, :], ab[:], False, True,
                            reads=[abk, "consts"], writes=[psDk])
                    yield
                d_, dk_ = dm.next()
                self.tt(d_[:], psD[:].rearrange("p (h c) -> p h c", h=4),
                        cst[:, C_NEGINCL, :].unsqueeze(1).to_broadcast([128, 4, 128]), ALU.add, reads=[psDk, "consts"],
                        writes=[dk_])
                yield
                self.act(d_[:], d_[:], AF.Exp, reads=[dk_], writes=[dk_])
                yield
                M_, Mk_ = MT.next()
                self.tt(M_[:], d_[:], psS[:, 0:128].unsqueeze(1).to_broadcast([128, 4, 128]), ALU.mult,
                        reads=[dk_, psSk], writes=[Mk_])
                yield
                psY, psYk = self.psum.next()
                for hh in range(4):
                    cs = slice(hh * 64, (hh + 1) * 64)
                    self.mm(psY[:, cs], M_[:, hh, :], xdt[:, t, cs], True, False, reads=[Mk_, "sxdt"], writes=[psYk])
                    yield
                    self.mm(psY[:, cs], self.ident_b(), xD[:, cs], False, True, reads=["cbf", xDk], writes=[psYk])
                    yield
                yd, ydk = ydr.next()
                self.copy(yd[:], psY[:, 0:256], reads=[psYk], writes=[ydk], eng="act")
                yield
                psZ, psZk = self.psum.next()
                for kc in range(KC):
                    self.mm(psZ[:, 0:256], self.hT[:, kc, tsl], wbf[:, kc, 0:256], kc == 0, kc == KC - 1,
                            reads=[(wk, 0), (wk, 1), ("hT", kc, t // 4)], writes=[psZk])
                    yield
                sz, szk = szr.next()
                self.act(sz[:], psZ[:, 0:256], AF.Silu, reads=[psZk], writes=[szk])
                yield
                I["stt"] = []
                for c in range(2):
                    psT, psTk = self.psum.next()
                    self.mm(psT[:, 0:256], B_tok[64 * c:64 * c + 64, t, :], xdtd[64 * c:64 * c + 64, t, :], True, True,
                            reads=[(btk, t // 4), "sxdtd"], writes=[psTk])
                    yield
                    sx, sxk = sttr.next()
                    self.copy(sx[:], psT[:, 0:256], reads=[psTk], writes=[sxk], eng=("act" if c else "dve"))
                    yield
                    I["stt"].append((sx, sxk))
                I.update(yd=yd, ydk=ydk, sz=sz, szk=szk)
                yield

            def stageQ(t, I):
                tsl = slice(t * 128, (t + 1) * 128)
                psO, psOk = self.psum_acc.next()
                for c in range(2):
                    j = 2 * t + c
                    sx, sxk = I["stt"][c]
                    self.mm(psO[64 * c:64 * c + 64, 0:256], CT[:, t * 128 + 64 * c:t * 128 + 64 * c + 64], state_bf[:], True, True,
                            reads=["sCT", sbk], writes=[psOk])
                    yield
                    self.tt(state[:].rearrange("p (h c) -> p h c", h=4), state[:].rearrange("p (h c) -> p h c", h=4),
                            eatot[:, j, hs].unsqueeze(2).to_broadcast([128, 4, 64]), ALU.mult, reads=[stk, "eatot"],
                            writes=[stk])
                    yield
                    self.tt(state[:], state[:], sx[:], ALU.add, reads=[stk, sxk], writes=[stk])
                    yield
                    self.copy(state_bf[:], state[:], reads=[stk], writes=[sbk], eng="act")
                    yield
                t1, t1k = t1r.next()
                self.tt(t1[:].rearrange("p (h c) -> p h c", h=4), psO[:, 0:256].rearrange("p (h c) -> p h c", h=4),
                        eacum[:, t, hs].unsqueeze(2).to_broadcast([128, 4, 64]), ALU.mult, reads=[psOk, "eacum"],
                        writes=[t1k])
                yield
                y, yk = yr.next()
                self.tt(y[:], t1[:], I["yd"][:], ALU.add, reads=[t1k, I["ydk"]], writes=[yk])
                yield
                self.tt(y[:], y[:], I["sz"][:], ALU.mult, reads=[yk, I["szk"]], writes=[yk])
                yield
                jk_, jkk = jr.next()
                ss, ssk = ssr.next()
                self.S.op("act", lambda e, jk_=jk_, y=y, ss=ss: e.activation(out=jk_[:], in_=y[:], func=AF.Square,
                                                                          accum_out=ss[:, 0:1]),
                          reads=[yk], writes=[jkk, ssk])
                yield
                self.act(ss[:, 1:2], ss[:, 0:1], AF.Ln, reads=[ssk, "consts"], writes=[ssk], scale=1.0 / 256,
                         bias=EPS)
                yield
                self.act(ss[:, 1:2], ss[:, 1:2], AF.Exp, reads=[ssk], writes=[ssk], scale=-0.5)
                yield
                ob, obk = obr.next()
                self.stt(ob[:], y[:], ss[:, 1:2], nwb[:], ALU.mult, ALU.mult, reads=[yk, ssk, nwbk], writes=[obk])
                yield
                psX, psXk = self.psum.next()
                pxb = psX[:].bitcast(BF16)
                for ch in range(2):
                    self.tr(pxb[:, ch * 128:(ch + 1) * 128], ob[:, ch * 128:(ch + 1) * 128], self.ident_b(),
                            reads=[obk, "cbf"], writes=[psXk])
                    yield
                self.copy(oT[:, :, tsl], pxb[:, 0:256].rearrange("p (c t) -> p c t", c=2), reads=[psXk],
                          writes=[("soT", t)], eng="act")
                yield

            def run2(ga, gb):
                gens = [g_ for g_ in (ga, gb) if g_ is not None]
                while gens:
                    for g_ in list(gens):
                        try:
                            next(g_)
                        except StopIteration:
                            gens.remove(g_)

            infos = {0: {}}
            run2(stageP(0, infos[0]), None)
            for t in range(16):
                gp = None
                if t + 1 < 16:
                    infos[t + 1] = {}
                    gp = stageP(t + 1, infos[t + 1])
                run2(gp, stageQ(t, infos.pop(t)))
            for ch in range(2):
                oc = 2 * g + ch
                self.dma(self.o_scr[2][oc * 128:(oc + 1) * 128, :], oT[:, ch, :], reads=[("soT", t) for t in range(16)],
                         writes=[("o_scr", 2, oc)])


    def dn_phase(self, l):
        A, S = self.A, self.S
        scale = self.nw[:, 2 * l, :]
        cst = self.consts
        one_col = cst[:, C_ONES, 0:1]
        ones_f = cst[:, C_ONES, :]
        wast = A.alloc("dwast", [128, KC, 16], F32)
        wa = A.alloc("dwa", [128, KC, 16], BF16)
        self.dma(wast[:], self.w_in_d[l][:, O_DNA:O_DNA + 16].rearrange("(k p) n -> p k n", p=128), [], ["dwast"])
        for kc in range(KC):
            self.act(wa[:, kc, :], wast[:, kc, :], AF.Copy, reads=["dwast", "nw"], writes=["dwa"], scale=scale[:, kc:kc + 1])
        dtb, dtbk = self.bload("ddtb", self.dn_dt_bias_d[l], 8)
        alog, alogk = self.bload("dalog", self.dn_a_log_d[l], 8)
        gv = A.alloc("dg", [128, 16, 8], F32)
        beta = A.alloc("dbeta", [128, 16, 8], F32)
        gc = A.alloc("dgc", [128, 16, 8], F32)
        egc = self.dn_egc
        bg = A.alloc("dbg", [128, 16, 8], F32)
        kdec = A.alloc("dkdec", [128, 16, 8], F32)
        eglast = self.dn_eglast
        tmp = A.alloc("dtmp", [128, 16, 8], F32)
        ps, pk = self.psum.next()
        for t in range(16):
            for kc in range(KC):
                self.mm(ps[:, t * 16:(t + 1) * 16], self.hT[:, kc, t * 128:(t + 1) * 128], wa[:, kc, :], kc == 0,
                        kc == KC - 1, reads=["dwa", ("hT", kc, t // 4)], writes=[pk])
        pv = ps[:, 0:256].rearrange("p (t h) -> p t h", t=16)
        self.act(beta[:], pv[:, :, 8:16], AF.Exp, reads=[pk], writes=["dbeta"], scale=-1.0)
        self.ts(beta[:], beta[:], 1.0, ALU.add, reads=["dbeta"], writes=["dbeta"])
        self.S.op("dve", lambda e: e.reciprocal(out=beta[:], in_=beta[:]), reads=["dbeta"], writes=["dbeta"])
        self.tt(gv[:], pv[:, :, 0:8], dtb[:].unsqueeze(1).to_broadcast([128, 16, 8]), ALU.add, reads=[pk, dtbk], writes=["dg"])
        self.act(tmp[:], gv[:], AF.Exp, reads=["dg"], writes=["dtmp"])
        self.act(gv[:], tmp[:], AF.Ln, reads=["dtmp", "consts"], writes=["dg"], bias=1.0, scale=1.0)
        self.act(alog[:], alog[:], AF.Exp, reads=[alogk], writes=[alogk])
        self.S.op("dve", lambda e: e.scalar_tensor_tensor(out=gv[:], in0=gv[:], scalar=-1.0,
                                                          in1=alog[:].unsqueeze(1).to_broadcast([128, 16, 8]),
                                                          op0=ALU.mult, op1=ALU.mult),
                  reads=["dg", alogk], writes=["dg"])
        ps, pk = self.psum.next()
        for t in range(16):
            self.mm(ps[:, t * 8:(t + 1) * 8], cst[:, C_TRI2, :], gv[:, t, :], True, True, reads=["dg", "consts"], writes=[pk])
        self.copy(gc[:], ps[:, 0:128].rearrange("p (t h) -> p t h", t=16), reads=[pk], writes=["dgc"])
        self.act(egc[:], gc[:], AF.Exp, reads=["dgc"], writes=["degc"])
        self.tt(bg[:], egc[:], beta[:], ALU.mult, reads=["degc", "dbeta"], writes=["dbg"])
        ps, pk = self.psum.next()
        for t in range(16):
            self.mm(ps[:, t * 8:(t + 1) * 8], cst[:, C_BD, :], gv[:, t, :], True, True, reads=["dg", "consts"], writes=[pk])
        self.tt(kdec[:], ps[:, 0:128].rearrange("p (t h) -> p t h", t=16), gc[:], ALU.subtract, reads=[pk, "dgc"],
                writes=["dkdec"])
        self.act(kdec[:], kdec[:], AF.Exp, reads=["dkdec"], writes=["dkdec"])
        ps, pk = self.psum.next()
        for t in range(16):
            for c in range(2):
                j = 2 * t + c
                self.mm(ps[:, j * 8:(j + 1) * 8], cst[:, C_IND0 + c, :], gv[:, t, :], True, True,
                        reads=["dg", "consts"], writes=[pk])
        self.act(eglast[:].rearrange("p j h -> p (j h)"), ps[:, 0:256], AF.Exp, reads=[pk], writes=["deglast"])

        mG = A.mark()
        for h in range(8):
            A.reset(mG)
            S.barrier()
            wg = A.alloc("dwg", [128, KC, 128], BF16)
            qTn = A.alloc("dqTn", [128, S_LEN], BF16)
            kTn = A.alloc("dkTn", [128, S_LEN], BF16)
            k_tok = A.alloc("dktok", [128, 16, 128], BF16)
            v_tok = A.alloc("dvtok", [128, 16, 128], BF16)
            mA = A.mark()
            wst = self.ring("dwst", 2, [128, KC, 128], F32)
            wbx = A.alloc("dwbx", [128, KC, 384], BF16)
            raws = [A.alloc("draw", [128, S_LEN + 4], F32) for _ in range(2)]
            accs = [A.alloc("dacc", [128, S_LEN], F32) for _ in range(2)]
            vT = A.alloc("dvT", [128, S_LEN], BF16)
            cw = self.ring("dcw", 2, [128, 8], F32)
            sqr = self.ring("dsq", 2, [128, 512], BF16)
            rsr = self.ring("drs", 2, [128, 512], F32)
            rawks = [self.key("draw"), self.key("draw")]
            for r_, rk__ in zip(raws, rawks):
                self.memset(r_[:, 0:3], 0.0, writes=[(rk__, "pad")])
            cols = [O_DNQKV + 128 * h, O_DNQKV + 1024 + 128 * h, O_DNQKV + 2048 + 128 * h, O_DNGATE + 128 * h]
            wk = self.key("dwb")
            for j, c0 in enumerate(cols):
                st, sk = wst.next()
                self.dma(st[:], self.w_in_d[l][:, c0:c0 + 128].rearrange("(k p) n -> p k n", p=128), [], [sk])
                for kc in range(KC):
                    wd_ = wbx[:, kc, j * 128:(j + 1) * 128] if j < 3 else wg[:, kc, :]
                    self.act(wd_, st[:, kc, :], AF.Copy, reads=[sk, "nw"], writes=[(wk, j)], scale=scale[:, kc:kc + 1])
            accks = [self.key("dacc"), self.key("dacc")]
            ktk = self.key("dktok")
            vtk = self.key("dvtok")
            def dn_P(j):
                return self.conv_P(l, wbx, (wk, j), j * 128, self.dn_conv_w_d, None, j * 1024 + 128 * h, cw)

            def dn_C(j, H):
                raw, rawk, acc, acck = raws[j % 2], rawks[j % 2], accs[j % 2], accks[j % 2]
                if j == 2:
                    self.conv_C(H, raw, rawk, acc, acck, vT[:], "dvT")
                else:
                    self.conv_C(H, raw, rawk, acc, acck, acc[:], acck)

            def dn_N(j):
                acc, acck = accs[j % 2], accks[j % 2]
                if j == 2:
                    self.to_tok(vT, "dvT", lambda tq: v_tok[:, tq * 4:(tq + 1) * 4, :], vtk)
                    return
                dst, dk = (qTn, "dqTn") if j == 0 else (kTn, "dkTn")
                for g in range(4):
                    sl = slice(g * 512, (g + 1) * 512)
                    q_, qk_ = sqr.next()
                    self.act(q_[:], acc[:, sl], AF.Square, reads=[acck], writes=[qk_])
                    ps, pk = self.psum.next()
                    self.mm(ps[:], self.ones_b(), q_[:], True, True, reads=[qk_, "cbf"], writes=[pk])
                    r_, rk_ = rsr.next()
                    self.act(r_[:], ps[:], AF.Ln, reads=[pk], writes=[rk_], scale=1.0, bias=EPS)
                    self.act(r_[:], r_[:], AF.Exp, reads=[rk_], writes=[rk_], scale=-0.5)
                    if j == 0:
                        self.stt(dst[:, sl], acc[:, sl], 128.0 ** -0.5, r_[:], ALU.mult, ALU.mult, reads=[acck, rk_],
                                 writes=[(dk, g)])
                    else:
                        self.tt(dst[:, sl], acc[:, sl], r_[:], ALU.mult, reads=[acck, rk_], writes=[(dk, g)])
                if j == 1:
                    self.to_tok(kTn, [("dkTn", g) for g in range(4)], lambda tq: k_tok[:, tq * 4:(tq + 1) * 4, :], ktk)

            H = {0: dn_P(0)}
            self.conv_E(H[0], raws[0], rawks[0])
            for j in range(3):
                if j + 1 < 3:
                    H[j + 1] = dn_P(j + 1)
                dn_C(j, H[j])
                if j + 1 < 3:
                    self.conv_E(H[j + 1], raws[(j + 1) % 2], rawks[(j + 1) % 2])
                dn_N(j)
            S.barrier()
            A.reset(mA)
            attnT = A.alloc("dattnT", [128, 16, 128], BF16)
            P = A.alloc("dP", [128, 16, 128], BF16)
            PL = A.alloc("dPL", [128, 16, 128], BF16)
            R32 = A.alloc("dR32", [128, 16, 128], F32)
            Rb = A.alloc("dRb", [128, 16, 128], BF16)
            vb = v_tok
            kbg = k_tok
            kd = A.alloc("dkd", [128, 16, 128], BF16)
            u = A.alloc("du", [128, 16, 128], BF16)
            wT = A.alloc("dwT", [128, 16, 128], BF16)
            abrs = [self.ring("dab", 2, [128, 128], F32) for _ in range(4)]
            dcs = [A.alloc("ddec", [128, 4, 128], F32) for _ in range(4)]
            bms = [A.alloc("dbm", [128, 4, 128], F32) for _ in range(4)]
            ktks = [(ktk, tq) for tq in range(4)]
            vtks = [(vtk, tq) for tq in range(4)]
            bc3 = lambda t_: t_[:, :, h:h + 1].to_broadcast([128, 16, 128])
            self.tt(vb[:], v_tok[:], bc3(beta), ALU.mult, reads=vtks + ["dbeta"], writes=vtks + ["dvb"])
            self.tt(kd[:], k_tok[:], bc3(kdec), ALU.mult, reads=ktks + ["dkdec"], writes=["dkd"])
            self.tt(kbg[:], k_tok[:], bc3(bg), ALU.mult, reads=ktks + ["dbg", "dkd"], writes=ktks + ["dkbg"])
            kTk = [("dkTn", g) for g in range(4)]
            identf4 = cst[:, C_IDENT, :].unsqueeze(1).to_broadcast([128, 4, 128])
            def chain(q):
                pr = Ring([self.pst[2 * q], self.pst[2 * q + 1]], ("dnps", q))
                tq = slice(4 * q, 4 * q + 4)
                abr = abrs[q]
                dc, dck = dcs[q], ("ddec", q)
                bm, bmk = bms[q], ("dbm", q)
                v4 = lambda ps_: ps_[:].rearrange("p (a b) -> p a b", a=4)
                tiles = [(4 * q + i4, slice((4 * q + i4) * 128, (4 * q + i4 + 1) * 128), slice(i4 * 128, (i4 + 1) * 128))
                         for i4 in range(4)]
                psD, psDk = pr.next()
                for t, tsl, cs in tiles:
                    ab, abk = abr.next()
                    self.ts(ab[:], ones_f, gv[:, t, h:h + 1], ALU.mult, reads=["dg", "consts"], writes=[abk])
                    yield
                    self.mm(psD[:, cs], ab[:], cst[:, C_TRI2, :], True, False, reads=[abk, "consts"], writes=[psDk])
                    self.mm(psD[:, cs], cst[:, C_TRI2NEG, :], ab[:], False, True, reads=[abk, "consts"], writes=[psDk])
                    yield
                self.tt(dc[:], v4(psD), cst[:, C_NEGINCL, :].unsqueeze(1).to_broadcast([128, 4, 128]), ALU.add,
                        reads=[psDk, "consts"], writes=[dck])
                yield
                self.act(dc[:], dc[:], AF.Exp, reads=[dck], writes=[dck])
                yield
                psQ, psQk = pr.next()
                for t, tsl, cs in tiles:
                    self.mm(psQ[:, cs], kTn[:, tsl], qTn[:, tsl], True, True, reads=[("dkTn", q), ("dqTn", q)], writes=[psQk])
                    yield
                self.tt(attnT[:, tq, :], v4(psQ), dc[:], ALU.mult, reads=[psQk, dck], writes=[("dattnT", q)])
                yield
                psB, psBk = pr.next()
                for t, tsl, cs in tiles:
                    db, dbk = abr.next()
                    self.ts(db[:], cst[:, C_IDENT, :], beta[:, t, h:h + 1], ALU.mult, reads=["dbeta", "consts"], writes=[dbk])
                    yield
                    self.mm(psB[:, cs], ones_f, db[:], True, True, reads=[dbk, "consts"], writes=[psBk])
                    yield
                self.tt(bm[:], v4(psB), cst[:, C_MSTRICT2, :].unsqueeze(1).to_broadcast([128, 4, 128]), ALU.mult,
                        reads=[psBk, "consts"], writes=[bmk])
                yield
                psK, psKk = pr.next()
                for t, tsl, cs in tiles:
                    self.mm(psK[:, cs], kTn[:, tsl], kTn[:, tsl], True, True, reads=[("dkTn", q)], writes=[psKk])
                    yield
                self.tt(dc[:], v4(psK), dc[:], ALU.mult, reads=[psKk, dck], writes=[dck])
                yield
                self.tt(P[:, tq, :], dc[:], bm[:], ALU.mult, reads=[dck, bmk], writes=[("dP", q)])
                yield
                psT, psTk = pr.next()
                ptb = psT[:].bitcast(BF16)
                for t, tsl, cs in tiles:
                    self.tr(ptb[:, cs], P[:, t, :], self.ident_b(), reads=[("dP", q), "cbf"], writes=[psTk])
                yield
                self.copy(PL[:, tq, :], ptb[:, 0:512].rearrange("p (a b) -> p a b", a=4), reads=[psTk],
                          writes=[("dPL", q)], eng="act")
                yield
                self.tt(R32[:, tq, :], identf4, P[:, tq, :], ALU.subtract, reads=[("dP", q), "consts"], writes=[("dR32", q)])
                yield
                self.copy(Rb[:, tq, :], R32[:, tq, :], reads=[("dR32", q)], writes=[("dRb", q)], eng="act")
                yield

            S.barrier()
            for q0 in (0, 2):
                gens = [chain(q0), chain(q0 + 1)]
                while gens:
                    for g_ in list(gens):
                        try:
                            next(g_)
                        except StopIteration:
                            gens.remove(g_)
            S.barrier()
            for lev in range(5):
                last = lev == 4
                for q in range(4):
                    if not last:
                        psP, psPk = self.psum.next()
                    psL, psLk = self.psum.next()
                    for i4 in range(4):
                        t = 4 * q + i4
                        cs = slice(i4 * 128, (i4 + 1) * 128)
                        if not last:
                            self.mm(psP[:, cs], PL[:, t, :], P[:, t, :], True, True, reads=[("dPL", q), ("dP", q)],
                                    writes=[psPk])
                        self.mm(psL[:, cs], P[:, t, :], PL[:, t, :], True, True, reads=[("dPL", q), ("dP", q)],
                                writes=[psLk])
                    if not last:
                        self.copy(P[:, 4 * q:4 * q + 4, :], psP[:].rearrange("p (a b) -> p a b", a=4), reads=[psPk],
                                  writes=[("dP", q)], eng="dve")
                    self.copy(PL[:, 4 * q:4 * q + 4, :], psL[:].rearrange("p (a b) -> p a b", a=4), reads=[psLk],
                              writes=[("dPL", q)], eng="act")
                for q in range(4):
                    psR, psRk = self.psum.next()
                    for i4 in range(4):
                        t = 4 * q + i4
                        cs = slice(i4 * 128, (i4 + 1) * 128)
                        self.mm(psR[:, cs], PL[:, t, :], Rb[:, t, :], True, True, reads=[("dPL", q), ("dRb", q)],
                                writes=[psRk])
                    self.tt(R32[:, 4 * q:4 * q + 4, :], R32[:, 4 * q:4 * q + 4, :],
                            psR[:].rearrange("p (a b) -> p a b", a=4), ALU.add, reads=[psRk, ("dR32", q)],
                            writes=[("dR32", q)])
                    self.copy(Rb[:, 4 * q:4 * q + 4, :], R32[:, 4 * q:4 * q + 4, :], reads=[("dR32", q)],
                              writes=[("dRb", q)], eng="act")
            for q in range(4):
                psU, psUk = self.psum.next()
                psW, psWk = self.psum.next()
                for i4 in range(4):
                    t = 4 * q + i4
                    cs = slice(i4 * 128, (i4 + 1) * 128)
                    self.mm(psU[:, cs], Rb[:, t, :], vb[:, t, :], True, True, reads=[("dRb", q), "dvb"], writes=[psUk])
                    self.mm(psW[:, cs], kbg[:, t, :], Rb[:, t, :], True, True, reads=[("dRb", q), "dkbg"], writes=[psWk])
                self.copy(u[:, 4 * q:4 * q + 4, :], psU[:].rearrange("p (a b) -> p a b", a=4), reads=[psUk],
                          writes=[("du", q)], eng="dve")
                self.copy(wT[:, 4 * q:4 * q + 4, :], psW[:].rearrange("p (a b) -> p a b", a=4), reads=[psWk],
                          writes=[("dwT", q)], eng="act")
            sgt = P
            for q in range(4):
                psG, psGk = self.psum.next()
                for i4 in range(4):
                    t = 4 * q + i4
                    for kc in range(KC):
                        self.mm(psG[:, i4 * 128:(i4 + 1) * 128], self.hT[:, kc, t * 128:(t + 1) * 128], wg[:, kc, :],
                                kc == 0, kc == KC - 1, reads=[(wk, 3), ("hT", kc, q)], writes=[psGk])
                self.act(sgt[:, 4 * q:4 * q + 4, :], psG[:].rearrange("p (a b) -> p a b", a=4), AF.Silu, reads=[psGk],
                         writes=[("dP", q)])
            fl = lambda t_: t_[:].rearrange("p a b -> p (a b)")
            for nm, src, keys in (("u", fl(u), [("du", q) for q in range(4)]),
                                  ("wT", fl(wT), [("dwT", q) for q in range(4)]),
                                  ("q", qTn[:], [("dqTn", g) for g in range(4)]),
                                  ("attnT", fl(attnT), [("dattnT", q) for q in range(4)]),
                                  ("kd", fl(kd), ["dkd"]),
                                  ("sg", fl(sgt), [("dP", q) for q in range(4)])):
                self.dma(self.dn_scr[nm][h], src, reads=keys, writes=[("dn_scr", nm, h)])

    def final_out(self):
        A, S = self.A, self.S
        m = A.mark()
        sq = self.ring("fsq", 3, [128, 512], BF16)
        rs = self.ring("frs", 2, [128, 512], F32)
        hf = self.ring("hf", 3, [128, 512], F32)
        ost = self.ring("ost", 2, [128, 4, D], F32)
        for g in range(4):
            ps, pk = self.psum.next()
            sl = slice(g * 512, (g + 1) * 512)
            for kc in range(KC):
                q, qk = sq.next()
                self.act(q[:], self.xT[:, kc, sl], AF.Square, reads=[("xT", kc, g)], writes=[qk])
                self.mm(ps[:], self.ones_b(), q[:], kc == 0, kc == KC - 1, reads=[qk, "cbf"], writes=[pk])
            r, rk = rs.next()
            self.act(r[:], ps[:], AF.Ln, reads=[pk], writes=[rk], scale=1.0 / D, bias=EPS)
            self.act(r[:], r[:], AF.Exp, reads=[rk], writes=[rk], scale=-0.5)
            o, ok = ost.next()
            for kc in range(KC):
                h, hk = hf.next()
                self.stt(h[:], self.xT[:, kc, sl], self.nw[:, 2 * DEPTH, kc:kc + 1], r[:], ALU.mult, ALU.mult,
                         reads=[("xT", kc, g), rk, "nw"], writes=[hk])
                ps2, pk2 = self.psum.next()
                for t in range(4):
                    self.tr(ps2[:, t * 128:(t + 1) * 128], h[:, t * 128:(t + 1) * 128], self.ident_f(),
                            reads=[hk, "consts"], writes=[pk2])
                self.copy(o[:, :, kc * 128:(kc + 1) * 128], ps2[:].rearrange("p (t c) -> p t c", t=4),
                          reads=[pk2], writes=[ok], eng=("act" if kc % 2 else "dve"))
            self.dma(self.out_d[g * 512:(g + 1) * 512, :].rearrange("(t p) d -> p t d", p=128), o[:],
                     reads=[ok], writes=[("out", g)])
        S.barrier()
        A.reset(m)


C_IDENT, C_ONES, C_EPS, C_MSTRICT, C_TINC = 0, 1, 2, 3, 4
C_TRI2, C_TRI2NEG, C_BD, C_IND0, C_IND1, C_NEGINCL, C_NEGSB, C_MINCL2, C_MSTRICT2, C_MSTRICT2T = 5, 6, 7, 8, 9, 10, 11, 12, 13, 14
NCONST = 15
NEG = -30000.0


def make_consts():
    c = np.zeros((128, NCONST, 128), np.float32)
    c[:, C_IDENT, :] = np.eye(128, dtype=np.float32)
    c[:, C_ONES, :] = 1.0
    c[:, C_EPS, :] = EPS
    ii = np.arange(128)
    c[:, C_MSTRICT, :] = (ii[:, None] < ii[None, :]).astype(np.float32)
    c[:, C_TINC, :] = (ii[:, None] >= ii[None, :]).astype(np.float32)
    same = (ii[:, None] // 64) == (ii[None, :] // 64)
    le = ii[:, None] <= ii[None, :]
    lt = ii[:, None] < ii[None, :]
    c[:, C_TRI2, :] = (same & le).astype(np.float32)
    c[:, C_TRI2NEG, :] = -c[:, C_TRI2, :]
    c[:, C_BD, :] = same.astype(np.float32)
    c[:, C_IND0, :] = (ii[:, None] < 64).astype(np.float32) * np.ones((1, 128), np.float32)
    c[:, C_IND1, :] = (ii[:, None] >= 64).astype(np.float32) * np.ones((1, 128), np.float32)
    c[:, C_NEGINCL, :] = np.where(same & le, 0.0, NEG)
    c[:, C_NEGSB, :] = np.where(lt, 0.0, 8.0 * NEG)
    c[:, C_MINCL2, :] = (same & le).astype(np.float32)
    c[:, C_MSTRICT2, :] = (same & lt).astype(np.float32)
    c[:, C_MSTRICT2T, :] = (same & lt).T.astype(np.float32)
    return c.reshape(128, NCONST * 128)


_CACHE = {}
_RUN_KW = {}


def get_nc(**kw):
    key = tuple(sorted((k, str(v)) for k, v in kw.items()))
    if key not in _CACHE:
        b = Builder(**kw)
        b.build()
        _CACHE[key] = b
    return _CACHE[key]


def run(inputs, **kw):
    b = get_nc(**kw)
    consts = make_consts()
    common = {
        "consts": consts,
        "w_in": np.ascontiguousarray(inputs["w_in"], dtype=np.float32),
        "norm_mix": np.ascontiguousarray(inputs["norm_mix"], dtype=np.float32),
        "norm_mlp": np.ascontiguousarray(inputs["norm_mlp"], dtype=np.float32),
        "norm_final": np.ascontiguousarray(inputs["norm_final"], dtype=np.float32).reshape(1, D),
        "w_branch": np.ascontiguousarray(inputs["w_branch"], dtype=np.float32),
        "w_out": np.ascontiguousarray(inputs["w_out"], dtype=np.float32),
        "w_up": np.ascontiguousarray(inputs["w_up"], dtype=np.float32),
        "w_down": np.ascontiguousarray(inputs["w_down"], dtype=np.float32),
    }
    for nm in ("ssm_conv_w", "ssm_conv_b", "ssm_a_log", "ssm_dt_bias", "ssm_d", "ssm_norm_w", "dn_conv_w", "dn_a_log",
               "dn_dt_bias", "dn_norm_w"):
        common[nm] = np.ascontiguousarray(inputs[nm], dtype=np.float32)
    x = np.asarray(inputs["x"], dtype=np.float32)
    in_maps = []
    for c in range(NCORES):
        m = dict(common)
        m["x"] = np.ascontiguousarray(x[c])
        in_maps.append(m)
    res = run_bass_kernel_spmd(b.nc, in_maps, core_ids=list(range(NCORES)), **_RUN_KW)
    return res


def kernel(**inputs):
    res = run(inputs)
    out = np.stack([np.asarray(r["out"], dtype=np.float32) for r in res.results], axis=0)
    return out
```

```python
import numpy as np
import concourse.bass as bass
import concourse.mybir as mybir
from concourse.bass_utils import run_bass_kernel_spmd

F32 = mybir.dt.float32
BF16 = mybir.dt.bfloat16
AF = mybir.ActivationFunctionType
ALU = mybir.AluOpType

S_LEN = 2048
D = 1024
KC = 8
NCORES = 8
DEPTH = 2
D_FF = 4096
EPS = 1e-6
IN_SIZES = (3072, 1024, 8, 8, 3072, 1024, 2048, 16, 3072)
IN_DIM = sum(IN_SIZES)
OFF = [0]
for _s in IN_SIZES:
    OFF.append(OFF[-1] + _s)
(O_DNQKV, O_DNGATE, O_DNA, O_DNB, O_SBQKV, O_SSMZ, O_SSMXBC, O_SSMDT, O_GATE, _) = OFF


class _Op:
    __slots__ = ("id", "eng", "fn", "dma", "waits", "idx", "sig", "reuse", "seen")


class Sched:
    ENG = ("pe", "act", "dve", "pool", "sp")
    NSLOT = 12
    SEM_LIMIT = 20000

    def __init__(self, nc):
        self.nc = nc
        self.ops = []
        self.by_eng = {e: [] for e in self.ENG}
        self.kstate = {}
        self.seen = {e: {p: -1 for p in self.ENG} for e in self.ENG}
        self.seen_dma = {e: set() for e in self.ENG}
        self.pending = {e: set() for e in self.ENG}
        self.open_dma = []

    def op(self, eng, fn, reads=(), writes=(), dma=False):
        o = _Op()
        o.id = len(self.ops)
        o.eng = eng
        o.fn = fn
        o.dma = dma
        o.sig = None
        o.reuse = None
        deps = {}
        for k in reads:
            st = self.kstate.get(k)
            if st is not None and st[0] is not None:
                deps[st[0]] = True
        for k in writes:
            st = self.kstate.get(k)
            if st is not None:
                if st[0] is not None:
                    deps.setdefault(st[0], False)
                for r in st[1].values():
                    deps.setdefault(r, False)
                for r in st[2]:
                    deps.setdefault(r, False)
        for d in self.pending[eng]:
            deps[d] = True
        self.pending[eng] = set()
        o.idx = len(self.by_eng[eng])
        seen = self.seen[eng]
        best = {}
        waits = []
        for d, raw in deps.items():
            p = self.ops[d]
            if p.dma:
                if d in self.seen_dma[eng]:
                    continue
                self.seen_dma[eng].add(d)
                waits.append(d)
            else:
                if p.eng == eng and not dma:
                    if eng == "pe" or (not raw and eng != "pool"):
                        continue
                if p.idx <= seen[p.eng]:
                    continue
                if p.eng not in best or self.ops[best[p.eng]].idx < p.idx:
                    best[p.eng] = d
        for pe, d in best.items():
            p = self.ops[d]
            waits.append(d)
            for e2, v in p.seen.items():
                if v > seen[e2]:
                    seen[e2] = v
            if p.idx > seen[pe]:
                seen[pe] = p.idx
        o.waits = waits
        o.seen = dict(seen)
        for k in reads:
            st = self.kstate.get(k)
            if st is None:
                st = [None, {}, []]
                self.kstate[k] = st
            if dma:
                st[2].append(o.id)
            else:
                st[1][eng] = o.id
        for k in writes:
            self.kstate[k] = [o.id, {}, []]
        self.ops.append(o)
        self.by_eng[eng].append(o)
        if dma:
            self.open_dma.append(o.id)
        return o

    def barrier(self):
        last = []
        for e in self.ENG:
            for o in reversed(self.by_eng[e]):
                if o.dma:
                    break
                if o.fn is not None:
                    last.append(o.id)
                    break
        for e in self.ENG:
            self.pending[e] |= set(last) | set(self.open_dma[-self.NSLOT:])
        self.open_dma = self.open_dma[-self.NSLOT:]

    def finalize(self):
        nc = self.nc
        needed = set()
        for o in self.ops:
            needed.update(o.waits)
        for e in self.ENG:
            sem = None
            cnt = 0
            ndma = 0
            slots = None
            for o in self.by_eng[e]:
                if o.dma:
                    if slots is None:
                        slots = [nc.alloc_semaphore(f"dq_{e}_{i}") for i in range(self.NSLOT)]
                    s = ndma % self.NSLOT
                    r = ndma // self.NSLOT
                    o.sig = (slots[s], 16 * (r + 1))
                    if r > 0:
                        o.reuse = (slots[s], 16 * r)
                    ndma += 1
                elif o.id in needed:
                    if sem is None or cnt >= self.SEM_LIMIT:
                        sem = nc.alloc_semaphore(f"pg_{e}_{o.id}")
                        cnt = 0
                    cnt += 1
                    o.sig = (sem, cnt)

    def emit(self, ename, eng):
        ops = self.ops
        for o in self.by_eng[ename]:
            for d in o.waits:
                s, v = ops[d].sig
                eng.wait_ge(s, v)
            if o.reuse is not None:
                eng.wait_ge(o.reuse[0], o.reuse[1])
            if o.fn is None:
                continue
            ins = o.fn(eng)
            if o.sig is not None:
                ins.then_inc(o.sig[0], 16 if o.dma else 1)


class Arena:
    def __init__(self, nc, limit=208 * 1024):
        self.nc = nc
        self.off = 16 * 1024
        self.limit = limit
        self.n = 0
        self.peak = 0

    def alloc(self, name, shape, dtype):
        per = 1
        for s in shape[1:]:
            per *= s
        nbytes = per * (4 if dtype == F32 else 2)
        nbytes = (nbytes + 63) // 64 * 64
        assert self.off + nbytes <= self.limit, f"SBUF arena overflow at {name}: {self.off}+{nbytes}"
        self.n += 1
        t = self.nc.alloc_sbuf_tensor_at(f"{name}_{self.n}", list(shape), dtype, offset=self.off)
        self.off += nbytes
        self.peak = max(self.peak, self.off)
        return t

    def mark(self):
        return self.off

    def reset(self, m):
        self.off = m


class Ring:
    def __init__(self, tiles, name):
        self.tiles = tiles
        self.name = name
        self.i = 0

    def next(self):
        j = self.i % len(self.tiles)
        self.i += 1
        return self.tiles[j], (self.name, j)


class Builder:
    def __init__(self, nlayers=DEPTH, mixers=("dn", "sb", "ssm"), do_mlp=True, debug=()):
        self.nlayers = nlayers
        self.mixers = mixers
        self.do_mlp = do_mlp
        self.debug = debug
        nc = bass.Bass("TRN2", target_bir_lowering=False)
        self.nc = nc
        self.S = Sched(nc)
        self.A = Arena(nc)
        self.uid = 0

    def dram_in(self, name, shape, dtype=F32):
        return self.nc.dram_tensor(name, list(shape), dtype, kind="ExternalInput").ap()

    def dram_out(self, name, shape, dtype=F32):
        return self.nc.dram_tensor(name, list(shape), dtype, kind="ExternalOutput").ap()

    def dram_tmp(self, name, shape, dtype):
        return self.nc.dram_tensor(name, list(shape), dtype, kind="Internal").ap()

    def key(self, base):
        self.uid += 1
        return (base, self.uid)

    def ring(self, name, n, shape, dtype):
        return Ring([self.A.alloc(name, shape, dtype) for _ in range(n)], self.key(name))

    def dma(self, out, in_, reads, writes, slow=False):
        if slow:
            fn = lambda e, out=out, in_=in_: e.dma_start(out=out, in_=in_, allow_slow_non_contiguous=True)
        else:
            fn = lambda e, out=out, in_=in_: e.dma_start(out=out, in_=in_)
        return self.S.op("sp", fn, reads=reads, writes=writes, dma=True)

    def mm(self, out, lhsT, rhs, start, stop, reads, writes, **kw):
        return self.S.op(
            "pe",
            lambda e, out=out, lhsT=lhsT, rhs=rhs, start=start, stop=stop, kw=kw: e.matmul(
                out, lhsT=lhsT, rhs=rhs, start=start, stop=stop, **kw
            ),
            reads=reads,
            writes=writes,
        )

    def tr(self, out, in_, ident, reads, writes):
        return self.S.op(
            "pe",
            lambda e, out=out, in_=in_, ident=ident: e.transpose(out, in_, ident),
            reads=reads,
            writes=writes,
        )

    def act(self, out, in_, func, reads, writes, eng="act", **kw):
        return self.S.op(
            eng,
            lambda e, out=out, in_=in_, func=func, kw=kw: e.activation(out=out, in_=in_, func=func, **kw),
            reads=reads,
            writes=writes,
        )

    def tt(self, out, in0, in1, op, reads, writes, eng="dve"):
        return self.S.op(
            eng,
            lambda e, out=out, in0=in0, in1=in1, op=op: e.tensor_tensor(out=out, in0=in0, in1=in1, op=op),
            reads=reads,
            writes=writes,
        )

    def ts(self, out, in0, s1, op0, reads, writes, s2=None, op1=None, eng="dve"):
        def fn(e, out=out, in0=in0, s1=s1, op0=op0, s2=s2, op1=op1):
            if op1 is None:
                return e.tensor_scalar(out=out, in0=in0, scalar1=s1, scalar2=None, op0=op0)
            return e.tensor_scalar(out=out, in0=in0, scalar1=s1, scalar2=s2, op0=op0, op1=op1)

        return self.S.op(eng, fn, reads=reads, writes=writes)

    def stt(self, out, in0, scalar, in1, op0, op1, reads, writes):
        return self.S.op(
            "dve",
            lambda e, out=out, in0=in0, scalar=scalar, in1=in1, op0=op0, op1=op1: e.scalar_tensor_tensor(
                out=out, in0=in0, scalar=scalar, in1=in1, op0=op0, op1=op1
            ),
            reads=reads,
            writes=writes,
        )

    def copy(self, out, in_, reads, writes, eng="dve"):
        if eng == "act":
            return self.act(out, in_, AF.Copy, reads, writes)
        return self.S.op(
            eng, lambda e, out=out, in_=in_: e.tensor_copy(out=out, in_=in_), reads=reads, writes=writes
        )

    def memset(self, ap, val, writes, eng="dve"):
        return self.S.op(eng, lambda e, ap=ap, val=val: e.memset(ap, val), reads=(), writes=writes)

    def build(self):
        nc, S, A = self.nc, self.S, self.A
        L = self.nlayers
        self.x_d = self.dram_in("x", [S_LEN, D])
        self.consts_d = self.dram_in("consts", [128, NCONST * 128])
        self.w_in_d = self.dram_in("w_in", [DEPTH, D, IN_DIM])
        self.norm_mix_d = self.dram_in("norm_mix", [DEPTH, D])
        self.norm_mlp_d = self.dram_in("norm_mlp", [DEPTH, D])
        self.norm_final_d = self.dram_in("norm_final", [1, D])
        self.w_branch_d = self.dram_in("w_branch", [DEPTH, 3, D, D])
        self.w_out_d = self.dram_in("w_out", [DEPTH, D, D])
        self.w_up_d = self.dram_in("w_up", [DEPTH, D, D_FF])
        self.w_down_d = self.dram_in("w_down", [DEPTH, D_FF, D])
        self.ssm_conv_w_d = self.dram_in("ssm_conv_w", [DEPTH, 4, 2048])
        self.ssm_conv_b_d = self.dram_in("ssm_conv_b", [DEPTH, 2048])
        self.ssm_a_log_d = self.dram_in("ssm_a_log", [DEPTH, 16])
        self.ssm_dt_bias_d = self.dram_in("ssm_dt_bias", [DEPTH, 16])
        self.ssm_d_d = self.dram_in("ssm_d", [DEPTH, 16])
        self.ssm_norm_w_d = self.dram_in("ssm_norm_w", [DEPTH, D])
        self.dn_conv_w_d = self.dram_in("dn_conv_w", [DEPTH, 4, 3072])
        self.dn_a_log_d = self.dram_in("dn_a_log", [DEPTH, 8])
        self.dn_dt_bias_d = self.dram_in("dn_dt_bias", [DEPTH, 8])
        self.dn_norm_w_d = self.dram_in("dn_norm_w", [DEPTH, 128])
        self.out_d = self.dram_out("out", [S_LEN, D])
        self.u_scr = self.dram_tmp("u_scr", [D_FF, S_LEN], BF16)
        if self.debug:
            self.o_scr = self.dram_out("o_scr", [3, D, S_LEN], BF16)
        else:
            self.o_scr = self.dram_tmp("o_scr", [3, D, S_LEN], BF16)
        self.g_scr = self.dram_tmp("g_scr", [3, D, S_LEN], BF16)
        self.dn_scr = {nm: self.dram_tmp("dn_" + nm, [8, 128, S_LEN], BF16) for nm in ("u", "wT", "q", "attnT", "kd", "sg")}

        pst = [nc.alloc_psum_tensor(f"ps{i}", [128, 512], F32) for i in range(8)]
        self.pst = pst
        self.psum = Ring(pst[:6], "psum")
        self.psum_acc = Ring(pst[6:], "psacc")

        self.consts = A.alloc("consts", [128, NCONST, 128], F32)
        self.cbf = A.alloc("cbf", [128, NCONST, 128], BF16)
        self.xT = A.alloc("xT", [128, KC, S_LEN], F32)
        self.nw = A.alloc("nw", [128, 2 * DEPTH + 1, KC], F32)
        KCON = "consts"
        self.dma(self.consts[:].rearrange("p a b -> p (a b)"), self.consts_d, reads=[], writes=[KCON])
        self.copy(self.cbf[:], self.consts[:], reads=[KCON], writes=["cbf"])
        for l in range(DEPTH):
            self.dma(self.nw[:, 2 * l, :], self.norm_mix_d[l].rearrange("(k p) -> p k", p=128), [], ["nw"], slow=True)
            self.dma(self.nw[:, 2 * l + 1, :], self.norm_mlp_d[l].rearrange("(k p) -> p k", p=128), [], ["nw"], slow=True)
        self.dma(self.nw[:, 2 * DEPTH, :], self.norm_final_d[0].rearrange("(k p) -> p k", p=128), [], ["nw"], slow=True)

        self.load_x()
        for l in range(L):
            self.layer(l)
        self.final_out()
        S.barrier()
        S.op("sp", None)
        S.finalize()

        with nc.Block() as block:

            @block.tensor
            def _(e):
                S.emit("pe", e)

            @block.scalar
            def _(e):
                S.emit("act", e)

            @block.vector
            def _(e):
                S.emit("dve", e)

            @block.gpsimd
            def _(e):
                S.emit("pool", e)

            @block.sync
            def _(e):
                S.emit("sp", e)

        return nc

    def ident_f(self):
        return self.consts[:, C_IDENT, :]

    def ident_b(self):
        return self.cbf[:, C_IDENT, :]

    def ones_b(self):
        return self.cbf[:, C_ONES, :]

    def load_x(self):
        A, S = self.A, self.S
        m = A.mark()
        stg = self.ring("xstg", 2, [128, 4, D], F32)
        for tg in range(4):
            st, sk = stg.next()
            self.dma(
                st[:],
                self.x_d[tg * 512:(tg + 1) * 512, :].rearrange("(t p) d -> p t d", p=128),
                reads=[],
                writes=[sk],
            )
            for kc in range(KC):
                ps, pk = self.psum.next()
                for t in range(4):
                    self.tr(ps[:, t * 128:(t + 1) * 128], st[:, t, kc * 128:(kc + 1) * 128], self.ident_f(),
                            reads=[sk, "consts"], writes=[pk])
                self.copy(self.xT[:, kc, tg * 512:(tg + 1) * 512], ps[:], reads=[pk], writes=[("xT", kc, tg)],
                          eng=("act" if kc % 2 else "dve"))
        S.barrier()
        A.reset(m)

    def rmsnorm(self):
        A, S = self.A, self.S
        m = A.mark()
        sq = self.ring("sq", 3, [128, 512], BF16)
        rs = self.ring("rs", 2, [128, 512], F32)
        for g in range(4):
            ps, pk = self.psum.next()
            sl = slice(g * 512, (g + 1) * 512)
            for kc in range(KC):
                q, qk = sq.next()
                self.act(q[:], self.xT[:, kc, sl], AF.Square, reads=[("xT", kc, g)], writes=[qk])
                self.mm(ps[:], self.ones_b(), q[:], kc == 0, kc == KC - 1, reads=[qk, "cbf"], writes=[pk])
            r, rk = rs.next()
            self.act(r[:], ps[:], AF.Ln, reads=[pk], writes=[rk], scale=1.0 / D, bias=EPS)
            self.act(r[:], r[:], AF.Exp, reads=[rk], writes=[rk], scale=-0.5)
            for kc in range(KC):
                self.tt(self.hT[:, kc, sl], self.xT[:, kc, sl], r[:], ALU.mult,
                        reads=[("xT", kc, g), rk], writes=[("hT", kc, g)])
        self.rstd_ring = rs
        S.barrier()
        A.reset(m)

    def eps_ap(self):
        return self.consts[:, C_EPS, 0:1]

    def wload(self, w_ap, kcn, n, wst_ring, wbf_ring, scale=None):
        st, sk = wst_ring.next()
        wb, wk = wbf_ring.next()
        self.dma(st[:, :kcn, :n], w_ap.rearrange("(k p) n -> p k n", p=128), reads=[], writes=[sk])
        for kc in range(kcn):
            if scale is not None:
                self.act(wb[:, kc, :n], st[:, kc, :n], AF.Copy, reads=[sk, "nw"], writes=[wk],
                         scale=scale[:, kc:kc + 1])
            else:
                self.act(wb[:, kc, :n], st[:, kc, :n], AF.Copy, reads=[sk], writes=[wk])
        return wb, wk

    def hT_keys(self, g):
        return [("hT", kc, g) for kc in range(KC)]

    def linear_T(self, wb, wk, ncols, kcn, rhs_fn, rhs_keys_fn, evac, groups=range(4)):
        for c0 in range(0, ncols, 128):
            n = min(128, ncols - c0)
            for g in groups:
                ps, pk = self.psum.next()
                for kc in range(kcn):
                    self.mm(ps[:n, :], wb[:, kc, c0:c0 + n], rhs_fn(kc, g), kc == 0, kc == kcn - 1,
                            reads=[wk] + rhs_keys_fn(g), writes=[pk])
                evac(c0 // 128, g, ps, pk, n)

    def mlp(self, l):
        A, S = self.A, self.S
        m = A.mark()
        self.hT = A.alloc("hT", [128, KC, S_LEN], BF16)
        self.rmsnorm()
        wst = self.ring("wst", 2, [128, KC, 512], F32)
        wbf = self.ring("wbf", 2, [128, KC, 512], BF16)
        ub = self.ring("ub", 3, [128, S_LEN], BF16)
        rr = self.ring("rr", 3, [128, 512], F32)
        scale = self.nw[:, 2 * l + 1, :]
        nxt = self.wload(self.w_up_d[l][:, 0:512], KC, 512, wst, wbf, scale=scale)
        for cb in range(D_FF // 512):
            wb, wk = nxt
            if cb + 1 < D_FF // 512:
                nxt = self.wload(self.w_up_d[l][:, (cb + 1) * 512:(cb + 2) * 512], KC, 512, wst, wbf, scale=scale)
            cur = {}

            def evac(ci, g, ps, pk, n, cb=cb, cur=cur):
                if g == 0:
                    cur["u"] = ub.next()
                u, uk = cur["u"]
                r, rk = rr.next()
                self.ts(r[:], ps[:], 0.0, ALU.max, reads=[pk], writes=[rk])
                self.act(u[:, g * 512:(g + 1) * 512], r[:], AF.Square, reads=[rk], writes=[uk])
                if g == 3:
                    f = cb * 4 + ci
                    self.dma(self.u_scr[f * 128:(f + 1) * 128, :], u[:], reads=[uk], writes=[("u_scr", f)])

            self.linear_T(wb, wk, 512, KC, lambda kc, g: self.hT[:, kc, g * 512:(g + 1) * 512],
                          self.hT_keys, evac)
        S.barrier()
        A.reset(m)
        wst = self.ring("wdst", 2, [128, 4, 512], F32)
        wbf = self.ring("wdbf", 1, [128, 32, 512], BF16)
        ur = self.ring("ur", 2, [128, 32, 512], BF16)

        def load_down(half):
            wb, wk = wbf.next()
            for q in range(8):
                st, sk = wst.next()
                self.dma(st[:], self.w_down_d[l][q * 512:(q + 1) * 512, half * 512:(half + 1) * 512]
                         .rearrange("(k p) n -> p k n", p=128), reads=[], writes=[sk])
                for k4 in range(4):
                    self.act(wb[:, q * 4 + k4, :], st[:, k4, :], AF.Copy, reads=[sk], writes=[wk])
            return wb, wk

        for half in range(2):
            wb, wk = load_down(half)
            for g in range(4):
                u, uk = ur.next()
                self.dma(u[:], self.u_scr[:, g * 512:(g + 1) * 512].rearrange("(k p) t -> p k t", p=128),
                         reads=[("u_scr", f) for f in range(32)], writes=[uk])
                for ci in range(4):
                    ps, pk = self.psum.next()
                    for kc in range(32):
                        self.mm(ps[:], wb[:, kc, ci * 128:(ci + 1) * 128], u[:, kc, :], kc == 0, kc == 31,
                                reads=[wk, uk], writes=[pk])
                    oc = half * 4 + ci
                    sl = slice(g * 512, (g + 1) * 512)
                    self.tt(self.xT[:, oc, sl], self.xT[:, oc, sl], ps[:], ALU.add,
                            reads=[pk, ("xT", oc, g)], writes=[("xT", oc, g)])
        S.barrier()
        A.reset(m)

    def layer(self, l):
        if self.mixers:
            self.mixer(l)
        if self.do_mlp:
            self.mlp(l)


    def mixer(self, l):
        A, S = self.A, self.S
        m00 = A.mark()
        if "dn" in self.mixers:
            self.dn_egc = A.alloc("degc", [128, 16, 8], F32)
            self.dn_eglast = A.alloc("deglast", [128, 32, 8], F32)
            self.dn_nwb, self.dn_nwbk = self.bload("dnwb", self.dn_norm_w_d[l], 128)
        m0 = A.mark()
        self.hT = A.alloc("hT", [128, KC, S_LEN], BF16)
        self.rmsnorm()
        m1 = A.mark()
        if "sb" in self.mixers:
            self.sb_phase(l)
            S.barrier()
            A.reset(m1)
        if "ssm" in self.mixers:
            self.ssm_phase(l)
            S.barrier()
            A.reset(m1)
        if "dn" in self.mixers:
            self.dn_phase(l)
            S.barrier()
            A.reset(m1)
        self.gates_phase(l)
        S.barrier()
        A.reset(m0)
        if "dn" in self.mixers:
            self.dn_phase2(l)
            S.barrier()
            A.reset(m0)
        self.merge_phase(l)
        S.barrier()
        A.reset(m00)

    def dn_phase2(self, l):
        A, S = self.A, self.S
        cst = self.consts
        pst = self.pst
        NH = 8
        names = ["u", "wT", "q", "attnT", "kd", "sg"]
        sets = [[A.alloc(f"d2{n}", [128, NH, 512], BF16) for n in names] for _ in range(1)]
        Sst = A.alloc("d2S", [128, NH, 128], F32)
        Sbf = A.alloc("d2Sbf", [128, NH, 128], BF16)
        vnr = self.ring("d2vn", 2, [128, NH, 128], BF16)
        t1 = A.alloc("d2t1", [128, NH, 128], F32)
        ot = A.alloc("d2ot", [128, NH, 128], F32)
        sq = A.alloc("d2sq", [128, NH, 128], F32)
        ssq = A.alloc("d2ssq", [128, 2, NH], F32)
        obr = self.ring("d2ob", 2, [128, NH, 128], BF16)
        otl = self.ring("d2otl", 2, [128, NH, 128], BF16)
        egc, eglast, nwb, nwbk = self.dn_egc, self.dn_eglast, self.dn_nwb, self.dn_nwbk
        self.memset(Sst[:], 0.0, writes=[("d2S", 0), ("d2S", 1)])
        self.memset(Sbf[:], 0.0, writes=[("d2Sbf", 0), ("d2Sbf", 1)])
        bO = [(pst[i], ("p2O", i)) for i in range(4)]
        bW = [(pst[4], ("p2W", 0)), (pst[5], ("p2W", 1))]
        bS = [(pst[6], ("p2S", 0)), (pst[7], ("p2S", 1))]
        o_view = self.o_scr[0].rearrange("(h e) s -> e h s", h=NH)
        for q in range(4):
            st = sets[0]
            sk = ("d2set", 0)
            for ni, nm in enumerate(names):
                self.dma(st[ni][:], self.dn_scr[nm][:, :, q * 512:(q + 1) * 512].rearrange("h p t -> p h t"),
                         reads=[("dn_scr", nm, h) for h in range(NH)], writes=[(sk, ni)])
            U, WT, Q, AT, KD, SG = st
            for tl in range(4):
                t = 4 * q + tl
                cs = slice(tl * 128, (tl + 1) * 128)
                vn, vnk = vnr.next()
                for c in range(2):
                    j = 2 * t + c
                    pb = 64 * c
                    prt = slice(pb, pb + 64)
                    ccs = slice(tl * 128 + pb, tl * 128 + pb + 64)
                    for h in range(NH):
                        pw, pwk = bW[h // 4]
                        self.mm(pw[prt, (h % 4) * 128:(h % 4 + 1) * 128], WT[:, h, ccs], Sbf[:, h, :], True, True,
                                reads=[(sk, 1), ("d2Sbf", h // 4)], writes=[pwk])
                    for hb in range(2):
                        pw, pwk = bW[hb]
                        self.tt(vn[prt, 4 * hb:4 * hb + 4, :], U[prt, 4 * hb:4 * hb + 4, cs],
                                pw[prt, :].rearrange("p (a b) -> p a b", a=4), ALU.subtract, reads=[(sk, 0), pwk],
                                writes=[(vnk, c, hb)])
                    for h in range(NH):
                        po, pok = bO[h // 2]
                        co = (h % 2) * 256
                        self.mm(po[prt, co:co + 128], Q[:, h, ccs], Sbf[:, h, :], True, True,
                                reads=[(sk, 2), ("d2Sbf", h // 4)], writes=[(pok, c)])
                        self.mm(po[prt, co + 128:co + 256], AT[prt, h, ccs], vn[prt, h, :], True, True,
                                reads=[(sk, 3), (vnk, c, h // 4)], writes=[(pok, c)])
                        ps_, psk_ = bS[h // 4]
                        self.mm(ps_[:, (h % 4) * 128:(h % 4 + 1) * 128], KD[prt, h, cs], vn[prt, h, :], True, True,
                                reads=[(sk, 4), (vnk, c, h // 4)], writes=[psk_])
                    for hb in range(2):
                        ps_, psk_ = bS[hb]
                        hs = slice(4 * hb, 4 * hb + 4)
                        self.tt(Sst[:, hs, :], Sst[:, hs, :], eglast[:, j, hs].unsqueeze(2).to_broadcast([128, 4, 128]),
                                ALU.mult, reads=[("d2S", hb), "deglast"], writes=[("d2S", hb)])
                        self.tt(Sst[:, hs, :], Sst[:, hs, :], ps_[:].rearrange("p (a b) -> p a b", a=4), ALU.add,
                                reads=[("d2S", hb), psk_], writes=[("d2S", hb)])
                        self.copy(Sbf[:, hs, :], Sst[:, hs, :], reads=[("d2S", hb)], writes=[("d2Sbf", hb)], eng="act")
                for b4 in range(4):
                    po, pok = bO[b4]
                    hs = slice(2 * b4, 2 * b4 + 2)
                    pv = po[:].rearrange("p (a b) -> p a b", a=2)
                    self.tt(t1[:, hs, :], pv[:, :, 0:128], egc[:, t, hs].unsqueeze(2).to_broadcast([128, 2, 128]), ALU.mult,
                            reads=[(pok, 0), (pok, 1), "degc"], writes=[("d2t1", b4)])
                    self.tt(ot[:, hs, :], t1[:, hs, :], pv[:, :, 128:256], ALU.add,
                            reads=[("d2t1", b4), (pok, 0), (pok, 1)], writes=[("d2ot", b4)])
                otk = [("d2ot", b4) for b4 in range(4)]
                self.act(sq[:], ot[:], AF.Square, reads=otk, writes=["d2sq"])
                self.S.op("dve", lambda e: e.tensor_reduce(out=ssq[:, 0, :], in_=sq[:], op=ALU.add,
                                                           axis=mybir.AxisListType.X),
                          reads=["d2sq"], writes=["d2ssq"])
                self.act(ssq[:, 1, :], ssq[:, 0, :], AF.Ln, reads=["d2ssq", "consts"], writes=["d2ssq"], scale=1.0 / 128,
                         bias=EPS)
                self.act(ssq[:, 1, :], ssq[:, 1, :], AF.Exp, reads=["d2ssq"], writes=["d2ssq"], scale=-0.5)
                self.tt(ot[:], ot[:], ssq[:, 1, :].unsqueeze(2).to_broadcast([128, NH, 128]), ALU.mult,
                        reads=otk + ["d2ssq"], writes=otk)
                self.tt(ot[:], ot[:], nwb[:].unsqueeze(1).to_broadcast([128, NH, 128]), ALU.mult, reads=otk + [nwbk],
                        writes=otk)
                ob, obk = obr.next()
                self.tt(ob[:], ot[:], SG[:, :, cs], ALU.mult, reads=otk + [(sk, 5)], writes=[obk])
                px, pxk = bW[0]
                pxb = px[:].bitcast(BF16)
                for h in range(NH):
                    self.tr(pxb[:, h * 128:(h + 1) * 128], ob[:, h, :], self.ident_b(), reads=[obk, "cbf"], writes=[pxk])
                otile, otlk = otl.next()
                self.copy(otile[:], pxb[:].rearrange("p (a b) -> p a b", a=NH), reads=[pxk], writes=[otlk], eng="act")
                self.dma(o_view[:, :, t * 128:(t + 1) * 128], otile[:], reads=[otlk], writes=[("o_scr", 0, "t", t)])

    def branches(self):
        return [i for i, n in enumerate(("dn", "sb", "ssm")) if n in self.mixers]

    def gates_phase(self, l):
        wst = self.ring("gwst", 2, [128, KC, 512], F32)
        wbf = self.ring("gwbf", 2, [128, KC, 512], BF16)
        gb = self.ring("gb", 3, [128, S_LEN], BF16)
        scale = self.nw[:, 2 * l, :]
        blocks = [(i, cb) for i in self.branches() for cb in range(2)]

        def gload(bi):
            i_, cb_ = blocks[bi]
            c0_ = O_GATE + i_ * D + cb_ * 512
            return self.wload(self.w_in_d[l][:, c0_:c0_ + 512], KC, 512, wst, wbf, scale=scale)

        nxt = gload(0)
        for bi_, (i, cb) in enumerate(blocks):
            if True:
                wb, wk = nxt
                if bi_ + 1 < len(blocks):
                    nxt = gload(bi_ + 1)
                cur = {}

                def evac(ci, g, ps, pk, n, cb=cb, i=i, cur=cur):
                    if g == 0:
                        cur["t"] = gb.next()
                    t, tk = cur["t"]
                    self.act(t[:, g * 512:(g + 1) * 512], ps[:], AF.Sigmoid, reads=[pk], writes=[tk])
                    if g == 3:
                        oc = cb * 4 + ci
                        self.dma(self.g_scr[i][oc * 128:(oc + 1) * 128, :], t[:], reads=[tk],
                                 writes=[("g_scr", i, oc)])

                self.linear_T(wb, wk, 512, KC, lambda kc, g: self.hT[:, kc, g * 512:(g + 1) * 512],
                              self.hT_keys, evac)

    def merge_phase(self, l):
        A, S = self.A, self.S
        wst = self.ring("mwst", 1, [128, KC, 512], F32)
        wbf = self.ring("mwbf", 2, [128, KC, 512], BF16)
        mg = A.alloc("mg", [128, KC, 1024], F32)
        mgb = A.alloc("mgb", [128, KC, 1024], BF16)
        ob = self.ring("ob", 1, [128, KC, 1024], BF16)
        gt = self.ring("gt", 3, [128, 1024], BF16)
        self.mtmp = self.ring("mtmp", 3, [128, 512], F32)
        brs = self.branches()
        mseq = []
        for half_ in range(2):
            for i_ in brs:
                for cb_ in range(2):
                    mseq.append(self.w_branch_d[l][i_][:, cb_ * 512:(cb_ + 1) * 512])
            for cb_ in range(2):
                mseq.append(self.w_out_d[l][:, cb_ * 512:(cb_ + 1) * 512])

        def mload(pos):
            if pos >= len(mseq):
                return None
            return self.wload(mseq[pos], KC, 512, wst, wbf)

        mpos = [0]
        mnxt = [mload(0)]
        for half in range(2):
            tsl = slice(half * 1024, (half + 1) * 1024)
            for bi, i in enumerate(brs):
                o, ok = ob.next()
                self.dma(o[:], self.o_scr[i][:, tsl].rearrange("(k p) t -> p k t", p=128),
                         reads=[("o_scr", i, c) for c in range(8)], writes=[ok])
                for cb in range(2):
                    wb, wk = mnxt[0]
                    mnxt[0] = mload(mpos[0] + 1)
                    mpos[0] += 1
                    cur = {}

                    def evac(ci, g, ps, pk, n, cb=cb, i=i, bi=bi, cur=cur, half=half, tsl=tsl):
                        oc = cb * 4 + ci
                        gl = g - 2 * half
                        if gl == 0:
                            cur["g"] = gt.next()
                            t, tk = cur["g"]
                            self.dma(t[:], self.g_scr[i][oc * 128:(oc + 1) * 128, tsl],
                                     reads=[("g_scr", i, oc)], writes=[tk])
                        t, tk = cur["g"]
                        dst = mg[:, oc, gl * 512:(gl + 1) * 512]
                        mk = ("mg", oc, gl)
                        if bi == 0:
                            self.tt(dst, ps[:], t[:, gl * 512:(gl + 1) * 512], ALU.mult,
                                    reads=[pk, tk], writes=[mk])
                        else:
                            tmp, tmk = self.mtmp.next()
                            self.tt(tmp[:], ps[:], t[:, gl * 512:(gl + 1) * 512], ALU.mult,
                                    reads=[pk, tk], writes=[tmk])
                            self.tt(dst, dst, tmp[:], ALU.add, reads=[tmk, mk], writes=[mk], eng="pool")
                        if bi == len(brs) - 1:
                            self.copy(mgb[:, oc, gl * 512:(gl + 1) * 512], dst, reads=[mk],
                                      writes=[("mgb", oc, gl)], eng="act")

                    self.linear_T(wb, wk, 512, KC, lambda kc, g, o=o, half=half: o[:, kc, (g - 2 * half) * 512:(g - 2 * half + 1) * 512],
                                  lambda g, ok=ok: [ok], evac, groups=(2 * half, 2 * half + 1))
            for cb in range(2):
                wb, wk = mnxt[0]
                mnxt[0] = mload(mpos[0] + 1)
                mpos[0] += 1

                def evac2(ci, g, ps, pk, n, cb=cb):
                    oc = cb * 4 + ci
                    sl = slice(g * 512, (g + 1) * 512)
                    self.tt(self.xT[:, oc, sl], self.xT[:, oc, sl], ps[:], ALU.add,
                            reads=[pk, ("xT", oc, g)], writes=[("xT", oc, g)])

                self.linear_T(wb, wk, 512, KC,
                              lambda kc, g, half=half: mgb[:, kc, (g - 2 * half) * 512:(g - 2 * half + 1) * 512],
                              lambda g, half=half: [("mgb", kc, g - 2 * half) for kc in range(KC)], evac2,
                              groups=(2 * half, 2 * half + 1))

    def sb_phase(self, l):
        A, S = self.A, self.S
        R = {}
        R["wst"] = self.ring("sbwst", 1, [128, KC, 384], F32)
        R["wbf"] = self.ring("sbwbf", 2, [128, KC, 384], BF16)
        R["qT"] = self.ring("sbq", 2, [128, 2, S_LEN], BF16)
        for i_, t_ in enumerate(R["qT"].tiles):
            self.memset(t_[:], 0.0, writes=[(R["qT"].name, i_)], eng="pool")
        R["kT"] = self.ring("sbk", 2, [128, S_LEN], BF16)
        R["v"] = self.ring("sbv", 2, [128, 16, 128], BF16)
        R["osb"] = self.ring("sbo", 1, [128, S_LEN], BF16)
        R["e"] = self.ring("sbe", 4, [128, 512], F32)
        R["spb"] = self.ring("sbsp", 4, [128, 512], BF16)
        R["xa"] = self.ring("sbxa", 2, [128, 512], F32)
        R["w"] = self.ring("sbw", 3, [128, 512], BF16)
        R["pa"] = Ring(self.pst[0:4], "psum")
        wnext = self.sb_wload(l, 0, R)
        for hp in range(8):
            wcur = wnext
            wnext = self.sb_wload(l, hp + 1, R) if hp + 1 < 8 else None
            self.sb_unit(l, hp, R, wcur)

    def sb_wload(self, l, hp, R):
        scale = self.nw[:, 2 * l, :]
        st, sk = R["wst"].next()
        wb, wk = R["wbf"].next()
        for j in range(3):
            base = O_SBQKV + j * 1024 + 128 * hp
            self.dma(st[:, :, j * 128:(j + 1) * 128],
                     self.w_in_d[l][:, base:base + 128].rearrange("(k p) n -> p k n", p=128),
                     reads=[], writes=[(sk, j)])
        for kc in range(KC):
            self.ts(wb[:, kc, :], st[:, kc, :], scale[:, kc:kc + 1], ALU.mult, reads=[(sk, 0), (sk, 1), (sk, 2), "nw"],
                    writes=[wk], s2=1.0, op1=ALU.mult, eng="pool")
        return wb, wk

    def sb_unit(self, l, hp, R, wcur):
        wb, wk = wcur
        qT, qk = R["qT"].next()
        kT, kk = R["kT"].next()
        v, vk = R["v"].next()

        def evac_qk(ci, g, ps, pk, n):
            if ci == 0:
                self.copy(qT[0:64, 0, g * 512:(g + 1) * 512], ps[0:64, :], reads=[pk, qk], writes=[(qk, g)], eng="dve")
                self.copy(qT[64:128, 1, g * 512:(g + 1) * 512], ps[64:128, :], reads=[pk, qk], writes=[(qk, g)], eng="dve")
            else:
                self.copy(kT[:, g * 512:(g + 1) * 512], ps[:], reads=[pk], writes=[(kk, g)], eng="dve")

        self.linear_T(wb, wk, 256, KC, lambda kc, g: self.hT[:, kc, g * 512:(g + 1) * 512], self.hT_keys, evac_qk)
        for tq in range(4):
            ps, pk = self.psum.next()
            for t in range(4):
                tt_ = tq * 4 + t
                for kc in range(KC):
                    self.mm(ps[:, t * 128:(t + 1) * 128], self.hT[:, kc, tt_ * 128:(tt_ + 1) * 128],
                            wb[:, kc, 256:384], kc == 0, kc == KC - 1, reads=[wk, ("hT", kc, tq)], writes=[pk])
            self.copy(v[:, tq * 4:(tq + 1) * 4, :], ps[:].rearrange("p (t c) -> p t c", t=4), reads=[pk],
                      writes=[(vk, tq)], eng="dve")
        osb, ok = R["osb"].next()
        mstrict = self.consts[:, C_MSTRICT, :]
        tinc = self.cbf[:, C_TINC, :]
        tlow = self.cbf[:, C_MSTRICT, :]
        one_col = self.consts[:, C_ONES, 0:1]
        pst = self.pst
        pa_ring = R["pa"]
        for g in range(4):
            racc = [(pst[4], ("psum", 4)), (pst[5], ("psum", 5))]
            po = [(pst[6], ("psacc", 0)), (pst[7], ("psacc", 1))]
            tiles = []
            for kb in range(4 * g + 3, -1, -1):
                for e in range(2):
                    t0 = max(kb * 128, g * 512)
                    tiles.append(dict(e=e, kb=kb, pb=64 * e, t0=t0, N=(g + 1) * 512 - t0, c0=t0 - g * 512,
                                      diag=kb * 128 >= g * 512, first=(kb == 4 * g + 3), last=(kb == 0)))

            def stage0(T):
                pa, pak = pa_ring.next()
                pb, N, t0, kb = T["pb"], T["N"], T["t0"], T["kb"]
                self.mm(pa[:, :N], kT[:, kb * 128:(kb + 1) * 128], qT[:, T["e"], t0:t0 + N], True, not T["diag"],
                        reads=[(kk, kb // 4), (qk, g)], writes=[pak])
                if T["diag"]:
                    self.mm(pa[:, 0:128], self.ident_b(), self.cbf[:, C_NEGSB, :], False, True, reads=["cbf"], writes=[pak])
                T["pa"], T["pak"] = pa, pak

            def stage1(T):
                N, c0 = T["N"], T["c0"]
                ee, ek = R["e"].next()
                self.act(ee[:, :N], T["pa"][:, :N], AF.Exp, reads=[T["pak"]], writes=[ek], scale=0.125)
                spb, spk = R["spb"].next()
                self.act(spb[:, :N], ee[:, :N], AF.Ln, reads=[ek], writes=[spk], bias=1.0, scale=1.0)
                ra, rak = racc[T["e"]]
                self.mm(ra[:, c0:512], tinc, spb[:, :N], T["first"], False, reads=[spk, "cbf"], writes=[rak],
                        skip_group_check=True)
                T.update(ee=ee, ek=ek, spb=spb, spk=spk)

            def stage2(T):
                N, c0, pb, kb = T["N"], T["c0"], T["pb"], T["kb"]
                ra, rak = racc[T["e"]]
                xa, xk = R["xa"].next()
                self.act(xa[:, :N], ra[:, c0:512], AF.Exp, reads=[rak], writes=[xk], scale=-1.0)
                if not T["last"]:
                    self.mm(ra[:, c0:512], tlow, T["spb"][:, :N], False, True, reads=[T["spk"], "cbf"], writes=[rak],
                            skip_group_check=True)
                w, wwk = R["w"].next()
                self.tt(w[:, :N], T["ee"][:, :N], xa[:, :N], ALU.mult, reads=[T["ek"], xk], writes=[wwk])
                pp, ppk = po[T["e"]]
                self.mm(pp[:, c0:512], v[:, kb, :], w[:, :N], T["first"], T["last"],
                        reads=[(vk, kb // 4), wwk], writes=[ppk], skip_group_check=True)

            n = len(tiles)
            for i in range(min(2, n)):
                stage0(tiles[i])
            for i in range(n + 2):
                if i - 2 >= 0:
                    stage2(tiles[i - 2])
                if i < n:
                    stage1(tiles[i])
                if i + 2 < n:
                    stage0(tiles[i + 2])
            for e in range(2):
                pp, ppk = po[e]
                pb = 64 * e
                self.copy(osb[pb:pb + 64, g * 512:(g + 1) * 512], pp[pb:pb + 64, :], reads=[ppk],
                          writes=[(ok, e, g)], eng="dve")
        self.dma(self.o_scr[1][hp * 128:(hp + 1) * 128, :], osb[:],
                 reads=[(ok, e, g) for e in range(2) for g in range(4)], writes=[("o_scr", 1, hp)])


    def bload(self, name, dram_row_ap, n):
        t = self.A.alloc(name, [128, n], F32)
        k = self.key(name)
        self.dma(t[:], dram_row_ap.partition_broadcast(128), reads=[], writes=[k])
        return t, k

    def conv_silu(self, l, wb, wk, wcol, conv_w_d, conv_b_d, ch, raw, rawk, acc, acck, cw, dst, dstk, func=AF.Silu):
        c, ck = cw.next()
        self.dma(c[:, 0:4], conv_w_d[l][:, ch:ch + 128].rearrange("k c -> c k"), reads=[], writes=[(ck, 0)], slow=True)
        if conv_b_d is not None:
            self.dma(c[:, 4:5], conv_b_d[l][ch:ch + 128].rearrange("(c o) -> c o", o=1), reads=[], writes=[(ck, 1)],
                     slow=True)
        else:
            self.memset(c[:, 4:5], 0.0, writes=[(ck, 1)])

        def evac(ci, g, ps, pk, n):
            self.copy(raw[:, 3 + g * 512:3 + (g + 1) * 512], ps[:], reads=[pk], writes=[(rawk, g)],
                      eng=("act" if g % 2 else "dve"))

        for g in range(4):
            ps, pk = self.psum.next()
            for kc in range(KC):
                self.mm(ps[:], wb[:, kc, wcol:wcol + 128], self.hT[:, kc, g * 512:(g + 1) * 512], kc == 0, kc == KC - 1,
                        reads=[wk] + self.hT_keys(g), writes=[pk])
            evac(0, g, ps, pk, 128)
        rk = [(rawk, g) for g in range(4)] + [(rawk, "pad")]
        self.ts(acc[:], raw[:, 0:S_LEN], c[:, 0:1], ALU.mult, reads=rk + [(ck, 0), (ck, 1)], writes=[acck],
                s2=c[:, 4:5], op1=ALU.add)
        for k in range(1, 4):
            self.stt(acc[:], raw[:, k:k + S_LEN], c[:, k:k + 1], acc[:], ALU.mult, ALU.add,
                     reads=rk + [(ck, 0), acck], writes=[acck])
        self.act(dst, acc[:], func, reads=[acck], writes=[dstk])

    def conv_P(self, l, wb, wk, wcol, conv_w_d, conv_b_d, ch, cw):
        c, ck = cw.next()
        self.dma(c[:, 0:4], conv_w_d[l][:, ch:ch + 128].rearrange("k c -> c k"), reads=[], writes=[(ck, 0)], slow=True)
        if conv_b_d is not None:
            self.dma(c[:, 4:5], conv_b_d[l][ch:ch + 128].rearrange("(c o) -> c o", o=1), reads=[], writes=[(ck, 1)],
                     slow=True)
        else:
            self.memset(c[:, 4:5], 0.0, writes=[(ck, 1)])
        banks = []
        for g in range(4):
            ps, pk = self.psum.next()
            for kc in range(KC):
                self.mm(ps[:], wb[:, kc, wcol:wcol + 128], self.hT[:, kc, g * 512:(g + 1) * 512], kc == 0, kc == KC - 1,
                        reads=[wk] + self.hT_keys(g), writes=[pk])
            banks.append((ps, pk))
        return dict(c=c, ck=ck, banks=banks)

    def conv_E(self, H, raw, rawk):
        for g, (ps, pk) in enumerate(H["banks"]):
            self.copy(raw[:, 3 + g * 512:3 + (g + 1) * 512], ps[:], reads=[pk], writes=[(rawk, g)],
                      eng=("act" if g % 2 else "dve"))

    def conv_C(self, H, raw, rawk, acc, acck, dst, dstk, func=AF.Silu):
        c, ck = H["c"], H["ck"]
        rk = [(rawk, g) for g in range(4)] + [(rawk, "pad")]
        self.ts(acc[:], raw[:, 0:S_LEN], c[:, 0:1], ALU.mult, reads=rk + [(ck, 0), (ck, 1)], writes=[acck],
                s2=c[:, 4:5], op1=ALU.add)
        for k in range(1, 4):
            self.stt(acc[:], raw[:, k:k + S_LEN], c[:, k:k + 1], acc[:], ALU.mult, ALU.add,
                     reads=rk + [(ck, 0), acck], writes=[acck])
        self.act(dst, acc[:], func, reads=[acck], writes=[dstk])

    def to_tok(self, src, srck, dst_fn, dstk):
        for tq in range(4):
            ps, pk = self.psum.next()
            pb = ps[:].bitcast(BF16)
            for t in range(4):
                tt_ = tq * 4 + t
                self.tr(pb[:, t * 128:(t + 1) * 128], src[:, tt_ * 128:(tt_ + 1) * 128], self.ident_b(),
                        reads=(list(srck) if isinstance(srck, list) else [srck]) + ["cbf"], writes=[pk])
            self.copy(dst_fn(tq), pb[:, 0:512].rearrange("p (t c) -> p t c", t=4), reads=[pk], writes=[(dstk, tq)],
                      eng=("act" if tq % 2 else "dve"))

    def ssm_phase(self, l):
        A, S = self.A, self.S
        scale = self.nw[:, 2 * l, :]
        cst = self.consts
        wdst = A.alloc("wdtst", [128, KC, 16], F32)
        wdt = A.alloc("wdt", [128, KC, 16], BF16)
        self.dma(wdst[:], self.w_in_d[l][:, O_SSMDT:O_SSMDT + 16].rearrange("(k p) n -> p k n", p=128), [], ["wdtst"])
        for kc in range(KC):
            self.act(wdt[:, kc, :], wdst[:, kc, :], AF.Copy, reads=["wdtst", "nw"], writes=["wdt"], scale=scale[:, kc:kc + 1])
        dtb, dtbk = self.bload("dtb", self.ssm_dt_bias_d[l], 16)
        alog, alogk = self.bload("alog", self.ssm_a_log_d[l], 16)
        dbc, dbck = self.bload("dbc", self.ssm_d_d[l], 16)
        dt = A.alloc("dt", [128, 16, 16], F32)
        av = A.alloc("av", [128, 16, 16], F32)
        acum = A.alloc("acum", [128, 16, 16], F32)
        eacum = A.alloc("eacum", [128, 16, 16], F32)
        dtds = A.alloc("dtds", [128, 16, 16], F32)
        eatot = A.alloc("eatot", [128, 32, 16], F32)
        tmp = A.alloc("ptmp", [128, 16, 16], F32)
        ps, pk = self.psum.next()
        for t in range(16):
            for kc in range(KC):
                self.mm(ps[:, t * 16:(t + 1) * 16], self.hT[:, kc, t * 128:(t + 1) * 128], wdt[:, kc, :], kc == 0,
                        kc == KC - 1, reads=["wdt", ("hT", kc, t // 4)], writes=[pk])
        self.tt(dt[:], ps[:, 0:256].rearrange("p (t h) -> p t h", t=16),
                dtb[:].unsqueeze(1).to_broadcast([128, 16, 16]), ALU.add, reads=[pk, dtbk], writes=["dt"])
        one_col = cst[:, C_ONES, 0:1]
        self.act(tmp[:], dt[:], AF.Exp, reads=["dt"], writes=["ptmp"])
        self.act(dt[:], tmp[:], AF.Ln, reads=["ptmp", "consts"], writes=["dt"], bias=1.0, scale=1.0)
        self.act(alog[:], alog[:], AF.Exp, reads=[alogk], writes=[alogk])
        self.S.op("dve", lambda e: e.scalar_tensor_tensor(out=av[:], in0=dt[:], scalar=-1.0,
                                                          in1=alog[:].unsqueeze(1).to_broadcast([128, 16, 16]),
                                                          op0=ALU.mult, op1=ALU.mult),
                  reads=["dt", alogk], writes=["av"])
        ps, pk = self.psum.next()
        for t in range(16):
            self.mm(ps[:, t * 16:(t + 1) * 16], cst[:, C_TRI2, :], av[:, t, :], True, True, reads=["av", "consts"], writes=[pk])
        self.copy(acum[:], ps[:, 0:256].rearrange("p (t h) -> p t h", t=16), reads=[pk], writes=["acum"])
        self.act(eacum[:], acum[:], AF.Exp, reads=["acum"], writes=["eacum"])
        ps, pk = self.psum.next()
        for t in range(16):
            self.mm(ps[:, t * 16:(t + 1) * 16], cst[:, C_BD, :], av[:, t, :], True, True, reads=["av", "consts"], writes=[pk])
        self.tt(tmp[:], ps[:, 0:256].rearrange("p (t h) -> p t h", t=16), acum[:], ALU.subtract, reads=[pk, "acum"],
                writes=["ptmp"])
        self.act(tmp[:], tmp[:], AF.Exp, reads=["ptmp"], writes=["ptmp"])
        self.tt(dtds[:], tmp[:], dt[:], ALU.mult, reads=["ptmp", "dt"], writes=["dtds"])
        ps, pk = self.psum.next()
        for t in range(16):
            for c in range(2):
                j = 2 * t + c
                self.mm(ps[:, j * 16:(j + 1) * 16], cst[:, C_IND0 + c, :], av[:, t, :], True, True,
                        reads=["av", "consts"], writes=[pk])
        self.act(eatot[:].rearrange("p j h -> p (j h)"), ps[:], AF.Exp, reads=[pk], writes=["eatot"])

        mG = A.mark()
        for g in range(4):
            A.reset(mG)
            S.barrier()
            wbf = A.alloc("swz", [128, KC, 256], BF16)
            BT = A.alloc("sBT", [128, S_LEN], BF16)
            CT = A.alloc("sCT", [128, S_LEN], BF16)
            x_tok = A.alloc("sxtok", [128, 16, 256], BF16)
            B_tok = A.alloc("sBtok", [128, 16, 128], BF16)
            nwb, nwbk = self.bload("snwb", self.ssm_norm_w_d[l][256 * g:256 * (g + 1)], 256)
            mA = A.mark()
            wst = self.ring("swst", 2, [128, KC, 128], F32)
            wbx = A.alloc("swbx", [128, KC, 512], BF16)
            raw = A.alloc("sraw", [128, S_LEN + 4], F32)
            acc = A.alloc("sacc", [128, S_LEN], F32)
            xc = self.ring("sxc", 2, [128, S_LEN], BF16)
            cw = self.ring("scw", 2, [128, 8], F32)
            rawk = self.key("sraw")
            self.memset(raw[:, 0:3], 0.0, writes=[(rawk, "pad")])
            cols = [O_SSMZ + 256 * g, O_SSMZ + 256 * g + 128, O_SSMXBC + 256 * g, O_SSMXBC + 256 * g + 128,
                    O_SSMXBC + 1024 + 128 * g, O_SSMXBC + 1536 + 128 * g]
            wk = self.key("swbf")
            for j, c0 in enumerate(cols):
                st, sk = wst.next()
                self.dma(st[:], self.w_in_d[l][:, c0:c0 + 128].rearrange("(k p) n -> p k n", p=128), [], [sk])
                for kc in range(KC):
                    wdst_ = wbf[:, kc, j * 128:(j + 1) * 128] if j < 2 else wbx[:, kc, (j - 2) * 128:(j - 1) * 128]
                    self.act(wdst_, st[:, kc, :], AF.Copy, reads=[sk, "nw"], writes=[(wk, j)],
                             scale=scale[:, kc:kc + 1])
            chs = [256 * g, 256 * g + 128, 1024 + 128 * g, 1536 + 128 * g]
            acck = self.key("sacc")
            xtk = self.key("sxtok")
            btk = self.key("sBtok")
            def s_P(jj):
                return self.conv_P(l, wbx, (wk, jj + 2), jj * 128, self.ssm_conv_w_d, self.ssm_conv_b_d, chs[jj], cw)

            dsts = {}

            def s_C(jj, H_):
                if jj < 2:
                    dst, dk = xc.next()
                elif jj == 2:
                    dst, dk = BT, "sBT"
                else:
                    dst, dk = CT, "sCT"
                self.conv_C(H_, raw, rawk, acc, acck, dst[:], dk)
                dsts[jj] = (dst, dk)

            def s_N(jj):
                dst, dk = dsts[jj]
                if jj < 2:
                    self.to_tok(dst, dk, lambda tq, jj=jj: x_tok[:, tq * 4:(tq + 1) * 4, jj * 128:(jj + 1) * 128], (xtk, jj))
                elif jj == 2:
                    self.to_tok(dst, dk, lambda tq: B_tok[:, tq * 4:(tq + 1) * 4, :], btk)

            Hs = {0: s_P(0)}
            self.conv_E(Hs[0], raw, rawk)
            for jj in range(4):
                if jj + 1 < 4:
                    Hs[jj + 1] = s_P(jj + 1)
                s_C(jj, Hs[jj])
                if jj + 1 < 4:
                    self.conv_E(Hs[jj + 1], raw, rawk)
                s_N(jj)
            S.barrier()
            A.reset(mA)
            xdt = A.alloc("sxdt", [128, 16, 256], BF16)
            xdtd = A.alloc("sxdtd", [128, 16, 256], BF16)
            oT = A.alloc("soT", [128, 2, S_LEN], BF16)
            state = A.alloc("sstate", [128, 256], F32)
            state_bf = A.alloc("sstatebf", [128, 256], BF16)
            abc = self.ring("sabc", 2, [128, 128], F32)
            dm = self.ring("sdm", 2, [128, 4, 128], F32)
            MT = self.ring("sMT", 2, [128, 4, 128], BF16)
            xDr = self.ring("sxD", 2, [128, 256], BF16)
            szr = self.ring("ssz", 2, [128, 256], F32)
            ydr = self.ring("syd", 2, [128, 256], F32)
            sttr = self.ring("sstt", 4, [128, 256], F32)
            t1r = self.ring("st1", 1, [128, 256], F32)
            yr = self.ring("sy", 2, [128, 256], F32)
            jr = self.ring("sjunk", 1, [128, 256], F32)
            ssr = self.ring("sssq", 4, [128, 2], F32)
            obr = self.ring("sob", 2, [128, 256], BF16)
            xtks = [(xtk, jj, tq) for jj in range(2) for tq in range(4)]
            x4 = x_tok[:].rearrange("p t (h c) -> p t h c", h=4)
            hs = slice(4 * g, 4 * g + 4)
            self.tt(xdt[:].rearrange("p t (h c) -> p t h c", h=4), x4,
                    dt[:, :, hs].unsqueeze(3).to_broadcast([128, 16, 4, 64]), ALU.mult, reads=xtks + ["dt"], writes=["sxdt"])
            self.tt(xdtd[:].rearrange("p t (h c) -> p t h c", h=4), x4,
                    dtds[:, :, hs].unsqueeze(3).to_broadcast([128, 16, 4, 64]), ALU.mult, reads=xtks + ["dtds"],
                    writes=["sxdtd"])
            stk = self.key("sstate")
            sbk = self.key("sstatebf")
            self.memset(state[:], 0.0, writes=[stk])
            self.memset(state_bf[:], 0.0, writes=[sbk])
            ones_f = cst[:, C_ONES, :]

            def stageP(t, I):
                tsl = slice(t * 128, (t + 1) * 128)
                xD, xDk = xDr.next()
                self.tt(xD[:].rearrange("p (h c) -> p h c", h=4), x_tok[:, t, :].rearrange("p (h c) -> p h c", h=4),
                        dbc[:, hs].unsqueeze(2).to_broadcast([128, 4, 64]), ALU.mult, reads=xtks + [dbck],
                        writes=[xDk], eng="pool")
                yield
                psS, psSk = self.psum.next()
                self.mm(psS[:, 0:128], BT[:, tsl], CT[:, tsl], True, True, reads=["sBT", "sCT"], writes=[psSk])
                yield
                psD, psDk = self.psum.next()
                for hh in range(4):
                    ab, abk = abc.next()
                    self.ts(ab[:], ones_f, av[:, t, 4 * g + hh:4 * g + hh + 1], ALU.mult, reads=["av", "consts"], writes=[abk])
                    yield
                    self.mm(psD[:, hh * 128:(hh + 1) * 128], ab[:], cst[:, C_TRI2, :], True, False, reads=[abk, "consts"],
                            writes=[psDk])
                    yield
                    self.mm(psD[:, hh * 128:(hh + 1) * 128], cst[:, C_TRI2NEG, :], ab[:], False, True,
                            reads=[abk, "consts"], writes=[psDk])
                    yield
                d_, dk_ = dm.next()
                self.tt(d_[:], psD[:].rearrange("p (h c) -> p h c", h=4),
                        cst[:, C_NEGINCL, :].unsqueeze(1).to_broadcast([128, 4, 128]), ALU.add, reads=[psDk, "consts"],
                        writes=[dk_])
                yield
                self.act(d_[:], d_[:], AF.Exp, reads=[dk_], writes=[dk_])
                yield
                M_, Mk_ = MT.next()
                self.tt(M_[:], d_[:], psS[:, 0:128].unsqueeze(1).to_broadcast([128, 4, 128]), ALU.mult,
                        reads=[dk_, psSk], writes=[Mk_])
                yield
                psY, psYk = self.psum.next()
                for hh in range(4):
                    cs = slice(hh * 64, (hh + 1) * 64)
                    self.mm(psY[:, cs], M_[:, hh, :], xdt[:, t, cs], True, False, reads=[Mk_, "sxdt"], writes=[psYk])
                    yield
                    self.mm(psY[:, cs], self.ident_b(), xD[:, cs], False, True, reads=["cbf", xDk], writes=[psYk])
                    yield
                yd, ydk = ydr.next()
                self.copy(yd[:], psY[:, 0:256], reads=[psYk], writes=[ydk], eng="act")
                yield
                psZ, psZk = self.psum.next()
                for kc in range(KC):
                    self.mm(psZ[:, 0:256], self.hT[:, kc, tsl], wbf[:, kc, 0:256], kc == 0, kc == KC - 1,
                            reads=[(wk, 0), (wk, 1), ("hT", kc, t // 4)], writes=[psZk])
                    yield
                sz, szk = szr.next()
                self.act(sz[:], psZ[:, 0:256], AF.Silu, reads=[psZk], writes=[szk])
                yield
                I["stt"] = []
                for c in range(2):
                    psT, psTk = self.psum.next()
                    self.mm(psT[:, 0:256], B_tok[64 * c:64 * c + 64, t, :], xdtd[64 * c:64 * c + 64, t, :], True, True,
                            reads=[(btk, t // 4), "sxdtd"], writes=[psTk])
                    yield
                    sx, sxk = sttr.next()
                    self.copy(sx[:], psT[:, 0:256], reads=[psTk], writes=[sxk], eng=("act" if c else "dve"))
                    yield
                    I["stt"].append((sx, sxk))
                I.update(yd=yd, ydk=ydk, sz=sz, szk=szk)
                yield

            def stageQ(t, I):
                tsl = slice(t * 128, (t + 1) * 128)
                psO, psOk = self.psum_acc.next()
                for c in range(2):
                    j = 2 * t + c
                    sx, sxk = I["stt"][c]
                    self.mm(psO[64 * c:64 * c + 64, 0:256], CT[:, t * 128 + 64 * c:t * 128 + 64 * c + 64], state_bf[:], True, True,
                            reads=["sCT", sbk], writes=[psOk])
                    yield
                    self.tt(state[:].rearrange("p (h c) -> p h c", h=4), state[:].rearrange("p (h c) -> p h c", h=4),
                            eatot[:, j, hs].unsqueeze(2).to_broadcast([128, 4, 64]), ALU.mult, reads=[stk, "eatot"],
                            writes=[stk])
                    yield
                    self.tt(state[:], state[:], sx[:], ALU.add, reads=[stk, sxk], writes=[stk])
                    yield
                    self.copy(state_bf[:], state[:], reads=[stk], writes=[sbk], eng="act")
                    yield
                t1, t1k = t1r.next()
                self.tt(t1[:].rearrange("p (h c) -> p h c", h=4), psO[:, 0:256].rearrange("p (h c) -> p h c", h=4),
                        eacum[:, t, hs].unsqueeze(2).to_broadcast([128, 4, 64]), ALU.mult, reads=[psOk, "eacum"],
                        writes=[t1k])
                yield
                y, yk = yr.next()
                self.tt(y[:], t1[:], I["yd"][:], ALU.add, reads=[t1k, I["ydk"]], writes=[yk])
                yield
                self.tt(y[:], y[:], I["sz"][:], ALU.mult, reads=[yk, I["szk"]], writes=[yk])
                yield
                jk_, jkk = jr.next()
                ss, ssk = ssr.next()
                self.S.op("act", lambda e, jk_=jk_, y=y, ss=ss: e.activation(out=jk_[:], in_=y[:], func=AF.Square,
                                                                          accum_out=ss[:, 0:1]),
                          reads=[yk], writes=[jkk, ssk])
                yield
                self.act(ss[:, 1:2], ss[:, 0:1], AF.Ln, reads=[ssk, "consts"], writes=[ssk], scale=1.0 / 256,
                         bias=EPS)
                yield
                self.act(ss[:, 1:2], ss[:, 1:2], AF.Exp, reads=[ssk], writes=[ssk], scale=-0.5)
                yield
                ob, obk = obr.next()
                self.stt(ob[:], y[:], ss[:, 1:2], nwb[:], ALU.mult, ALU.mult, reads=[yk, ssk, nwbk], writes=[obk])
                yield
                psX, psXk = self.psum.next()
                pxb = psX[:].bitcast(BF16)
                for ch in range(2):
                    self.tr(pxb[:, ch * 128:(ch + 1) * 128], ob[:, ch * 128:(ch + 1) * 128], self.ident_b(),
                            reads=[obk, "cbf"], writes=[psXk])
                    yield
                self.copy(oT[:, :, tsl], pxb[:, 0:256].rearrange("p (c t) -> p c t", c=2), reads=[psXk],
                          writes=[("soT", t)], eng="act")
                yield

            def run2(ga, gb):
                gens = [g_ for g_ in (ga, gb) if g_ is not None]
                while gens:
                    for g_ in list(gens):
                        try:
                            next(g_)
                        except StopIteration:
                            gens.remove(g_)

            infos = {0: {}}
            run2(stageP(0, infos[0]), None)
            for t in range(16):
                gp = None
                if t + 1 < 16:
                    infos[t + 1] = {}
                    gp = stageP(t + 1, infos[t + 1])
                run2(gp, stageQ(t, infos.pop(t)))
            for ch in range(2):
                oc = 2 * g + ch
                self.dma(self.o_scr[2][oc * 128:(oc + 1) * 128, :], oT[:, ch, :], reads=[("soT", t) for t in range(16)],
                         writes=[("o_scr", 2, oc)])


    def dn_phase(self, l):
        A, S = self.A, self.S
        scale = self.nw[:, 2 * l, :]
        cst = self.consts
        one_col = cst[:, C_ONES, 0:1]
        ones_f = cst[:, C_ONES, :]
        wast = A.alloc("dwast", [128, KC, 16], F32)
        wa = A.alloc("dwa", [128, KC, 16], BF16)
        self.dma(wast[:], self.w_in_d[l][:, O_DNA:O_DNA + 16].rearrange("(k p) n -> p k n", p=128), [], ["dwast"])
        for kc in range(KC):
            self.act(wa[:, kc, :], wast[:, kc, :], AF.Copy, reads=["dwast", "nw"], writes=["dwa"], scale=scale[:, kc:kc + 1])
        dtb, dtbk = self.bload("ddtb", self.dn_dt_bias_d[l], 8)
        alog, alogk = self.bload("dalog", self.dn_a_log_d[l], 8)
        gv = A.alloc("dg", [128, 16, 8], F32)
        beta = A.alloc("dbeta", [128, 16, 8], F32)
        gc = A.alloc("dgc", [128, 16, 8], F32)
        egc = self.dn_egc
        bg = A.alloc("dbg", [128, 16, 8], F32)
        kdec = A.alloc("dkdec", [128, 16, 8], F32)
        eglast = self.dn_eglast
        tmp = A.alloc("dtmp", [128, 16, 8], F32)
        ps, pk = self.psum.next()
        for t in range(16):
            for kc in range(KC):
                self.mm(ps[:, t * 16:(t + 1) * 16], self.hT[:, kc, t * 128:(t + 1) * 128], wa[:, kc, :], kc == 0,
                        kc == KC - 1, reads=["dwa", ("hT", kc, t // 4)], writes=[pk])
        pv = ps[:, 0:256].rearrange("p (t h) -> p t h", t=16)
        self.act(beta[:], pv[:, :, 8:16], AF.Exp, reads=[pk], writes=["dbeta"], scale=-1.0)
        self.ts(beta[:], beta[:], 1.0, ALU.add, reads=["dbeta"], writes=["dbeta"])
        self.S.op("dve", lambda e: e.reciprocal(out=beta[:], in_=beta[:]), reads=["dbeta"], writes=["dbeta"])
        self.tt(gv[:], pv[:, :, 0:8], dtb[:].unsqueeze(1).to_broadcast([128, 16, 8]), ALU.add, reads=[pk, dtbk], writes=["dg"])
        self.act(tmp[:], gv[:], AF.Exp, reads=["dg"], writes=["dtmp"])
        self.act(gv[:], tmp[:], AF.Ln, reads=["dtmp", "consts"], writes=["dg"], bias=1.0, scale=1.0)
        self.act(alog[:], alog[:], AF.Exp, reads=[alogk], writes=[alogk])
        self.S.op("dve", lambda e: e.scalar_tensor_tensor(out=gv[:], in0=gv[:], scalar=-1.0,
                                                          in1=alog[:].unsqueeze(1).to_broadcast([128, 16, 8]),
                                                          op0=ALU.mult, op1=ALU.mult),
                  reads=["dg", alogk], writes=["dg"])
        ps, pk = self.psum.next()
        for t in range(16):
            self.mm(ps[:, t * 8:(t + 1) * 8], cst[:, C_TRI2, :], gv[:, t, :], True, True, reads=["dg", "consts"], writes=[pk])
        self.copy(gc[:], ps[:, 0:128].rearrange("p (t h) -> p t h", t=16), reads=[pk], writes=["dgc"])
        self.act(egc[:], gc[:], AF.Exp, reads=["dgc"], writes=["degc"])
        self.tt(bg[:], egc[:], beta[:], ALU.mult, reads=["degc", "dbeta"], writes=["dbg"])
        ps, pk = self.psum.next()
        for t in range(16):
            self.mm(ps[:, t * 8:(t + 1) * 8], cst[:, C_BD, :], gv[:, t, :], True, True, reads=["dg", "consts"], writes=[pk])
        self.tt(kdec[:], ps[:, 0:128].rearrange("p (t h) -> p t h", t=16), gc[:], ALU.subtract, reads=[pk, "dgc"],
                writes=["dkdec"])
        self.act(kdec[:], kdec[:], AF.Exp, reads=["dkdec"], writes=["dkdec"])
        ps, pk = self.psum.next()
        for t in range(16):
            for c in range(2):
                j = 2 * t + c
                self.mm(ps[:, j * 8:(j + 1) * 8], cst[:, C_IND0 + c, :], gv[:, t, :], True, True,
                        reads=["dg", "consts"], writes=[pk])
        self.act(eglast[:].rearrange("p j h -> p (j h)"), ps[:, 0:256], AF.Exp, reads=[pk], writes=["deglast"])

        mG = A.mark()
        for h in range(8):
            A.reset(mG)
            S.barrier()
            wg = A.alloc("dwg", [128, KC, 128], BF16)
            qTn = A.alloc("dqTn", [128, S_LEN], BF16)
            kTn = A.alloc("dkTn", [128, S_LEN], BF16)
            k_tok = A.alloc("dktok", [128, 16, 128], BF16)
            v_tok = A.alloc("dvtok", [128, 16, 128], BF16)
            mA = A.mark()
            wst = self.ring("dwst", 2, [128, KC, 128], F32)
            wbx = A.alloc("dwbx", [128, KC, 384], BF16)
            raws = [A.alloc("draw", [128, S_LEN + 4], F32) for _ in range(2)]
            accs = [A.alloc("dacc", [128, S_LEN], F32) for _ in range(2)]
            vT = A.alloc("dvT", [128, S_LEN], BF16)
            cw = self.ring("dcw", 2, [128, 8], F32)
            sqr = self.ring("dsq", 2, [128, 512], BF16)
            rsr = self.ring("drs", 2, [128, 512], F32)
            rawks = [self.key("draw"), self.key("draw")]
            for r_, rk__ in zip(raws, rawks):
                self.memset(r_[:, 0:3], 0.0, writes=[(rk__, "pad")])
            cols = [O_DNQKV + 128 * h, O_DNQKV + 1024 + 128 * h, O_DNQKV + 2048 + 128 * h, O_DNGATE + 128 * h]
            wk = self.key("dwb")
            for j, c0 in enumerate(cols):
                st, sk = wst.next()
                self.dma(st[:], self.w_in_d[l][:, c0:c0 + 128].rearrange("(k p) n -> p k n", p=128), [], [sk])
                for kc in range(KC):
                    wd_ = wbx[:, kc, j * 128:(j + 1) * 128] if j < 3 else wg[:, kc, :]
                    self.act(wd_, st[:, kc, :], AF.Copy, reads=[sk, "nw"], writes=[(wk, j)], scale=scale[:, kc:kc + 1])
            accks = [self.key("dacc"), self.key("dacc")]
            ktk = self.key("dktok")
            vtk = self.key("dvtok")
            def dn_P(j):
                return self.conv_P(l, wbx, (wk, j), j * 128, self.dn_conv_w_d, None, j * 1024 + 128 * h, cw)

            def dn_C(j, H):
                raw, rawk, acc, acck = raws[j % 2], rawks[j % 2], accs[j % 2], accks[j % 2]
                if j == 2:
                    self.conv_C(H, raw, rawk, acc, acck, vT[:], "dvT")
                else:
                    self.conv_C(H, raw, rawk, acc, acck, acc[:], acck)

            def dn_N(j):
                acc, acck = accs[j % 2], accks[j % 2]
                if j == 2:
                    self.to_tok(vT, "dvT", lambda tq: v_tok[:, tq * 4:(tq + 1) * 4, :], vtk)
                    return
                dst, dk = (qTn, "dqTn") if j == 0 else (kTn, "dkTn")
                for g in range(4):
                    sl = slice(g * 512, (g + 1) * 512)
                    q_, qk_ = sqr.next()
                    self.act(q_[:], acc[:, sl], AF.Square, reads=[acck], writes=[qk_])
                    ps, pk = self.psum.next()
                    self.mm(ps[:], self.ones_b(), q_[:], True, True, reads=[qk_, "cbf"], writes=[pk])
                    r_, rk_ = rsr.next()
                    self.act(r_[:], ps[:], AF.Ln, reads=[pk], writes=[rk_], scale=1.0, bias=EPS)
                    self.act(r_[:], r_[:], AF.Exp, reads=[rk_], writes=[rk_], scale=-0.5)
                    if j == 0:
                        self.stt(dst[:, sl], acc[:, sl], 128.0 ** -0.5, r_[:], ALU.mult, ALU.mult, reads=[acck, rk_],
                                 writes=[(dk, g)])
                    else:
                        self.tt(dst[:, sl], acc[:, sl], r_[:], ALU.mult, reads=[acck, rk_], writes=[(dk, g)])
                if j == 1:
                    self.to_tok(kTn, [("dkTn", g) for g in range(4)], lambda tq: k_tok[:, tq * 4:(tq + 1) * 4, :], ktk)

            H = {0: dn_P(0)}
            self.conv_E(H[0], raws[0], rawks[0])
            for j in range(3):
                if j + 1 < 3:
                    H[j + 1] = dn_P(j + 1)
                dn_C(j, H[j])
                if j + 1 < 3:
                    self.conv_E(H[j + 1], raws[(j + 1) % 2], rawks[(j + 1) % 2])
                dn_N(j)
            S.barrier()
            A.reset(mA)
            attnT = A.alloc("dattnT", [128, 16, 128], BF16)
            P = A.alloc("dP", [128, 16, 128], BF16)
            PL = A.alloc("dPL", [128, 16, 128], BF16)
            R32 = A.alloc("dR32", [128, 16, 128], F32)
            Rb = A.alloc("dRb", [128, 16, 128], BF16)
            vb = v_tok
            kbg = k_tok
            kd = A.alloc("dkd", [128, 16, 128], BF16)
            u = A.alloc("du", [128, 16, 128], BF16)
            wT = A.alloc("dwT", [128, 16, 128], BF16)
            abrs = [self.ring("dab", 2, [128, 128], F32) for _ in range(4)]
            dcs = [A.alloc("ddec", [128, 4, 128], F32) for _ in range(4)]
            bms = [A.alloc("dbm", [128, 4, 128], F32) for _ in range(4)]
            ktks = [(ktk, tq) for tq in range(4)]
            vtks = [(vtk, tq) for tq in range(4)]
            bc3 = lambda t_: t_[:, :, h:h + 1].to_broadcast([128, 16, 128])
            self.tt(vb[:], v_tok[:], bc3(beta), ALU.mult, reads=vtks + ["dbeta"], writes=vtks + ["dvb"])
            self.tt(kd[:], k_tok[:], bc3(kdec), ALU.mult, reads=ktks + ["dkdec"], writes=["dkd"])
            self.tt(kbg[:], k_tok[:], bc3(bg), ALU.mult, reads=ktks + ["dbg", "dkd"], writes=ktks + ["dkbg"])
            kTk = [("dkTn", g) for g in range(4)]
            identf4 = cst[:, C_IDENT, :].unsqueeze(1).to_broadcast([128, 4, 128])
            def chain(q):
                pr = Ring([self.pst[2 * q], self.pst[2 * q + 1]], ("dnps", q))
                tq = slice(4 * q, 4 * q + 4)
                abr = abrs[q]
                dc, dck = dcs[q], ("ddec", q)
                bm, bmk = bms[q], ("dbm", q)
                v4 = lambda ps_: ps_[:].rearrange("p (a b) -> p a b", a=4)
                tiles = [(4 * q + i4, slice((4 * q + i4) * 128, (4 * q + i4 + 1) * 128), slice(i4 * 128, (i4 + 1) * 128))
                         for i4 in range(4)]
                psD, psDk = pr.next()
                for t, tsl, cs in tiles:
                    ab, abk = abr.next()
                    self.ts(ab[:], ones_f, gv[:, t, h:h + 1], ALU.mult, reads=["dg", "consts"], writes=[abk])
                    yield
                    self.mm(psD[:, cs], ab[:], cst[:, C_TRI2, :], True, False, reads=[abk, "consts"], writes=[psDk])
                    self.mm(psD[:, cs], cst[:, C_TRI2NEG, :], ab[:], False, True, reads=[abk, "consts"], writes=[psDk])
                    yield
                self.tt(dc[:], v4(psD), cst[:, C_NEGINCL, :].unsqueeze(1).to_broadcast([128, 4, 128]), ALU.add,
                        reads=[psDk, "consts"], writes=[dck])
                yield
                self.act(dc[:], dc[:], AF.Exp, reads=[dck], writes=[dck])
                yield
                psQ, psQk = pr.next()
                for t, tsl, cs in tiles:
                    self.mm(psQ[:, cs], kTn[:, tsl], qTn[:, tsl], True, True, reads=[("dkTn", q), ("dqTn", q)], writes=[psQk])
                    yield
                self.tt(attnT[:, tq, :], v4(psQ), dc[:], ALU.mult, reads=[psQk, dck], writes=[("dattnT", q)])
                yield
                psB, psBk = pr.next()
                for t, tsl, cs in tiles:
                    db, dbk = abr.next()
                    self.ts(db[:], cst[:, C_IDENT, :], beta[:, t, h:h + 1], ALU.mult, reads=["dbeta", "consts"], writes=[dbk])
                    yield
                    self.mm(psB[:, cs], ones_f, db[:], True, True, reads=[dbk, "consts"], writes=[psBk])
                    yield
                self.tt(bm[:], v4(psB), cst[:, C_MSTRICT2, :].unsqueeze(1).to_broadcast([128, 4, 128]), ALU.mult,
                        reads=[psBk, "consts"], writes=[bmk])
                yield
                psK, psKk = pr.next()
                for t, tsl, cs in tiles:
                    self.mm(psK[:, cs], kTn[:, tsl], kTn[:, tsl], True, True, reads=[("dkTn", q)], writes=[psKk])
                    yield
                self.tt(dc[:], v4(psK), dc[:], ALU.mult, reads=[psKk, dck], writes=[dck])
                yield
                self.tt(P[:, tq, :], dc[:], bm[:], ALU.mult, reads=[dck, bmk], writes=[("dP", q)])
                yield
                psT, psTk = pr.next()
                ptb = psT[:].bitcast(BF16)
                for t, tsl, cs in tiles:
                    self.tr(ptb[:, cs], P[:, t, :], self.ident_b(), reads=[("dP", q), "cbf"], writes=[psTk])
                yield
                self.copy(PL[:, tq, :], ptb[:, 0:512].rearrange("p (a b) -> p a b", a=4), reads=[psTk],
                          writes=[("dPL", q)], eng="act")
                yield
                self.tt(R32[:, tq, :], identf4, P[:, tq, :], ALU.subtract, reads=[("dP", q), "consts"], writes=[("dR32", q)])
                yield
                self.copy(Rb[:, tq, :], R32[:, tq, :], reads=[("dR32", q)], writes=[("dRb", q)], eng="act")
                yield

            S.barrier()
            for q0 in (0, 2):
                gens = [chain(q0), chain(q0 + 1)]
                while gens:
                    for g_ in list(gens):
                        try:
                            next(g_)
                        except StopIteration:
                            gens.remove(g_)
            S.barrier()
            for lev in range(5):
                last = lev == 4
                for q in range(4):
                    if not last:
                        psP, psPk = self.psum.next()
                    psL, psLk = self.psum.next()
                    for i4 in range(4):
                        t = 4 * q + i4
                        cs = slice(i4 * 128, (i4 + 1) * 128)
                        if not last:
                            self.mm(psP[:, cs], PL[:, t, :], P[:, t, :], True, True, reads=[("dPL", q), ("dP", q)],
                                    writes=[psPk])
                        self.mm(psL[:, cs], P[:, t, :], PL[:, t, :], True, True, reads=[("dPL", q), ("dP", q)],
                                writes=[psLk])
                    if not last:
                        self.copy(P[:, 4 * q:4 * q + 4, :], psP[:].rearrange("p (a b) -> p a b", a=4), reads=[psPk],
                                  writes=[("dP", q)], eng="dve")
                    self.copy(PL[:, 4 * q:4 * q + 4, :], psL[:].rearrange("p (a b) -> p a b", a=4), reads=[psLk],
                              writes=[("dPL", q)], eng="act")
                for q in range(4):
                    psR, psRk = self.psum.next()
                    for i4 in range(4):
                        t = 4 * q + i4
                        cs = slice(i4 * 128, (i4 + 1) * 128)
                        self.mm(psR[:, cs], PL[:, t, :], Rb[:, t, :], True, True, reads=[("dPL", q), ("dRb", q)],
                                writes=[psRk])
                    self.tt(R32[:, 4 * q:4 * q + 4, :], R32[:, 4 * q:4 * q + 4, :],
                            psR[:].rearrange("p (a b) -> p a b", a=4), ALU.add, reads=[psRk, ("dR32", q)],
                            writes=[("dR32", q)])
                    self.copy(Rb[:, 4 * q:4 * q + 4, :], R32[:, 4 * q:4 * q + 4, :], reads=[("dR32", q)],
                              writes=[("dRb", q)], eng="act")
            for q in range(4):
                psU, psUk = self.psum.next()
                psW, psWk = self.psum.next()
                for i4 in range(4):
                    t = 4 * q + i4
                    cs = slice(i4 * 128, (i4 + 1) * 128)
                    self.mm(psU[:, cs], Rb[:, t, :], vb[:, t, :], True, True, reads=[("dRb", q), "dvb"], writes=[psUk])
                    self.mm(psW[:, cs], kbg[:, t, :], Rb[:, t, :], True, True, reads=[("dRb", q), "dkbg"], writes=[psWk])
                self.copy(u[:, 4 * q:4 * q + 4, :], psU[:].rearrange("p (a b) -> p a b", a=4), reads=[psUk],
                          writes=[("du", q)], eng="dve")
                self.copy(wT[:, 4 * q:4 * q + 4, :], psW[:].rearrange("p (a b) -> p a b", a=4), reads=[psWk],
                          writes=[("dwT", q)], eng="act")
            sgt = P
            for q in range(4):
                psG, psGk = self.psum.next()
                for i4 in range(4):
                    t = 4 * q + i4
                    for kc in range(KC):
                        self.mm(psG[:, i4 * 128:(i4 + 1) * 128], self.hT[:, kc, t * 128:(t + 1) * 128], wg[:, kc, :],
                                kc == 0, kc == KC - 1, reads=[(wk, 3), ("hT", kc, q)], writes=[psGk])
                self.act(sgt[:, 4 * q:4 * q + 4, :], psG[:].rearrange("p (a b) -> p a b", a=4), AF.Silu, reads=[psGk],
                         writes=[("dP", q)])
            fl = lambda t_: t_[:].rearrange("p a b -> p (a b)")
            for nm, src, keys in (("u", fl(u), [("du", q) for q in range(4)]),
                                  ("wT", fl(wT), [("dwT", q) for q in range(4)]),
                                  ("q", qTn[:], [("dqTn", g) for g in range(4)]),
                                  ("attnT", fl(attnT), [("dattnT", q) for q in range(4)]),
                                  ("kd", fl(kd), ["dkd"]),
                                  ("sg", fl(sgt), [("dP", q) for q in range(4)])):
                self.dma(self.dn_scr[nm][h], src, reads=keys, writes=[("dn_scr", nm, h)])

    def final_out(self):
        A, S = self.A, self.S
        m = A.mark()
        sq = self.ring("fsq", 3, [128, 512], BF16)
        rs = self.ring("frs", 2, [128, 512], F32)
        hf = self.ring("hf", 3, [128, 512], F32)
        ost = self.ring("ost", 2, [128, 4, D], F32)
        for g in range(4):
            ps, pk = self.psum.next()
            sl = slice(g * 512, (g + 1) * 512)
            for kc in range(KC):
                q, qk = sq.next()
                self.act(q[:], self.xT[:, kc, sl], AF.Square, reads=[("xT", kc, g)], writes=[qk])
                self.mm(ps[:], self.ones_b(), q[:], kc == 0, kc == KC - 1, reads=[qk, "cbf"], writes=[pk])
            r, rk = rs.next()
            self.act(r[:], ps[:], AF.Ln, reads=[pk], writes=[rk], scale=1.0 / D, bias=EPS)
            self.act(r[:], r[:], AF.Exp, reads=[rk], writes=[rk], scale=-0.5)
            o, ok = ost.next()
            for kc in range(KC):
                h, hk = hf.next()
                self.stt(h[:], self.xT[:, kc, sl], self.nw[:, 2 * DEPTH, kc:kc + 1], r[:], ALU.mult, ALU.mult,
                         reads=[("xT", kc, g), rk, "nw"], writes=[hk])
                ps2, pk2 = self.psum.next()
                for t in range(4):
                    self.tr(ps2[:, t * 128:(t + 1) * 128], h[:, t * 128:(t + 1) * 128], self.ident_f(),
                            reads=[hk, "consts"], writes=[pk2])
                self.copy(o[:, :, kc * 128:(kc + 1) * 128], ps2[:].rearrange("p (t c) -> p t c", t=4),
                          reads=[pk2], writes=[ok], eng=("act" if kc % 2 else "dve"))
            self.dma(self.out_d[g * 512:(g + 1) * 512, :].rearrange("(t p) d -> p t d", p=128), o[:],
                     reads=[ok], writes=[("out", g)])
        S.barrier()
        A.reset(m)


C_IDENT, C_ONES, C_EPS, C_MSTRICT, C_TINC = 0, 1, 2, 3, 4
C_TRI2, C_TRI2NEG, C_BD, C_IND0, C_IND1, C_NEGINCL, C_NEGSB, C_MINCL2, C_MSTRICT2, C_MSTRICT2T = 5, 6, 7, 8, 9, 10, 11, 12, 13, 14
NCONST = 15
NEG = -30000.0


def make_consts():
    c = np.zeros((128, NCONST, 128), np.float32)
    c[:, C_IDENT, :] = np.eye(128, dtype=np.float32)
    c[:, C_ONES, :] = 1.0
    c[:, C_EPS, :] = EPS
    ii = np.arange(128)
    c[:, C_MSTRICT, :] = (ii[:, None] < ii[None, :]).astype(np.float32)
    c[:, C_TINC, :] = (ii[:, None] >= ii[None, :]).astype(np.float32)
    same = (ii[:, None] // 64) == (ii[None, :] // 64)
    le = ii[:, None] <= ii[None, :]
    lt = ii[:, None] < ii[None, :]
    c[:, C_TRI2, :] = (same & le).astype(np.float32)
    c[:, C_TRI2NEG, :] = -c[:, C_TRI2, :]
    c[:, C_BD, :] = same.astype(np.float32)
    c[:, C_IND0, :] = (ii[:, None] < 64).astype(np.float32) * np.ones((1, 128), np.float32)
    c[:, C_IND1, :] = (ii[:, None] >= 64).astype(np.float32) * np.ones((1, 128), np.float32)
    c[:, C_NEGINCL, :] = np.where(same & le, 0.0, NEG)
    c[:, C_NEGSB, :] = np.where(lt, 0.0, 8.0 * NEG)
    c[:, C_MINCL2, :] = (same & le).astype(np.float32)
    c[:, C_MSTRICT2, :] = (same & lt).astype(np.float32)
    c[:, C_MSTRICT2T, :] = (same & lt).T.astype(np.float32)
    return c.reshape(128, NCONST * 128)


_CACHE = {}
_RUN_KW = {}


def get_nc(**kw):
    key = tuple(sorted((k, str(v)) for k, v in kw.items()))
    if key not in _CACHE:
        b = Builder(**kw)
        b.build()
        _CACHE[key] = b
    return _CACHE[key]


def run(inputs, **kw):
    b = get_nc(**kw)
    consts = make_consts()
    common = {
        "consts": consts,
        "w_in": np.ascontiguousarray(inputs["w_in"], dtype=np.float32),
        "norm_mix": np.ascontiguousarray(inputs["norm_mix"], dtype=np.float32),
        "norm_mlp": np.ascontiguousarray(inputs["norm_mlp"], dtype=np.float32),
        "norm_final": np.ascontiguousarray(inputs["norm_final"], dtype=np.float32).reshape(1, D),
        "w_branch": np.ascontiguousarray(inputs["w_branch"], dtype=np.float32),
        "w_out": np.ascontiguousarray(inputs["w_out"], dtype=np.float32),
        "w_up": np.ascontiguousarray(inputs["w_up"], dtype=np.float32),
        "w_down": np.ascontiguousarray(inputs["w_down"], dtype=np.float32),
    }
    for nm in ("ssm_conv_w", "ssm_conv_b", "ssm_a_log", "ssm_dt_bias", "ssm_d", "ssm_norm_w", "dn_conv_w", "dn_a_log",
               "dn_dt_bias", "dn_norm_w"):
        common[nm] = np.ascontiguousarray(inputs[nm], dtype=np.float32)
    x = np.asarray(inputs["x"], dtype=np.float32)
    in_maps = []
    for c in range(NCORES):
        m = dict(common)
        m["x"] = np.ascontiguousarray(x[c])
        in_maps.append(m)
    res = run_bass_kernel_spmd(b.nc, in_maps, core_ids=list(range(NCORES)), **_RUN_KW)
    return res


def kernel(**inputs):
    res = run(inputs)
    out = np.stack([np.asarray(r["out"], dtype=np.float32) for r in res.results], axis=0)
    return out
```
